# Optimizing a Trainium2 kernel written in Bass

```python
import math
import jax, jax.numpy as jnp
from jax import lax
import numpy as np

D_MODEL = 1024
BATCH = 8
SEQ = 4096
DEPTH = 1

HEAD_DIM = 64
BLK = 128
RMS_EPS = 1e-6
NSA_HEADS = 8
NSA_KV_GROUPS = 2
NSA_CMP_LEN = 32
NSA_CMP_STRIDE = 16
NSA_CMP_HIDDEN = 256
NSA_SLC_LEN = 64
NSA_N_SEL = 16
NSA_WINDOW = 512
DIL_CONFIGS = ((128, 1), (512, 4), (2048, 16))
DIL_HEADS_PER_GROUP = 4
DIL_HEADS = DIL_HEADS_PER_GROUP * len(DIL_CONFIGS)
D_FF = ((8 * D_MODEL // 3 + 127) // 128) * 128
Q_NSA = NSA_HEADS * HEAD_DIM
KV_NSA = NSA_KV_GROUPS * HEAD_DIM
GATE_NSA = NSA_HEADS * 3
QKV_DIL = DIL_HEADS * HEAD_DIM
IN_DIM = Q_NSA + 6 * KV_NSA + GATE_NSA + 3 * QKV_DIL + 2 * D_MODEL
DIL_OUT = DIL_HEADS_PER_GROUP * HEAD_DIM

kernel_name = "hybrid_nsa_dilated_macaron_block"


def rms_norm(x, g):
    xf = x.astype(jnp.float32)
    y = xf * lax.rsqrt(jnp.mean(xf * xf, axis=-1, keepdims=True) + RMS_EPS)
    return (y * g.astype(jnp.float32)).astype(x.dtype)


def swiglu(x, w_gate, w_up, w_down):
    return (jax.nn.silu(x @ w_gate) * (x @ w_up)) @ w_down


def alibi_slopes(n):
    return jnp.asarray(2.0 ** (-8.0 * np.arange(1, n + 1) / n), dtype=jnp.float32)


def masked_softmax(s, mask):
    s = jnp.where(mask, s, -jnp.inf)
    m = jnp.max(s, axis=-1, keepdims=True)
    m = jnp.where(jnp.isfinite(m), m, 0.0)
    p = jnp.exp(s - m)
    den = jnp.maximum(jnp.sum(p, axis=-1, keepdims=True), 1e-30)
    return p / den, (m + jnp.log(den))[..., 0]


def banded_attention(q, k, v, n_prev, max_back, step, slopes):
    N, L, H, hd = q.shape
    G = k.shape[2]
    rep = H // G
    nb = L // BLK
    W = (n_prev + 1) * BLK
    qb = q.reshape(N, nb, BLK, G, rep, hd)

    def windows(t):
        tp = jnp.pad(t, ((0, 0), (n_prev * BLK, 0), (0, 0), (0, 0))).reshape(N, nb + n_prev, BLK, G, hd)
        return jnp.concatenate([tp[:, j:j + nb] for j in range(n_prev + 1)], axis=2)

    kw, vw = windows(k), windows(v)
    s = jnp.einsum('nbqgrd,nbkgd->nbgrqk', qb, kw, preferred_element_type=jnp.float32) * hd ** -0.5
    diff = (jnp.arange(BLK)[:, None] + n_prev * BLK) - jnp.arange(W)[None, :]
    key_abs = jnp.arange(nb)[:, None] * BLK - n_prev * BLK + jnp.arange(W)[None, :]
    s = s - slopes.reshape(G, rep)[:, :, None, None] * (diff * step).astype(jnp.float32)
    mask = ((diff >= 0) & (diff <= max_back))[None, None, None, None] & (key_abs >= 0)[None, :, None, None, None, :]
    p, lse = masked_softmax(s, mask)
    o = jnp.einsum('nbgrqk,nbkgd->nbqgrd', p.astype(v.dtype), vw).reshape(N, L, H, hd)
    lse = lse.transpose(0, 1, 4, 2, 3).reshape(N, L, H)
    return o, lse


def nsa_attention(q, k_cmp, v_cmp, k_slc, v_slc, k_win, v_win, gates,
                  pe_k, w_ck1, w_ck2, pe_v, w_cv1, w_cv2):
    B, S, H, hd = q.shape
    G = NSA_KV_GROUPS
    rep = H // G
    scale = hd ** -0.5
    slopes = alibi_slopes(H)
    sl_gr = slopes.reshape(G, rep)
    pos = jnp.arange(S)
    qg = q.reshape(B, S, G, rep, hd)

    n_cmp = (S - NSA_CMP_LEN) // NSA_CMP_STRIDE + 1
    blk_idx = jnp.arange(n_cmp)[:, None] * NSA_CMP_STRIDE + jnp.arange(NSA_CMP_LEN)[None, :]

    def compress(t, pe, w1, w2):
        tb = t[:, blk_idx] + pe[None, None, :, None, :]
        tb = tb.transpose(0, 1, 3, 2, 4).reshape(B, n_cmp, G, NSA_CMP_LEN * hd)
        return jax.nn.gelu(tb @ w1) @ w2

    kc = compress(k_cmp, pe_k, w_ck1, w_ck2)
    vc = compress(v_cmp, pe_v, w_cv1, w_cv2)
    cmp_end = jnp.arange(n_cmp) * NSA_CMP_STRIDE + NSA_CMP_LEN - 1
    dist_c = pos[:, None] - cmp_end[None, :]
    s_c = jnp.einsum('bsgrd,bcgd->bgrsc', qg, kc, preferred_element_type=jnp.float32) * scale
    s_c = s_c - sl_gr[:, :, None, None] * dist_c.astype(jnp.float32)
    p_cmp, _ = masked_softmax(s_c, dist_c >= 0)
    o_cmp = jnp.einsum('bgrsc,bcgd->bsgrd', p_cmp.astype(vc.dtype), vc).reshape(B, S, H, hd)

    n_slc = S // NSA_SLC_LEN
    c_start = jnp.arange(n_cmp) * NSA_CMP_STRIDE
    s_start = jnp.arange(n_slc) * NSA_SLC_LEN
    overlap = jnp.clip(jnp.minimum(c_start[:, None] + NSA_CMP_LEN, s_start[None, :] + NSA_SLC_LEN)
                       - jnp.maximum(c_start[:, None], s_start[None, :]), 0, None).astype(jnp.float32) / NSA_CMP_LEN
    imp = jnp.einsum('bgrsc,cj->bsgj', p_cmp, overlap)
    jb = jnp.arange(n_slc)[None, :]
    own = (pos // NSA_SLC_LEN)[:, None]
    forced = (jb == 0) | (jb == own) | (jb == own - 1)
    valid = jb * NSA_SLC_LEN <= pos[:, None]
    imp = jnp.where(forced[None, :, None, :], jnp.inf, jnp.where(valid[None, :, None, :], imp, -jnp.inf))
    n_sel = min(NSA_N_SEL, n_slc)
    _, sel = lax.top_k(imp, n_sel)

    kb = k_slc.reshape(B, n_slc, NSA_SLC_LEN, G, hd).transpose(0, 3, 1, 2, 4)
    vb = v_slc.reshape(B, n_slc, NSA_SLC_LEN, G, hd).transpose(0, 3, 1, 2, 4)
    nqb = S // BLK
    bi = jnp.arange(B)[:, None, None, None]
    gi = jnp.arange(G)[None, :, None, None]
    n_keys = n_sel * NSA_SLC_LEN

    def slc_block(args):
        qblk, selblk, qpos = args
        idx = selblk.transpose(0, 2, 1, 3)
        kg = kb[bi, gi, idx].reshape(B, G, BLK, n_keys, hd)
        vg = vb[bi, gi, idx].reshape(B, G, BLK, n_keys, hd)
        kpos = (idx[..., None] * NSA_SLC_LEN + jnp.arange(NSA_SLC_LEN)).reshape(B, G, BLK, n_keys)
        dist = qpos[None, None, :, None] - kpos
        s = jnp.einsum('bqgrd,bgqkd->bgrqk', qblk, kg, preferred_element_type=jnp.float32) * scale
        s = s - sl_gr[None, :, :, None, None] * dist[:, :, None].astype(jnp.float32)
        p, _ = masked_softmax(s, (dist >= 0)[:, :, None])
        return jnp.einsum('bgrqk,bgqkd->bqgrd', p.astype(vg.dtype), vg)

    xs = (qg.reshape(B, nqb, BLK, G, rep, hd).swapaxes(0, 1),
          sel.reshape(B, nqb, BLK, G, n_sel).swapaxes(0, 1),
          pos.reshape(nqb, BLK))
    o_slc = lax.map(slc_block, xs).swapaxes(0, 1).reshape(B, S, H, hd)

    o_win, _ = banded_attention(q, k_win, v_win, NSA_WINDOW // BLK, NSA_WINDOW - 1, 1, slopes)

    g = jax.nn.sigmoid(gates)
    return g[..., 0:1] * o_cmp + g[..., 1:2] * o_slc + g[..., 2:3] * o_win


def dilated_attention(q, k, v):
    B, S, _, hd = q.shape
    hpg = DIL_HEADS_PER_GROUP
    slopes = alibi_slopes(DIL_HEADS).reshape(len(DIL_CONFIGS), hpg)
    outs, lses = [], []
    for gidx, (window, dil) in enumerate(DIL_CONFIGS):
        Lp = -(-S // (dil * BLK)) * dil * BLK
        Ls = Lp // dil

        def strided(t):
            t = jnp.pad(t[:, :, gidx * hpg:(gidx + 1) * hpg], ((0, 0), (0, Lp - S), (0, 0), (0, 0)))
            return t.reshape(B, Ls, dil, hpg, hd).transpose(0, 2, 1, 3, 4).reshape(B * dil, Ls, hpg, hd)

        o, lse = banded_attention(strided(q), strided(k), strided(v), 1, window // dil, dil, slopes[gidx])
        outs.append(o.reshape(B, dil, Ls, hpg, hd).transpose(0, 2, 1, 3, 4).reshape(B, Lp, hpg, hd)[:, :S])
        lses.append(lse.reshape(B, dil, Ls, hpg).transpose(0, 2, 1, 3).reshape(B, Lp, hpg)[:, :S])
    w = jax.nn.softmax(jnp.stack(lses), axis=0)
    o = jnp.sum(w[..., None] * jnp.stack(outs).astype(jnp.float32), axis=0)
    return o.astype(q.dtype)


def token_mixer(h, w_in, pe_k, w_ck1, w_ck2, pe_v, w_cv1, w_cv2, w_nsa_o, w_dil_o, w_mix_out):
    B, S, _ = h.shape
    proj = h @ w_in
    sizes = [Q_NSA, 6 * KV_NSA, GATE_NSA, 3 * QKV_DIL, 2 * D_MODEL]
    q_n, kv_n, g_n, qkv_d, merge = jnp.split(proj, np.cumsum(sizes)[:-1].tolist(), axis=-1)
    q_n = q_n.reshape(B, S, NSA_HEADS, HEAD_DIM)
    kv_n = kv_n.reshape(B, S, 6, NSA_KV_GROUPS, HEAD_DIM)
    g_n = g_n.reshape(B, S, NSA_HEADS, 3)
    o_nsa = nsa_attention(q_n, kv_n[:, :, 0], kv_n[:, :, 1], kv_n[:, :, 2], kv_n[:, :, 3],
                          kv_n[:, :, 4], kv_n[:, :, 5], g_n, pe_k, w_ck1, w_ck2, pe_v, w_cv1, w_cv2)
    qkv_d = qkv_d.reshape(B, S, 3, DIL_HEADS, HEAD_DIM)
    o_dil = dilated_attention(qkv_d[:, :, 0], qkv_d[:, :, 1], qkv_d[:, :, 2])
    y_nsa = o_nsa.reshape(B, S, Q_NSA) @ w_nsa_o
    y_dil = o_dil.reshape(B, S, DIL_OUT) @ w_dil_o
    g_a, g_b = jnp.split(merge, 2, axis=-1)
    y = jax.nn.sigmoid(g_a) * y_nsa + jax.nn.sigmoid(g_b) * y_dil
    return y @ w_mix_out


def setup_inputs(seed: int = 0) -> dict:
    key = jax.random.key(seed)
    ks = iter(jax.random.split(key, 32))

    def w(shape, fan_in):
        return jax.random.normal(next(ks), (DEPTH,) + shape, jnp.float32) * fan_in ** -0.5

    def gain():
        return 1.0 + 0.01 * jax.random.normal(next(ks), (DEPTH, D_MODEL), jnp.float32)

    inp = {}
    inp["x"] = jax.random.normal(next(ks), (BATCH, SEQ, D_MODEL), jnp.float32)
    inp["ffn1_pre"] = gain()
    inp["ffn1_post"] = gain()
    inp["ffn1_w_gate"] = w((D_MODEL, D_FF), D_MODEL)
    inp["ffn1_w_up"] = w((D_MODEL, D_FF), D_MODEL)
    inp["ffn1_w_down"] = w((D_FF, D_MODEL), D_FF)
    inp["mix_pre"] = gain()
    inp["mix_post"] = gain()
    inp["w_in"] = w((D_MODEL, IN_DIM), D_MODEL)
    inp["nsa_pe_k"] = 0.1 * jax.random.normal(next(ks), (DEPTH, NSA_CMP_LEN, HEAD_DIM), jnp.float32)
    inp["nsa_w_ck1"] = w((NSA_CMP_LEN * HEAD_DIM, NSA_CMP_HIDDEN), NSA_CMP_LEN * HEAD_DIM)
    inp["nsa_w_ck2"] = w((NSA_CMP_HIDDEN, HEAD_DIM), NSA_CMP_HIDDEN)
    inp["nsa_pe_v"] = 0.1 * jax.random.normal(next(ks), (DEPTH, NSA_CMP_LEN, HEAD_DIM), jnp.float32)
    inp["nsa_w_cv1"] = w((NSA_CMP_LEN * HEAD_DIM, NSA_CMP_HIDDEN), NSA_CMP_LEN * HEAD_DIM)
    inp["nsa_w_cv2"] = w((NSA_CMP_HIDDEN, HEAD_DIM), NSA_CMP_HIDDEN)
    inp["w_nsa_o"] = w((Q_NSA, D_MODEL), Q_NSA)
    inp["w_dil_o"] = w((DIL_OUT, D_MODEL), DIL_OUT)
    inp["w_mix_out"] = w((D_MODEL, D_MODEL), D_MODEL)
    inp["ffn2_pre"] = gain()
    inp["ffn2_post"] = gain()
    inp["ffn2_w_gate"] = w((D_MODEL, D_FF), D_MODEL)
    inp["ffn2_w_up"] = w((D_MODEL, D_FF), D_MODEL)
    inp["ffn2_w_down"] = w((D_FF, D_MODEL), D_FF)
    return inp


def reference(x, ffn1_pre, ffn1_post, ffn1_w_gate, ffn1_w_up, ffn1_w_down,
              mix_pre, mix_post, w_in, nsa_pe_k, nsa_w_ck1, nsa_w_ck2,
              nsa_pe_v, nsa_w_cv1, nsa_w_cv2, w_nsa_o, w_dil_o, w_mix_out,
              ffn2_pre, ffn2_post, ffn2_w_gate, ffn2_w_up, ffn2_w_down):
    for l in range(DEPTH):
        h = rms_norm(x, ffn1_pre[l])
        x = x + 0.5 * rms_norm(swiglu(h, ffn1_w_gate[l], ffn1_w_up[l], ffn1_w_down[l]), ffn1_post[l])
        h = rms_norm(x, mix_pre[l])
        y = token_mixer(h, w_in[l], nsa_pe_k[l], nsa_w_ck1[l], nsa_w_ck2[l], nsa_pe_v[l],
                        nsa_w_cv1[l], nsa_w_cv2[l], w_nsa_o[l], w_dil_o[l], w_mix_out[l])
        x = x + rms_norm(y, mix_post[l])
        h = rms_norm(x, ffn2_pre[l])
        x = x + 0.5 * rms_norm(swiglu(h, ffn2_w_gate[l], ffn2_w_up[l], ffn2_w_down[l]), ffn2_post[l])
    return x
```

```python
import numpy as np
import concourse.bass as bass
import concourse.mybir as mybir
from concourse.alu_op_type import AluOpType as ALU
from concourse.bass_utils import run_bass_kernel_spmd

F32 = mybir.dt.float32
BF16 = mybir.dt.bfloat16
AF = mybir.ActivationFunctionType

S = 4096
D = 1024
DFF = 2816
NFF = 22
EPS = 1e-6
SBUF_BYTES = 212000
LIMIT = 9
STAGE = 3


class Res:
    __slots__ = ("name", "lw", "rdc", "rdd")

    def __init__(self, name=""):
        self.name = name
        self.lw = None
        self.rdc = {}
        self.rdd = []


ENGS = ("pe", "act", "dve", "pool", "sp")
SAME_ENG_SKIP = ("pe", "sp")


class Sched:
    def __init__(self, nring=28):
        self.ops = {e: [] for e in ENGS}
        self.ndma = 0
        self.nring = nring
        self.last_dma = {}

    def _deps(self, reads, writes):
        dc = {}
        dd = set()

        def add(ev):
            if ev is None:
                return
            if ev[0] == "c":
                if dc.get(ev[1], -1) < ev[2]:
                    dc[ev[1]] = ev[2]
            else:
                dd.add(ev)

        for r in reads:
            add(r.lw)
        for w in writes:
            add(w.lw)
            for e, s in w.rdc.items():
                add(("c", e, s))
            for ev in w.rdd:
                add(ev)
        return dc, dd

    def _mark(self, ev, reads, writes):
        for r in reads:
            if ev[0] == "c":
                if r.rdc.get(ev[1], -1) < ev[2]:
                    r.rdc[ev[1]] = ev[2]
            else:
                r.rdd.append(ev)
        for w in writes:
            w.lw = ev
            w.rdc = {}
            w.rdd = []

    def op(self, eng, fn, reads=(), writes=()):
        dc, dd = self._deps(reads, writes)
        seq = len(self.ops[eng])
        ev = ("c", eng, seq)
        self.ops[eng].append(dict(fn=fn, dc=dc, dd=dd, dma=None))
        self._mark(ev, reads, writes)
        return ev

    def dma(self, fn, reads=(), writes=(), q="sp"):
        dc, dd = self._deps(reads, writes)
        k = self.ndma
        self.ndma += 1
        slot = k % self.nring
        val = 16 * (k // self.nring + 1)
        if k >= self.nring:
            dd.add(("d", slot, val - 16))
        ev = ("d", slot, val)
        self.ops[q].append(dict(fn=fn, dc=dc, dd=dd, dma=slot))
        self._mark(ev, reads, writes)
        self.last_dma[slot] = val
        return ev

    def barrier(self):
        last = {}
        for e in ENGS:
            last[e] = -1
            for i in range(len(self.ops[e]) - 1, -1, -1):
                if self.ops[e][i]["fn"] is not None and self.ops[e][i]["dma"] is None:
                    last[e] = i
                    break
        dds = set(("d", s, v) for s, v in self.last_dma.items())
        for e in ENGS:
            dc = {o: last[o] for o in ENGS if o != e and o != 'sp' and last[o] >= 0}
            self.ops[e].append(dict(fn=None, dc=dc, dd=set(dds), dma=None))

    def emit(self, nc):
        needed = {e: set() for e in ENGS}
        for e in ENGS:
            for o in self.ops[e]:
                for oe, s in o["dc"].items():
                    if oe == e and e in SAME_ENG_SKIP:
                        continue
                    needed[oe].add(s)
        rank = {e: {s: i + 1 for i, s in enumerate(sorted(needed[e]))} for e in ENGS}
        ops = self.ops
        nring = self.nring
        import contextlib

        with contextlib.ExitStack() as st:
            sems = {e: st.enter_context(nc.semaphore("s_" + e)) for e in ENGS}
            ring = [st.enter_context(nc.semaphore("r%d" % i)) for i in range(nring)]
            block = st.enter_context(nc.Block())

            def run(ename):
                def body(eng):
                    known = {}
                    for seq, o in enumerate(ops[ename]):
                        waits = {}
                        for oe, s in o["dc"].items():
                            if oe == ename and ename in SAME_ENG_SKIP:
                                continue
                            key = ("c", oe)
                            v = rank[oe][s]
                            if known.get(key, 0) >= v:
                                continue
                            if waits.get(key, 0) < v:
                                waits[key] = v
                        for ev in o["dd"]:
                            key = ("d", ev[1])
                            v = ev[2]
                            if known.get(key, 0) >= v:
                                continue
                            if waits.get(key, 0) < v:
                                waits[key] = v
                        for key, v in waits.items():
                            sem = sems[key[1]] if key[0] == "c" else ring[key[1]]
                            eng.wait_ge(sem, v)
                            known[key] = v
                        if o["fn"] is None:
                            continue
                        ins = o["fn"](eng)
                        if o["dma"] is not None:
                            ins.then_inc(ring[o["dma"]], 16)
                        elif seq in rank[ename]:
                            ins.then_inc(sems[ename], 1)

                return body

            block.tensor(run("pe"))
            block.scalar(run("act"))
            block.vector(run("dve"))
            block.gpsimd(run("pool"))
            block.sync(run("sp"))


class Mem:
    def __init__(self, big, ps):
        self.big = big
        self.ps = ps
        self.off = 0

    def reset(self, off=0):
        self.off = off

    def f32(self, n, p0=0, p1=128):
        o = (self.off + 31) // 32 * 32
        self.off = o + 4 * n
        assert self.off <= SBUF_BYTES, self.off
        return self.big[p0:p1, o // 4:o // 4 + n]

    def bf(self, n, p0=0, p1=128):
        o = (self.off + 31) // 32 * 32
        self.off = o + 2 * n
        assert self.off <= SBUF_BYTES, self.off
        return self.big[p0:p1, o // 4:o // 4 + (n + 1) // 2].bitcast(BF16)

    def bank(self, b, n=512, o=0):
        return self.ps[:, b * 512 + o:b * 512 + o + n]

    def bank_bf(self, b):
        return self.ps[:, b * 512:(b + 1) * 512].bitcast(BF16)


def r3(ap, a):
    return ap.rearrange("p (a b) -> p a b", a=a)


def ffn_phase(sc, M, C, x_src, x_dst, wg, wu, wd, gpre, gpost):
    M.reset(C["const_end"])
    ident = C["ident"]
    Wg = r3(M.bf(8 * DFF), 8)
    Wu = r3(M.bf(8 * DFF), 8)
    Wd = r3(M.bf(NFF * D), NFF)
    stage = [M.f32(1024) for _ in range(3)]
    gp = M.f32(1024)
    gq = M.f32(1024)
    xp0 = M.f32(1024)
    xp = [xp0, xp0]
    xr = [M.f32(1024) for _ in range(2)]
    hb0 = M.bf(1024)
    hb = [hb0, hb0]
    hT = r3(M.bf(8 * 512), 8)
    AT = r3(M.bf(NFF * 512), NFF)
    sg0 = M.f32(512)
    sg = [sg0, sg0]
    yt = M.f32(1024)
    small = M.f32(32)

    r_stage = [Res("stage%d" % i) for i in range(3)]
    r_Wg = [[Res() for _ in range(4)] for _ in range(8)]
    r_Wu = [[Res() for _ in range(4)] for _ in range(8)]
    r_Wd = [Res() for _ in range(NFF)]
    r_gp, r_gq = Res(), Res()
    r_xp0 = Res()
    r_xp = [r_xp0, r_xp0]
    r_xr = [Res(), Res()]
    r_hb0 = Res()
    r_hb = [r_hb0, r_hb0]
    r_hT, r_AT = Res(), [Res() for _ in range(NFF)]
    r_sg0 = Res()
    r_sg = [r_sg0, r_sg0]
    r_yt = Res()
    r_small = [Res() for _ in range(8)]
    r_T = Res()
    r_G = [Res(), Res()]
    r_U = [Res(), Res()]
    r_Y = Res()
    r_xdst = C["r_dram"][id(x_dst)] if id(x_dst) in C["r_dram"] else Res()
    r_xsrc = C["r_dram"].get(id(x_src), Res())
    C["r_dram"][id(x_dst)] = r_xdst

    sc.dma(lambda e: e.dma_start(out=gp, in_=gpre[0, :].partition_broadcast(128)), writes=[r_gp])
    sc.dma(lambda e: e.dma_start(out=gq, in_=gpost[0, :].partition_broadcast(128)), writes=[r_gq])
    sc.op("act", lambda e: e.mul(gq, gq, 0.5), reads=[r_gq], writes=[r_gq])

    cnt = [0]
    CB = [(0, 768), (768, 768), (1536, 768), (2304, 512)]

    def load_piece(dst_ap, src_ap, n, rdst):
        i = cnt[0] % 3
        eng = ("pool", "act", "dve")[cnt[0] % 3]
        cnt[0] += 1
        st = stage[i][:, 0:n]
        sc.dma(lambda e: e.dma_start(out=st, in_=src_ap), writes=[r_stage[i]])
        if eng == "act":
            sc.op("act", lambda e: e.copy(out=dst_ap, in_=st), reads=[r_stage[i]], writes=[rdst])
        else:
            sc.op(eng, lambda e: e.tensor_copy(out=dst_ap, in_=st), reads=[r_stage[i]], writes=[rdst])

    def load_weights():
        for cb, (c0, w) in enumerate(CB):
            for kc in range(8):
                for (W, wsrc, rW) in ((Wg, wg, r_Wg), (Wu, wu, r_Wu)):
                    load_piece(W[:, kc, c0:c0 + w], wsrc[kc * 128:(kc + 1) * 128, c0:c0 + w], w, rW[kc][cb])
        for f in range(NFF):
            load_piece(Wd[:, f, :], wd[f * 128:(f + 1) * 128, :], 1024, r_Wd[f])

    inv_d = 1.0 / D

    def rstd_from(ss_ap, out_ap, rs):
        sc.op("dve", lambda e: e.tensor_scalar(out=out_ap, in0=ss_ap, scalar1=inv_d, scalar2=EPS,
                                               op0=ALU.mult, op1=ALU.add), reads=[rs], writes=[rs])
        sc.op("act", lambda e: e.sqrt(out_ap, out_ap), reads=[rs], writes=[rs])
        sc.op("dve", lambda e: e.reciprocal(out=out_ap, in_=out_ap), reads=[rs], writes=[rs])

    NT = S // 512
    Tb = M.bank_bf(0)

    def prep(i):
        for j in range(4):
            t = i * 4 + j
            b = t % 2
            x_ap = xp[b]
            sc.dma(lambda e, x_ap=x_ap, t=t: e.dma_start(out=x_ap, in_=x_src[t * 128:(t + 1) * 128, :]),
                   reads=[r_xsrc], writes=[r_xp[b]])
            ss = small[:, b:b + 1]
            sc.op("pool", lambda e, ss=ss: e.memset(ss, 0.0), writes=[r_small[b]])
            sc.op("act", lambda e, x_ap=x_ap, b=b, ss=ss: e.activation(out=hb[b], in_=x_ap, func=AF.Square, accum_out=ss),
                  reads=[r_xp[b]], writes=[r_hb[b], r_small[b]])
            rstd_from(ss, ss, r_small[b])
            sc.op("dve", lambda e, x_ap=x_ap, b=b, ss=ss: e.scalar_tensor_tensor(
                out=hb[b], in0=x_ap, scalar=ss, in1=gp, op0=ALU.mult, op1=ALU.mult),
                reads=[r_xp[b], r_small[b], r_gp], writes=[r_hb[b]])
            for kc in range(8):
                sc.op("pe", lambda e, b=b, kc=kc: e.transpose(out=Tb[:, kc * 128:(kc + 1) * 128],
                                                             in_=hb[b][:, kc * 128:(kc + 1) * 128], identity=ident),
                      reads=[r_hb[b]], writes=[r_T])
            sc.op("act", lambda e, j=j: e.copy(out=hT[:, :, j * 128:(j + 1) * 128], in_=r3(Tb, 8)),
                  reads=[r_T], writes=[r_hT])

    def gateup(i):
        for f in range(NFF):
            pb = f % 2
            G = M.bank(1 + pb)
            U = M.bank(3 + pb)
            for kc in range(8):
                sc.op("pe", lambda e, G=G, kc=kc, f=f: e.matmul(G, lhsT=Wg[:, kc, f * 128:(f + 1) * 128], rhs=hT[:, kc, :],
                                                               start=(kc == 0), stop=(kc == 7)),
                      reads=[r_Wg[kc][min(3, (f * 128) // 768)], r_hT], writes=[r_G[pb]])
            for kc in range(8):
                sc.op("pe", lambda e, U=U, kc=kc, f=f: e.matmul(U, lhsT=Wu[:, kc, f * 128:(f + 1) * 128], rhs=hT[:, kc, :],
                                                               start=(kc == 0), stop=(kc == 7)),
                      reads=[r_Wu[kc][min(3, (f * 128) // 768)], r_hT], writes=[r_U[pb]])
            sc.op("act", lambda e, G=G, pb=pb: e.activation(out=sg[pb], in_=G, func=AF.Silu),
                  reads=[r_G[pb]], writes=[r_sg[pb]])
            sc.op("dve", lambda e, U=U, pb=pb, f=f: e.tensor_tensor(out=AT[:, f, :], in0=sg[pb], in1=U, op=ALU.mult),
                  reads=[r_sg[pb], r_U[pb]], writes=[r_AT[f]])

    YB = [(5, 6), (7, 0)]
    r_Yp = [[Res()], [Res(), r_T]]

    def down(i):
        for j in range(4):
            t = i * 4 + j
            b = t % 2
            yp = j % 2
            rY = r_Yp[yp]
            sc.dma(lambda e, b=b, t=t: e.dma_start(out=xr[b], in_=x_src[t * 128:(t + 1) * 128, :]),
                   reads=[r_xsrc], writes=[r_xr[b]])
            for dh in range(2):
                Y = M.bank(YB[yp][dh])
                for f in range(NFF):
                    sc.op("pe", lambda e, Y=Y, f=f, j=j, dh=dh: e.matmul(
                        Y, lhsT=AT[:, f, j * 128:(j + 1) * 128], rhs=Wd[:, f, dh * 512:(dh + 1) * 512],
                        start=(f == 0), stop=(f == NFF - 1)),
                        reads=[r_AT[f], r_Wd[f]], writes=rY)
            ssa = small[:, 4:5]
            ssb = small[:, 5:6]
            Y0, Y1 = M.bank(YB[yp][0]), M.bank(YB[yp][1])
            sc.op("pool", lambda e: e.memset(small[:, 4:6], 0.0), writes=[r_small[4]])
            sc.op("act", lambda e, ssa=ssa, Y0=Y0: e.activation(out=yt[:, 0:512], in_=Y0, func=AF.Square, accum_out=ssa),
                  reads=rY, writes=[r_yt, r_small[4]])
            sc.op("act", lambda e, ssb=ssb, Y1=Y1: e.activation(out=yt[:, 512:1024], in_=Y1, func=AF.Square, accum_out=ssb),
                  reads=rY, writes=[r_yt, r_small[4]])
            sc.op("dve", lambda e, ssa=ssa, ssb=ssb: e.tensor_tensor(out=ssa, in0=ssa, in1=ssb, op=ALU.add),
                  reads=[r_small[4]], writes=[r_small[4]])
            rstd_from(ssa, ssa, r_small[4])
            for dh in range(2):
                Yd = M.bank(YB[yp][dh])
                sc.op("dve", lambda e, dh=dh, ssa=ssa, Yd=Yd: e.scalar_tensor_tensor(
                    out=yt[:, dh * 512:(dh + 1) * 512], in0=Yd, scalar=ssa,
                    in1=gq[:, dh * 512:(dh + 1) * 512], op0=ALU.mult, op1=ALU.mult),
                    reads=rY + [r_small[4], r_gq], writes=[r_yt])
            sc.op("dve", lambda e, b=b: e.tensor_tensor(out=xr[b], in0=yt, in1=xr[b], op=ALU.add),
                  reads=[r_yt, r_xr[b]], writes=[r_xr[b]])
            sc.dma(lambda e, b=b, t=t: e.dma_start(out=x_dst[t * 128:(t + 1) * 128, :], in_=xr[b]),
                   reads=[r_xr[b]], writes=[r_xdst], q="act")

    if LIMIT >= 1:
        prep(0)
    load_weights()
    for i in range(NT):
        if LIMIT == 2 and i == 0:
            gateup(i)
        if LIMIT == 3 and i == 0:
            gateup(i)
            down(i)
        if LIMIT < 9:
            continue
        gateup(i)
        if i + 1 < NT:
            prep(i + 1)
        down(i)
    sc.barrier()


NEG = -30000.0
ZEROS = [None]
NSA_SL = [2.0 ** (-(i + 1)) for i in range(8)]
DIL_SL = [2.0 ** (-8.0 * (i + 1) / 12) for i in range(12)]
DILS = (1, 4, 16)
C_QN, C_KV, C_GN, C_QD, C_KD, C_VD, C_MG = 0, 512, 1280, 1304, 2072, 2840, 3608


def v4(ap, a, b):
    return ap.rearrange("p (a b c) -> p a b c", a=a, b=b)


def norm_to_hT(sc, M, x_src, t, xp, hb, small, gp, ident, rr, dstT, r_dst, Tb, r_T):
    b = t % 2
    r_xp, r_hb, r_small, r_gp = rr
    sc.dma(lambda e: e.dma_start(out=xp[b], in_=x_src[t * 128:(t + 1) * 128, :]), writes=[r_xp[b]])
    ss = small[:, b:b + 1]
    sc.op("pool", lambda e: e.memset(ss, 0.0), writes=[r_small[b]])
    sc.op("act", lambda e: e.activation(out=hb[b], in_=xp[b], func=AF.Square, accum_out=ss),
          reads=[r_xp[b]], writes=[r_hb[b], r_small[b]])
    sc.op("dve", lambda e: e.tensor_scalar(out=ss, in0=ss, scalar1=1.0 / D, scalar2=EPS, op0=ALU.mult, op1=ALU.add),
          reads=[r_small[b]], writes=[r_small[b]])
    sc.op("act", lambda e: e.sqrt(ss, ss), reads=[r_small[b]], writes=[r_small[b]])
    sc.op("dve", lambda e: e.reciprocal(out=ss, in_=ss), reads=[r_small[b]], writes=[r_small[b]])
    sc.op("dve", lambda e: e.scalar_tensor_tensor(out=hb[b], in0=xp[b], scalar=ss, in1=gp, op0=ALU.mult, op1=ALU.mult),
          reads=[r_xp[b], r_small[b], r_gp], writes=[r_hb[b]])
    for kc in range(8):
        sc.op("pe", lambda e, kc=kc: e.transpose(out=Tb[:, kc * 128:(kc + 1) * 128],
                                                 in_=hb[b][:, kc * 128:(kc + 1) * 128], identity=ident),
              reads=[r_hb[b]], writes=[r_T])
    sc.op("act", lambda e: e.copy(out=dstT, in_=r3(Tb, 8)), reads=[r_T], writes=[r_dst])


def proj_phase(sc, M, C, x1, w_in, gmix, scr):
    M.reset(C["const_end"])
    ident = C["ident"]
    h2T = r3(M.bf(8 * S), 8)
    Wb = [r3(M.bf(8 * 512), 8) for _ in range(2)]
    stage = [M.f32(512) for _ in range(2)]
    gp = M.f32(1024)
    xp = [M.f32(1024) for _ in range(2)]
    hb = [M.bf(1024) for _ in range(2)]
    small = M.f32(8)
    evf = [M.f32(512) for _ in range(2)]
    evb = [M.bf(512) for _ in range(2)]
    vaug = [M.bf(4 * 65) for _ in range(2)]
    r_h2T = [Res() for _ in range(32)]
    r_Wb = [Res(), Res()]
    r_stage = [Res(), Res()]
    r_gp = Res()
    rr = ([Res(), Res()], [Res(), Res()], [Res(), Res()], r_gp)
    r_ev = [Res(), Res()]
    r_va = [Res(), Res()]
    r_T = Res()
    r_ps = [Res(), Res()]
    Tb = M.bank_bf(0)
    sc.dma(lambda e: e.dma_start(out=gp, in_=gmix[0, :].partition_broadcast(128)), writes=[r_gp])
    for i in range(2):
        sc.op("pool", lambda e, i=i: e.memset(vaug[i], 1.0), writes=[r_va[i]])
    for t in range(32):
        norm_to_hT(sc, M, x1, t, xp, hb, small, gp, ident, rr, h2T[:, :, t * 128:(t + 1) * 128], r_h2T[t], Tb, r_T)

    cnt = [0]
    nspec = [0]

    def load_wb(bi, cols):
        for kc in range(8):
            off = 0
            for (c, w) in cols:
                i = cnt[0] % 2
                cnt[0] += 1
                st = stage[i][:, 0:w]
                sc.dma(lambda e, st=st, c=c, w=w, kc=kc: e.dma_start(out=st, in_=w_in[kc * 128:(kc + 1) * 128, c:c + w]),
                       writes=[r_stage[i]])
                dst = Wb[bi][:, kc, off:off + w]
                sc.op("pool", lambda e, st=st, dst=dst: e.tensor_copy(out=dst, in_=st), reads=[r_stage[i]], writes=[r_Wb[bi]])
                off += w

    ecnt = [0]

    def run_fm(cols, dst, fp32, dil):
        bi = nspec[0] % 2
        nspec[0] += 1
        load_wb(bi, cols)
        Ls = S // dil
        Cn = min(512, Ls)
        h4 = h2T.rearrange("p k (i r) -> p k i r", r=dil)
        for p0 in range(0, S, Cn):
            r, i0 = divmod(p0, Ls)
            k = ecnt[0] % 2
            ecnt[0] += 1
            PS = M.bank(1 + k)[:, 0:Cn]
            for kc in range(8):
                sc.op("pe", lambda e, PS=PS, kc=kc, i0=i0, r=r: e.matmul(
                    PS, lhsT=Wb[bi][:, kc, 0:128], rhs=h4[:, kc, i0:i0 + Cn, r], start=(kc == 0), stop=(kc == 7)),
                    reads=[r_Wb[bi]] + r_h2T, writes=[r_ps[k]])
            evt = (evf[k] if fp32 else evb[k])[:, 0:Cn]
            sc.op("act", lambda e, PS=PS, evt=evt: e.copy(out=evt, in_=PS), reads=[r_ps[k]], writes=[r_ev[k]])
            sc.dma(lambda e, evt=evt, p0=p0: e.dma_start(out=dst[:, p0:p0 + Cn], in_=evt), reads=[r_ev[k]], q="act")

    def run_tm(c0, n, dst, kind, dil, nh=0):
        bi = nspec[0] % 2
        nspec[0] += 1
        load_wb(bi, [(c0, n)])
        Ls = S // dil
        h4 = h2T.rearrange("p k (i r) -> p k i r", r=dil)
        for t in range(32):
            p0 = t * 128
            r, i0 = divmod(p0, Ls)
            k = ecnt[0] % 2
            ecnt[0] += 1
            PS = M.bank(1 + k)[:, 0:n]
            for kc in range(8):
                sc.op("pe", lambda e, PS=PS, kc=kc, i0=i0, r=r: e.matmul(
                    PS, lhsT=h4[:, kc, i0:i0 + 128, r], rhs=Wb[bi][:, kc, 0:n], start=(kc == 0), stop=(kc == 7)),
                    reads=[r_Wb[bi]] + r_h2T, writes=[r_ps[k]])
            if kind == "aug":
                va = vaug[k][:, 0:nh * 65]
                sc.op("act", lambda e, PS=PS, va=va: e.copy(out=r3(va, nh)[:, :, 0:64], in_=r3(PS, nh)),
                      reads=[r_ps[k]], writes=[r_va[k]])
                sc.dma(lambda e, va=va, p0=p0: e.dma_start(out=dst[p0:p0 + 128, :], in_=va), reads=[r_va[k]], q="act")
            elif kind == "sigf":
                evt = evf[k][:, 0:n]
                sc.op("act", lambda e, PS=PS, evt=evt: e.activation(out=evt, in_=PS, func=AF.Sigmoid),
                      reads=[r_ps[k]], writes=[r_ev[k]])
                sc.dma(lambda e, evt=evt, p0=p0: e.dma_start(out=dst[p0:p0 + 128, :], in_=evt), reads=[r_ev[k]], q="act")
            else:
                evt = evb[k][:, 0:n]
                sc.op("act", lambda e, PS=PS, evt=evt: e.activation(out=evt, in_=PS, func=AF.Sigmoid),
                      reads=[r_ps[k]], writes=[r_ev[k]])
                sc.dma(lambda e, evt=evt, p0=p0: e.dma_start(out=dst[p0:p0 + 128, :], in_=evt), reads=[r_ev[k]], q="act")

    for p in range(4):
        run_fm([(C_QN + p * 128, 128)], scr["QN"][p], False, 1)
    run_fm([(C_KV, 128)], scr["KC"][0], True, 1)
    run_fm([(C_KV + 128, 128)], scr["KC"][1], True, 1)
    for g in range(2):
        run_fm([(C_KV + 256 + g * 64, 64)] * 2, scr["KS"][g], False, 1)
        run_fm([(C_KV + 512 + g * 64, 64)] * 2, scr["KW"][g], False, 1)
    run_tm(C_KV + 384, 128, scr["VS"], "aug", 1, nh=2)
    run_tm(C_KV + 640, 128, scr["VW"], "aug", 1, nh=2)
    run_tm(C_GN, 24, scr["GN"], "sigf", 1)
    for gi in range(3):
        for pp in range(2):
            run_fm([(C_QD + gi * 256 + pp * 128, 128)], scr["QD"][gi * 2 + pp], False, DILS[gi])
            run_fm([(C_KD + gi * 256 + pp * 128, 128)], scr["KD"][gi * 2 + pp], False, DILS[gi])
        run_tm(C_VD + gi * 256, 256, scr["VD"][gi], "aug", DILS[gi], nh=4)
    for q in range(4):
        run_tm(C_MG + q * 512, 512, scr["MG"][:, q * 512:(q + 1) * 512], "sigb", 1)
    sc.barrier()


def cmp_phase(sc, M, C, scr, P):
    M.reset(C["const_end"])
    kcT, VC = C["kcT"], C["VC"]
    kcf = M.f32(S)
    tb = r3(M.bf(32 * 256), 32)
    W1b = r3(M.bf(32 * 256), 32)
    W2b = v4(M.bf(2 * 128), 2, 2)
    st1 = [M.f32(256) for _ in range(2)]
    st2 = M.f32(128)
    peT = M.f32(32)
    hx = [M.f32(256) for _ in range(4)]
    hu = M.f32(256)
    gT = [M.bf(256) for _ in range(4)]
    r_kcf, r_tb, r_W1, r_W2, r_st2, r_pe = Res(), Res(), Res(), Res(), Res(), Res()
    r_st1 = [Res(), Res()]
    r_hx = [Res() for _ in range(4)]
    r_hu = Res()
    r_gT = [Res() for _ in range(4)]
    r_ps = [Res(), Res(), Res()]
    r_kcT, r_VC = C["r_kcT"], C["r_VC"]
    VC4 = v4(VC, 2, 2)
    sc.op("pool", lambda e: e.memset(VC, 1.0), writes=[r_VC])
    for ct in range(2):
        for g in range(2):
            sc.dma(lambda e, ct=ct, g=g: e.dma_start(out=VC4[:, ct, g, 65:128], in_=P["c_ov"][ct * 128:(ct + 1) * 128, 1:64]),
                   writes=[r_VC])
    def do_kv(kv):
        pe_d = P["nsa_pe_k"] if kv == 0 else P["nsa_pe_v"]
        w1_d = P["nsa_w_ck1"] if kv == 0 else P["nsa_w_cv1"]
        w2_d = P["nsa_w_ck2"] if kv == 0 else P["nsa_w_cv2"]
        sc.dma(lambda e: e.dma_start(out=kcf, in_=scr["KC"][kv]), writes=[r_kcf])
        for g in range(2):
            sc.dma(lambda e, g=g, pe_d=pe_d: e.dma_start(out=peT[g * 64:(g + 1) * 64, :], in_=pe_d.rearrange("l d -> d l"),
                                                        allow_slow_non_contiguous=True), writes=[r_pe])
        sc.op("pool", lambda e: e.memset(tb.rearrange("p a b -> p (a b)"), 0.0), writes=[r_tb])
        kcf3 = kcf.rearrange("p (a b) -> p a b", b=16)
        for l in range(32):
            src = kcf3[:, 0:255, l] if l < 16 else kcf3[:, 1:256, l - 16]
            sc.op("dve", lambda e, l=l, src=src: e.tensor_scalar(out=tb[:, l, 0:255], in0=src, scalar1=peT[:, l:l + 1],
                                                                 scalar2=None, op0=ALU.add),
                  reads=[r_kcf, r_pe], writes=[r_tb])
        w1v = w1_d.rearrange("(l d) h -> d l h", d=64)
        for l in range(32):
            i = l % 2
            for g in range(2):
                sc.dma(lambda e, l=l, g=g, i=i: e.dma_start(out=st1[i][g * 64:(g + 1) * 64, :], in_=w1v[:, l, :]),
                       writes=[r_st1[i]])
            sc.op("pool", lambda e, l=l, i=i: e.tensor_copy(out=W1b[:, l, :], in_=st1[i]), reads=[r_st1[i]], writes=[r_W1])
        sc.dma(lambda e: e.dma_start(out=r3(st2, 2), in_=w2_d.rearrange("(hc p) d -> p hc d", p=128)), writes=[r_st2])
        sc.op("pool", lambda e: e.tensor_copy(out=W2b[:, :, 0, :], in_=r3(st2, 2)), reads=[r_st2], writes=[r_W2])
        sc.op("pool", lambda e: e.memset(W2b[:, :, 1, :], 0.0), writes=[r_W2])
        for g in range(2):
            for hc in range(2):
                idx = g * 2 + hc
                PS = M.bank(1 + idx % 2)[:, 0:256]
                rp = r_ps[idx % 2]
                for l in range(32):
                    sc.op("pe", lambda e, PS=PS, l=l, g=g, hc=hc: e.matmul(
                        PS, lhsT=W1b[g * 64:(g + 1) * 64, l, hc * 128:(hc + 1) * 128], rhs=tb[g * 64:(g + 1) * 64, l, :],
                        start=(l == 0), stop=(l == 31)), reads=[r_W1, r_tb], writes=[rp])
                x_ = hx[idx]
                sc.op("act", lambda e, PS=PS, x_=x_: e.copy(out=x_, in_=PS), reads=[rp], writes=[r_hx[idx]])
                sc.op("act", lambda e, x_=x_: e.activation(out=hu, in_=x_, func=AF.Square), reads=[r_hx[idx]], writes=[r_hu])
                sc.op("dve", lambda e: e.tensor_scalar(out=hu, in0=hu, scalar1=0.044715, scalar2=1.0, op0=ALU.mult, op1=ALU.add),
                      reads=[r_hu], writes=[r_hu])
                sc.op("dve", lambda e, x_=x_: e.tensor_tensor(out=hu, in0=hu, in1=x_, op=ALU.mult),
                      reads=[r_hu, r_hx[idx]], writes=[r_hu])
                sc.op("act", lambda e: e.activation(out=hu, in_=hu, func=AF.Sigmoid, scale=1.5957691216057308),
                      reads=[r_hu], writes=[r_hu])
                sc.op("dve", lambda e, x_=x_, idx=idx: e.tensor_tensor(out=gT[idx], in0=hu, in1=x_, op=ALU.mult),
                      reads=[r_hu, r_hx[idx]], writes=[r_gT[idx]])
        for g in range(2):
            if kv == 0:
                PS = M.bank(3)[:, 0:256]
                for hc in range(2):
                    sc.op("pe", lambda e, PS=PS, g=g, hc=hc: e.matmul(
                        PS, lhsT=W2b[:, hc, :, :].rearrange("p a b -> p (a b)"), rhs=gT[g * 2 + hc],
                        start=(hc == 0), stop=(hc == 1)), reads=[r_W2, r_gT[g * 2 + hc]], writes=[r_ps[2]])
                sc.op("act", lambda e, PS=PS, g=g: e.copy(out=kcT[:, g, :], in_=PS), reads=[r_ps[2]], writes=[r_kcT])
            else:
                for ct in range(2):
                    PS = M.bank(3)[:, 0:64]
                    for hc in range(2):
                        sc.op("pe", lambda e, PS=PS, g=g, hc=hc, ct=ct: e.matmul(
                            PS, lhsT=gT[g * 2 + hc][:, ct * 128:(ct + 1) * 128], rhs=W2b[:, hc, 0, :],
                            start=(hc == 0), stop=(hc == 1)), reads=[r_W2, r_gT[g * 2 + hc]], writes=[r_ps[2]])
                    sc.op("act", lambda e, PS=PS, g=g, ct=ct: e.copy(out=VC4[:, ct, g, 0:64], in_=PS),
                          reads=[r_ps[2]], writes=[r_VC])

    for kv in range(2):
        do_kv(kv)
    sc.barrier()


class AttnPipe:
    def __init__(self, sc, M, sbanks, ntmp=3):
        self.sc, self.M = sc, M
        self.sb = sbanks
        self.r_s = [Res() for _ in sbanks]
        self.tmp = [M.f32(512) for _ in range(ntmp)]
        self.Pt = [M.bf(512) for _ in range(ntmp)]
        self.r_tmp = [Res() for _ in range(ntmp)]
        self.r_P = [Res() for _ in range(ntmp)]
        self.n = 0

    def run_stream(self, items, LA=3):
        sc, M = self.sc, self.M
        jobs = [it for it in items if isinstance(it, dict)]
        order = []
        ji = 0
        for it in items:
            if isinstance(it, dict):
                order.append(("job", ji))
                ji += 1
            else:
                order.append(("call", it))
        n = len(jobs)
        slots = {}
        done_pv = [0]
        pending = []
        DELAY = 4

        def emit_pv(j):
            jb = jobs[j]
            ti = slots[j]
            if "parts" in jb:
                while pending and pending[0][0] <= j:
                    pending.pop(0)[2]()
                acc, M_rows, W = jb["acc"], jb["M_rows"], jb["W"]
                nk = jb["nk"]
                if jb["first"]:
                    z = ZEROS[0]
                    sc.op("pe", lambda e: e.matmul(acc[0:M_rows, 0:W], lhsT=z[:, 0:M_rows], rhs=z[:, 0:W], start=True, stop=False),
                          writes=[jb["r_acc"]])
                np_ = len(jb["parts"])
                for pi_, (_l, _r, _rd, off_, n_, pv_, pvr_, c0_) in enumerate(jb["parts"]):
                    Pq = self.Pt[ti][0:nk, off_:off_ + n_]
                    lastp = jb["last"] and pi_ == np_ - 1
                    sc.op("pe", lambda e, Pq=Pq, pv_=pv_, c0_=c0_, n_=n_, lastp=lastp: e.matmul(
                        acc[0:M_rows, c0_:c0_ + n_], lhsT=pv_, rhs=Pq, start=False, stop=lastp),
                        reads=[self.r_P[ti]] + pvr_, writes=[jb["r_acc"]])
                if jb.get("after") is not None:
                    key, fa, fb = jb["after"]
                    while any(p[1] == key for p in pending):
                        pending.pop(0)[2]()
                    fa()
                    pending.append((j + DELAY, key, fb))
                return
            nk, ncol, c0 = jb["nk"], jb["nc"], jb["c0"]
            Pp = self.Pt[ti][0:nk, 0:ncol]
            pv = jb["pv"]
            acc, M_rows = jb["acc"], jb["M_rows"]
            first, last = jb["first"], jb["last"]
            while pending and pending[0][0] <= j:
                pending.pop(0)[2]()
            W = jb.get("W", 512)
            if first and (c0 != 0 or ncol != W):
                z = ZEROS[0]
                sc.op("pe", lambda e: e.matmul(acc[0:M_rows, 0:W], lhsT=z[:, 0:M_rows], rhs=z[:, 0:W], start=True, stop=False),
                      writes=[jb["r_acc"]])
                first = False
            sc.op("pe", lambda e: e.matmul(acc[0:M_rows, c0:c0 + ncol], lhsT=pv, rhs=Pp, start=first, stop=last),
                  reads=[self.r_P[ti]] + jb["pv_reads"], writes=[jb["r_acc"]])
            if jb.get("after") is not None:
                key, fa, fb = jb["after"]
                while any(p[1] == key for p in pending):
                    pending.pop(0)[2]()
                fa()
                pending.append((j + DELAY, key, fb))

        for kind, v in order:
            if kind == "call":
                v()
                continue
            i = v
            jb = jobs[i]
            k = self.n
            self.n += 1
            si = k % len(self.sb)
            ti = k % len(self.tmp)
            slots[i] = ti
            nk, ncol = jb["nk"], jb["nc"]
            Sps = M.bank(self.sb[si])[0:nk, 0:ncol]
            if "parts" in jb:
                for (l_, r_, rd, off_, n_, _pv, _pvr, _c0) in jb["parts"]:
                    Sp_ = M.bank(self.sb[si])[0:nk, off_:off_ + n_]
                    sc.op("pe", lambda e, l_=l_, r_=r_, Sp_=Sp_: e.matmul(Sp_, lhsT=l_, rhs=r_, start=True, stop=True),
                          reads=rd, writes=[self.r_s[si]])
            else:
                nq = len(jb["qk"])
                for qi, (l_, r_, rd) in enumerate(jb["qk"]):
                    sc.op("pe", lambda e, l_=l_, r_=r_, qi=qi, Sps=Sps, nq=nq: e.matmul(
                        Sps, lhsT=l_, rhs=r_, start=(qi == 0), stop=(qi == nq - 1)), reads=rd, writes=[self.r_s[si]])
            Pp = self.Pt[ti][0:nk, 0:ncol]
            bias = jb.get("bias")
            if "parts" in jb:
                tm = self.tmp[ti][0:nk, 0:ncol]
                T = jb["T"]
                sc.op("dve", lambda e, tm=tm, Sps=Sps, T=T: e.scalar_tensor_tensor(
                    out=tm, in0=Sps, scalar=0.125, in1=T, op0=ALU.mult, op1=ALU.add),
                    reads=[self.r_s[si]] + jb.get("T_reads", []), writes=[self.r_tmp[ti]])
                sc.op("act", lambda e, Pp=Pp, tm=tm: e.activation(out=Pp, in_=tm, func=AF.Exp),
                      reads=[self.r_tmp[ti]], writes=[self.r_P[ti]])
            elif jb.get("direct"):
                sc.op("act", lambda e, Pp=Pp, Sps=Sps, bias=bias: e.activation(out=Pp, in_=Sps, func=AF.Exp, bias=bias, scale=0.125),
                      reads=[self.r_s[si]] + jb.get("bias_reads", []), writes=[self.r_P[ti]])
            else:
                tm = self.tmp[ti][0:nk, 0:ncol]
                T = jb["T"]
                sc.op("dve", lambda e, tm=tm, Sps=Sps, T=T: e.scalar_tensor_tensor(
                    out=tm, in0=Sps, scalar=0.125, in1=T, op0=ALU.mult, op1=ALU.add),
                    reads=[self.r_s[si]] + jb.get("T_reads", []), writes=[self.r_tmp[ti]])
                for (off, mk, mrd) in jb["masks"]:
                    w = mk.shape[-1]
                    tmm = self.tmp[ti][0:nk, off:off + w]
                    sc.op("pool", lambda e, tmm=tmm, mk=mk: e.tensor_tensor(out=tmm, in0=tmm, in1=mk, op=ALU.add),
                          reads=[self.r_tmp[ti]] + mrd, writes=[self.r_tmp[ti]])
                sc.op("act", lambda e, Pp=Pp, tm=tm, bias=bias: e.activation(out=Pp, in_=tm, func=AF.Exp, bias=bias),
                      reads=[self.r_tmp[ti]] + jb.get("bias_reads", []), writes=[self.r_P[ti]])
            while done_pv[0] <= i - LA:
                emit_pv(done_pv[0])
                done_pv[0] += 1
        while done_pv[0] < n:
            emit_pv(done_pv[0])
            done_pv[0] += 1
        while pending:
            pending.pop(0)[2]()

    def run(self, jobs, acc, r_acc, M_rows):
        sc, M = self.sc, self.M
        LA = 2
        n = len(jobs)
        slots = []
        for i in range(n + LA):
            if i < n:
                jb = jobs[i]
                k = self.n
                self.n += 1
                si = k % len(self.sb)
                ti = k % len(self.tmp)
                slots.append(ti)
                nk, ncol = jb["nk"], jb["nc"]
                Sps = M.bank(self.sb[si])[0:nk, 0:ncol]
                nq = len(jb["qk"])
                for qi, (l_, r_, rd) in enumerate(jb["qk"]):
                    sc.op("pe", lambda e, Sps=Sps, l_=l_, r_=r_, qi=qi, nq=nq: e.matmul(
                        Sps, lhsT=l_, rhs=r_, start=(qi == 0), stop=(qi == nq - 1)), reads=rd, writes=[self.r_s[si]])
                tm = self.tmp[ti][0:nk, 0:ncol]
                T = jb["T"]
                sc.op("dve", lambda e, tm=tm, Sps=Sps, T=T: e.scalar_tensor_tensor(out=tm, in0=Sps, scalar=0.125, in1=T, op0=ALU.mult, op1=ALU.add),
                      reads=[self.r_s[si]] + jb.get("T_reads", []), writes=[self.r_tmp[ti]])
                for (off, mk, mrd) in jb["masks"]:
                    w = mk.shape[-1]
                    tmm = self.tmp[ti][0:nk, off:off + w]
                    sc.op("pool", lambda e, tmm=tmm, mk=mk: e.tensor_tensor(out=tmm, in0=tmm, in1=mk, op=ALU.add),
                          reads=[self.r_tmp[ti]] + mrd, writes=[self.r_tmp[ti]])
                Pp = self.Pt[ti][0:nk, 0:ncol]
                bias = jb["bias"]
                sc.op("act", lambda e, Pp=Pp, tm=tm, bias=bias: e.activation(out=Pp, in_=tm, func=AF.Exp, bias=bias),
                      reads=[self.r_tmp[ti]] + jb.get("bias_reads", []), writes=[self.r_P[ti]])
            if i >= LA:
                j = i - LA
                jb = jobs[j]
                ti = slots[j]
                nk, ncol, c0 = jb["nk"], jb["nc"], jb["c0"]
                Pp = self.Pt[ti][0:nk, 0:ncol]
                pv = jb["pv"]
                sc.op("pe", lambda e, Pp=Pp, pv=pv, c0=c0, ncol=ncol, j=j: e.matmul(
                    acc[0:M_rows, c0:c0 + ncol], lhsT=pv, rhs=Pp, start=(j == 0), stop=(j == n - 1)),
                    reads=[self.r_P[ti]] + jb["pv_reads"], writes=[r_acc])


def nsa_phase(sc, M, C, scr, P):
    M.reset(C["const_end"])
    identf = C["identf"]
    ident = C["ident"]
    kcT, VC = C["kcT"], C["VC"]
    r_kcT, r_VC = C["r_kcT"], C["r_VC"]
    VC4 = v4(VC, 2, 2)
    KS = r3(M.bf(2 * S), 2)
    KW = r3(M.bf(2 * S), 2)
    VS = v4(M.bf(32 * 256), 32, 2)
    VW = v4(M.bf(32 * 256), 32, 2)
    TQA = r3(M.f32(8 * 512), 8)
    MASKO = M.f32(512)
    TQ8 = r3(M.f32(8 * 4), 8)
    QN = [r3(M.bf(8 * 512), 8) for _ in range(2)]
    TQ = r3(M.f32(8 * 512), 8)
    MK = r3(M.f32(3 * 128), 3)
    BC = r3(M.f32(8 * 35), 8)
    BCC = v4(M.f32(8 * 8 * 2), 8, 8)
    Mc = [r3(M.f32(2 * 512), 2) for _ in range(2)]
    accS = [M.f32(512) for _ in range(2)]
    onsa = r3(M.f32(4 * 512), 4)
    onsab = r3(M.bf(4 * 512), 4)
    impacc = v4(M.f32(2 * 4 * 64), 2, 4)
    GN = [r3(M.f32(4 * 24), 4) for _ in range(2)]
    SELM = [v4(M.f32(4 * 2 * 64), 4, 2) for _ in range(2)]
    rr_ = M.f32(4)
    sc4 = M.f32(4)
    m1 = M.f32(8)
    m2 = M.f32(8)
    wk = M.f32(64)
    impp = r3(M.f32(4 * 64), 4)
    mbf = M.bf(128)
    tnum = r3(M.f32(4 * 64), 4)
    pipe = AttnPipe(sc, M, [0, 1, 2, 6], ntmp=4)
    r_KS, r_KW, r_VS, r_VW, r_c = Res(), Res(), Res(), Res(), Res()
    r_QN = [[Res(), Res()], [Res(), Res()]]
    r_Mc = [Res(), Res()]
    r_GN = [Res(), Res()]
    r_SELM = [Res(), Res()]
    r_accS = [Res(), Res()]
    r_acc = [Res(), Res()]
    r_TP0 = Res()
    r_TP = [r_TP0, r_TP0]
    r_onsa, r_onsab, r_imp, r_MBT = Res(), Res(), Res(), Res()
    r_sm = Res()
    r_sel = Res()
    r_mbf = Res()
    r_tnum = Res()
    r_TP7 = Res()
    sc.op("pool", lambda e: e.memset(KW.rearrange("p a b -> p (a b)"), 0.0), writes=[r_KW])
    sc.op("pool", lambda e: e.memset(VS.rearrange("p a b c -> p (a b c)"), 0.0), writes=[r_VS])
    sc.op("pool", lambda e: e.memset(VW.rearrange("p a b c -> p (a b c)"), 0.0), writes=[r_VW])
    for qb_ in range(2):
        sc.op("pool", lambda e, qb_=qb_: e.memset(QN[qb_].rearrange("p a b -> p (a b)"), 0.0),
              writes=[r_QN[qb_][0], r_QN[qb_][1]])
    ohf = P["c_oh"].rearrange("j a b -> j (a b)")
    for g in range(2):
        sc.dma(lambda e, g=g: e.dma_start(out=KS[0:64, g, :], in_=scr["KS"][g][0:64, :]), writes=[r_KS])
        sc.dma(lambda e, g=g: e.dma_start(out=KS[64:128, g, :], in_=ohf), writes=[r_KS])
        sc.dma(lambda e, g=g: e.dma_start(out=KW[0:64, g, :], in_=scr["KW"][g][0:64, :]), writes=[r_KW])
    vsv = scr["VS"].rearrange("(kt p) (g c) -> p kt g c", p=128, g=2)
    vwv = scr["VW"].rearrange("(kt p) (g c) -> p kt g c", p=128, g=2)
    for q4 in range(4):
        for g in range(2):
            sc.dma(lambda e, q4=q4, g=g: e.dma_start(out=VS[:, q4 * 8:(q4 + 1) * 8, g, 0:65],
                                                      in_=vsv[:, q4 * 8:(q4 + 1) * 8, g, :]), writes=[r_VS])
            sc.dma(lambda e, q4=q4, g=g: e.dma_start(out=VW[:, q4 * 8:(q4 + 1) * 8, g, 0:65],
                                                      in_=vwv[:, q4 * 8:(q4 + 1) * 8, g, :]), writes=[r_VW])
    sc.dma(lambda e: e.dma_start(out=TQ.rearrange("p a b -> p (a b)"),
                                 in_=P["c_tq"].rearrange("a b -> (a b)").partition_broadcast(128)), writes=[r_c])
    sc.dma(lambda e: e.dma_start(out=TQA, in_=P["c_tqa"].rearrange("h k q -> k h q")), writes=[r_c])
    sc.dma(lambda e: e.dma_start(out=MASKO, in_=P["c_masko"]), writes=[r_c])
    sc.dma(lambda e: e.dma_start(out=TQ8, in_=P["c_tq8"]), writes=[r_c])
    sc.dma(lambda e: e.dma_start(out=MK, in_=P["c_masks"].rearrange("m k q -> k m q")), writes=[r_c])
    sc.dma(lambda e: e.dma_start(out=BC, in_=P["c_bc"]), writes=[r_c])
    sc.dma(lambda e: e.dma_start(out=BCC, in_=P["c_bcc"]), writes=[r_c])
    sc.barrier()

    def finish_head(acc_bank, ai, M_rows, h, br, qc, first):
        a = accS[ai]
        tb_ = 5
        TP = r3(M.bank(tb_), 4)
        for sub in range(4):
            MR = M_rows + (M_rows % 2)
            sc.op("pe", lambda e, sub=sub, MR=MR: e.transpose(out=TP[:, sub, 0:MR], in_=a[0:MR, sub * 128:(sub + 1) * 128],
                                                       identity=identf[0:MR, 0:MR]),
                  reads=[r_accS[ai]], writes=[r_TP[ai]])
        den = TP[:, :, 64:65]
        if br == 0:
            sc.op("dve", lambda e: e.tensor_scalar(out=rr_.unsqueeze(2), in0=den, scalar1=1e-30, scalar2=None, op0=ALU.max),
                  reads=[r_TP[ai]], writes=[r_sm])
            sc.op("dve", lambda e: e.reciprocal(out=rr_, in_=rr_), reads=[r_sm], writes=[r_sm])
        else:
            sc.op("dve", lambda e: e.reciprocal(out=rr_.unsqueeze(2), in_=den), reads=[r_TP[ai]], writes=[r_sm])
        gidx = h * 3 + br
        sc.op("dve", lambda e: e.tensor_tensor(out=sc4, in0=rr_, in1=GN[qc % 2][:, :, gidx], op=ALU.mult),
              reads=[r_sm, r_GN[qc % 2]], writes=[r_sm])
        dst = onsa[:, :, h * 64:(h + 1) * 64]
        if first:
            sc.op("dve", lambda e: e.tensor_tensor(out=dst, in0=TP[:, :, 0:64], in1=sc4.unsqueeze(2).to_broadcast([128, 4, 64]),
                                                   op=ALU.mult), reads=[r_TP[ai], r_sm], writes=[r_onsa])
        else:
            sc.op("dve", lambda e: e.tensor_tensor(out=tnum, in0=TP[:, :, 0:64], in1=sc4.unsqueeze(2).to_broadcast([128, 4, 64]),
                                                   op=ALU.mult), reads=[r_TP[ai], r_sm], writes=[r_tnum])
            sc.op("dve", lambda e: e.tensor_tensor(out=dst, in0=dst, in1=tnum, op=ALU.add),
                  reads=[r_tnum, r_onsa], writes=[r_onsa])
        return TP

    hcount = [0]

    mbfs = [[r3(M.bf(4 * 128), 4) for _ in range(4)] for _ in range(2)]
    r_mbfs = [[Res() for _ in range(4)] for _ in range(2)]
    for g_ in range(2):
        for s_ in range(4):
            sc.op("pool", lambda e, g_=g_, s_=s_: e.memset(mbfs[g_][s_].rearrange("p a b -> p (a b)"), 0.0),
                  writes=[r_mbfs[g_][s_]])

    def do_chunk(qc):
        q0 = qc * 512
        qb = qc % 2
        for h_ in range(8):
            sc.dma(lambda e, h_=h_: e.dma_start(out=QN[qb][0:64, h_, :],
                                                 in_=scr["QN"][h_ // 2][(h_ % 2) * 64:(h_ % 2) * 64 + 64, q0:q0 + 512]),
                   writes=[r_QN[qb][h_ // 4]])
        sc.dma(lambda e: e.dma_start(out=GN[qb], in_=scr["GN"][q0:q0 + 512, :].rearrange("(s p) c -> p s c", p=128)),
               writes=[r_GN[qb]])
        sc.dma(lambda e: e.dma_start(out=SELM[qb].rearrange("p a b c -> p a (b c)"),
                                     in_=P["c_selm"][q0:q0 + 512].rearrange("(s p) a c -> p s (a c)", p=128)),
               writes=[r_SELM[qb]])
        sc.dma(lambda e: e.dma_start(out=Mc[qb], in_=P["c_mc"][qc]), writes=[r_Mc[qb]])
        nct = 2 if qc >= 4 else 1
        items = []

        def sel_dve(g):
            sc.op("pool", lambda e: e.memset(impacc[:, g, :, 0:1], 0.0), reads=[r_imp], writes=[r_imp])
            sc.op("dve", lambda e: e.tensor_tensor(out=impp, in0=impacc[:, g, :, :], in1=SELM[qb][:, :, 0, :], op=ALU.mult),
                  reads=[r_imp, r_SELM[qb]], writes=[r_sel])
            sc.op("dve", lambda e: e.tensor_tensor(out=impp, in0=impp, in1=SELM[qb][:, :, 1, :], op=ALU.add),
                  reads=[r_sel, r_SELM[qb]], writes=[r_sel])
            for sub in range(4):
                iv = impp[:, sub, :]
                mb_ = mbfs[g][sub]
                sc.op("dve", lambda e, iv=iv: e.max(out=m1, in_=iv), reads=[r_sel], writes=[r_sel])
                sc.op("dve", lambda e, iv=iv: e.match_replace(out=wk, in_to_replace=m1, in_values=iv, imm_value=-3.0e38),
                      reads=[r_sel], writes=[r_sel])
                sc.op("dve", lambda e: e.max(out=m2, in_=wk), reads=[r_sel], writes=[r_sel])
                sc.op("dve", lambda e, iv=iv: e.tensor_scalar(
                    out=wk, in0=iv, scalar1=m2[:, 7:8], scalar2=NEG, op0=ALU.is_lt, op1=ALU.mult),
                    reads=[r_sel], writes=[r_sel])
                for hh_ in range(4):
                    sc.op("dve", lambda e, hh_=hh_, mb_=mb_, sub=sub: e.tensor_scalar(
                        out=mb_[:, hh_, 64:128], in0=wk, scalar1=TQ8[:, g * 4 + hh_, sub:sub + 1], scalar2=None, op0=ALU.add),
                        reads=[r_sel, r_c], writes=[r_mbfs[g][sub]])

        def sel_pe(g):
            Tb = M.bank_bf(7)
            for sub in range(4):
                mb_ = mbfs[g][sub]
                for hh_ in range(4):
                    sc.op("pe", lambda e, mb_=mb_, hh_=hh_: e.transpose(out=Tb[:, hh_ * 128:(hh_ + 1) * 128], in_=mb_[:, hh_, :],
                                                                       identity=ident),
                          reads=[r_mbfs[g][sub]], writes=[r_TP7])
                sc.op("act", lambda e, sub=sub: e.copy(
                    out=QN[qb][64:128, g * 4:(g + 1) * 4, sub * 128:(sub + 1) * 128],
                    in_=r3(Tb[64:128, 0:512], 4)),
                    reads=[r_TP7], writes=[r_QN[qb][g]])

        def mk_after(ai, M_rows, h, br, g, hh):
            def fa():
                a = accS[ai]
                sc.op("dve", lambda e: e.tensor_copy(out=a[0:M_rows, :], in_=M.bank(3 + ai)[0:M_rows, :]),
                      reads=[r_acc[ai]], writes=[r_accS[ai]])

            def after():
                first = (br == 0)
                TP = finish_head(3 + ai, ai, M_rows, h, br, qc, first)
                if br == 0:
                    ia = impacc[:, g, :, 1:64]
                    if hh == 0:
                        sc.op("dve", lambda e: e.tensor_tensor(
                            out=ia, in0=TP[:, :, 65:128], in1=rr_.unsqueeze(2).to_broadcast([128, 4, 63]), op=ALU.mult),
                            reads=[r_TP[ai], r_sm], writes=[r_imp])
                    else:
                        sc.op("dve", lambda e: e.tensor_tensor(
                            out=tnum[:, :, 0:63], in0=TP[:, :, 65:128], in1=rr_.unsqueeze(2).to_broadcast([128, 4, 63]),
                            op=ALU.mult), reads=[r_TP[ai], r_sm], writes=[r_tnum])
                        sc.op("dve", lambda e: e.tensor_tensor(out=ia, in0=ia, in1=tnum[:, :, 0:63], op=ALU.add),
                              reads=[r_tnum, r_imp], writes=[r_imp])
                    if hh == 3:
                        sel_dve(g)
            return (ai, fa, after)

        for g in range(2):
            for hh in range(4):
                h = g * 4 + hh
                pr, half = h // 2, h % 2
                lo, hi = half * 64, half * 64 + 64
                ai = hcount[0] % 2
                hcount[0] += 1
                for ct in range(nct):
                    items.append(dict(
                        qk=[(kcT[:, g, ct * 128:(ct + 1) * 128], QN[qb][:, h, :], [r_kcT, r_QN[qb][g]])],
                        nk=128, c0=0, nc=512, T=TQ[:, h, :], T_reads=[r_c],
                        masks=[(0, Mc[qb][:, ct, :], [r_Mc[qb]])],
                        bias=BCC[:, h, qc, ct:ct + 1], bias_reads=[r_c],
                        pv=VC4[:, ct, g, :], pv_reads=[r_VC],
                        acc=M.bank(3 + ai), r_acc=r_acc[ai], M_rows=128, first=(ct == 0), last=(ct == nct - 1),
                        after=(mk_after(ai, 128, h, 0, g, hh) if ct == nct - 1 else None)))

        def branch_jobs(br, g):
            for hh in range(4):
                h = g * 4 + hh
                pr, half = h // 2, h % 2
                lo, hi = half * 64, half * 64 + 64
                ai = hcount[0] % 2
                hcount[0] += 1
                if br == 1:
                    kts = list(range(0, 4 * qc + 4))
                else:
                    kts = list(range(max(0, 4 * qc - 4), 4 * qc + 4))
                for ki_, kt in enumerate(kts):
                    dlt = 4 * qc - kt
                    masks = []
                    if dlt > 0:
                        c0, ncol = 0, 512
                        Tt = TQ
                        bidx = dlt + 3
                        if br == 2:
                            m = 4 - dlt
                            ncol = 128 * (m + 1)
                            masks = [(128 * m, MK[:, 1, :], [r_c])]
                    else:
                        j = -dlt
                        c0, ncol = 128 * j, 512 - 128 * j
                        Tt = TQA
                        bidx = 3
                    Kt = KS if br == 1 else KW
                    rK = r_KS if br == 1 else r_KW
                    qk = [(Kt[:, g, kt * 128:(kt + 1) * 128], QN[qb][:, h, c0:c0 + ncol], [rK, r_QN[qb][g]])]
                    Vt = VS if br == 1 else VW
                    lastj = (ki_ == len(kts) - 1)
                    Tap = Tt[:, h, 0:ncol]
                    direct = False
                    if br == 1:
                        bidx = dlt + 3
                        if dlt > 0:
                            direct = True
                        else:
                            Tap = MASKO[:, 0:ncol]
                    items.append(dict(
                        qk=qk, nk=128, c0=c0, nc=ncol, T=Tap, T_reads=[r_c], masks=masks, direct=direct,
                        bias=BC[:, h, bidx:bidx + 1], bias_reads=[r_c],
                        pv=Vt[:, kt, g, :], pv_reads=[r_VS if br == 1 else r_VW],
                        acc=M.bank(3 + ai), r_acc=r_acc[ai], M_rows=128, first=(ki_ == 0), last=lastj,
                        after=(mk_after(ai, 65, h, br, g, hh) if lastj else None)))

        branch_jobs(2, 0)
        items.append(lambda: sel_pe(0))
        branch_jobs(2, 1)
        items.append(lambda: sel_pe(1))
        branch_jobs(1, 0)
        branch_jobs(1, 1)
        pipe.run_stream(items)
        sc.op("act", lambda e: e.copy(out=onsab, in_=onsa), reads=[r_onsa], writes=[r_onsab])
        sc.dma(lambda e: e.dma_start(out=scr["ON"][q0:q0 + 512, :].rearrange("(s p) c -> p s c", p=128), in_=onsab),
               reads=[r_onsab])

    for qc in range(8):
        do_chunk(qc)
    sc.barrier()


def dil_phase(sc, M, C, scr, P):
    identf = C["identf"]

    def do_group(gi):
        M.reset(C["const_end"])
        dil = DILS[gi]
        Ls = S // dil
        Cn = min(512, Ls)
        nsub = Cn // 128
        KD = v4(M.bf(4 * S), 2, 2)
        VD = v4(M.bf(32 * 512), 32, 4)
        QD = [r3(M.bf(2 * 512), 2) for _ in range(2)]
        TD = r3(M.f32(4 * 1152), 4)
        BCD = r3(M.f32(4 * 5), 4)
        accS = [M.f32(512) for _ in range(2)]
        osb = [v4(M.f32(4 * 260), 4, 4) for _ in range(2)]
        pipe = AttnPipe(sc, M, [0, 1, 2, 6], ntmp=4)
        r_KD, r_VD, r_c = Res(), Res(), Res()
        r_QD = [Res(), Res()]
        r_accS = [Res(), Res()]
        r_acc = [Res(), Res()]
        r_TP0 = Res()
        r_TP = [r_TP0, r_TP0]
        r_osb = [Res(), Res()]
        sc.op("pool", lambda e: e.memset(KD.rearrange("p a b c -> p (a b c)"), 0.0), writes=[r_KD])
        sc.op("pool", lambda e: e.memset(VD.rearrange("p a b c -> p (a b c)"), 0.0), writes=[r_VD])
        for pp in range(2):
            for hf in range(2):
                sc.dma(lambda e, pp=pp, hf=hf: e.dma_start(out=KD[hf * 64:(hf + 1) * 64, pp, hf, :],
                                                            in_=scr["KD"][gi * 2 + pp][hf * 64:(hf + 1) * 64, :]), writes=[r_KD])
        vdv = scr["VD"][gi].rearrange("(kt p) (h c) -> p kt h c", p=128, h=4)
        for q4 in range(4):
            for hh_ in range(4):
                sc.dma(lambda e, q4=q4, hh_=hh_: e.dma_start(out=VD[:, q4 * 8:(q4 + 1) * 8, hh_, 0:65],
                                                              in_=vdv[:, q4 * 8:(q4 + 1) * 8, hh_, :]), writes=[r_VD])
        sc.dma(lambda e: e.dma_start(out=TD, in_=P["c_tds"][gi * 4:(gi + 1) * 4].rearrange("h k q -> k h q")), writes=[r_c])
        sc.dma(lambda e: e.dma_start(out=BCD, in_=P["c_bcd"][:, gi * 4:(gi + 1) * 4, :]), writes=[r_c])
        sc.barrier()
        ODv = scr["OD"][gi].rearrange("(i r) c -> i r c", r=dil)
        hcount = [0]
        items = []
        loadqs = []
        marks = []

        def do_chunk(ci, p0):
            r, i0 = divmod(p0, Ls)
            qb = ci % 2
            ob = osb[qb]

            def loadq():
                for pp in range(2):
                    sc.dma(lambda e, pp=pp: e.dma_start(out=QD[qb][:, pp, 0:Cn], in_=scr["QD"][gi * 2 + pp][:, p0:p0 + Cn]),
                           writes=[r_QD[qb]])
            loadqs.append(loadq)
            if ci == 0:
                items.append(loadq)
            mark = len(items)
            items.append(None)
            marks.append(mark)

            def mk_after(ai, hh):
                a = accS[ai]

                def fa():
                    sc.op("dve", lambda e: e.tensor_copy(out=a[0:65, 0:Cn], in_=M.bank(3 + ai)[0:65, 0:Cn]),
                          reads=[r_acc[ai]], writes=[r_accS[ai]])

                def after():
                    TP = r3(M.bank(5), 4)
                    for sub in range(nsub):
                        sc.op("pe", lambda e, sub=sub: e.transpose(
                            out=TP[:, sub, 0:66], in_=a[0:66, sub * 128:(sub + 1) * 128], identity=identf[0:66, 0:66]),
                            reads=[r_accS[ai]], writes=[r_TP[ai]])
                    sc.op("dve", lambda e: e.tensor_copy(out=ob[:, 0:nsub, hh, :], in_=TP[:, 0:nsub, 0:65]),
                          reads=[r_TP[ai]], writes=[r_osb[qb]])
                    if hh == 3:
                        for sub in range(nsub):
                            ii = i0 + sub * 128
                            sc.dma(lambda e, sub=sub, ii=ii: e.dma_start(
                                out=ODv[ii:ii + 128, r, :], in_=ob[:, sub, :, :].rearrange("p a b -> p (a b)")),
                                reads=[r_osb[qb]], q="act")
                return (ai, fa, after)

            for hh in range(4):
                pp, half = hh // 2, hh % 2
                lo, hi = half * 64, half * 64 + 64
                ai = hcount[0] % 2
                hcount[0] += 1
                js = [j for j in range(0, nsub + 1) if not (j == 0 and i0 == 0)]
                groups, cur, tot = [], [], 0
                for j in js:
                    n_ = 128 if (j == 0 or j == nsub) else 256
                    if tot + n_ > 512:
                        groups.append(cur)
                        cur, tot = [], 0
                    cur.append((j, n_))
                    tot += n_
                if cur:
                    groups.append(cur)
                for gn, grp in enumerate(groups):
                    parts = []
                    off = 0
                    j_first = grp[0][0]
                    t0 = 0 if j_first == 0 else 128 + 256 * (j_first - 1)
                    for (j, n_) in grp:
                        k0 = p0 + 128 * (j - 1)
                        kt = k0 // 128
                        c0 = 0 if j == 0 else 128 * (j - 1)
                        parts.append((KD[:, pp, half, k0:k0 + 128], QD[qb][:, pp, c0:c0 + n_], [r_KD, r_QD[qb]],
                                      off, n_, VD[:, kt, hh, :], [r_VD], c0))
                        off += n_
                    lastg = (gn == len(groups) - 1)
                    items.append(dict(
                        parts=parts, nk=128, nc=off, c0=0, T=TD[:, hh, t0:t0 + off], T_reads=[r_c],
                        acc=M.bank(3 + ai), r_acc=r_acc[ai], M_rows=128, first=(gn == 0), last=lastg, W=Cn,
                        after=(mk_after(ai, hh) if lastg else None)))

        for ci, p0 in enumerate(range(0, S, Cn)):
            do_chunk(ci, p0)
        for ci, mark in enumerate(marks):
            items[mark] = loadqs[ci + 1] if ci + 1 < len(loadqs) else (lambda: None)
        pipe.run_stream(items)
        sc.barrier()

    for gi in range(3):
        do_group(gi)


def final_phase(sc, M, C, scr, P, x1, x2, gpost):
    M.reset(C["const_end"])
    ident = C["ident"]
    Wno = r3(M.bf(4 * 1024), 4)
    Wdo = r3(M.bf(2 * 1024), 2)
    Wmo = r3(M.bf(8 * 1024), 8)
    stage = [M.f32(1024) for _ in range(2)]
    gq = M.f32(1024)
    onb = [M.bf(512) for _ in range(2)]
    odf = [v4(M.f32(3 * 260), 3, 4) for _ in range(2)]
    mg = [M.bf(2048) for _ in range(2)]
    xr = [M.f32(1024) for _ in range(2)]
    onT = r3(M.bf(4 * 128), 4)
    odT = r3(M.bf(2 * 128), 2)
    odn = r3(M.f32(4 * 65), 4)
    odb = M.bf(256)
    ya = M.f32(1024)
    yb = M.f32(1024)
    ybf = M.bf(1024)
    yT = r3(M.bf(8 * 128), 8)
    yt = M.f32(1024)
    small = M.f32(8)
    r_W = Res()
    r_stage = [Res(), Res()]
    r_gq = Res()
    r_in = [Res(), Res()]
    r_onT, r_odT, r_odn, r_odb, r_ya, r_yb, r_ybf, r_yT, r_yt, r_sm = (Res() for _ in range(10))
    r_T, r_Y1, r_Y2, r_Y3 = Res(), Res(), Res(), Res()
    cnt = [0]

    def load_piece(dst_ap, src_ap, n):
        i = cnt[0] % 2
        cnt[0] += 1
        st = stage[i][:, 0:n]
        sc.dma(lambda e: e.dma_start(out=st, in_=src_ap), writes=[r_stage[i]])
        sc.op("pool", lambda e: e.tensor_copy(out=dst_ap, in_=st), reads=[r_stage[i]], writes=[r_W])

    for k in range(4):
        load_piece(Wno[:, k, :], P["w_nsa_o"][k * 128:(k + 1) * 128, :], 1024)
    for k in range(2):
        load_piece(Wdo[:, k, :], P["w_dil_o"][k * 128:(k + 1) * 128, :], 1024)
    for k in range(8):
        load_piece(Wmo[:, k, :], P["w_mix_out"][k * 128:(k + 1) * 128, :], 1024)
    sc.dma(lambda e: e.dma_start(out=gq, in_=gpost[0, :].partition_broadcast(128)), writes=[r_gq])
    for t in range(32):
        b = t % 2
        rows = slice(t * 128, (t + 1) * 128)
        sc.dma(lambda e, b=b, rows=rows: e.dma_start(out=onb[b], in_=scr["ON"][rows, :]), writes=[r_in[b]])
        for gi in range(3):
            sc.dma(lambda e, b=b, rows=rows, gi=gi: e.dma_start(out=odf[b][:, gi, :, :].rearrange("p a b -> p (a b)"),
                                                               in_=scr["OD"][gi][rows, :]), writes=[r_in[b]])
        sc.dma(lambda e, b=b, rows=rows: e.dma_start(out=mg[b], in_=scr["MG"][rows, :]), writes=[r_in[b]])
        sc.dma(lambda e, b=b, rows=rows: e.dma_start(out=xr[b], in_=x1[rows, :]), writes=[r_in[b]])
        sc.op("dve", lambda e, b=b: e.tensor_tensor(out=odn, in0=odf[b][:, 0, :, :], in1=odf[b][:, 1, :, :], op=ALU.add),
              reads=[r_in[b]], writes=[r_odn])
        sc.op("dve", lambda e, b=b: e.tensor_tensor(out=odn, in0=odn, in1=odf[b][:, 2, :, :], op=ALU.add),
              reads=[r_in[b], r_odn], writes=[r_odn])
        sc.op("dve", lambda e: e.reciprocal(out=small[:, 0:4].unsqueeze(2), in_=odn[:, :, 64:65]), reads=[r_odn], writes=[r_sm])
        sc.op("dve", lambda e: e.tensor_tensor(out=r3(odb, 4), in0=odn[:, :, 0:64],
                                               in1=small[:, 0:4].unsqueeze(2).to_broadcast([128, 4, 64]), op=ALU.mult),
              reads=[r_odn, r_sm], writes=[r_odb])
        Tb = M.bank_bf(0)
        for k in range(4):
            sc.op("pe", lambda e, b=b, k=k: e.transpose(out=Tb[:, k * 128:(k + 1) * 128], in_=onb[b][:, k * 128:(k + 1) * 128],
                                                        identity=ident), reads=[r_in[b]], writes=[r_T])
        for k in range(2):
            sc.op("pe", lambda e, k=k: e.transpose(out=Tb[:, (4 + k) * 128:(5 + k) * 128], in_=odb[:, k * 128:(k + 1) * 128],
                                                   identity=ident), reads=[r_odb], writes=[r_T])
        sc.op("act", lambda e: e.copy(out=onT, in_=r3(Tb[:, 0:512], 4)), reads=[r_T], writes=[r_onT])
        sc.op("act", lambda e: e.copy(out=odT, in_=r3(Tb[:, 512:768], 2)), reads=[r_T], writes=[r_odT])
        for dh in range(2):
            Y1 = M.bank(1 + dh)
            for k in range(4):
                sc.op("pe", lambda e, Y1=Y1, k=k, dh=dh: e.matmul(Y1, lhsT=onT[:, k, :], rhs=Wno[:, k, dh * 512:(dh + 1) * 512],
                                                                  start=(k == 0), stop=(k == 3)),
                      reads=[r_onT, r_W], writes=[r_Y1])
            Y2 = M.bank(3 + dh)
            for k in range(2):
                sc.op("pe", lambda e, Y2=Y2, k=k, dh=dh: e.matmul(Y2, lhsT=odT[:, k, :], rhs=Wdo[:, k, dh * 512:(dh + 1) * 512],
                                                                  start=(k == 0), stop=(k == 1)),
                      reads=[r_odT, r_W], writes=[r_Y2])
            sl = slice(dh * 512, (dh + 1) * 512)
            sc.op("dve", lambda e, Y1=Y1, b=b, sl=sl: e.tensor_tensor(out=ya[:, sl], in0=Y1, in1=mg[b][:, sl], op=ALU.mult),
                  reads=[r_Y1, r_in[b]], writes=[r_ya])
            sl2 = slice(1024 + dh * 512, 1024 + (dh + 1) * 512)
            sc.op("dve", lambda e, Y2=Y2, b=b, sl=sl, sl2=sl2: e.tensor_tensor(out=yb[:, sl], in0=Y2, in1=mg[b][:, sl2], op=ALU.mult),
                  reads=[r_Y2, r_in[b]], writes=[r_yb])
        sc.op("pool", lambda e: e.tensor_tensor(out=ybf, in0=ya, in1=yb, op=ALU.add), reads=[r_ya, r_yb], writes=[r_ybf])
        Tb2 = M.bank_bf(7)
        for k in range(8):
            sc.op("pe", lambda e, k=k: e.transpose(out=Tb2[:, k * 128:(k + 1) * 128], in_=ybf[:, k * 128:(k + 1) * 128],
                                                   identity=ident), reads=[r_ybf], writes=[r_Y3])
        sc.op("act", lambda e: e.copy(out=yT, in_=r3(Tb2, 8)), reads=[r_Y3], writes=[r_yT])
        for dh in range(2):
            Y = M.bank(5 + dh)
            for k in range(8):
                sc.op("pe", lambda e, Y=Y, k=k, dh=dh: e.matmul(Y, lhsT=yT[:, k, :], rhs=Wmo[:, k, dh * 512:(dh + 1) * 512],
                                                                start=(k == 0), stop=(k == 7)),
                      reads=[r_yT, r_W], writes=[r_T if False else r_Y1 if False else r_yt_ps])
        ssa, ssb = small[:, 4:5], small[:, 5:6]
        sc.op("pool", lambda e: e.memset(small[:, 4:6], 0.0), writes=[r_sm])
        sc.op("act", lambda e: e.activation(out=yt[:, 0:512], in_=M.bank(5), func=AF.Square, accum_out=ssa),
              reads=[r_yt_ps], writes=[r_yt, r_sm])
        sc.op("act", lambda e: e.activation(out=yt[:, 512:1024], in_=M.bank(6), func=AF.Square, accum_out=ssb),
              reads=[r_yt_ps], writes=[r_yt, r_sm])
        sc.op("dve", lambda e: e.tensor_tensor(out=ssa, in0=ssa, in1=ssb, op=ALU.add), reads=[r_sm], writes=[r_sm])
        sc.op("dve", lambda e: e.tensor_scalar(out=ssa, in0=ssa, scalar1=1.0 / D, scalar2=EPS, op0=ALU.mult, op1=ALU.add),
              reads=[r_sm], writes=[r_sm])
        sc.op("act", lambda e: e.sqrt(ssa, ssa), reads=[r_sm], writes=[r_sm])
        sc.op("dve", lambda e: e.reciprocal(out=ssa, in_=ssa), reads=[r_sm], writes=[r_sm])
        for dh in range(2):
            sc.op("dve", lambda e, dh=dh: e.scalar_tensor_tensor(
                out=yt[:, dh * 512:(dh + 1) * 512], in0=M.bank(5 + dh), scalar=ssa,
                in1=gq[:, dh * 512:(dh + 1) * 512], op0=ALU.mult, op1=ALU.mult),
                reads=[r_yt_ps, r_sm, r_gq], writes=[r_yt])
        sc.op("dve", lambda e, b=b: e.tensor_tensor(out=xr[b], in0=yt, in1=xr[b], op=ALU.add),
              reads=[r_yt, r_in[b]], writes=[r_in[b]])
        sc.dma(lambda e, b=b, rows=rows: e.dma_start(out=x2[rows, :], in_=xr[b]), reads=[r_in[b]], q="act")
    sc.barrier()


r_yt_ps = Res()


IN_SHAPES = (
    ("ffn1_pre", [1, D]), ("ffn1_post", [1, D]), ("ffn1_w_gate", [D, DFF]), ("ffn1_w_up", [D, DFF]),
    ("ffn1_w_down", [DFF, D]), ("mix_pre", [1, D]), ("mix_post", [1, D]), ("w_in", [D, 5656]),
    ("nsa_pe_k", [32, 64]), ("nsa_w_ck1", [2048, 256]), ("nsa_w_ck2", [256, 64]),
    ("nsa_pe_v", [32, 64]), ("nsa_w_cv1", [2048, 256]), ("nsa_w_cv2", [256, 64]),
    ("w_nsa_o", [512, D]), ("w_dil_o", [256, D]), ("w_mix_out", [D, D]),
    ("ffn2_pre", [1, D]), ("ffn2_post", [1, D]), ("ffn2_w_gate", [D, DFF]),
    ("ffn2_w_up", [D, DFF]), ("ffn2_w_down", [DFF, D]))


def _consts():
    import ml_dtypes
    bf = ml_dtypes.bfloat16
    c = {}
    c["c_ident"] = np.eye(128, dtype=np.float32).astype(bf)
    c["c_identf"] = np.eye(128, dtype=np.float32)
    n_cmp = 255
    cs = np.arange(256) * 16
    ss = np.arange(64) * 64
    ov = np.clip(np.minimum(cs[:, None] + 32, ss[None, :] + 64) - np.maximum(cs[:, None], ss[None, :]), 0, None) / 32.0
    ov[255] = 0
    c["c_ov"] = ov.astype(np.float32).astype(bf)
    qi = np.arange(512, dtype=np.float64)
    c["c_tq"] = np.stack([-s_ * qi for s_ in NSA_SL]).astype(np.float32)
    c["c_td"] = np.stack([-DIL_SL[h] * DILS[h // 4] * qi for h in range(12)]).astype(np.float32)
    ki = np.arange(128)[:, None]
    qq = np.arange(128)[None, :]
    mk = np.zeros((3, 128, 128), np.float32)
    mk[0] = np.where(qq >= ki, 0.0, NEG)
    mk[1] = np.where(qq < ki, 0.0, NEG)
    mk[2] = np.where(qq <= ki, 0.0, NEG)
    c["c_masks"] = mk
    tqa = np.zeros((8, 128, 512), np.float64)
    for h in range(8):
        tqa[h] = -NSA_SL[h] * qi[None, :]
        tqa[h][:, 0:128] += mk[0]
    c["c_tqa"] = tqa.astype(np.float32)
    mo = np.zeros((128, 512), np.float32)
    mo[:, 0:128] = mk[0]
    c["c_masko"] = mo
    tq8 = np.zeros((128, 8, 4), np.float64)
    for h in range(8):
        for sb in range(4):
            tq8[:, h, sb] = -8.0 * NSA_SL[h] * (128 * sb + np.arange(128))
    c["c_tq8"] = tq8.astype(np.float32)
    tdm = np.zeros((12, 128, 384), np.float64)
    for h in range(12):
        sl = DIL_SL[h] * DILS[h // 4]
        tdm[h][:, 0:256] = -sl * qi[None, 0:256]
        tdm[h][:, 0:128] += mk[0]
        tdm[h][:, 128:256] += mk[2]
        tdm[h][:, 256:384] = -sl * qi[None, 0:128] + mk[2]
    c["c_tdm"] = tdm.astype(np.float32)
    tds = np.zeros((12, 128, 1152), np.float64)
    kcol = np.arange(128, dtype=np.float64)[:, None]
    for h in range(12):
        sl = DIL_SL[h] * DILS[h // 4]
        t2a = -sl * (qi[None, 0:256] - kcol)
        t2a[:, 0:128] += mk[0]
        t2a[:, 128:256] += mk[2]
        t2b = -sl * (128.0 + qi[None, 0:128] - kcol) + mk[2]
        tds[h][:, 0:128] = t2b
        for r_ in range(4):
            tds[h][:, 128 + 256 * r_:128 + 256 * (r_ + 1)] = t2a
    c["c_tds"] = tds.astype(np.float32)
    kk = np.arange(128, dtype=np.float64)
    bc = np.zeros((128, 8, 35), np.float64)
    for h in range(8):
        for idx in range(35):
            bc[:, h, idx] = NSA_SL[h] * (kk - 128 * (idx - 3))
    c["c_bc"] = bc.astype(np.float32)
    bcc = np.zeros((128, 8, 8, 2), np.float64)
    for h in range(8):
        for qc in range(8):
            for ct in range(2):
                bcc[:, h, qc, ct] = NSA_SL[h] * (16 * (128 * ct + kk) + 31 - 512 * qc)
    c["c_bcc"] = bcc.astype(np.float32)
    bcd = np.zeros((128, 12, 5), np.float64)
    for h in range(12):
        for j in range(5):
            bcd[:, h, j] = DIL_SL[h] * DILS[h // 4] * (kk - 128 * (1 - j))
    c["c_bcd"] = bcd.astype(np.float32)
    mc = np.zeros((8, 128, 2, 512), np.float32)
    for qc in range(8):
        for ct in range(2):
            cc = 128 * ct + np.arange(128)[:, None]
            qpos = 512 * qc + np.arange(512)[None, :]
            ok = (qpos >= 16 * cc + 31) & (cc < 255)
            mc[qc, :, ct, :] = np.where(ok, 0.0, NEG)
    c["c_mc"] = mc
    oh = np.zeros((64, 32, 128), np.float32)
    for kt in range(32):
        oh[2 * kt, kt, 0:64] = 1
        oh[2 * kt + 1, kt, 64:128] = 1
    c["c_oh"] = oh.astype(bf)
    pos = np.arange(S)[:, None]
    jb = np.arange(64)[None, :]
    own = pos // 64
    forced = (jb == 0) | (jb == own) | (jb == own - 1)
    valid = jb * 64 <= pos
    m1 = np.where(forced, 0.0, np.where(valid, 1.0, 0.0))
    m2 = np.where(forced, 1.0e9 + 1.0e4 * jb, np.where(valid, 0.0, -1.0e9 - 1.0e4 * jb))
    c["c_selm"] = np.stack([m1, m2], axis=1).astype(np.float32)
    return c


CONST_SHAPES = (("c_ident", [128, 128], BF16), ("c_identf", [128, 128], F32), ("c_ov", [256, 64], BF16),
                ("c_tq", [8, 512], F32), ("c_td", [12, 512], F32), ("c_masks", [3, 128, 128], F32),
                ("c_bc", [128, 8, 35], F32), ("c_bcc", [128, 8, 8, 2], F32), ("c_bcd", [128, 12, 5], F32),
                ("c_mc", [8, 128, 2, 512], F32), ("c_oh", [64, 32, 128], BF16), ("c_selm", [S, 2, 64], F32),
                ("c_tqa", [8, 128, 512], F32), ("c_tdm", [12, 128, 384], F32),
                ("c_masko", [128, 512], F32), ("c_tq8", [128, 8, 4], F32),
                ("c_tds", [12, 128, 1152], F32))


def build_nc():
    nc = bass.Bass("TRN2", target_bir_lowering=False)
    dt = lambda name, shape, dtype=F32, kind="ExternalInput": nc.dram_tensor(name, list(shape), dtype, kind=kind).ap()
    x = dt("x", [S, D])
    out = dt("out", [S, D], kind="ExternalOutput")
    P = {}
    for nm, shp in IN_SHAPES:
        P[nm] = dt(nm, shp)
    for nm, shp, ty in CONST_SHAPES:
        P[nm] = dt(nm, shp, ty)
    I = "Internal"
    if DBG_OUT:
        I = "ExternalOutput"
    x1 = dt("x1", [S, D], kind=I)
    x2 = dt("x2", [S, D], kind=I)
    scr = {
        "QN": dt("s_qn", [4, 128, S], BF16, I), "KC": dt("s_kc", [2, 128, S], F32, I),
        "KS": dt("s_ks", [2, 128, S], BF16, I), "KW": dt("s_kw", [2, 128, S], BF16, I),
        "VS": dt("s_vs", [S, 130], BF16, I), "VW": dt("s_vw", [S, 130], BF16, I),
        "GN": dt("s_gn", [S, 24], F32, I), "QD": dt("s_qd", [6, 128, S], BF16, I),
        "KD": dt("s_kd", [6, 128, S], BF16, I), "VD": dt("s_vd", [3, S, 260], BF16, I),
        "MG": dt("s_mg", [S, 2048], BF16, I), "ON": dt("s_on", [S, 512], BF16, I),
        "OD": dt("s_od", [3, S, 260], F32, I),
    }

    import contextlib
    with contextlib.ExitStack() as st:
        big = st.enter_context(nc.sbuf_tensor("big", [128, SBUF_BYTES // 4], F32))
        ps = st.enter_context(nc.psum_tensor("ps", [128, 4096], F32))
        M = Mem(big, ps)
        sc = Sched()
        C = {"r_dram": {}}
        ident = M.bf(128)
        identf = M.f32(128)
        C["kcT"] = r3(M.bf(2 * 256), 2)
        C["VC"] = M.bf(2 * 2 * 128)
        C["r_kcT"], C["r_VC"] = Res(), Res()
        r_ident = Res()
        sc.dma(lambda e: e.dma_start(out=ident, in_=P["c_ident"]), writes=[r_ident])
        sc.dma(lambda e: e.dma_start(out=identf, in_=P["c_identf"]), writes=[r_ident])
        C["ident"] = ident
        C["identf"] = identf
        zeros = M.bf(512)
        r_z = Res()
        sc.op("pool", lambda e: e.memset(zeros, 0.0), writes=[r_z])
        C["zeros"] = zeros
        ZEROS[0] = zeros
        C["const_end"] = M.off
        sc.barrier()
        if STAGE == 1:
            ffn_phase(sc, M, C, x, out, P["ffn1_w_gate"], P["ffn1_w_up"], P["ffn1_w_down"], P["ffn1_pre"], P["ffn1_post"])
        else:
            if DBG_SKIP_FFN1:
                x1 = x
            else:
                ffn_phase(sc, M, C, x, x1, P["ffn1_w_gate"], P["ffn1_w_up"], P["ffn1_w_down"], P["ffn1_pre"], P["ffn1_post"])
            if DBG_UPTO >= 1:
                proj_phase(sc, M, C, x1, P["w_in"], P["mix_pre"], scr)
            if DBG_UPTO >= 2:
                cmp_phase(sc, M, C, scr, P)
            if DBG_UPTO >= 3:
                nsa_phase(sc, M, C, scr, P)
            if DBG_UPTO >= 4:
                dil_phase(sc, M, C, scr, P)
            if DBG_UPTO >= 5:
                final_phase(sc, M, C, scr, P, x1, out if STAGE == 2 else x2, P["mix_post"])
            if STAGE >= 3:
                ffn_phase(sc, M, C, x2, out, P["ffn2_w_gate"], P["ffn2_w_up"], P["ffn2_w_down"], P["ffn2_pre"], P["ffn2_post"])
        sc.barrier()
        sc.emit(nc)
    return nc


DBG_SKIP_FFN1 = False
DBG_UPTO = 9
DBG_OUT = False
DBG_RES = None
_NC = None


def kernel(**inputs):
    global _NC
    if _NC is None:
        _NC = build_nc()
    nc = _NC
    x = np.ascontiguousarray(inputs["x"], dtype=np.float32)
    consts = _consts()
    shared = {}
    for nm, shp in IN_SHAPES:
        shared[nm] = np.ascontiguousarray(np.asarray(inputs[nm], dtype=np.float32)[0])
    in_maps = []
    for b in range(8):
        m = {"x": x[b]}
        m.update(shared)
        m.update(consts)
        in_maps.append(m)
    res = run_bass_kernel_spmd(nc, in_maps, core_ids=list(range(8)))
    if DBG_OUT:
        global DBG_RES
        DBG_RES = res.results[0]
    return np.stack([np.asarray(r["out"]) for r in res.results], axis=0).astype(np.float32)
```

```python
import numpy as np
import concourse.bass as bass
import concourse.mybir as mybir
from concourse.alu_op_type import AluOpType as ALU
from concourse.bass_utils import run_bass_kernel_spmd

F32 = mybir.dt.float32
BF16 = mybir.dt.bfloat16
AF = mybir.ActivationFunctionType

S = 4096
D = 1024
DFF = 2816
NFF = 22
EPS = 1e-6
SBUF_BYTES = 212000
LIMIT = 9
STAGE = 3


class Res:
    __slots__ = ("name", "lw", "rdc", "rdd")

    def __init__(self, name=""):
        self.name = name
        self.lw = None
        self.rdc = {}
        self.rdd = []


ENGS = ("pe", "act", "dve", "pool", "sp")
SAME_ENG_SKIP = ("pe", "sp")


class Sched:
    def __init__(self, nring=28):
        self.ops = {e: [] for e in ENGS}
        self.ndma = 0
        self.nring = nring
        self.last_dma = {}

    def _deps(self, reads, writes):
        dc = {}
        dd = set()

        def add(ev):
            if ev is None:
                return
            if ev[0] == "c":
                if dc.get(ev[1], -1) < ev[2]:
                    dc[ev[1]] = ev[2]
            else:
                dd.add(ev)

        for r in reads:
            add(r.lw)
        for w in writes:
            add(w.lw)
            for e, s in w.rdc.items():
                add(("c", e, s))
            for ev in w.rdd:
                add(ev)
        return dc, dd

    def _mark(self, ev, reads, writes):
        for r in reads:
            if ev[0] == "c":
                if r.rdc.get(ev[1], -1) < ev[2]:
                    r.rdc[ev[1]] = ev[2]
            else:
                r.rdd.append(ev)
        for w in writes:
            w.lw = ev
            w.rdc = {}
            w.rdd = []

    def op(self, eng, fn, reads=(), writes=()):
        dc, dd = self._deps(reads, writes)
        seq = len(self.ops[eng])
        ev = ("c", eng, seq)
        self.ops[eng].append(dict(fn=fn, dc=dc, dd=dd, dma=None))
        self._mark(ev, reads, writes)
        return ev

    def dma(self, fn, reads=(), writes=(), q="sp"):
        dc, dd = self._deps(reads, writes)
        k = self.ndma
        self.ndma += 1
        slot = k % self.nring
        val = 16 * (k // self.nring + 1)
        if k >= self.nring:
            dd.add(("d", slot, val - 16))
        ev = ("d", slot, val)
        self.ops[q].append(dict(fn=fn, dc=dc, dd=dd, dma=slot))
        self._mark(ev, reads, writes)
        self.last_dma[slot] = val
        return ev

    def barrier(self):
        last = {}
        for e in ENGS:
            last[e] = -1
            for i in range(len(self.ops[e]) - 1, -1, -1):
                if self.ops[e][i]["fn"] is not None and self.ops[e][i]["dma"] is None:
                    last[e] = i
                    break
        dds = set(("d", s, v) for s, v in self.last_dma.items())
        for e in ENGS:
            dc = {o: last[o] for o in ENGS if o != e and o != 'sp' and last[o] >= 0}
            self.ops[e].append(dict(fn=None, dc=dc, dd=set(dds), dma=None))

    def emit(self, nc):
        needed = {e: set() for e in ENGS}
        for e in ENGS:
            for o in self.ops[e]:
                for oe, s in o["dc"].items():
                    if oe == e and e in SAME_ENG_SKIP:
                        continue
                    needed[oe].add(s)
        rank = {e: {s: i + 1 for i, s in enumerate(sorted(needed[e]))} for e in ENGS}
        ops = self.ops
        nring = self.nring
        import contextlib

        with contextlib.ExitStack() as st:
            sems = {e: st.enter_context(nc.semaphore("s_" + e)) for e in ENGS}
            ring = [st.enter_context(nc.semaphore("r%d" % i)) for i in range(nring)]
            block = st.enter_context(nc.Block())

            def run(ename):
                def body(eng):
                    known = {}
                    for seq, o in enumerate(ops[ename]):
                        waits = {}
                        for oe, s in o["dc"].items():
                            if oe == ename and ename in SAME_ENG_SKIP:
                                continue
                            key = ("c", oe)
                            v = rank[oe][s]
                            if known.get(key, 0) >= v:
                                continue
                            if waits.get(key, 0) < v:
                                waits[key] = v
                        for ev in o["dd"]:
                            key = ("d", ev[1])
                            v = ev[2]
                            if known.get(key, 0) >= v:
                                continue
                            if waits.get(key, 0) < v:
                                waits[key] = v
                        for key, v in waits.items():
                            sem = sems[key[1]] if key[0] == "c" else ring[key[1]]
                            eng.wait_ge(sem, v)
                            known[key] = v
                        if o["fn"] is None:
                            continue
                        ins = o["fn"](eng)
                        if o["dma"] is not None:
                            ins.then_inc(ring[o["dma"]], 16)
                        elif seq in rank[ename]:
                            ins.then_inc(sems[ename], 1)

                return body

            block.tensor(run("pe"))
            block.scalar(run("act"))
            block.vector(run("dve"))
            block.gpsimd(run("pool"))
            block.sync(run("sp"))


class Mem:
    def __init__(self, big, ps):
        self.big = big
        self.ps = ps
        self.off = 0

    def reset(self, off=0):
        self.off = off

    def f32(self, n, p0=0, p1=128):
        o = (self.off + 31) // 32 * 32
        self.off = o + 4 * n
        assert self.off <= SBUF_BYTES, self.off
        return self.big[p0:p1, o // 4:o // 4 + n]

    def bf(self, n, p0=0, p1=128):
        o = (self.off + 31) // 32 * 32
        self.off = o + 2 * n
        assert self.off <= SBUF_BYTES, self.off
        return self.big[p0:p1, o // 4:o // 4 + (n + 1) // 2].bitcast(BF16)

    def bank(self, b, n=512, o=0):
        return self.ps[:, b * 512 + o:b * 512 + o + n]

    def bank_bf(self, b):
        return self.ps[:, b * 512:(b + 1) * 512].bitcast(BF16)


def r3(ap, a):
    return ap.rearrange("p (a b) -> p a b", a=a)


def ffn_phase(sc, M, C, x_src, x_dst, wg, wu, wd, gpre, gpost):
    M.reset(C["const_end"])
    ident = C["ident"]
    Wg = r3(M.bf(8 * DFF), 8)
    Wu = r3(M.bf(8 * DFF), 8)
    Wd = r3(M.bf(NFF * D), NFF)
    stage = [M.f32(1024) for _ in range(3)]
    gp = M.f32(1024)
    gq = M.f32(1024)
    xp0 = M.f32(1024)
    xp = [xp0, xp0]
    xr = [M.f32(1024) for _ in range(2)]
    hb0 = M.bf(1024)
    hb = [hb0, hb0]
    hT = r3(M.bf(8 * 512), 8)
    AT = r3(M.bf(NFF * 512), NFF)
    sg0 = M.f32(512)
    sg = [sg0, sg0]
    yt = M.f32(1024)
    small = M.f32(32)

    r_stage = [Res("stage%d" % i) for i in range(3)]
    r_Wg = [[Res() for _ in range(4)] for _ in range(8)]
    r_Wu = [[Res() for _ in range(4)] for _ in range(8)]
    r_Wd = [Res() for _ in range(NFF)]
    r_gp, r_gq = Res(), Res()
    r_xp0 = Res()
    r_xp = [r_xp0, r_xp0]
    r_xr = [Res(), Res()]
    r_hb0 = Res()
    r_hb = [r_hb0, r_hb0]
    r_hT, r_AT = Res(), [Res() for _ in range(NFF)]
    r_sg0 = Res()
    r_sg = [r_sg0, r_sg0]
    r_yt = Res()
    r_small = [Res() for _ in range(8)]
    r_T = Res()
    r_G = [Res(), Res()]
    r_U = [Res(), Res()]
    r_Y = Res()
    r_xdst = C["r_dram"][id(x_dst)] if id(x_dst) in C["r_dram"] else Res()
    r_xsrc = C["r_dram"].get(id(x_src), Res())
    C["r_dram"][id(x_dst)] = r_xdst

    sc.dma(lambda e: e.dma_start(out=gp, in_=gpre[0, :].partition_broadcast(128)), writes=[r_gp])
    sc.dma(lambda e: e.dma_start(out=gq, in_=gpost[0, :].partition_broadcast(128)), writes=[r_gq])
    sc.op("act", lambda e: e.mul(gq, gq, 0.5), reads=[r_gq], writes=[r_gq])

    cnt = [0]
    CB = [(0, 768), (768, 768), (1536, 768), (2304, 512)]

    def load_piece(dst_ap, src_ap, n, rdst):
        i = cnt[0] % 3
        eng = ("pool", "act", "dve")[cnt[0] % 3]
        cnt[0] += 1
        st = stage[i][:, 0:n]
        sc.dma(lambda e: e.dma_start(out=st, in_=src_ap), writes=[r_stage[i]])
        if eng == "act":
            sc.op("act", lambda e: e.copy(out=dst_ap, in_=st), reads=[r_stage[i]], writes=[rdst])
        else:
            sc.op(eng, lambda e: e.tensor_copy(out=dst_ap, in_=st), reads=[r_stage[i]], writes=[rdst])

    def load_weights():
        for cb, (c0, w) in enumerate(CB):
            for kc in range(8):
                for (W, wsrc, rW) in ((Wg, wg, r_Wg), (Wu, wu, r_Wu)):
                    load_piece(W[:, kc, c0:c0 + w], wsrc[kc * 128:(kc + 1) * 128, c0:c0 + w], w, rW[kc][cb])
        for f in range(NFF):
            load_piece(Wd[:, f, :], wd[f * 128:(f + 1) * 128, :], 1024, r_Wd[f])

    inv_d = 1.0 / D

    def rstd_from(ss_ap, out_ap, rs):
        sc.op("dve", lambda e: e.tensor_scalar(out=out_ap, in0=ss_ap, scalar1=inv_d, scalar2=EPS,
                                               op0=ALU.mult, op1=ALU.add), reads=[rs], writes=[rs])
        sc.op("act", lambda e: e.sqrt(out_ap, out_ap), reads=[rs], writes=[rs])
        sc.op("dve", lambda e: e.reciprocal(out=out_ap, in_=out_ap), reads=[rs], writes=[rs])

    NT = S // 512
    Tb = M.bank_bf(0)

    def prep(i):
        for j in range(4):
            t = i * 4 + j
            b = t % 2
            x_ap = xp[b]
            sc.dma(lambda e, x_ap=x_ap, t=t: e.dma_start(out=x_ap, in_=x_src[t * 128:(t + 1) * 128, :]),
                   reads=[r_xsrc], writes=[r_xp[b]])
            ss = small[:, b:b + 1]
            sc.op("pool", lambda e, ss=ss: e.memset(ss, 0.0), writes=[r_small[b]])
            sc.op("act", lambda e, x_ap=x_ap, b=b, ss=ss: e.activation(out=hb[b], in_=x_ap, func=AF.Square, accum_out=ss),
                  reads=[r_xp[b]], writes=[r_hb[b], r_small[b]])
            rstd_from(ss, ss, r_small[b])
            sc.op("dve", lambda e, x_ap=x_ap, b=b, ss=ss: e.scalar_tensor_tensor(
                out=hb[b], in0=x_ap, scalar=ss, in1=gp, op0=ALU.mult, op1=ALU.mult),
                reads=[r_xp[b], r_small[b], r_gp], writes=[r_hb[b]])
            for kc in range(8):
                sc.op("pe", lambda e, b=b, kc=kc: e.transpose(out=Tb[:, kc * 128:(kc + 1) * 128],
                                                             in_=hb[b][:, kc * 128:(kc + 1) * 128], identity=ident),
                      reads=[r_hb[b]], writes=[r_T])
            sc.op("act", lambda e, j=j: e.copy(out=hT[:, :, j * 128:(j + 1) * 128], in_=r3(Tb, 8)),
                  reads=[r_T], writes=[r_hT])

    def gateup(i):
        for f in range(NFF):
            pb = f % 2
            G = M.bank(1 + pb)
            U = M.bank(3 + pb)
            for kc in range(8):
                sc.op("pe", lambda e, G=G, kc=kc, f=f: e.matmul(G, lhsT=Wg[:, kc, f * 128:(f + 1) * 128], rhs=hT[:, kc, :],
                                                               start=(kc == 0), stop=(kc == 7)),
                      reads=[r_Wg[kc][min(3, (f * 128) // 768)], r_hT], writes=[r_G[pb]])
            for kc in range(8):
                sc.op("pe", lambda e, U=U, kc=kc, f=f: e.matmul(U, lhsT=Wu[:, kc, f * 128:(f + 1) * 128], rhs=hT[:, kc, :],
                                                               start=(kc == 0), stop=(kc == 7)),
                      reads=[r_Wu[kc][min(3, (f * 128) // 768)], r_hT], writes=[r_U[pb]])
            sc.op("act", lambda e, G=G, pb=pb: e.activation(out=sg[pb], in_=G, func=AF.Silu),
                  reads=[r_G[pb]], writes=[r_sg[pb]])
            sc.op("dve", lambda e, U=U, pb=pb, f=f: e.tensor_tensor(out=AT[:, f, :], in0=sg[pb], in1=U, op=ALU.mult),
                  reads=[r_sg[pb], r_U[pb]], writes=[r_AT[f]])

    YB = [(5, 6), (7, 0)]
    r_Yp = [[Res()], [Res(), r_T]]

    def down(i):
        for j in range(4):
            t = i * 4 + j
            b = t % 2
            yp = j % 2
            rY = r_Yp[yp]
            sc.dma(lambda e, b=b, t=t: e.dma_start(out=xr[b], in_=x_src[t * 128:(t + 1) * 128, :]),
                   reads=[r_xsrc], writes=[r_xr[b]])
            for dh in range(2):
                Y = M.bank(YB[yp][dh])
                for f in range(NFF):
                    sc.op("pe", lambda e, Y=Y, f=f, j=j, dh=dh: e.matmul(
                        Y, lhsT=AT[:, f, j * 128:(j + 1) * 128], rhs=Wd[:, f, dh * 512:(dh + 1) * 512],
                        start=(f == 0), stop=(f == NFF - 1)),
                        reads=[r_AT[f], r_Wd[f]], writes=rY)
            ssa = small[:, 4:5]
            ssb = small[:, 5:6]
            Y0, Y1 = M.bank(YB[yp][0]), M.bank(YB[yp][1])
            sc.op("pool", lambda e: e.memset(small[:, 4:6], 0.0), writes=[r_small[4]])
            sc.op("act", lambda e, ssa=ssa, Y0=Y0: e.activation(out=yt[:, 0:512], in_=Y0, func=AF.Square, accum_out=ssa),
                  reads=rY, writes=[r_yt, r_small[4]])
            sc.op("act", lambda e, ssb=ssb, Y1=Y1: e.activation(out=yt[:, 512:1024], in_=Y1, func=AF.Square, accum_out=ssb),
                  reads=rY, writes=[r_yt, r_small[4]])
            sc.op("dve", lambda e, ssa=ssa, ssb=ssb: e.tensor_tensor(out=ssa, in0=ssa, in1=ssb, op=ALU.add),
                  reads=[r_small[4]], writes=[r_small[4]])
            rstd_from(ssa, ssa, r_small[4])
            for dh in range(2):
                Yd = M.bank(YB[yp][dh])
                sc.op("dve", lambda e, dh=dh, ssa=ssa, Yd=Yd: e.scalar_tensor_tensor(
                    out=yt[:, dh * 512:(dh + 1) * 512], in0=Yd, scalar=ssa,
                    in1=gq[:, dh * 512:(dh + 1) * 512], op0=ALU.mult, op1=ALU.mult),
                    reads=rY + [r_small[4], r_gq], writes=[r_yt])
            sc.op("dve", lambda e, b=b: e.tensor_tensor(out=xr[b], in0=yt, in1=xr[b], op=ALU.add),
                  reads=[r_yt, r_xr[b]], writes=[r_xr[b]])
            sc.dma(lambda e, b=b, t=t: e.dma_start(out=x_dst[t * 128:(t + 1) * 128, :], in_=xr[b]),
                   reads=[r_xr[b]], writes=[r_xdst], q="act")

    if LIMIT >= 1:
        prep(0)
    load_weights()
    for i in range(NT):
        if LIMIT == 2 and i == 0:
            gateup(i)
        if LIMIT == 3 and i == 0:
            gateup(i)
            down(i)
        if LIMIT < 9:
            continue
        gateup(i)
        if i + 1 < NT:
            prep(i + 1)
        down(i)
    sc.barrier()


NEG = -30000.0
ZEROS = [None]
NSA_SL = [2.0 ** (-(i + 1)) for i in range(8)]
DIL_SL = [2.0 ** (-8.0 * (i + 1) / 12) for i in range(12)]
DILS = (1, 4, 16)
C_QN, C_KV, C_GN, C_QD, C_KD, C_VD, C_MG = 0, 512, 1280, 1304, 2072, 2840, 3608


def v4(ap, a, b):
    return ap.rearrange("p (a b c) -> p a b c", a=a, b=b)


def norm_to_hT(sc, M, x_src, t, xp, hb, small, gp, ident, rr, dstT, r_dst, Tb, r_T):
    b = t % 2
    r_xp, r_hb, r_small, r_gp = rr
    sc.dma(lambda e: e.dma_start(out=xp[b], in_=x_src[t * 128:(t + 1) * 128, :]), writes=[r_xp[b]])
    ss = small[:, b:b + 1]
    sc.op("pool", lambda e: e.memset(ss, 0.0), writes=[r_small[b]])
    sc.op("act", lambda e: e.activation(out=hb[b], in_=xp[b], func=AF.Square, accum_out=ss),
          reads=[r_xp[b]], writes=[r_hb[b], r_small[b]])
    sc.op("dve", lambda e: e.tensor_scalar(out=ss, in0=ss, scalar1=1.0 / D, scalar2=EPS, op0=ALU.mult, op1=ALU.add),
          reads=[r_small[b]], writes=[r_small[b]])
    sc.op("act", lambda e: e.sqrt(ss, ss), reads=[r_small[b]], writes=[r_small[b]])
    sc.op("dve", lambda e: e.reciprocal(out=ss, in_=ss), reads=[r_small[b]], writes=[r_small[b]])
    sc.op("dve", lambda e: e.scalar_tensor_tensor(out=hb[b], in0=xp[b], scalar=ss, in1=gp, op0=ALU.mult, op1=ALU.mult),
          reads=[r_xp[b], r_small[b], r_gp], writes=[r_hb[b]])
    for kc in range(8):
        sc.op("pe", lambda e, kc=kc: e.transpose(out=Tb[:, kc * 128:(kc + 1) * 128],
                                                 in_=hb[b][:, kc * 128:(kc + 1) * 128], identity=ident),
              reads=[r_hb[b]], writes=[r_T])
    sc.op("act", lambda e: e.copy(out=dstT, in_=r3(Tb, 8)), reads=[r_T], writes=[r_dst])


def proj_phase(sc, M, C, x1, w_in, gmix, scr):
    M.reset(C["const_end"])
    ident = C["ident"]
    h2T = r3(M.bf(8 * S), 8)
    Wb = [r3(M.bf(8 * 512), 8) for _ in range(2)]
    stage = [M.f32(512) for _ in range(2)]
    gp = M.f32(1024)
    xp = [M.f32(1024) for _ in range(2)]
    hb = [M.bf(1024) for _ in range(2)]
    small = M.f32(8)
    evf = [M.f32(512) for _ in range(2)]
    evb = [M.bf(512) for _ in range(2)]
    vaug = [M.bf(4 * 65) for _ in range(2)]
    r_h2T = [Res() for _ in range(32)]
    r_Wb = [Res(), Res()]
    r_stage = [Res(), Res()]
    r_gp = Res()
    rr = ([Res(), Res()], [Res(), Res()], [Res(), Res()], r_gp)
    r_ev = [Res(), Res()]
    r_va = [Res(), Res()]
    r_T = Res()
    r_ps = [Res(), Res()]
    Tb = M.bank_bf(0)
    sc.dma(lambda e: e.dma_start(out=gp, in_=gmix[0, :].partition_broadcast(128)), writes=[r_gp])
    for i in range(2):
        sc.op("pool", lambda e, i=i: e.memset(vaug[i], 1.0), writes=[r_va[i]])
    for t in range(32):
        norm_to_hT(sc, M, x1, t, xp, hb, small, gp, ident, rr, h2T[:, :, t * 128:(t + 1) * 128], r_h2T[t], Tb, r_T)

    cnt = [0]
    nspec = [0]

    def load_wb(bi, cols):
        for kc in range(8):
            off = 0
            for (c, w) in cols:
                i = cnt[0] % 2
                cnt[0] += 1
                st = stage[i][:, 0:w]
                sc.dma(lambda e, st=st, c=c, w=w, kc=kc: e.dma_start(out=st, in_=w_in[kc * 128:(kc + 1) * 128, c:c + w]),
                       writes=[r_stage[i]])
                dst = Wb[bi][:, kc, off:off + w]
                sc.op("pool", lambda e, st=st, dst=dst: e.tensor_copy(out=dst, in_=st), reads=[r_stage[i]], writes=[r_Wb[bi]])
                off += w

    ecnt = [0]

    def run_fm(cols, dst, fp32, dil):
        bi = nspec[0] % 2
        nspec[0] += 1
        load_wb(bi, cols)
        Ls = S // dil
        Cn = min(512, Ls)
        h4 = h2T.rearrange("p k (i r) -> p k i r", r=dil)
        for p0 in range(0, S, Cn):
            r, i0 = divmod(p0, Ls)
            k = ecnt[0] % 2
            ecnt[0] += 1
            PS = M.bank(1 + k)[:, 0:Cn]
            for kc in range(8):
                sc.op("pe", lambda e, PS=PS, kc=kc, i0=i0, r=r: e.matmul(
                    PS, lhsT=Wb[bi][:, kc, 0:128], rhs=h4[:, kc, i0:i0 + Cn, r], start=(kc == 0), stop=(kc == 7)),
                    reads=[r_Wb[bi]] + r_h2T, writes=[r_ps[k]])
            evt = (evf[k] if fp32 else evb[k])[:, 0:Cn]
            sc.op("act", lambda e, PS=PS, evt=evt: e.copy(out=evt, in_=PS), reads=[r_ps[k]], writes=[r_ev[k]])
            sc.dma(lambda e, evt=evt, p0=p0: e.dma_start(out=dst[:, p0:p0 + Cn], in_=evt), reads=[r_ev[k]], q="act")

    def run_tm(c0, n, dst, kind, dil, nh=0):
        bi = nspec[0] % 2
        nspec[0] += 1
        load_wb(bi, [(c0, n)])
        Ls = S // dil
        h4 = h2T.rearrange("p k (i r) -> p k i r", r=dil)
        for t in range(32):
            p0 = t * 128
            r, i0 = divmod(p0, Ls)
            k = ecnt[0] % 2
            ecnt[0] += 1
            PS = M.bank(1 + k)[:, 0:n]
            for kc in range(8):
                sc.op("pe", lambda e, PS=PS, kc=kc, i0=i0, r=r: e.matmul(
                    PS, lhsT=h4[:, kc, i0:i0 + 128, r], rhs=Wb[bi][:, kc, 0:n], start=(kc == 0), stop=(kc == 7)),
                    reads=[r_Wb[bi]] + r_h2T, writes=[r_ps[k]])
            if kind == "aug":
                va = vaug[k][:, 0:nh * 65]
                sc.op("act", lambda e, PS=PS, va=va: e.copy(out=r3(va, nh)[:, :, 0:64], in_=r3(PS, nh)),
                      reads=[r_ps[k]], writes=[r_va[k]])
                sc.dma(lambda e, va=va, p0=p0: e.dma_start(out=dst[p0:p0 + 128, :], in_=va), reads=[r_va[k]], q="act")
            elif kind == "sigf":
                evt = evf[k][:, 0:n]
                sc.op("act", lambda e, PS=PS, evt=evt: e.activation(out=evt, in_=PS, func=AF.Sigmoid),
                      reads=[r_ps[k]], writes=[r_ev[k]])
                sc.dma(lambda e, evt=evt, p0=p0: e.dma_start(out=dst[p0:p0 + 128, :], in_=evt), reads=[r_ev[k]], q="act")
            else:
                evt = evb[k][:, 0:n]
                sc.op("act", lambda e, PS=PS, evt=evt: e.activation(out=evt, in_=PS, func=AF.Sigmoid),
                      reads=[r_ps[k]], writes=[r_ev[k]])
                sc.dma(lambda e, evt=evt, p0=p0: e.dma_start(out=dst[p0:p0 + 128, :], in_=evt), reads=[r_ev[k]], q="act")

    for p in range(4):
        run_fm([(C_QN + p * 128, 128)], scr["QN"][p], False, 1)
    run_fm([(C_KV, 128)], scr["KC"][0], True, 1)
    run_fm([(C_KV + 128, 128)], scr["KC"][1], True, 1)
    for g in range(2):
        run_fm([(C_KV + 256 + g * 64, 64)] * 2, scr["KS"][g], False, 1)
        run_fm([(C_KV + 512 + g * 64, 64)] * 2, scr["KW"][g], False, 1)
    run_tm(C_KV + 384, 128, scr["VS"], "aug", 1, nh=2)
    run_tm(C_KV + 640, 128, scr["VW"], "aug", 1, nh=2)
    run_tm(C_GN, 24, scr["GN"], "sigf", 1)
    for gi in range(3):
        for pp in range(2):
            run_fm([(C_QD + gi * 256 + pp * 128, 128)], scr["QD"][gi * 2 + pp], False, DILS[gi])
            run_fm([(C_KD + gi * 256 + pp * 128, 128)], scr["KD"][gi * 2 + pp], False, DILS[gi])
        run_tm(C_VD + gi * 256, 256, scr["VD"][gi], "aug", DILS[gi], nh=4)
    for q in range(4):
        run_tm(C_MG + q * 512, 512, scr["MG"][:, q * 512:(q + 1) * 512], "sigb", 1)
    sc.barrier()


def cmp_phase(sc, M, C, scr, P):
    M.reset(C["const_end"])
    kcT, VC = C["kcT"], C["VC"]
    kcf = M.f32(S)
    tb = r3(M.bf(32 * 256), 32)
    W1b = r3(M.bf(32 * 256), 32)
    W2b = v4(M.bf(2 * 128), 2, 2)
    st1 = [M.f32(256) for _ in range(2)]
    st2 = M.f32(128)
    peT = M.f32(32)
    hx = [M.f32(256) for _ in range(4)]
    hu = M.f32(256)
    gT = [M.bf(256) for _ in range(4)]
    r_kcf, r_tb, r_W1, r_W2, r_st2, r_pe = Res(), Res(), Res(), Res(), Res(), Res()
    r_st1 = [Res(), Res()]
    r_hx = [Res() for _ in range(4)]
    r_hu = Res()
    r_gT = [Res() for _ in range(4)]
    r_ps = [Res(), Res(), Res()]
    r_kcT, r_VC = C["r_kcT"], C["r_VC"]
    VC4 = v4(VC, 2, 2)
    sc.op("pool", lambda e: e.memset(VC, 1.0), writes=[r_VC])
    for ct in range(2):
        for g in range(2):
            sc.dma(lambda e, ct=ct, g=g: e.dma_start(out=VC4[:, ct, g, 65:128], in_=P["c_ov"][ct * 128:(ct + 1) * 128, 1:64]),
                   writes=[r_VC])
    def do_kv(kv):
        pe_d = P["nsa_pe_k"] if kv == 0 else P["nsa_pe_v"]
        w1_d = P["nsa_w_ck1"] if kv == 0 else P["nsa_w_cv1"]
        w2_d = P["nsa_w_ck2"] if kv == 0 else P["nsa_w_cv2"]
        sc.dma(lambda e: e.dma_start(out=kcf, in_=scr["KC"][kv]), writes=[r_kcf])
        for g in range(2):
            sc.dma(lambda e, g=g, pe_d=pe_d: e.dma_start(out=peT[g * 64:(g + 1) * 64, :], in_=pe_d.rearrange("l d -> d l"),
                                                        allow_slow_non_contiguous=True), writes=[r_pe])
        sc.op("pool", lambda e: e.memset(tb.rearrange("p a b -> p (a b)"), 0.0), writes=[r_tb])
        kcf3 = kcf.rearrange("p (a b) -> p a b", b=16)
        for l in range(32):
            src = kcf3[:, 0:255, l] if l < 16 else kcf3[:, 1:256, l - 16]
            sc.op("dve", lambda e, l=l, src=src: e.tensor_scalar(out=tb[:, l, 0:255], in0=src, scalar1=peT[:, l:l + 1],
                                                                 scalar2=None, op0=ALU.add),
                  reads=[r_kcf, r_pe], writes=[r_tb])
        w1v = w1_d.rearrange("(l d) h -> d l h", d=64)
        for l in range(32):
            i = l % 2
            for g in range(2):
                sc.dma(lambda e, l=l, g=g, i=i: e.dma_start(out=st1[i][g * 64:(g + 1) * 64, :], in_=w1v[:, l, :]),
                       writes=[r_st1[i]])
            sc.op("pool", lambda e, l=l, i=i: e.tensor_copy(out=W1b[:, l, :], in_=st1[i]), reads=[r_st1[i]], writes=[r_W1])
        sc.dma(lambda e: e.dma_start(out=r3(st2, 2), in_=w2_d.rearrange("(hc p) d -> p hc d", p=128)), writes=[r_st2])
        sc.op("pool", lambda e: e.tensor_copy(out=W2b[:, :, 0, :], in_=r3(st2, 2)), reads=[r_st2], writes=[r_W2])
        sc.op("pool", lambda e: e.memset(W2b[:, :, 1, :], 0.0), writes=[r_W2])
        for g in range(2):
            for hc in range(2):
                idx = g * 2 + hc
                PS = M.bank(1 + idx % 2)[:, 0:256]
                rp = r_ps[idx % 2]
                for l in range(32):
                    sc.op("pe", lambda e, PS=PS, l=l, g=g, hc=hc: e.matmul(
                        PS, lhsT=W1b[g * 64:(g + 1) * 64, l, hc * 128:(hc + 1) * 128], rhs=tb[g * 64:(g + 1) * 64, l, :],
                        start=(l == 0), stop=(l == 31)), reads=[r_W1, r_tb], writes=[rp])
                x_ = hx[idx]
                sc.op("act", lambda e, PS=PS, x_=x_: e.copy(out=x_, in_=PS), reads=[rp], writes=[r_hx[idx]])
                sc.op("act", lambda e, x_=x_: e.activation(out=hu, in_=x_, func=AF.Square), reads=[r_hx[idx]], writes=[r_hu])
                sc.op("dve", lambda e: e.tensor_scalar(out=hu, in0=hu, scalar1=0.044715, scalar2=1.0, op0=ALU.mult, op1=ALU.add),
                      reads=[r_hu], writes=[r_hu])
                sc.op("dve", lambda e, x_=x_: e.tensor_tensor(out=hu, in0=hu, in1=x_, op=ALU.mult),
                      reads=[r_hu, r_hx[idx]], writes=[r_hu])
                sc.op("act", lambda e: e.activation(out=hu, in_=hu, func=AF.Sigmoid, scale=1.5957691216057308),
                      reads=[r_hu], writes=[r_hu])
                sc.op("dve", lambda e, x_=x_, idx=idx: e.tensor_tensor(out=gT[idx], in0=hu, in1=x_, op=ALU.mult),
                      reads=[r_hu, r_hx[idx]], writes=[r_gT[idx]])
        for g in range(2):
            if kv == 0:
                PS = M.bank(3)[:, 0:256]
                for hc in range(2):
                    sc.op("pe", lambda e, PS=PS, g=g, hc=hc: e.matmul(
                        PS, lhsT=W2b[:, hc, :, :].rearrange("p a b -> p (a b)"), rhs=gT[g * 2 + hc],
                        start=(hc == 0), stop=(hc == 1)), reads=[r_W2, r_gT[g * 2 + hc]], writes=[r_ps[2]])
                sc.op("act", lambda e, PS=PS, g=g: e.copy(out=kcT[:, g, :], in_=PS), reads=[r_ps[2]], writes=[r_kcT])
            else:
                for ct in range(2):
                    PS = M.bank(3)[:, 0:64]
                    for hc in range(2):
                        sc.op("pe", lambda e, PS=PS, g=g, hc=hc, ct=ct: e.matmul(
                            PS, lhsT=gT[g * 2 + hc][:, ct * 128:(ct + 1) * 128], rhs=W2b[:, hc, 0, :],
                            start=(hc == 0), stop=(hc == 1)), reads=[r_W2, r_gT[g * 2 + hc]], writes=[r_ps[2]])
                    sc.op("act", lambda e, PS=PS, g=g, ct=ct: e.copy(out=VC4[:, ct, g, 0:64], in_=PS),
                          reads=[r_ps[2]], writes=[r_VC])

    for kv in range(2):
        do_kv(kv)
    sc.barrier()


class AttnPipe:
    def __init__(self, sc, M, sbanks, ntmp=3):
        self.sc, self.M = sc, M
        self.sb = sbanks
        self.r_s = [Res() for _ in sbanks]
        self.tmp = [M.f32(512) for _ in range(ntmp)]
        self.Pt = [M.bf(512) for _ in range(ntmp)]
        self.r_tmp = [Res() for _ in range(ntmp)]
        self.r_P = [Res() for _ in range(ntmp)]
        self.n = 0

    def run_stream(self, items, LA=3):
        sc, M = self.sc, self.M
        jobs = [it for it in items if isinstance(it, dict)]
        order = []
        ji = 0
        for it in items:
            if isinstance(it, dict):
                order.append(("job", ji))
                ji += 1
            else:
                order.append(("call", it))
        n = len(jobs)
        slots = {}
        done_pv = [0]
        pending = []
        DELAY = 4

        def emit_pv(j):
            jb = jobs[j]
            ti = slots[j]
            if "parts" in jb:
                while pending and pending[0][0] <= j:
                    pending.pop(0)[2]()
                acc, M_rows, W = jb["acc"], jb["M_rows"], jb["W"]
                nk = jb["nk"]
                if jb["first"]:
                    z = ZEROS[0]
                    sc.op("pe", lambda e: e.matmul(acc[0:M_rows, 0:W], lhsT=z[:, 0:M_rows], rhs=z[:, 0:W], start=True, stop=False),
                          writes=[jb["r_acc"]])
                np_ = len(jb["parts"])
                for pi_, (_l, _r, _rd, off_, n_, pv_, pvr_, c0_) in enumerate(jb["parts"]):
                    Pq = self.Pt[ti][0:nk, off_:off_ + n_]
                    lastp = jb["last"] and pi_ == np_ - 1
                    sc.op("pe", lambda e, Pq=Pq, pv_=pv_, c0_=c0_, n_=n_, lastp=lastp: e.matmul(
                        acc[0:M_rows, c0_:c0_ + n_], lhsT=pv_, rhs=Pq, start=False, stop=lastp),
                        reads=[self.r_P[ti]] + pvr_, writes=[jb["r_acc"]])
                if jb.get("after") is not None:
                    key, fa, fb = jb["after"]
                    while any(p[1] == key for p in pending):
                        pending.pop(0)[2]()
                    fa()
                    pending.append((j + DELAY, key, fb))
                return
            nk, ncol, c0 = jb["nk"], jb["nc"], jb["c0"]
            Pp = self.Pt[ti][0:nk, 0:ncol]
            pv = jb["pv"]
            acc, M_rows = jb["acc"], jb["M_rows"]
            first, last = jb["first"], jb["last"]
            while pending and pending[0][0] <= j:
                pending.pop(0)[2]()
            W = jb.get("W", 512)
            if first and (c0 != 0 or ncol != W):
                z = ZEROS[0]
                sc.op("pe", lambda e: e.matmul(acc[0:M_rows, 0:W], lhsT=z[:, 0:M_rows], rhs=z[:, 0:W], start=True, stop=False),
                      writes=[jb["r_acc"]])
                first = False
            sc.op("pe", lambda e: e.matmul(acc[0:M_rows, c0:c0 + ncol], lhsT=pv, rhs=Pp, start=first, stop=last),
                  reads=[self.r_P[ti]] + jb["pv_reads"], writes=[jb["r_acc"]])
            if jb.get("after") is not None:
                key, fa, fb = jb["after"]
                while any(p[1] == key for p in pending):
                    pending.pop(0)[2]()
                fa()
                pending.append((j + DELAY, key, fb))

        for kind, v in order:
            if kind == "call":
                v()
                continue
            i = v
            jb = jobs[i]
            k = self.n
            self.n += 1
            si = k % len(self.sb)
            ti = k % len(self.tmp)
            slots[i] = ti
            nk, ncol = jb["nk"], jb["nc"]
            Sps = M.bank(self.sb[si])[0:nk, 0:ncol]
            if "parts" in jb:
                for (l_, r_, rd, off_, n_, _pv, _pvr, _c0) in jb["parts"]:
                    Sp_ = M.bank(self.sb[si])[0:nk, off_:off_ + n_]
                    sc.op("pe", lambda e, l_=l_, r_=r_, Sp_=Sp_: e.matmul(Sp_, lhsT=l_, rhs=r_, start=True, stop=True),
                          reads=rd, writes=[self.r_s[si]])
            else:
                nq = len(jb["qk"])
                for qi, (l_, r_, rd) in enumerate(jb["qk"]):
                    sc.op("pe", lambda e, l_=l_, r_=r_, qi=qi, Sps=Sps, nq=nq: e.matmul(
                        Sps, lhsT=l_, rhs=r_, start=(qi == 0), stop=(qi == nq - 1)), reads=rd, writes=[self.r_s[si]])
            Pp = self.Pt[ti][0:nk, 0:ncol]
            bias = jb.get("bias")
            if "parts" in jb:
                tm = self.tmp[ti][0:nk, 0:ncol]
                T = jb["T"]
                sc.op("dve", lambda e, tm=tm, Sps=Sps, T=T: e.scalar_tensor_tensor(
                    out=tm, in0=Sps, scalar=0.125, in1=T, op0=ALU.mult, op1=ALU.add),
                    reads=[self.r_s[si]] + jb.get("T_reads", []), writes=[self.r_tmp[ti]])
                sc.op("act", lambda e, Pp=Pp, tm=tm: e.activation(out=Pp, in_=tm, func=AF.Exp),
                      reads=[self.r_tmp[ti]], writes=[self.r_P[ti]])
            elif jb.get("direct"):
                sc.op("act", lambda e, Pp=Pp, Sps=Sps, bias=bias: e.activation(out=Pp, in_=Sps, func=AF.Exp, bias=bias, scale=0.125),
                      reads=[self.r_s[si]] + jb.get("bias_reads", []), writes=[self.r_P[ti]])
            else:
                tm = self.tmp[ti][0:nk, 0:ncol]
                T = jb["T"]
                sc.op("dve", lambda e, tm=tm, Sps=Sps, T=T: e.scalar_tensor_tensor(
                    out=tm, in0=Sps, scalar=0.125, in1=T, op0=ALU.mult, op1=ALU.add),
                    reads=[self.r_s[si]] + jb.get("T_reads", []), writes=[self.r_tmp[ti]])
                for (off, mk, mrd) in jb["masks"]:
                    w = mk.shape[-1]
                    tmm = self.tmp[ti][0:nk, off:off + w]
                    sc.op("pool", lambda e, tmm=tmm, mk=mk: e.tensor_tensor(out=tmm, in0=tmm, in1=mk, op=ALU.add),
                          reads=[self.r_tmp[ti]] + mrd, writes=[self.r_tmp[ti]])
                sc.op("act", lambda e, Pp=Pp, tm=tm, bias=bias: e.activation(out=Pp, in_=tm, func=AF.Exp, bias=bias),
                      reads=[self.r_tmp[ti]] + jb.get("bias_reads", []), writes=[self.r_P[ti]])
            while done_pv[0] <= i - LA:
                emit_pv(done_pv[0])
                done_pv[0] += 1
        while done_pv[0] < n:
            emit_pv(done_pv[0])
            done_pv[0] += 1
        while pending:
            pending.pop(0)[2]()

    def run(self, jobs, acc, r_acc, M_rows):
        sc, M = self.sc, self.M
        LA = 2
        n = len(jobs)
        slots = []
        for i in range(n + LA):
            if i < n:
                jb = jobs[i]
                k = self.n
                self.n += 1
                si = k % len(self.sb)
                ti = k % len(self.tmp)
                slots.append(ti)
                nk, ncol = jb["nk"], jb["nc"]
                Sps = M.bank(self.sb[si])[0:nk, 0:ncol]
                nq = len(jb["qk"])
                for qi, (l_, r_, rd) in enumerate(jb["qk"]):
                    sc.op("pe", lambda e, Sps=Sps, l_=l_, r_=r_, qi=qi, nq=nq: e.matmul(
                        Sps, lhsT=l_, rhs=r_, start=(qi == 0), stop=(qi == nq - 1)), reads=rd, writes=[self.r_s[si]])
                tm = self.tmp[ti][0:nk, 0:ncol]
                T = jb["T"]
                sc.op("dve", lambda e, tm=tm, Sps=Sps, T=T: e.scalar_tensor_tensor(out=tm, in0=Sps, scalar=0.125, in1=T, op0=ALU.mult, op1=ALU.add),
                      reads=[self.r_s[si]] + jb.get("T_reads", []), writes=[self.r_tmp[ti]])
                for (off, mk, mrd) in jb["masks"]:
                    w = mk.shape[-1]
                    tmm = self.tmp[ti][0:nk, off:off + w]
                    sc.op("pool", lambda e, tmm=tmm, mk=mk: e.tensor_tensor(out=tmm, in0=tmm, in1=mk, op=ALU.add),
                          reads=[self.r_tmp[ti]] + mrd, writes=[self.r_tmp[ti]])
                Pp = self.Pt[ti][0:nk, 0:ncol]
                bias = jb["bias"]
                sc.op("act", lambda e, Pp=Pp, tm=tm, bias=bias: e.activation(out=Pp, in_=tm, func=AF.Exp, bias=bias),
                      reads=[self.r_tmp[ti]] + jb.get("bias_reads", []), writes=[self.r_P[ti]])
            if i >= LA:
                j = i - LA
                jb = jobs[j]
                ti = slots[j]
                nk, ncol, c0 = jb["nk"], jb["nc"], jb["c0"]
                Pp = self.Pt[ti][0:nk, 0:ncol]
                pv = jb["pv"]
                sc.op("pe", lambda e, Pp=Pp, pv=pv, c0=c0, ncol=ncol, j=j: e.matmul(
                    acc[0:M_rows, c0:c0 + ncol], lhsT=pv, rhs=Pp, start=(j == 0), stop=(j == n - 1)),
                    reads=[self.r_P[ti]] + jb["pv_reads"], writes=[r_acc])


def nsa_phase(sc, M, C, scr, P):
    M.reset(C["const_end"])
    identf = C["identf"]
    ident = C["ident"]
    kcT, VC = C["kcT"], C["VC"]
    r_kcT, r_VC = C["r_kcT"], C["r_VC"]
    VC4 = v4(VC, 2, 2)
    KS = r3(M.bf(2 * S), 2)
    KW = r3(M.bf(2 * S), 2)
    VS = v4(M.bf(32 * 256), 32, 2)
    VW = v4(M.bf(32 * 256), 32, 2)
    TQA = r3(M.f32(8 * 512), 8)
    MASKO = M.f32(512)
    TQ8 = r3(M.f32(8 * 4), 8)
    QN = [r3(M.bf(8 * 512), 8) for _ in range(2)]
    TQ = r3(M.f32(8 * 512), 8)
    MK = r3(M.f32(3 * 128), 3)
    BC = r3(M.f32(8 * 35), 8)
    BCC = v4(M.f32(8 * 8 * 2), 8, 8)
    Mc = [r3(M.f32(2 * 512), 2) for _ in range(2)]
    accS = [M.f32(512) for _ in range(2)]
    onsa = r3(M.f32(4 * 512), 4)
    onsab = r3(M.bf(4 * 512), 4)
    impacc = v4(M.f32(2 * 4 * 64), 2, 4)
    GN = [r3(M.f32(4 * 24), 4) for _ in range(2)]
    SELM = [v4(M.f32(4 * 2 * 64), 4, 2) for _ in range(2)]
    rr_ = M.f32(4)
    sc4 = M.f32(4)
    m1 = M.f32(8)
    m2 = M.f32(8)
    wk = M.f32(64)
    impp = r3(M.f32(4 * 64), 4)
    mbf = M.bf(128)
    tnum = r3(M.f32(4 * 64), 4)
    pipe = AttnPipe(sc, M, [0, 1, 2, 6], ntmp=4)
    r_KS, r_KW, r_VS, r_VW, r_c = Res(), Res(), Res(), Res(), Res()
    r_QN = [[Res(), Res()], [Res(), Res()]]
    r_Mc = [Res(), Res()]
    r_GN = [Res(), Res()]
    r_SELM = [Res(), Res()]
    r_accS = [Res(), Res()]
    r_acc = [Res(), Res()]
    r_TP0 = Res()
    r_TP = [r_TP0, r_TP0]
    r_onsa, r_onsab, r_imp, r_MBT = Res(), Res(), Res(), Res()
    r_sm = Res()
    r_sel = Res()
    r_mbf = Res()
    r_tnum = Res()
    r_TP7 = Res()
    sc.op("dve", lambda e: e.memset(KW.rearrange("p a b -> p (a b)"), 0.0), writes=[r_KW])
    sc.op("dve", lambda e: e.memset(VS.rearrange("p a b c -> p (a b c)"), 0.0), writes=[r_VS])
    sc.op("pool", lambda e: e.memset(VW.rearrange("p a b c -> p (a b c)"), 0.0), writes=[r_VW])
    for qb_ in range(2):
        sc.op("pool", lambda e, qb_=qb_: e.memset(QN[qb_].rearrange("p a b -> p (a b)"), 0.0),
              writes=[r_QN[qb_][0], r_QN[qb_][1]])
    ohf = P["c_oh"].rearrange("j a b -> j (a b)")
    for g in range(2):
        sc.dma(lambda e, g=g: e.dma_start(out=KS[0:64, g, :], in_=scr["KS"][g][0:64, :]), writes=[r_KS])
        sc.dma(lambda e, g=g: e.dma_start(out=KS[64:128, g, :], in_=ohf), writes=[r_KS])
        sc.dma(lambda e, g=g: e.dma_start(out=KW[0:64, g, :], in_=scr["KW"][g][0:64, :]), writes=[r_KW])
    vsv = scr["VS"].rearrange("(kt p) (g c) -> p kt g c", p=128, g=2)
    vwv = scr["VW"].rearrange("(kt p) (g c) -> p kt g c", p=128, g=2)
    for q4 in range(4):
        for g in range(2):
            sc.dma(lambda e, q4=q4, g=g: e.dma_start(out=VS[:, q4 * 8:(q4 + 1) * 8, g, 0:65],
                                                      in_=vsv[:, q4 * 8:(q4 + 1) * 8, g, :]), writes=[r_VS])
            sc.dma(lambda e, q4=q4, g=g: e.dma_start(out=VW[:, q4 * 8:(q4 + 1) * 8, g, 0:65],
                                                      in_=vwv[:, q4 * 8:(q4 + 1) * 8, g, :]), writes=[r_VW])
    sc.dma(lambda e: e.dma_start(out=TQ.rearrange("p a b -> p (a b)"),
                                 in_=P["c_tq"].rearrange("a b -> (a b)").partition_broadcast(128)), writes=[r_c])
    sc.dma(lambda e: e.dma_start(out=TQA, in_=P["c_tqa"].rearrange("h k q -> k h q")), writes=[r_c])
    sc.dma(lambda e: e.dma_start(out=MASKO, in_=P["c_masko"]), writes=[r_c])
    sc.dma(lambda e: e.dma_start(out=TQ8, in_=P["c_tq8"]), writes=[r_c])
    sc.dma(lambda e: e.dma_start(out=MK, in_=P["c_masks"].rearrange("m k q -> k m q")), writes=[r_c])
    sc.dma(lambda e: e.dma_start(out=BC, in_=P["c_bc"]), writes=[r_c])
    sc.dma(lambda e: e.dma_start(out=BCC, in_=P["c_bcc"]), writes=[r_c])
    sc.barrier()

    def finish_head(acc_bank, ai, M_rows, h, br, qc, first):
        a = accS[ai]
        tb_ = 5
        TP = r3(M.bank(tb_), 4)
        for sub in range(4):
            MR = M_rows + (M_rows % 2)
            sc.op("pe", lambda e, sub=sub, MR=MR: e.transpose(out=TP[:, sub, 0:MR], in_=a[0:MR, sub * 128:(sub + 1) * 128],
                                                       identity=identf[0:MR, 0:MR]),
                  reads=[r_accS[ai]], writes=[r_TP[ai]])
        den = TP[:, :, 64:65]
        if br == 0:
            sc.op("dve", lambda e: e.tensor_scalar(out=rr_.unsqueeze(2), in0=den, scalar1=1e-30, scalar2=None, op0=ALU.max),
                  reads=[r_TP[ai]], writes=[r_sm])
            sc.op("dve", lambda e: e.reciprocal(out=rr_, in_=rr_), reads=[r_sm], writes=[r_sm])
        else:
            sc.op("dve", lambda e: e.reciprocal(out=rr_.unsqueeze(2), in_=den), reads=[r_TP[ai]], writes=[r_sm])
        gidx = h * 3 + br
        sc.op("dve", lambda e: e.tensor_tensor(out=sc4, in0=rr_, in1=GN[qc % 2][:, :, gidx], op=ALU.mult),
              reads=[r_sm, r_GN[qc % 2]], writes=[r_sm])
        dst = onsa[:, :, h * 64:(h + 1) * 64]
        if first:
            sc.op("dve", lambda e: e.tensor_tensor(out=dst, in0=TP[:, :, 0:64], in1=sc4.unsqueeze(2).to_broadcast([128, 4, 64]),
                                                   op=ALU.mult), reads=[r_TP[ai], r_sm], writes=[r_onsa])
        else:
            sc.op("dve", lambda e: e.tensor_tensor(out=tnum, in0=TP[:, :, 0:64], in1=sc4.unsqueeze(2).to_broadcast([128, 4, 64]),
                                                   op=ALU.mult), reads=[r_TP[ai], r_sm], writes=[r_tnum])
            sc.op("dve", lambda e: e.tensor_tensor(out=dst, in0=dst, in1=tnum, op=ALU.add),
                  reads=[r_tnum, r_onsa], writes=[r_onsa])
        return TP

    hcount = [0]

    mbfs = [[r3(M.bf(4 * 128), 4) for _ in range(4)] for _ in range(2)]
    r_mbfs = [[Res() for _ in range(4)] for _ in range(2)]
    for g_ in range(2):
        for s_ in range(4):
            sc.op("pool", lambda e, g_=g_, s_=s_: e.memset(mbfs[g_][s_].rearrange("p a b -> p (a b)"), 0.0),
                  writes=[r_mbfs[g_][s_]])

    def do_chunk(qc):
        q0 = qc * 512
        qb = qc % 2
        for h_ in range(8):
            sc.dma(lambda e, h_=h_: e.dma_start(out=QN[qb][0:64, h_, :],
                                                 in_=scr["QN"][h_ // 2][(h_ % 2) * 64:(h_ % 2) * 64 + 64, q0:q0 + 512]),
                   writes=[r_QN[qb][h_ // 4]])
        sc.dma(lambda e: e.dma_start(out=GN[qb], in_=scr["GN"][q0:q0 + 512, :].rearrange("(s p) c -> p s c", p=128)),
               writes=[r_GN[qb]])
        sc.dma(lambda e: e.dma_start(out=SELM[qb].rearrange("p a b c -> p a (b c)"),
                                     in_=P["c_selm"][q0:q0 + 512].rearrange("(s p) a c -> p s (a c)", p=128)),
               writes=[r_SELM[qb]])
        sc.dma(lambda e: e.dma_start(out=Mc[qb], in_=P["c_mc"][qc]), writes=[r_Mc[qb]])
        nct = 2 if qc >= 4 else 1
        items = []

        def sel_dve(g):
            sc.op("pool", lambda e: e.memset(impacc[:, g, :, 0:1], 0.0), reads=[r_imp], writes=[r_imp])
            sc.op("dve", lambda e: e.tensor_tensor(out=impp, in0=impacc[:, g, :, :], in1=SELM[qb][:, :, 0, :], op=ALU.mult),
                  reads=[r_imp, r_SELM[qb]], writes=[r_sel])
            sc.op("dve", lambda e: e.tensor_tensor(out=impp, in0=impp, in1=SELM[qb][:, :, 1, :], op=ALU.add),
                  reads=[r_sel, r_SELM[qb]], writes=[r_sel])
            for sub in range(4):
                iv = impp[:, sub, :]
                mb_ = mbfs[g][sub]
                sc.op("dve", lambda e, iv=iv: e.max(out=m1, in_=iv), reads=[r_sel], writes=[r_sel])
                sc.op("dve", lambda e, iv=iv: e.match_replace(out=wk, in_to_replace=m1, in_values=iv, imm_value=-3.0e38),
                      reads=[r_sel], writes=[r_sel])
                sc.op("dve", lambda e: e.max(out=m2, in_=wk), reads=[r_sel], writes=[r_sel])
                sc.op("dve", lambda e, iv=iv: e.tensor_scalar(
                    out=wk, in0=iv, scalar1=m2[:, 7:8], scalar2=NEG, op0=ALU.is_lt, op1=ALU.mult),
                    reads=[r_sel], writes=[r_sel])
                for hh_ in range(4):
                    sc.op("dve", lambda e, hh_=hh_, mb_=mb_, sub=sub: e.tensor_scalar(
                        out=mb_[:, hh_, 64:128], in0=wk, scalar1=TQ8[:, g * 4 + hh_, sub:sub + 1], scalar2=None, op0=ALU.add),
                        reads=[r_sel, r_c], writes=[r_mbfs[g][sub]])

        def sel_pe(g):
            Tb = M.bank_bf(7)
            for sub in range(4):
                mb_ = mbfs[g][sub]
                for hh_ in range(4):
                    sc.op("pe", lambda e, mb_=mb_, hh_=hh_: e.transpose(out=Tb[:, hh_ * 128:(hh_ + 1) * 128], in_=mb_[:, hh_, :],
                                                                       identity=ident),
                          reads=[r_mbfs[g][sub]], writes=[r_TP7])
                sc.op("act", lambda e, sub=sub: e.copy(
                    out=QN[qb][64:128, g * 4:(g + 1) * 4, sub * 128:(sub + 1) * 128],
                    in_=r3(Tb[64:128, 0:512], 4)),
                    reads=[r_TP7], writes=[r_QN[qb][g]])

        def mk_after(ai, M_rows, h, br, g, hh):
            def fa():
                a = accS[ai]
                sc.op("dve", lambda e: e.tensor_copy(out=a[0:M_rows, :], in_=M.bank(3 + ai)[0:M_rows, :]),
                      reads=[r_acc[ai]], writes=[r_accS[ai]])

            def after():
                first = (br == 0)
                TP = finish_head(3 + ai, ai, M_rows, h, br, qc, first)
                if br == 0:
                    ia = impacc[:, g, :, 1:64]
                    if hh == 0:
                        sc.op("dve", lambda e: e.tensor_tensor(
                            out=ia, in0=TP[:, :, 65:128], in1=rr_.unsqueeze(2).to_broadcast([128, 4, 63]), op=ALU.mult),
                            reads=[r_TP[ai], r_sm], writes=[r_imp])
                    else:
                        sc.op("dve", lambda e: e.tensor_tensor(
                            out=tnum[:, :, 0:63], in0=TP[:, :, 65:128], in1=rr_.unsqueeze(2).to_broadcast([128, 4, 63]),
                            op=ALU.mult), reads=[r_TP[ai], r_sm], writes=[r_tnum])
                        sc.op("dve", lambda e: e.tensor_tensor(out=ia, in0=ia, in1=tnum[:, :, 0:63], op=ALU.add),
                              reads=[r_tnum, r_imp], writes=[r_imp])
                    if hh == 3:
                        sel_dve(g)
            return (ai, fa, after)

        for g in range(2):
            for hh in range(4):
                h = g * 4 + hh
                pr, half = h // 2, h % 2
                lo, hi = half * 64, half * 64 + 64
                ai = hcount[0] % 2
                hcount[0] += 1
                for ct in range(nct):
                    items.append(dict(
                        qk=[(kcT[:, g, ct * 128:(ct + 1) * 128], QN[qb][:, h, :], [r_kcT, r_QN[qb][g]])],
                        nk=128, c0=0, nc=512, T=TQ[:, h, :], T_reads=[r_c],
                        masks=[(0, Mc[qb][:, ct, :], [r_Mc[qb]])],
                        bias=BCC[:, h, qc, ct:ct + 1], bias_reads=[r_c],
                        pv=VC4[:, ct, g, :], pv_reads=[r_VC],
                        acc=M.bank(3 + ai), r_acc=r_acc[ai], M_rows=128, first=(ct == 0), last=(ct == nct - 1),
                        after=(mk_after(ai, 128, h, 0, g, hh) if ct == nct - 1 else None)))

        def branch_jobs(br, g):
            for hh in range(4):
                h = g * 4 + hh
                pr, half = h // 2, h % 2
                lo, hi = half * 64, half * 64 + 64
                ai = hcount[0] % 2
                hcount[0] += 1
                if br == 1:
                    kts = list(range(0, 4 * qc + 4))
                else:
                    kts = list(range(max(0, 4 * qc - 4), 4 * qc + 4))
                for ki_, kt in enumerate(kts):
                    dlt = 4 * qc - kt
                    masks = []
                    if dlt > 0:
                        c0, ncol = 0, 512
                        Tt = TQ
                        bidx = dlt + 3
                        if br == 2:
                            m = 4 - dlt
                            ncol = 128 * (m + 1)
                            masks = [(128 * m, MK[:, 1, :], [r_c])]
                    else:
                        j = -dlt
                        c0, ncol = 128 * j, 512 - 128 * j
                        Tt = TQA
                        bidx = 3
                    Kt = KS if br == 1 else KW
                    rK = r_KS if br == 1 else r_KW
                    qk = [(Kt[:, g, kt * 128:(kt + 1) * 128], QN[qb][:, h, c0:c0 + ncol], [rK, r_QN[qb][g]])]
                    Vt = VS if br == 1 else VW
                    lastj = (ki_ == len(kts) - 1)
                    Tap = Tt[:, h, 0:ncol]
                    direct = False
                    if br == 1:
                        bidx = dlt + 3
                        if dlt > 0:
                            direct = True
                        else:
                            Tap = MASKO[:, 0:ncol]
                    items.append(dict(
                        qk=qk, nk=128, c0=c0, nc=ncol, T=Tap, T_reads=[r_c], masks=masks, direct=direct,
                        bias=BC[:, h, bidx:bidx + 1], bias_reads=[r_c],
                        pv=Vt[:, kt, g, :], pv_reads=[r_VS if br == 1 else r_VW],
                        acc=M.bank(3 + ai), r_acc=r_acc[ai], M_rows=128, first=(ki_ == 0), last=lastj,
                        after=(mk_after(ai, 65, h, br, g, hh) if lastj else None)))

        branch_jobs(2, 0)
        items.append(lambda: sel_pe(0))
        branch_jobs(2, 1)
        items.append(lambda: sel_pe(1))
        branch_jobs(1, 0)
        branch_jobs(1, 1)
        pipe.run_stream(items)
        sc.op("act", lambda e: e.copy(out=onsab, in_=onsa), reads=[r_onsa], writes=[r_onsab])
        sc.dma(lambda e: e.dma_start(out=scr["ON"][q0:q0 + 512, :].rearrange("(s p) c -> p s c", p=128), in_=onsab),
               reads=[r_onsab])

    for qc in range(8):
        do_chunk(qc)
    sc.barrier()


def dil_phase(sc, M, C, scr, P):
    identf = C["identf"]

    def do_group(gi):
        M.reset(C["const_end"])
        dil = DILS[gi]
        Ls = S // dil
        Cn = min(512, Ls)
        nsub = Cn // 128
        KD = v4(M.bf(4 * S), 2, 2)
        VD = v4(M.bf(32 * 512), 32, 4)
        QD = [r3(M.bf(2 * 512), 2) for _ in range(2)]
        TD = r3(M.f32(4 * 1152), 4)
        BCD = r3(M.f32(4 * 5), 4)
        accS = [M.f32(512) for _ in range(2)]
        osb = [v4(M.f32(4 * 260), 4, 4) for _ in range(2)]
        pipe = AttnPipe(sc, M, [0, 1, 2, 6], ntmp=4)
        r_KD, r_VD, r_c = Res(), Res(), Res()
        r_QD = [Res(), Res()]
        r_accS = [Res(), Res()]
        r_acc = [Res(), Res()]
        r_TP0 = Res()
        r_TP = [r_TP0, r_TP0]
        r_osb = [Res(), Res()]
        if gi == 0:
            sc.op("dve", lambda e: e.memset(KD.rearrange("p a b c -> p (a b c)"), 0.0), writes=[r_KD])
            sc.op("pool", lambda e: e.memset(VD.rearrange("p a b c -> p (a b c)"), 0.0), writes=[r_VD])
        for pp in range(2):
            for hf in range(2):
                sc.dma(lambda e, pp=pp, hf=hf: e.dma_start(out=KD[hf * 64:(hf + 1) * 64, pp, hf, :],
                                                            in_=scr["KD"][gi * 2 + pp][hf * 64:(hf + 1) * 64, :]), writes=[r_KD])
        vdv = scr["VD"][gi].rearrange("(kt p) (h c) -> p kt h c", p=128, h=4)
        for q4 in range(4):
            for hh_ in range(4):
                sc.dma(lambda e, q4=q4, hh_=hh_: e.dma_start(out=VD[:, q4 * 8:(q4 + 1) * 8, hh_, 0:65],
                                                              in_=vdv[:, q4 * 8:(q4 + 1) * 8, hh_, :]), writes=[r_VD])
        sc.dma(lambda e: e.dma_start(out=TD, in_=P["c_tds"][gi * 4:(gi + 1) * 4].rearrange("h k q -> k h q")), writes=[r_c])
        sc.dma(lambda e: e.dma_start(out=BCD, in_=P["c_bcd"][:, gi * 4:(gi + 1) * 4, :]), writes=[r_c])
        sc.barrier()
        ODv = scr["OD"][gi].rearrange("(i r) c -> i r c", r=dil)
        hcount = [0]
        items = []
        loadqs = []
        marks = []

        def do_chunk(ci, p0):
            r, i0 = divmod(p0, Ls)
            qb = ci % 2
            ob = osb[qb]

            def loadq():
                for pp in range(2):
                    sc.dma(lambda e, pp=pp: e.dma_start(out=QD[qb][:, pp, 0:Cn], in_=scr["QD"][gi * 2 + pp][:, p0:p0 + Cn]),
                           writes=[r_QD[qb]])
            loadqs.append(loadq)
            if ci == 0:
                items.append(loadq)
            mark = len(items)
            items.append(None)
            marks.append(mark)

            def mk_after(ai, hh):
                a = accS[ai]

                def fa():
                    sc.op("dve", lambda e: e.tensor_copy(out=a[0:65, 0:Cn], in_=M.bank(3 + ai)[0:65, 0:Cn]),
                          reads=[r_acc[ai]], writes=[r_accS[ai]])

                def after():
                    TP = r3(M.bank(5), 4)
                    for sub in range(nsub):
                        sc.op("pe", lambda e, sub=sub: e.transpose(
                            out=TP[:, sub, 0:66], in_=a[0:66, sub * 128:(sub + 1) * 128], identity=identf[0:66, 0:66]),
                            reads=[r_accS[ai]], writes=[r_TP[ai]])
                    sc.op("dve", lambda e: e.tensor_copy(out=ob[:, 0:nsub, hh, :], in_=TP[:, 0:nsub, 0:65]),
                          reads=[r_TP[ai]], writes=[r_osb[qb]])
                    if hh == 3:
                        for sub in range(nsub):
                            ii = i0 + sub * 128
                            sc.dma(lambda e, sub=sub, ii=ii: e.dma_start(
                                out=ODv[ii:ii + 128, r, :], in_=ob[:, sub, :, :].rearrange("p a b -> p (a b)")),
                                reads=[r_osb[qb]], q="act")
                return (ai, fa, after)

            for hh in range(4):
                pp, half = hh // 2, hh % 2
                lo, hi = half * 64, half * 64 + 64
                ai = hcount[0] % 2
                hcount[0] += 1
                js = [j for j in range(0, nsub + 1) if not (j == 0 and i0 == 0)]
                groups, cur, tot = [], [], 0
                for j in js:
                    n_ = 128 if (j == 0 or j == nsub) else 256
                    if tot + n_ > 512:
                        groups.append(cur)
                        cur, tot = [], 0
                    cur.append((j, n_))
                    tot += n_
                if cur:
                    groups.append(cur)
                for gn, grp in enumerate(groups):
                    parts = []
                    off = 0
                    j_first = grp[0][0]
                    t0 = 0 if j_first == 0 else 128 + 256 * (j_first - 1)
                    for (j, n_) in grp:
                        k0 = p0 + 128 * (j - 1)
                        kt = k0 // 128
                        c0 = 0 if j == 0 else 128 * (j - 1)
                        parts.append((KD[:, pp, half, k0:k0 + 128], QD[qb][:, pp, c0:c0 + n_], [r_KD, r_QD[qb]],
                                      off, n_, VD[:, kt, hh, :], [r_VD], c0))
                        off += n_
                    lastg = (gn == len(groups) - 1)
                    items.append(dict(
                        parts=parts, nk=128, nc=off, c0=0, T=TD[:, hh, t0:t0 + off], T_reads=[r_c],
                        acc=M.bank(3 + ai), r_acc=r_acc[ai], M_rows=128, first=(gn == 0), last=lastg, W=Cn,
                        after=(mk_after(ai, hh) if lastg else None)))

        for ci, p0 in enumerate(range(0, S, Cn)):
            do_chunk(ci, p0)
        for ci, mark in enumerate(marks):
            items[mark] = loadqs[ci + 1] if ci + 1 < len(loadqs) else (lambda: None)
        pipe.run_stream(items)
        sc.barrier()

    for gi in range(3):
        do_group(gi)


def final_phase(sc, M, C, scr, P, x1, x2, gpost):
    M.reset(C["const_end"])
    ident = C["ident"]
    Wno = r3(M.bf(4 * 1024), 4)
    Wdo = r3(M.bf(2 * 1024), 2)
    Wmo = r3(M.bf(8 * 1024), 8)
    stage = [M.f32(1024) for _ in range(2)]
    gq = M.f32(1024)
    onb = [M.bf(512) for _ in range(2)]
    odf = [v4(M.f32(3 * 260), 3, 4) for _ in range(2)]
    mg = [M.bf(2048) for _ in range(2)]
    xr = [M.f32(1024) for _ in range(2)]
    onT = r3(M.bf(4 * 128), 4)
    odT = r3(M.bf(2 * 128), 2)
    odn = r3(M.f32(4 * 65), 4)
    odb = M.bf(256)
    ya = M.f32(1024)
    yb = M.f32(1024)
    ybf = M.bf(1024)
    yT = r3(M.bf(8 * 128), 8)
    yt = M.f32(1024)
    small = M.f32(8)
    r_W = Res()
    r_stage = [Res(), Res()]
    r_gq = Res()
    r_in = [Res(), Res()]
    r_onT, r_odT, r_odn, r_odb, r_ya, r_yb, r_ybf, r_yT, r_yt, r_sm = (Res() for _ in range(10))
    r_T, r_Y1, r_Y2, r_Y3 = Res(), Res(), Res(), Res()
    cnt = [0]

    def load_piece(dst_ap, src_ap, n):
        i = cnt[0] % 2
        cnt[0] += 1
        st = stage[i][:, 0:n]
        sc.dma(lambda e: e.dma_start(out=st, in_=src_ap), writes=[r_stage[i]])
        sc.op("pool", lambda e: e.tensor_copy(out=dst_ap, in_=st), reads=[r_stage[i]], writes=[r_W])

    for k in range(4):
        load_piece(Wno[:, k, :], P["w_nsa_o"][k * 128:(k + 1) * 128, :], 1024)
    for k in range(2):
        load_piece(Wdo[:, k, :], P["w_dil_o"][k * 128:(k + 1) * 128, :], 1024)
    for k in range(8):
        load_piece(Wmo[:, k, :], P["w_mix_out"][k * 128:(k + 1) * 128, :], 1024)
    sc.dma(lambda e: e.dma_start(out=gq, in_=gpost[0, :].partition_broadcast(128)), writes=[r_gq])
    for t in range(32):
        b = t % 2
        rows = slice(t * 128, (t + 1) * 128)
        sc.dma(lambda e, b=b, rows=rows: e.dma_start(out=onb[b], in_=scr["ON"][rows, :]), writes=[r_in[b]])
        for gi in range(3):
            sc.dma(lambda e, b=b, rows=rows, gi=gi: e.dma_start(out=odf[b][:, gi, :, :].rearrange("p a b -> p (a b)"),
                                                               in_=scr["OD"][gi][rows, :]), writes=[r_in[b]])
        sc.dma(lambda e, b=b, rows=rows: e.dma_start(out=mg[b], in_=scr["MG"][rows, :]), writes=[r_in[b]])
        sc.dma(lambda e, b=b, rows=rows: e.dma_start(out=xr[b], in_=x1[rows, :]), writes=[r_in[b]])
        sc.op("dve", lambda e, b=b: e.tensor_tensor(out=odn, in0=odf[b][:, 0, :, :], in1=odf[b][:, 1, :, :], op=ALU.add),
              reads=[r_in[b]], writes=[r_odn])
        sc.op("dve", lambda e, b=b: e.tensor_tensor(out=odn, in0=odn, in1=odf[b][:, 2, :, :], op=ALU.add),
              reads=[r_in[b], r_odn], writes=[r_odn])
        sc.op("dve", lambda e: e.reciprocal(out=small[:, 0:4].unsqueeze(2), in_=odn[:, :, 64:65]), reads=[r_odn], writes=[r_sm])
        sc.op("dve", lambda e: e.tensor_tensor(out=r3(odb, 4), in0=odn[:, :, 0:64],
                                               in1=small[:, 0:4].unsqueeze(2).to_broadcast([128, 4, 64]), op=ALU.mult),
              reads=[r_odn, r_sm], writes=[r_odb])
        Tb = M.bank_bf(0)
        for k in range(4):
            sc.op("pe", lambda e, b=b, k=k: e.transpose(out=Tb[:, k * 128:(k + 1) * 128], in_=onb[b][:, k * 128:(k + 1) * 128],
                                                        identity=ident), reads=[r_in[b]], writes=[r_T])
        for k in range(2):
            sc.op("pe", lambda e, k=k: e.transpose(out=Tb[:, (4 + k) * 128:(5 + k) * 128], in_=odb[:, k * 128:(k + 1) * 128],
                                                   identity=ident), reads=[r_odb], writes=[r_T])
        sc.op("act", lambda e: e.copy(out=onT, in_=r3(Tb[:, 0:512], 4)), reads=[r_T], writes=[r_onT])
        sc.op("act", lambda e: e.copy(out=odT, in_=r3(Tb[:, 512:768], 2)), reads=[r_T], writes=[r_odT])
        for dh in range(2):
            Y1 = M.bank(1 + dh)
            for k in range(4):
                sc.op("pe", lambda e, Y1=Y1, k=k, dh=dh: e.matmul(Y1, lhsT=onT[:, k, :], rhs=Wno[:, k, dh * 512:(dh + 1) * 512],
                                                                  start=(k == 0), stop=(k == 3)),
                      reads=[r_onT, r_W], writes=[r_Y1])
            Y2 = M.bank(3 + dh)
            for k in range(2):
                sc.op("pe", lambda e, Y2=Y2, k=k, dh=dh: e.matmul(Y2, lhsT=odT[:, k, :], rhs=Wdo[:, k, dh * 512:(dh + 1) * 512],
                                                                  start=(k == 0), stop=(k == 1)),
                      reads=[r_odT, r_W], writes=[r_Y2])
            sl = slice(dh * 512, (dh + 1) * 512)
            sc.op("dve", lambda e, Y1=Y1, b=b, sl=sl: e.tensor_tensor(out=ya[:, sl], in0=Y1, in1=mg[b][:, sl], op=ALU.mult),
                  reads=[r_Y1, r_in[b]], writes=[r_ya])
            sl2 = slice(1024 + dh * 512, 1024 + (dh + 1) * 512)
            sc.op("dve", lambda e, Y2=Y2, b=b, sl=sl, sl2=sl2: e.tensor_tensor(out=yb[:, sl], in0=Y2, in1=mg[b][:, sl2], op=ALU.mult),
                  reads=[r_Y2, r_in[b]], writes=[r_yb])
        sc.op("pool", lambda e: e.tensor_tensor(out=ybf, in0=ya, in1=yb, op=ALU.add), reads=[r_ya, r_yb], writes=[r_ybf])
        Tb2 = M.bank_bf(7)
        for k in range(8):
            sc.op("pe", lambda e, k=k: e.transpose(out=Tb2[:, k * 128:(k + 1) * 128], in_=ybf[:, k * 128:(k + 1) * 128],
                                                   identity=ident), reads=[r_ybf], writes=[r_Y3])
        sc.op("act", lambda e: e.copy(out=yT, in_=r3(Tb2, 8)), reads=[r_Y3], writes=[r_yT])
        for dh in range(2):
            Y = M.bank(5 + dh)
            for k in range(8):
                sc.op("pe", lambda e, Y=Y, k=k, dh=dh: e.matmul(Y, lhsT=yT[:, k, :], rhs=Wmo[:, k, dh * 512:(dh + 1) * 512],
                                                                start=(k == 0), stop=(k == 7)),
                      reads=[r_yT, r_W], writes=[r_T if False else r_Y1 if False else r_yt_ps])
        ssa, ssb = small[:, 4:5], small[:, 5:6]
        sc.op("pool", lambda e: e.memset(small[:, 4:6], 0.0), writes=[r_sm])
        sc.op("act", lambda e: e.activation(out=yt[:, 0:512], in_=M.bank(5), func=AF.Square, accum_out=ssa),
              reads=[r_yt_ps], writes=[r_yt, r_sm])
        sc.op("act", lambda e: e.activation(out=yt[:, 512:1024], in_=M.bank(6), func=AF.Square, accum_out=ssb),
              reads=[r_yt_ps], writes=[r_yt, r_sm])
        sc.op("dve", lambda e: e.tensor_tensor(out=ssa, in0=ssa, in1=ssb, op=ALU.add), reads=[r_sm], writes=[r_sm])
        sc.op("dve", lambda e: e.tensor_scalar(out=ssa, in0=ssa, scalar1=1.0 / D, scalar2=EPS, op0=ALU.mult, op1=ALU.add),
              reads=[r_sm], writes=[r_sm])
        sc.op("act", lambda e: e.sqrt(ssa, ssa), reads=[r_sm], writes=[r_sm])
        sc.op("dve", lambda e: e.reciprocal(out=ssa, in_=ssa), reads=[r_sm], writes=[r_sm])
        for dh in range(2):
            sc.op("dve", lambda e, dh=dh: e.scalar_tensor_tensor(
                out=yt[:, dh * 512:(dh + 1) * 512], in0=M.bank(5 + dh), scalar=ssa,
                in1=gq[:, dh * 512:(dh + 1) * 512], op0=ALU.mult, op1=ALU.mult),
                reads=[r_yt_ps, r_sm, r_gq], writes=[r_yt])
        sc.op("dve", lambda e, b=b: e.tensor_tensor(out=xr[b], in0=yt, in1=xr[b], op=ALU.add),
              reads=[r_yt, r_in[b]], writes=[r_in[b]])
        sc.dma(lambda e, b=b, rows=rows: e.dma_start(out=x2[rows, :], in_=xr[b]), reads=[r_in[b]], q="act")
    sc.barrier()


r_yt_ps = Res()


IN_SHAPES = (
    ("ffn1_pre", [1, D]), ("ffn1_post", [1, D]), ("ffn1_w_gate", [D, DFF]), ("ffn1_w_up", [D, DFF]),
    ("ffn1_w_down", [DFF, D]), ("mix_pre", [1, D]), ("mix_post", [1, D]), ("w_in", [D, 5656]),
    ("nsa_pe_k", [32, 64]), ("nsa_w_ck1", [2048, 256]), ("nsa_w_ck2", [256, 64]),
    ("nsa_pe_v", [32, 64]), ("nsa_w_cv1", [2048, 256]), ("nsa_w_cv2", [256, 64]),
    ("w_nsa_o", [512, D]), ("w_dil_o", [256, D]), ("w_mix_out", [D, D]),
    ("ffn2_pre", [1, D]), ("ffn2_post", [1, D]), ("ffn2_w_gate", [D, DFF]),
    ("ffn2_w_up", [D, DFF]), ("ffn2_w_down", [DFF, D]))


def _consts():
    import ml_dtypes
    bf = ml_dtypes.bfloat16
    c = {}
    c["c_ident"] = np.eye(128, dtype=np.float32).astype(bf)
    c["c_identf"] = np.eye(128, dtype=np.float32)
    n_cmp = 255
    cs = np.arange(256) * 16
    ss = np.arange(64) * 64
    ov = np.clip(np.minimum(cs[:, None] + 32, ss[None, :] + 64) - np.maximum(cs[:, None], ss[None, :]), 0, None) / 32.0
    ov[255] = 0
    c["c_ov"] = ov.astype(np.float32).astype(bf)
    qi = np.arange(512, dtype=np.float64)
    c["c_tq"] = np.stack([-s_ * qi for s_ in NSA_SL]).astype(np.float32)
    c["c_td"] = np.stack([-DIL_SL[h] * DILS[h // 4] * qi for h in range(12)]).astype(np.float32)
    ki = np.arange(128)[:, None]
    qq = np.arange(128)[None, :]
    mk = np.zeros((3, 128, 128), np.float32)
    mk[0] = np.where(qq >= ki, 0.0, NEG)
    mk[1] = np.where(qq < ki, 0.0, NEG)
    mk[2] = np.where(qq <= ki, 0.0, NEG)
    c["c_masks"] = mk
    tqa = np.zeros((8, 128, 512), np.float64)
    for h in range(8):
        tqa[h] = -NSA_SL[h] * qi[None, :]
        tqa[h][:, 0:128] += mk[0]
    c["c_tqa"] = tqa.astype(np.float32)
    mo = np.zeros((128, 512), np.float32)
    mo[:, 0:128] = mk[0]
    c["c_masko"] = mo
    tq8 = np.zeros((128, 8, 4), np.float64)
    for h in range(8):
        for sb in range(4):
            tq8[:, h, sb] = -8.0 * NSA_SL[h] * (128 * sb + np.arange(128))
    c["c_tq8"] = tq8.astype(np.float32)
    tdm = np.zeros((12, 128, 384), np.float64)
    for h in range(12):
        sl = DIL_SL[h] * DILS[h // 4]
        tdm[h][:, 0:256] = -sl * qi[None, 0:256]
        tdm[h][:, 0:128] += mk[0]
        tdm[h][:, 128:256] += mk[2]
        tdm[h][:, 256:384] = -sl * qi[None, 0:128] + mk[2]
    c["c_tdm"] = tdm.astype(np.float32)
    tds = np.zeros((12, 128, 1152), np.float64)
    kcol = np.arange(128, dtype=np.float64)[:, None]
    for h in range(12):
        sl = DIL_SL[h] * DILS[h // 4]
        t2a = -sl * (qi[None, 0:256] - kcol)
        t2a[:, 0:128] += mk[0]
        t2a[:, 128:256] += mk[2]
        t2b = -sl * (128.0 + qi[None, 0:128] - kcol) + mk[2]
        tds[h][:, 0:128] = t2b
        for r_ in range(4):
            tds[h][:, 128 + 256 * r_:128 + 256 * (r_ + 1)] = t2a
    c["c_tds"] = tds.astype(np.float32)
    kk = np.arange(128, dtype=np.float64)
    bc = np.zeros((128, 8, 35), np.float64)
    for h in range(8):
        for idx in range(35):
            bc[:, h, idx] = NSA_SL[h] * (kk - 128 * (idx - 3))
    c["c_bc"] = bc.astype(np.float32)
    bcc = np.zeros((128, 8, 8, 2), np.float64)
    for h in range(8):
        for qc in range(8):
            for ct in range(2):
                bcc[:, h, qc, ct] = NSA_SL[h] * (16 * (128 * ct + kk) + 31 - 512 * qc)
    c["c_bcc"] = bcc.astype(np.float32)
    bcd = np.zeros((128, 12, 5), np.float64)
    for h in range(12):
        for j in range(5):
            bcd[:, h, j] = DIL_SL[h] * DILS[h // 4] * (kk - 128 * (1 - j))
    c["c_bcd"] = bcd.astype(np.float32)
    mc = np.zeros((8, 128, 2, 512), np.float32)
    for qc in range(8):
        for ct in range(2):
            cc = 128 * ct + np.arange(128)[:, None]
            qpos = 512 * qc + np.arange(512)[None, :]
            ok = (qpos >= 16 * cc + 31) & (cc < 255)
            mc[qc, :, ct, :] = np.where(ok, 0.0, NEG)
    c["c_mc"] = mc
    oh = np.zeros((64, 32, 128), np.float32)
    for kt in range(32):
        oh[2 * kt, kt, 0:64] = 1
        oh[2 * kt + 1, kt, 64:128] = 1
    c["c_oh"] = oh.astype(bf)
    pos = np.arange(S)[:, None]
    jb = np.arange(64)[None, :]
    own = pos // 64
    forced = (jb == 0) | (jb == own) | (jb == own - 1)
    valid = jb * 64 <= pos
    m1 = np.where(forced, 0.0, np.where(valid, 1.0, 0.0))
    m2 = np.where(forced, 1.0e9 + 1.0e4 * jb, np.where(valid, 0.0, -1.0e9 - 1.0e4 * jb))
    c["c_selm"] = np.stack([m1, m2], axis=1).astype(np.float32)
    return c


CONST_SHAPES = (("c_ident", [128, 128], BF16), ("c_identf", [128, 128], F32), ("c_ov", [256, 64], BF16),
                ("c_tq", [8, 512], F32), ("c_td", [12, 512], F32), ("c_masks", [3, 128, 128], F32),
                ("c_bc", [128, 8, 35], F32), ("c_bcc", [128, 8, 8, 2], F32), ("c_bcd", [128, 12, 5], F32),
                ("c_mc", [8, 128, 2, 512], F32), ("c_oh", [64, 32, 128], BF16), ("c_selm", [S, 2, 64], F32),
                ("c_tqa", [8, 128, 512], F32), ("c_tdm", [12, 128, 384], F32),
                ("c_masko", [128, 512], F32), ("c_tq8", [128, 8, 4], F32),
                ("c_tds", [12, 128, 1152], F32))


def build_nc():
    nc = bass.Bass("TRN2", target_bir_lowering=False)
    dt = lambda name, shape, dtype=F32, kind="ExternalInput": nc.dram_tensor(name, list(shape), dtype, kind=kind).ap()
    x = dt("x", [S, D])
    out = dt("out", [S, D], kind="ExternalOutput")
    P = {}
    for nm, shp in IN_SHAPES:
        P[nm] = dt(nm, shp)
    for nm, shp, ty in CONST_SHAPES:
        P[nm] = dt(nm, shp, ty)
    I = "Internal"
    if DBG_OUT:
        I = "ExternalOutput"
    x1 = dt("x1", [S, D], kind=I)
    x2 = dt("x2", [S, D], kind=I)
    scr = {
        "QN": dt("s_qn", [4, 128, S], BF16, I), "KC": dt("s_kc", [2, 128, S], F32, I),
        "KS": dt("s_ks", [2, 128, S], BF16, I), "KW": dt("s_kw", [2, 128, S], BF16, I),
        "VS": dt("s_vs", [S, 130], BF16, I), "VW": dt("s_vw", [S, 130], BF16, I),
        "GN": dt("s_gn", [S, 24], F32, I), "QD": dt("s_qd", [6, 128, S], BF16, I),
        "KD": dt("s_kd", [6, 128, S], BF16, I), "VD": dt("s_vd", [3, S, 260], BF16, I),
        "MG": dt("s_mg", [S, 2048], BF16, I), "ON": dt("s_on", [S, 512], BF16, I),
        "OD": dt("s_od", [3, S, 260], F32, I),
    }

    import contextlib
    with contextlib.ExitStack() as st:
        big = st.enter_context(nc.sbuf_tensor("big", [128, SBUF_BYTES // 4], F32))
        ps = st.enter_context(nc.psum_tensor("ps", [128, 4096], F32))
        M = Mem(big, ps)
        sc = Sched()
        C = {"r_dram": {}}
        ident = M.bf(128)
        identf = M.f32(128)
        C["kcT"] = r3(M.bf(2 * 256), 2)
        C["VC"] = M.bf(2 * 2 * 128)
        C["r_kcT"], C["r_VC"] = Res(), Res()
        r_ident = Res()
        sc.dma(lambda e: e.dma_start(out=ident, in_=P["c_ident"]), writes=[r_ident])
        sc.dma(lambda e: e.dma_start(out=identf, in_=P["c_identf"]), writes=[r_ident])
        C["ident"] = ident
        C["identf"] = identf
        zeros = M.bf(512)
        r_z = Res()
        sc.op("pool", lambda e: e.memset(zeros, 0.0), writes=[r_z])
        C["zeros"] = zeros
        ZEROS[0] = zeros
        C["const_end"] = M.off
        sc.barrier()
        if STAGE == 1:
            ffn_phase(sc, M, C, x, out, P["ffn1_w_gate"], P["ffn1_w_up"], P["ffn1_w_down"], P["ffn1_pre"], P["ffn1_post"])
        else:
            if DBG_SKIP_FFN1:
                x1 = x
            else:
                ffn_phase(sc, M, C, x, x1, P["ffn1_w_gate"], P["ffn1_w_up"], P["ffn1_w_down"], P["ffn1_pre"], P["ffn1_post"])
            if DBG_UPTO >= 1:
                proj_phase(sc, M, C, x1, P["w_in"], P["mix_pre"], scr)
            if DBG_UPTO >= 2:
                cmp_phase(sc, M, C, scr, P)
            if DBG_UPTO >= 3:
                nsa_phase(sc, M, C, scr, P)
            if DBG_UPTO >= 4:
                dil_phase(sc, M, C, scr, P)
            if DBG_UPTO >= 5:
                final_phase(sc, M, C, scr, P, x1, out if STAGE == 2 else x2, P["mix_post"])
            if STAGE >= 3:
                ffn_phase(sc, M, C, x2, out, P["ffn2_w_gate"], P["ffn2_w_up"], P["ffn2_w_down"], P["ffn2_pre"], P["ffn2_post"])
        sc.barrier()
        sc.emit(nc)
    return nc


DBG_SKIP_FFN1 = False
DBG_UPTO = 9
DBG_OUT = False
DBG_RES = None
_NC = None


def kernel(**inputs):
    global _NC
    if _NC is None:
        _NC = build_nc()
    nc = _NC
    x = np.ascontiguousarray(inputs["x"], dtype=np.float32)
    consts = _consts()
    shared = {}
    for nm, shp in IN_SHAPES:
        shared[nm] = np.ascontiguousarray(np.asarray(inputs[nm], dtype=np.float32)[0])
    in_maps = []
    for b in range(8):
        m = {"x": x[b]}
        m.update(shared)
        m.update(consts)
        in_maps.append(m)
    res = run_bass_kernel_spmd(nc, in_maps, core_ids=list(range(8)))
    if DBG_OUT:
        global DBG_RES
        DBG_RES = res.results[0]
    return np.stack([np.asarray(r["out"]) for r in res.results], axis=0).astype(np.float32)
```

```python
import numpy as np
import concourse.bass as bass
import concourse.mybir as mybir
from concourse.alu_op_type import AluOpType as ALU
from concourse.bass_utils import run_bass_kernel_spmd

F32 = mybir.dt.float32
BF16 = mybir.dt.bfloat16
AF = mybir.ActivationFunctionType

S = 4096
D = 1024
DFF = 2816
NFF = 22
EPS = 1e-6
SBUF_BYTES = 212000
LIMIT = 9
STAGE = 3


class Res:
    __slots__ = ("name", "lw", "rdc", "rdd")

    def __init__(self, name=""):
        self.name = name
        self.lw = None
        self.rdc = {}
        self.rdd = []


ENGS = ("pe", "act", "dve", "pool", "sp")
SAME_ENG_SKIP = ("pe", "sp")


class Sched:
    def __init__(self, nring=28):
        self.ops = {e: [] for e in ENGS}
        self.ndma = 0
        self.nring = nring
        self.last_dma = {}

    def _deps(self, reads, writes):
        dc = {}
        dd = set()

        def add(ev):
            if ev is None:
                return
            if ev[0] == "c":
                if dc.get(ev[1], -1) < ev[2]:
                    dc[ev[1]] = ev[2]
            else:
                dd.add(ev)

        for r in reads:
            add(r.lw)
        for w in writes:
            add(w.lw)
            for e, s in w.rdc.items():
                add(("c", e, s))
            for ev in w.rdd:
                add(ev)
        return dc, dd

    def _mark(self, ev, reads, writes):
        for r in reads:
            if ev[0] == "c":
                if r.rdc.get(ev[1], -1) < ev[2]:
                    r.rdc[ev[1]] = ev[2]
            else:
                r.rdd.append(ev)
        for w in writes:
            w.lw = ev
            w.rdc = {}
            w.rdd = []

    def op(self, eng, fn, reads=(), writes=()):
        dc, dd = self._deps(reads, writes)
        seq = len(self.ops[eng])
        ev = ("c", eng, seq)
        self.ops[eng].append(dict(fn=fn, dc=dc, dd=dd, dma=None))
        self._mark(ev, reads, writes)
        return ev

    def dma(self, fn, reads=(), writes=(), q="sp"):
        dc, dd = self._deps(reads, writes)
        k = self.ndma
        self.ndma += 1
        slot = k % self.nring
        val = 16 * (k // self.nring + 1)
        if k >= self.nring:
            dd.add(("d", slot, val - 16))
        ev = ("d", slot, val)
        self.ops[q].append(dict(fn=fn, dc=dc, dd=dd, dma=slot))
        self._mark(ev, reads, writes)
        self.last_dma[slot] = val
        return ev

    def barrier(self):
        last = {}
        for e in ENGS:
            last[e] = -1
            for i in range(len(self.ops[e]) - 1, -1, -1):
                if self.ops[e][i]["fn"] is not None and self.ops[e][i]["dma"] is None:
                    last[e] = i
                    break
        dds = set(("d", s, v) for s, v in self.last_dma.items())
        for e in ENGS:
            dc = {o: last[o] for o in ENGS if o != e and o != 'sp' and last[o] >= 0}
            self.ops[e].append(dict(fn=None, dc=dc, dd=set(dds), dma=None))

    def emit(self, nc):
        needed = {e: set() for e in ENGS}
        for e in ENGS:
            for o in self.ops[e]:
                for oe, s in o["dc"].items():
                    if oe == e and e in SAME_ENG_SKIP:
                        continue
                    needed[oe].add(s)
        rank = {e: {s: i + 1 for i, s in enumerate(sorted(needed[e]))} for e in ENGS}
        ops = self.ops
        nring = self.nring
        import contextlib

        with contextlib.ExitStack() as st:
            sems = {e: st.enter_context(nc.semaphore("s_" + e)) for e in ENGS}
            ring = [st.enter_context(nc.semaphore("r%d" % i)) for i in range(nring)]
            block = st.enter_context(nc.Block())

            def run(ename):
                def body(eng):
                    known = {}
                    for seq, o in enumerate(ops[ename]):
                        waits = {}
                        for oe, s in o["dc"].items():
                            if oe == ename and ename in SAME_ENG_SKIP:
                                continue
                            key = ("c", oe)
                            v = rank[oe][s]
                            if known.get(key, 0) >= v:
                                continue
                            if waits.get(key, 0) < v:
                                waits[key] = v
                        for ev in o["dd"]:
                            key = ("d", ev[1])
                            v = ev[2]
                            if known.get(key, 0) >= v:
                                continue
                            if waits.get(key, 0) < v:
                                waits[key] = v
                        for key, v in waits.items():
                            sem = sems[key[1]] if key[0] == "c" else ring[key[1]]
                            eng.wait_ge(sem, v)
                            known[key] = v
                        if o["fn"] is None:
                            continue
                        ins = o["fn"](eng)
                        if o["dma"] is not None:
                            ins.then_inc(ring[o["dma"]], 16)
                        elif seq in rank[ename]:
                            ins.then_inc(sems[ename], 1)

                return body

            block.tensor(run("pe"))
            block.scalar(run("act"))
            block.vector(run("dve"))
            block.gpsimd(run("pool"))
            block.sync(run("sp"))


class Mem:
    def __init__(self, big, ps):
        self.big = big
        self.ps = ps
        self.off = 0

    def reset(self, off=0):
        self.off = off

    def f32(self, n, p0=0, p1=128):
        o = (self.off + 31) // 32 * 32
        self.off = o + 4 * n
        assert self.off <= SBUF_BYTES, self.off
        return self.big[p0:p1, o // 4:o // 4 + n]

    def bf(self, n, p0=0, p1=128):
        o = (self.off + 31) // 32 * 32
        self.off = o + 2 * n
        assert self.off <= SBUF_BYTES, self.off
        return self.big[p0:p1, o // 4:o // 4 + (n + 1) // 2].bitcast(BF16)

    def bank(self, b, n=512, o=0):
        return self.ps[:, b * 512 + o:b * 512 + o + n]

    def bank_bf(self, b):
        return self.ps[:, b * 512:(b + 1) * 512].bitcast(BF16)


def r3(ap, a):
    return ap.rearrange("p (a b) -> p a b", a=a)


def ffn_phase(sc, M, C, x_src, x_dst, wg, wu, wd, gpre, gpost):
    M.reset(C["const_end"])
    ident = C["ident"]
    Wg = r3(M.bf(8 * DFF), 8)
    Wu = r3(M.bf(8 * DFF), 8)
    Wd = r3(M.bf(NFF * D), NFF)
    stage = [M.f32(1024) for _ in range(3)]
    gp = M.f32(1024)
    gq = M.f32(1024)
    xp0 = M.f32(1024)
    xp = [xp0, xp0]
    xr = [M.f32(1024) for _ in range(2)]
    hb0 = M.bf(1024)
    hb = [hb0, hb0]
    hT = r3(M.bf(8 * 512), 8)
    AT = r3(M.bf(NFF * 512), NFF)
    sg0 = M.f32(512)
    sg = [sg0, sg0]
    yt = M.f32(1024)
    small = M.f32(32)

    r_stage = [Res("stage%d" % i) for i in range(3)]
    r_Wg = [[Res() for _ in range(4)] for _ in range(8)]
    r_Wu = [[Res() for _ in range(4)] for _ in range(8)]
    r_Wd = [Res() for _ in range(NFF)]
    r_gp, r_gq = Res(), Res()
    r_xp0 = Res()
    r_xp = [r_xp0, r_xp0]
    r_xr = [Res(), Res()]
    r_hb0 = Res()
    r_hb = [r_hb0, r_hb0]
    r_hT, r_AT = Res(), [Res() for _ in range(NFF)]
    r_sg0 = Res()
    r_sg = [r_sg0, r_sg0]
    r_yt = Res()
    r_small = [Res() for _ in range(8)]
    r_T = Res()
    r_G = [Res(), Res()]
    r_U = [Res(), Res()]
    r_Y = Res()
    r_xdst = C["r_dram"][id(x_dst)] if id(x_dst) in C["r_dram"] else Res()
    r_xsrc = C["r_dram"].get(id(x_src), Res())
    C["r_dram"][id(x_dst)] = r_xdst

    sc.dma(lambda e: e.dma_start(out=gp, in_=gpre[0, :].partition_broadcast(128)), writes=[r_gp])
    sc.dma(lambda e: e.dma_start(out=gq, in_=gpost[0, :].partition_broadcast(128)), writes=[r_gq])
    sc.op("act", lambda e: e.mul(gq, gq, 0.5), reads=[r_gq], writes=[r_gq])

    cnt = [0]
    CB = [(0, 768), (768, 768), (1536, 768), (2304, 512)]

    def load_piece(dst_ap, src_ap, n, rdst):
        i = cnt[0] % 3
        eng = ("pool", "act", "dve")[cnt[0] % 3]
        cnt[0] += 1
        st = stage[i][:, 0:n]
        sc.dma(lambda e: e.dma_start(out=st, in_=src_ap), writes=[r_stage[i]])
        if eng == "act":
            sc.op("act", lambda e: e.copy(out=dst_ap, in_=st), reads=[r_stage[i]], writes=[rdst])
        else:
            sc.op(eng, lambda e: e.tensor_copy(out=dst_ap, in_=st), reads=[r_stage[i]], writes=[rdst])

    def load_weights():
        for cb, (c0, w) in enumerate(CB):
            for kc in range(8):
                for (W, wsrc, rW) in ((Wg, wg, r_Wg), (Wu, wu, r_Wu)):
                    load_piece(W[:, kc, c0:c0 + w], wsrc[kc * 128:(kc + 1) * 128, c0:c0 + w], w, rW[kc][cb])
        for f in range(NFF):
            load_piece(Wd[:, f, :], wd[f * 128:(f + 1) * 128, :], 1024, r_Wd[f])

    inv_d = 1.0 / D

    def rstd_from(ss_ap, out_ap, rs):
        sc.op("dve", lambda e: e.tensor_scalar(out=out_ap, in0=ss_ap, scalar1=inv_d, scalar2=EPS,
                                               op0=ALU.mult, op1=ALU.add), reads=[rs], writes=[rs])
        sc.op("act", lambda e: e.sqrt(out_ap, out_ap), reads=[rs], writes=[rs])
        sc.op("dve", lambda e: e.reciprocal(out=out_ap, in_=out_ap), reads=[rs], writes=[rs])

    NT = S // 512
    Tb = M.bank_bf(0)

    def prep(i):
        for j in range(4):
            t = i * 4 + j
            b = t % 2
            x_ap = xp[b]
            sc.dma(lambda e, x_ap=x_ap, t=t: e.dma_start(out=x_ap, in_=x_src[t * 128:(t + 1) * 128, :]),
                   reads=[r_xsrc], writes=[r_xp[b]])
            ss = small[:, b:b + 1]
            sc.op("pool", lambda e, ss=ss: e.memset(ss, 0.0), writes=[r_small[b]])
            sc.op("act", lambda e, x_ap=x_ap, b=b, ss=ss: e.activation(out=hb[b], in_=x_ap, func=AF.Square, accum_out=ss),
                  reads=[r_xp[b]], writes=[r_hb[b], r_small[b]])
            rstd_from(ss, ss, r_small[b])
            sc.op("dve", lambda e, x_ap=x_ap, b=b, ss=ss: e.scalar_tensor_tensor(
                out=hb[b], in0=x_ap, scalar=ss, in1=gp, op0=ALU.mult, op1=ALU.mult),
                reads=[r_xp[b], r_small[b], r_gp], writes=[r_hb[b]])
            for kc in range(8):
                sc.op("pe", lambda e, b=b, kc=kc: e.transpose(out=Tb[:, kc * 128:(kc + 1) * 128],
                                                             in_=hb[b][:, kc * 128:(kc + 1) * 128], identity=ident),
                      reads=[r_hb[b]], writes=[r_T])
            sc.op("act", lambda e, j=j: e.copy(out=hT[:, :, j * 128:(j + 1) * 128], in_=r3(Tb, 8)),
                  reads=[r_T], writes=[r_hT])

    def gateup(i):
        for f in range(NFF):
            pb = f % 2
            G = M.bank(1 + pb)
            U = M.bank(3 + pb)
            for kc in range(8):
                sc.op("pe", lambda e, G=G, kc=kc, f=f: e.matmul(G, lhsT=Wg[:, kc, f * 128:(f + 1) * 128], rhs=hT[:, kc, :],
                                                               start=(kc == 0), stop=(kc == 7)),
                      reads=[r_Wg[kc][min(3, (f * 128) // 768)], r_hT], writes=[r_G[pb]])
            for kc in range(8):
                sc.op("pe", lambda e, U=U, kc=kc, f=f: e.matmul(U, lhsT=Wu[:, kc, f * 128:(f + 1) * 128], rhs=hT[:, kc, :],
                                                               start=(kc == 0), stop=(kc == 7)),
                      reads=[r_Wu[kc][min(3, (f * 128) // 768)], r_hT], writes=[r_U[pb]])
            sc.op("act", lambda e, G=G, pb=pb: e.activation(out=sg[pb], in_=G, func=AF.Silu),
                  reads=[r_G[pb]], writes=[r_sg[pb]])
            sc.op("dve", lambda e, U=U, pb=pb, f=f: e.tensor_tensor(out=AT[:, f, :], in0=sg[pb], in1=U, op=ALU.mult),
                  reads=[r_sg[pb], r_U[pb]], writes=[r_AT[f]])

    YB = [(5, 6), (7, 0)]
    r_Yp = [[Res()], [Res(), r_T]]

    def down(i):
        for j in range(4):
            t = i * 4 + j
            b = t % 2
            yp = j % 2
            rY = r_Yp[yp]
            sc.dma(lambda e, b=b, t=t: e.dma_start(out=xr[b], in_=x_src[t * 128:(t + 1) * 128, :]),
                   reads=[r_xsrc], writes=[r_xr[b]])
            for dh in range(2):
                Y = M.bank(YB[yp][dh])
                for f in range(NFF):
                    sc.op("pe", lambda e, Y=Y, f=f, j=j, dh=dh: e.matmul(
                        Y, lhsT=AT[:, f, j * 128:(j + 1) * 128], rhs=Wd[:, f, dh * 512:(dh + 1) * 512],
                        start=(f == 0), stop=(f == NFF - 1)),
                        reads=[r_AT[f], r_Wd[f]], writes=rY)
            ssa = small[:, 4:5]
            ssb = small[:, 5:6]
            Y0, Y1 = M.bank(YB[yp][0]), M.bank(YB[yp][1])
            sc.op("pool", lambda e: e.memset(small[:, 4:6], 0.0), writes=[r_small[4]])
            sc.op("act", lambda e, ssa=ssa, Y0=Y0: e.activation(out=yt[:, 0:512], in_=Y0, func=AF.Square, accum_out=ssa),
                  reads=rY, writes=[r_yt, r_small[4]])
            sc.op("act", lambda e, ssb=ssb, Y1=Y1: e.activation(out=yt[:, 512:1024], in_=Y1, func=AF.Square, accum_out=ssb),
                  reads=rY, writes=[r_yt, r_small[4]])
            sc.op("dve", lambda e, ssa=ssa, ssb=ssb: e.tensor_tensor(out=ssa, in0=ssa, in1=ssb, op=ALU.add),
                  reads=[r_small[4]], writes=[r_small[4]])
            rstd_from(ssa, ssa, r_small[4])
            for dh in range(2):
                Yd = M.bank(YB[yp][dh])
                sc.op("dve", lambda e, dh=dh, ssa=ssa, Yd=Yd: e.scalar_tensor_tensor(
                    out=yt[:, dh * 512:(dh + 1) * 512], in0=Yd, scalar=ssa,
                    in1=gq[:, dh * 512:(dh + 1) * 512], op0=ALU.mult, op1=ALU.mult),
                    reads=rY + [r_small[4], r_gq], writes=[r_yt])
            sc.op("dve", lambda e, b=b: e.tensor_tensor(out=xr[b], in0=yt, in1=xr[b], op=ALU.add),
                  reads=[r_yt, r_xr[b]], writes=[r_xr[b]])
            sc.dma(lambda e, b=b, t=t: e.dma_start(out=x_dst[t * 128:(t + 1) * 128, :], in_=xr[b]),
                   reads=[r_xr[b]], writes=[r_xdst], q="act")

    if LIMIT >= 1:
        prep(0)
    load_weights()
    for i in range(NT):
        if LIMIT == 2 and i == 0:
            gateup(i)
        if LIMIT == 3 and i == 0:
            gateup(i)
            down(i)
        if LIMIT < 9:
            continue
        gateup(i)
        if i + 1 < NT:
            prep(i + 1)
        down(i)
    sc.barrier()


NEG = -30000.0
ZEROS = [None]
NSA_SL = [2.0 ** (-(i + 1)) for i in range(8)]
DIL_SL = [2.0 ** (-8.0 * (i + 1) / 12) for i in range(12)]
DILS = (1, 4, 16)
C_QN, C_KV, C_GN, C_QD, C_KD, C_VD, C_MG = 0, 512, 1280, 1304, 2072, 2840, 3608


def v4(ap, a, b):
    return ap.rearrange("p (a b c) -> p a b c", a=a, b=b)


def norm_to_hT(sc, M, x_src, t, xp, hb, small, gp, ident, rr, dstT, r_dst, Tb, r_T):
    b = t % 2
    r_xp, r_hb, r_small, r_gp = rr
    sc.dma(lambda e: e.dma_start(out=xp[b], in_=x_src[t * 128:(t + 1) * 128, :]), writes=[r_xp[b]])
    ss = small[:, b:b + 1]
    sc.op("pool", lambda e: e.memset(ss, 0.0), writes=[r_small[b]])
    sc.op("act", lambda e: e.activation(out=hb[b], in_=xp[b], func=AF.Square, accum_out=ss),
          reads=[r_xp[b]], writes=[r_hb[b], r_small[b]])
    sc.op("dve", lambda e: e.tensor_scalar(out=ss, in0=ss, scalar1=1.0 / D, scalar2=EPS, op0=ALU.mult, op1=ALU.add),
          reads=[r_small[b]], writes=[r_small[b]])
    sc.op("act", lambda e: e.sqrt(ss, ss), reads=[r_small[b]], writes=[r_small[b]])
    sc.op("dve", lambda e: e.reciprocal(out=ss, in_=ss), reads=[r_small[b]], writes=[r_small[b]])
    sc.op("dve", lambda e: e.scalar_tensor_tensor(out=hb[b], in0=xp[b], scalar=ss, in1=gp, op0=ALU.mult, op1=ALU.mult),
          reads=[r_xp[b], r_small[b], r_gp], writes=[r_hb[b]])
    for kc in range(8):
        sc.op("pe", lambda e, kc=kc: e.transpose(out=Tb[:, kc * 128:(kc + 1) * 128],
                                                 in_=hb[b][:, kc * 128:(kc + 1) * 128], identity=ident),
              reads=[r_hb[b]], writes=[r_T])
    sc.op("act", lambda e: e.copy(out=dstT, in_=r3(Tb, 8)), reads=[r_T], writes=[r_dst])


def proj_phase(sc, M, C, x1, w_in, gmix, scr):
    M.reset(C["const_end"])
    ident = C["ident"]
    h2T = r3(M.bf(8 * S), 8)
    Wb = [r3(M.bf(8 * 512), 8) for _ in range(2)]
    stage = [M.f32(512) for _ in range(2)]
    gp = M.f32(1024)
    xp = [M.f32(1024) for _ in range(2)]
    hb = [M.bf(1024) for _ in range(2)]
    small = M.f32(8)
    evf = [M.f32(512) for _ in range(2)]
    evb = [M.bf(512) for _ in range(2)]
    vaug = [M.bf(4 * 65) for _ in range(2)]
    r_h2T = [Res() for _ in range(32)]
    r_Wb = [Res(), Res()]
    r_stage = [Res(), Res()]
    r_gp = Res()
    rr = ([Res(), Res()], [Res(), Res()], [Res(), Res()], r_gp)
    r_ev = [Res(), Res()]
    r_va = [Res(), Res()]
    r_T = Res()
    r_ps = [Res(), Res()]
    Tb = M.bank_bf(0)
    sc.dma(lambda e: e.dma_start(out=gp, in_=gmix[0, :].partition_broadcast(128)), writes=[r_gp])
    for i in range(2):
        sc.op("pool", lambda e, i=i: e.memset(vaug[i], 1.0), writes=[r_va[i]])
    for t in range(32):
        norm_to_hT(sc, M, x1, t, xp, hb, small, gp, ident, rr, h2T[:, :, t * 128:(t + 1) * 128], r_h2T[t], Tb, r_T)

    cnt = [0]
    nspec = [0]

    def load_wb(bi, cols):
        for kc in range(8):
            off = 0
            for (c, w) in cols:
                i = cnt[0] % 2
                cnt[0] += 1
                st = stage[i][:, 0:w]
                sc.dma(lambda e, st=st, c=c, w=w, kc=kc: e.dma_start(out=st, in_=w_in[kc * 128:(kc + 1) * 128, c:c + w]),
                       writes=[r_stage[i]])
                dst = Wb[bi][:, kc, off:off + w]
                sc.op("pool", lambda e, st=st, dst=dst: e.tensor_copy(out=dst, in_=st), reads=[r_stage[i]], writes=[r_Wb[bi]])
                off += w

    ecnt = [0]

    def run_fm(cols, dst, fp32, dil):
        bi = nspec[0] % 2
        nspec[0] += 1
        load_wb(bi, cols)
        Ls = S // dil
        Cn = min(512, Ls)
        h4 = h2T.rearrange("p k (i r) -> p k i r", r=dil)
        for p0 in range(0, S, Cn):
            r, i0 = divmod(p0, Ls)
            k = ecnt[0] % 2
            ecnt[0] += 1
            PS = M.bank(1 + k)[:, 0:Cn]
            for kc in range(8):
                sc.op("pe", lambda e, PS=PS, kc=kc, i0=i0, r=r: e.matmul(
                    PS, lhsT=Wb[bi][:, kc, 0:128], rhs=h4[:, kc, i0:i0 + Cn, r], start=(kc == 0), stop=(kc == 7)),
                    reads=[r_Wb[bi]] + r_h2T, writes=[r_ps[k]])
            evt = (evf[k] if fp32 else evb[k])[:, 0:Cn]
            sc.op("act", lambda e, PS=PS, evt=evt: e.copy(out=evt, in_=PS), reads=[r_ps[k]], writes=[r_ev[k]])
            sc.dma(lambda e, evt=evt, p0=p0: e.dma_start(out=dst[:, p0:p0 + Cn], in_=evt), reads=[r_ev[k]], q="act")

    def run_tm(c0, n, dst, kind, dil, nh=0):
        bi = nspec[0] % 2
        nspec[0] += 1
        load_wb(bi, [(c0, n)])
        Ls = S // dil
        h4 = h2T.rearrange("p k (i r) -> p k i r", r=dil)
        for t in range(32):
            p0 = t * 128
            r, i0 = divmod(p0, Ls)
            k = ecnt[0] % 2
            ecnt[0] += 1
            PS = M.bank(1 + k)[:, 0:n]
            for kc in range(8):
                sc.op("pe", lambda e, PS=PS, kc=kc, i0=i0, r=r: e.matmul(
                    PS, lhsT=h4[:, kc, i0:i0 + 128, r], rhs=Wb[bi][:, kc, 0:n], start=(kc == 0), stop=(kc == 7)),
                    reads=[r_Wb[bi]] + r_h2T, writes=[r_ps[k]])
            if kind == "aug":
                va = vaug[k][:, 0:nh * 65]
                sc.op("act", lambda e, PS=PS, va=va: e.copy(out=r3(va, nh)[:, :, 0:64], in_=r3(PS, nh)),
                      reads=[r_ps[k]], writes=[r_va[k]])
                sc.dma(lambda e, va=va, p0=p0: e.dma_start(out=dst[p0:p0 + 128, :], in_=va), reads=[r_va[k]], q="act")
            elif kind == "sigf":
                evt = evf[k][:, 0:n]
                sc.op("act", lambda e, PS=PS, evt=evt: e.activation(out=evt, in_=PS, func=AF.Sigmoid),
                      reads=[r_ps[k]], writes=[r_ev[k]])
                sc.dma(lambda e, evt=evt, p0=p0: e.dma_start(out=dst[p0:p0 + 128, :], in_=evt), reads=[r_ev[k]], q="act")
            else:
                evt = evb[k][:, 0:n]
                sc.op("act", lambda e, PS=PS, evt=evt: e.activation(out=evt, in_=PS, func=AF.Sigmoid),
                      reads=[r_ps[k]], writes=[r_ev[k]])
                sc.dma(lambda e, evt=evt, p0=p0: e.dma_start(out=dst[p0:p0 + 128, :], in_=evt), reads=[r_ev[k]], q="act")

    for p in range(4):
        run_fm([(C_QN + p * 128, 128)], scr["QN"][p], False, 1)
    run_fm([(C_KV, 128)], scr["KC"][0], True, 1)
    run_fm([(C_KV + 128, 128)], scr["KC"][1], True, 1)
    for g in range(2):
        run_fm([(C_KV + 256 + g * 64, 64)] * 2, scr["KS"][g], False, 1)
        run_fm([(C_KV + 512 + g * 64, 64)] * 2, scr["KW"][g], False, 1)
    run_tm(C_KV + 384, 128, scr["VS"], "aug", 1, nh=2)
    run_tm(C_KV + 640, 128, scr["VW"], "aug", 1, nh=2)
    run_tm(C_GN, 24, scr["GN"], "sigf", 1)
    for gi in range(3):
        for pp in range(2):
            run_fm([(C_QD + gi * 256 + pp * 128, 128)], scr["QD"][gi * 2 + pp], False, DILS[gi])
            run_fm([(C_KD + gi * 256 + pp * 128, 128)], scr["KD"][gi * 2 + pp], False, DILS[gi])
        run_tm(C_VD + gi * 256, 256, scr["VD"][gi], "aug", DILS[gi], nh=4)
    for q in range(4):
        run_tm(C_MG + q * 512, 512, scr["MG"][:, q * 512:(q + 1) * 512], "sigb", 1)
    sc.barrier()


def cmp_phase(sc, M, C, scr, P):
    M.reset(C["const_end"])
    kcT, VC = C["kcT"], C["VC"]
    kcf = M.f32(S)
    tb = r3(M.bf(32 * 256), 32)
    W1b = r3(M.bf(32 * 256), 32)
    W2b = v4(M.bf(2 * 128), 2, 2)
    st1 = [M.f32(256) for _ in range(2)]
    st2 = M.f32(128)
    peT = M.f32(32)
    hx = [M.f32(256) for _ in range(4)]
    hu = M.f32(256)
    gT = [M.bf(256) for _ in range(4)]
    r_kcf, r_tb, r_W1, r_W2, r_st2, r_pe = Res(), Res(), Res(), Res(), Res(), Res()
    r_st1 = [Res(), Res()]
    r_hx = [Res() for _ in range(4)]
    r_hu = Res()
    r_gT = [Res() for _ in range(4)]
    r_ps = [Res(), Res(), Res()]
    r_kcT, r_VC = C["r_kcT"], C["r_VC"]
    VC4 = v4(VC, 2, 2)
    sc.op("pool", lambda e: e.memset(VC, 1.0), writes=[r_VC])
    for ct in range(2):
        for g in range(2):
            sc.dma(lambda e, ct=ct, g=g: e.dma_start(out=VC4[:, ct, g, 65:128], in_=P["c_ov"][ct * 128:(ct + 1) * 128, 1:64]),
                   writes=[r_VC])
    def do_kv(kv):
        pe_d = P["nsa_pe_k"] if kv == 0 else P["nsa_pe_v"]
        w1_d = P["nsa_w_ck1"] if kv == 0 else P["nsa_w_cv1"]
        w2_d = P["nsa_w_ck2"] if kv == 0 else P["nsa_w_cv2"]
        sc.dma(lambda e: e.dma_start(out=kcf, in_=scr["KC"][kv]), writes=[r_kcf])
        for g in range(2):
            sc.dma(lambda e, g=g, pe_d=pe_d: e.dma_start(out=peT[g * 64:(g + 1) * 64, :], in_=pe_d.rearrange("l d -> d l"),
                                                        allow_slow_non_contiguous=True), writes=[r_pe])
        sc.op("pool", lambda e: e.memset(tb.rearrange("p a b -> p (a b)"), 0.0), writes=[r_tb])
        kcf3 = kcf.rearrange("p (a b) -> p a b", b=16)
        for l in range(32):
            src = kcf3[:, 0:255, l] if l < 16 else kcf3[:, 1:256, l - 16]
            sc.op("dve", lambda e, l=l, src=src: e.tensor_scalar(out=tb[:, l, 0:255], in0=src, scalar1=peT[:, l:l + 1],
                                                                 scalar2=None, op0=ALU.add),
                  reads=[r_kcf, r_pe], writes=[r_tb])
        w1v = w1_d.rearrange("(l d) h -> d l h", d=64)
        for l in range(32):
            i = l % 2
            for g in range(2):
                sc.dma(lambda e, l=l, g=g, i=i: e.dma_start(out=st1[i][g * 64:(g + 1) * 64, :], in_=w1v[:, l, :]),
                       writes=[r_st1[i]])
            sc.op("pool", lambda e, l=l, i=i: e.tensor_copy(out=W1b[:, l, :], in_=st1[i]), reads=[r_st1[i]], writes=[r_W1])
        sc.dma(lambda e: e.dma_start(out=r3(st2, 2), in_=w2_d.rearrange("(hc p) d -> p hc d", p=128)), writes=[r_st2])
        sc.op("pool", lambda e: e.tensor_copy(out=W2b[:, :, 0, :], in_=r3(st2, 2)), reads=[r_st2], writes=[r_W2])
        sc.op("pool", lambda e: e.memset(W2b[:, :, 1, :], 0.0), writes=[r_W2])
        for g in range(2):
            for hc in range(2):
                idx = g * 2 + hc
                PS = M.bank(1 + idx % 2)[:, 0:256]
                rp = r_ps[idx % 2]
                for l in range(32):
                    sc.op("pe", lambda e, PS=PS, l=l, g=g, hc=hc: e.matmul(
                        PS, lhsT=W1b[g * 64:(g + 1) * 64, l, hc * 128:(hc + 1) * 128], rhs=tb[g * 64:(g + 1) * 64, l, :],
                        start=(l == 0), stop=(l == 31)), reads=[r_W1, r_tb], writes=[rp])
                x_ = hx[idx]
                sc.op("act", lambda e, PS=PS, x_=x_: e.copy(out=x_, in_=PS), reads=[rp], writes=[r_hx[idx]])
                sc.op("act", lambda e, x_=x_: e.activation(out=hu, in_=x_, func=AF.Square), reads=[r_hx[idx]], writes=[r_hu])
                sc.op("dve", lambda e: e.tensor_scalar(out=hu, in0=hu, scalar1=0.044715, scalar2=1.0, op0=ALU.mult, op1=ALU.add),
                      reads=[r_hu], writes=[r_hu])
                sc.op("dve", lambda e, x_=x_: e.tensor_tensor(out=hu, in0=hu, in1=x_, op=ALU.mult),
                      reads=[r_hu, r_hx[idx]], writes=[r_hu])
                sc.op("act", lambda e: e.activation(out=hu, in_=hu, func=AF.Sigmoid, scale=1.5957691216057308),
                      reads=[r_hu], writes=[r_hu])
                sc.op("dve", lambda e, x_=x_, idx=idx: e.tensor_tensor(out=gT[idx], in0=hu, in1=x_, op=ALU.mult),
                      reads=[r_hu, r_hx[idx]], writes=[r_gT[idx]])
        for g in range(2):
            if kv == 0:
                PS = M.bank(3)[:, 0:256]
                for hc in range(2):
                    sc.op("pe", lambda e, PS=PS, g=g, hc=hc: e.matmul(
                        PS, lhsT=W2b[:, hc, :, :].rearrange("p a b -> p (a b)"), rhs=gT[g * 2 + hc],
                        start=(hc == 0), stop=(hc == 1)), reads=[r_W2, r_gT[g * 2 + hc]], writes=[r_ps[2]])
                sc.op("act", lambda e, PS=PS, g=g: e.copy(out=kcT[:, g, :], in_=PS), reads=[r_ps[2]], writes=[r_kcT])
            else:
                for ct in range(2):
                    PS = M.bank(3)[:, 0:64]
                    for hc in range(2):
                        sc.op("pe", lambda e, PS=PS, g=g, hc=hc, ct=ct: e.matmul(
                            PS, lhsT=gT[g * 2 + hc][:, ct * 128:(ct + 1) * 128], rhs=W2b[:, hc, 0, :],
                            start=(hc == 0), stop=(hc == 1)), reads=[r_W2, r_gT[g * 2 + hc]], writes=[r_ps[2]])
                    sc.op("act", lambda e, PS=PS, g=g, ct=ct: e.copy(out=VC4[:, ct, g, 0:64], in_=PS),
                          reads=[r_ps[2]], writes=[r_VC])

    for kv in range(2):
        do_kv(kv)
    sc.barrier()


class AttnPipe:
    def __init__(self, sc, M, sbanks, ntmp=3):
        self.sc, self.M = sc, M
        self.sb = sbanks
        self.r_s = [Res() for _ in sbanks]
        self.tmp = [M.f32(512) for _ in range(ntmp)]
        self.Pt = [M.bf(512) for _ in range(ntmp)]
        self.r_tmp = [Res() for _ in range(ntmp)]
        self.r_P = [Res() for _ in range(ntmp)]
        self.n = 0

    def run_stream(self, items, LA=3):
        sc, M = self.sc, self.M
        jobs = [it for it in items if isinstance(it, dict)]
        order = []
        ji = 0
        for it in items:
            if isinstance(it, dict):
                order.append(("job", ji))
                ji += 1
            else:
                order.append(("call", it))
        n = len(jobs)
        slots = {}
        done_pv = [0]
        pending = []
        DELAY = 4

        def emit_pv(j):
            jb = jobs[j]
            ti = slots[j]
            if "parts" in jb:
                while pending and pending[0][0] <= j:
                    pending.pop(0)[2]()
                acc, M_rows, W = jb["acc"], jb["M_rows"], jb["W"]
                nk = jb["nk"]
                if jb["first"]:
                    z = ZEROS[0]
                    sc.op("pe", lambda e: e.matmul(acc[0:M_rows, 0:W], lhsT=z[:, 0:M_rows], rhs=z[:, 0:W], start=True, stop=False),
                          writes=[jb["r_acc"]])
                np_ = len(jb["parts"])
                for pi_, (_l, _r, _rd, off_, n_, pv_, pvr_, c0_) in enumerate(jb["parts"]):
                    Pq = self.Pt[ti][0:nk, off_:off_ + n_]
                    lastp = jb["last"] and pi_ == np_ - 1
                    sc.op("pe", lambda e, Pq=Pq, pv_=pv_, c0_=c0_, n_=n_, lastp=lastp: e.matmul(
                        acc[0:M_rows, c0_:c0_ + n_], lhsT=pv_, rhs=Pq, start=False, stop=lastp),
                        reads=[self.r_P[ti]] + pvr_, writes=[jb["r_acc"]])
                if jb.get("after") is not None:
                    key, fa, fb = jb["after"]
                    while any(p[1] == key for p in pending):
                        pending.pop(0)[2]()
                    fa()
                    pending.append((j + DELAY, key, fb))
                return
            nk, ncol, c0 = jb["nk"], jb["nc"], jb["c0"]
            Pp = self.Pt[ti][0:nk, 0:ncol]
            pv = jb["pv"]
            acc, M_rows = jb["acc"], jb["M_rows"]
            first, last = jb["first"], jb["last"]
            while pending and pending[0][0] <= j:
                pending.pop(0)[2]()
            W = jb.get("W", 512)
            if first and (c0 != 0 or ncol != W):
                z = ZEROS[0]
                sc.op("pe", lambda e: e.matmul(acc[0:M_rows, 0:W], lhsT=z[:, 0:M_rows], rhs=z[:, 0:W], start=True, stop=False),
                      writes=[jb["r_acc"]])
                first = False
            sc.op("pe", lambda e: e.matmul(acc[0:M_rows, c0:c0 + ncol], lhsT=pv, rhs=Pp, start=first, stop=last),
                  reads=[self.r_P[ti]] + jb["pv_reads"], writes=[jb["r_acc"]])
            if jb.get("after") is not None:
                key, fa, fb = jb["after"]
                while any(p[1] == key for p in pending):
                    pending.pop(0)[2]()
                fa()
                pending.append((j + DELAY, key, fb))

        for kind, v in order:
            if kind == "call":
                v()
                continue
            i = v
            jb = jobs[i]
            k = self.n
            self.n += 1
            si = k % len(self.sb)
            ti = k % len(self.tmp)
            slots[i] = ti
            nk, ncol = jb["nk"], jb["nc"]
            Sps = M.bank(self.sb[si])[0:nk, 0:ncol]
            if "parts" in jb:
                for (l_, r_, rd, off_, n_, _pv, _pvr, _c0) in jb["parts"]:
                    Sp_ = M.bank(self.sb[si])[0:nk, off_:off_ + n_]
                    sc.op("pe", lambda e, l_=l_, r_=r_, Sp_=Sp_: e.matmul(Sp_, lhsT=l_, rhs=r_, start=True, stop=True),
                          reads=rd, writes=[self.r_s[si]])
            else:
                nq = len(jb["qk"])
                for qi, (l_, r_, rd) in enumerate(jb["qk"]):
                    sc.op("pe", lambda e, l_=l_, r_=r_, qi=qi, Sps=Sps, nq=nq: e.matmul(
                        Sps, lhsT=l_, rhs=r_, start=(qi == 0), stop=(qi == nq - 1)), reads=rd, writes=[self.r_s[si]])
            Pp = self.Pt[ti][0:nk, 0:ncol]
            bias = jb.get("bias")
            if "parts" in jb:
                tm = self.tmp[ti][0:nk, 0:ncol]
                T = jb["T"]
                sc.op("dve", lambda e, tm=tm, Sps=Sps, T=T: e.scalar_tensor_tensor(
                    out=tm, in0=Sps, scalar=0.125, in1=T, op0=ALU.mult, op1=ALU.add),
                    reads=[self.r_s[si]] + jb.get("T_reads", []), writes=[self.r_tmp[ti]])
                sc.op("act", lambda e, Pp=Pp, tm=tm: e.activation(out=Pp, in_=tm, func=AF.Exp),
                      reads=[self.r_tmp[ti]], writes=[self.r_P[ti]])
            elif jb.get("direct"):
                sc.op("act", lambda e, Pp=Pp, Sps=Sps, bias=bias: e.activation(out=Pp, in_=Sps, func=AF.Exp, bias=bias, scale=0.125),
                      reads=[self.r_s[si]] + jb.get("bias_reads", []), writes=[self.r_P[ti]])
            else:
                tm = self.tmp[ti][0:nk, 0:ncol]
                T = jb["T"]
                sc.op("dve", lambda e, tm=tm, Sps=Sps, T=T: e.scalar_tensor_tensor(
                    out=tm, in0=Sps, scalar=0.125, in1=T, op0=ALU.mult, op1=ALU.add),
                    reads=[self.r_s[si]] + jb.get("T_reads", []), writes=[self.r_tmp[ti]])
                for (off, mk, mrd) in jb["masks"]:
                    w = mk.shape[-1]
                    tmm = self.tmp[ti][0:nk, off:off + w]
                    sc.op("pool", lambda e, tmm=tmm, mk=mk: e.tensor_tensor(out=tmm, in0=tmm, in1=mk, op=ALU.add),
                          reads=[self.r_tmp[ti]] + mrd, writes=[self.r_tmp[ti]])
                sc.op("act", lambda e, Pp=Pp, tm=tm, bias=bias: e.activation(out=Pp, in_=tm, func=AF.Exp, bias=bias),
                      reads=[self.r_tmp[ti]] + jb.get("bias_reads", []), writes=[self.r_P[ti]])
            while done_pv[0] <= i - LA:
                emit_pv(done_pv[0])
                done_pv[0] += 1
        while done_pv[0] < n:
            emit_pv(done_pv[0])
            done_pv[0] += 1
        while pending:
            pending.pop(0)[2]()

    def run(self, jobs, acc, r_acc, M_rows):
        sc, M = self.sc, self.M
        LA = 2
        n = len(jobs)
        slots = []
        for i in range(n + LA):
            if i < n:
                jb = jobs[i]
                k = self.n
                self.n += 1
                si = k % len(self.sb)
                ti = k % len(self.tmp)
                slots.append(ti)
                nk, ncol = jb["nk"], jb["nc"]
                Sps = M.bank(self.sb[si])[0:nk, 0:ncol]
                nq = len(jb["qk"])
                for qi, (l_, r_, rd) in enumerate(jb["qk"]):
                    sc.op("pe", lambda e, Sps=Sps, l_=l_, r_=r_, qi=qi, nq=nq: e.matmul(
                        Sps, lhsT=l_, rhs=r_, start=(qi == 0), stop=(qi == nq - 1)), reads=rd, writes=[self.r_s[si]])
                tm = self.tmp[ti][0:nk, 0:ncol]
                T = jb["T"]
                sc.op("dve", lambda e, tm=tm, Sps=Sps, T=T: e.scalar_tensor_tensor(out=tm, in0=Sps, scalar=0.125, in1=T, op0=ALU.mult, op1=ALU.add),
                      reads=[self.r_s[si]] + jb.get("T_reads", []), writes=[self.r_tmp[ti]])
                for (off, mk, mrd) in jb["masks"]:
                    w = mk.shape[-1]
                    tmm = self.tmp[ti][0:nk, off:off + w]
                    sc.op("pool", lambda e, tmm=tmm, mk=mk: e.tensor_tensor(out=tmm, in0=tmm, in1=mk, op=ALU.add),
                          reads=[self.r_tmp[ti]] + mrd, writes=[self.r_tmp[ti]])
                Pp = self.Pt[ti][0:nk, 0:ncol]
                bias = jb["bias"]
                sc.op("act", lambda e, Pp=Pp, tm=tm, bias=bias: e.activation(out=Pp, in_=tm, func=AF.Exp, bias=bias),
                      reads=[self.r_tmp[ti]] + jb.get("bias_reads", []), writes=[self.r_P[ti]])
            if i >= LA:
                j = i - LA
                jb = jobs[j]
                ti = slots[j]
                nk, ncol, c0 = jb["nk"], jb["nc"], jb["c0"]
                Pp = self.Pt[ti][0:nk, 0:ncol]
                pv = jb["pv"]
                sc.op("pe", lambda e, Pp=Pp, pv=pv, c0=c0, ncol=ncol, j=j: e.matmul(
                    acc[0:M_rows, c0:c0 + ncol], lhsT=pv, rhs=Pp, start=(j == 0), stop=(j == n - 1)),
                    reads=[self.r_P[ti]] + jb["pv_reads"], writes=[r_acc])


def nsa_phase(sc, M, C, scr, P):
    M.reset(C["const_end"])
    identf = C["identf"]
    ident = C["ident"]
    kcT, VC = C["kcT"], C["VC"]
    r_kcT, r_VC = C["r_kcT"], C["r_VC"]
    VC4 = v4(VC, 2, 2)
    KS = r3(M.bf(2 * S), 2)
    KW = r3(M.bf(2 * S), 2)
    VS = v4(M.bf(32 * 256), 32, 2)
    VW = v4(M.bf(32 * 256), 32, 2)
    TQA = r3(M.f32(8 * 512), 8)
    MASKO = M.f32(512)
    TQ8 = r3(M.f32(8 * 4), 8)
    QN = [r3(M.bf(8 * 512), 8) for _ in range(2)]
    TQ = r3(M.f32(8 * 512), 8)
    MK = r3(M.f32(3 * 128), 3)
    BC = r3(M.f32(8 * 35), 8)
    BCC = v4(M.f32(8 * 8 * 2), 8, 8)
    Mc = [r3(M.f32(2 * 512), 2) for _ in range(2)]
    accS = [M.f32(512) for _ in range(2)]
    onsa = r3(M.f32(4 * 512), 4)
    onsab = r3(M.bf(4 * 512), 4)
    impacc = v4(M.f32(2 * 4 * 64), 2, 4)
    GN = [r3(M.f32(4 * 24), 4) for _ in range(2)]
    SELM = [v4(M.f32(4 * 2 * 64), 4, 2) for _ in range(2)]
    rr_ = M.f32(4)
    sc4 = M.f32(4)
    m1 = M.f32(8)
    m2 = M.f32(8)
    wk = M.f32(64)
    impp = r3(M.f32(4 * 64), 4)
    mbf = M.bf(128)
    tnum = r3(M.f32(4 * 64), 4)
    pipe = AttnPipe(sc, M, [0, 1, 2, 6], ntmp=4)
    r_KS, r_KW, r_VS, r_VW, r_c = Res(), Res(), Res(), Res(), Res()
    r_QN = [[Res(), Res()], [Res(), Res()]]
    r_Mc = [Res(), Res()]
    r_GN = [Res(), Res()]
    r_SELM = [Res(), Res()]
    r_accS = [Res(), Res()]
    r_acc = [Res(), Res()]
    r_TP0 = Res()
    r_TP = [r_TP0, r_TP0]
    r_onsa, r_onsab, r_imp, r_MBT = Res(), Res(), Res(), Res()
    r_sm = Res()
    r_sel = Res()
    r_mbf = Res()
    r_tnum = Res()
    r_TP7 = Res()
    sc.op("dve", lambda e: e.memset(KW.rearrange("p a b -> p (a b)"), 0.0), writes=[r_KW])
    sc.op("dve", lambda e: e.memset(VS.rearrange("p a b c -> p (a b c)"), 0.0), writes=[r_VS])
    sc.op("pool", lambda e: e.memset(VW.rearrange("p a b c -> p (a b c)"), 0.0), writes=[r_VW])
    for qb_ in range(2):
        sc.op("pool", lambda e, qb_=qb_: e.memset(QN[qb_].rearrange("p a b -> p (a b)"), 0.0),
              writes=[r_QN[qb_][0], r_QN[qb_][1]])
    ohf = P["c_oh"].rearrange("j a b -> j (a b)")
    for g in range(2):
        sc.dma(lambda e, g=g: e.dma_start(out=KS[0:64, g, :], in_=scr["KS"][g][0:64, :]), writes=[r_KS])
        sc.dma(lambda e, g=g: e.dma_start(out=KS[64:128, g, :], in_=ohf), writes=[r_KS])
        sc.dma(lambda e, g=g: e.dma_start(out=KW[0:64, g, :], in_=scr["KW"][g][0:64, :]), writes=[r_KW])
    vsv = scr["VS"].rearrange("(kt p) (g c) -> p kt g c", p=128, g=2)
    vwv = scr["VW"].rearrange("(kt p) (g c) -> p kt g c", p=128, g=2)
    for q4 in range(4):
        for g in range(2):
            sc.dma(lambda e, q4=q4, g=g: e.dma_start(out=VS[:, q4 * 8:(q4 + 1) * 8, g, 0:65],
                                                      in_=vsv[:, q4 * 8:(q4 + 1) * 8, g, :]), writes=[r_VS])
            sc.dma(lambda e, q4=q4, g=g: e.dma_start(out=VW[:, q4 * 8:(q4 + 1) * 8, g, 0:65],
                                                      in_=vwv[:, q4 * 8:(q4 + 1) * 8, g, :]), writes=[r_VW])
    sc.dma(lambda e: e.dma_start(out=TQ.rearrange("p a b -> p (a b)"),
                                 in_=P["c_tq"].rearrange("a b -> (a b)").partition_broadcast(128)), writes=[r_c])
    sc.dma(lambda e: e.dma_start(out=TQA, in_=P["c_tqa"].rearrange("h k q -> k h q")), writes=[r_c])
    sc.dma(lambda e: e.dma_start(out=MASKO, in_=P["c_masko"]), writes=[r_c])
    sc.dma(lambda e: e.dma_start(out=TQ8, in_=P["c_tq8"]), writes=[r_c])
    sc.dma(lambda e: e.dma_start(out=MK, in_=P["c_masks"].rearrange("m k q -> k m q")), writes=[r_c])
    sc.dma(lambda e: e.dma_start(out=BC, in_=P["c_bc"]), writes=[r_c])
    sc.dma(lambda e: e.dma_start(out=BCC, in_=P["c_bcc"]), writes=[r_c])
    sc.barrier()

    def finish_head(acc_bank, ai, M_rows, h, br, qc, first):
        a = accS[ai]
        tb_ = 5
        TP = r3(M.bank(tb_), 4)
        for sub in range(4):
            MR = M_rows + (M_rows % 2)
            sc.op("pe", lambda e, sub=sub, MR=MR: e.transpose(out=TP[:, sub, 0:MR], in_=a[0:MR, sub * 128:(sub + 1) * 128],
                                                       identity=identf[0:MR, 0:MR]),
                  reads=[r_accS[ai]], writes=[r_TP[ai]])
        den = TP[:, :, 64:65]
        if br == 0:
            sc.op("dve", lambda e: e.tensor_scalar(out=rr_.unsqueeze(2), in0=den, scalar1=1e-30, scalar2=None, op0=ALU.max),
                  reads=[r_TP[ai]], writes=[r_sm])
            sc.op("dve", lambda e: e.reciprocal(out=rr_, in_=rr_), reads=[r_sm], writes=[r_sm])
        else:
            sc.op("dve", lambda e: e.reciprocal(out=rr_.unsqueeze(2), in_=den), reads=[r_TP[ai]], writes=[r_sm])
        gidx = h * 3 + br
        sc.op("dve", lambda e: e.tensor_tensor(out=sc4, in0=rr_, in1=GN[qc % 2][:, :, gidx], op=ALU.mult),
              reads=[r_sm, r_GN[qc % 2]], writes=[r_sm])
        dst = onsa[:, :, h * 64:(h + 1) * 64]
        if first:
            sc.op("dve", lambda e: e.tensor_tensor(out=dst, in0=TP[:, :, 0:64], in1=sc4.unsqueeze(2).to_broadcast([128, 4, 64]),
                                                   op=ALU.mult), reads=[r_TP[ai], r_sm], writes=[r_onsa])
        else:
            sc.op("dve", lambda e: e.tensor_tensor(out=tnum, in0=TP[:, :, 0:64], in1=sc4.unsqueeze(2).to_broadcast([128, 4, 64]),
                                                   op=ALU.mult), reads=[r_TP[ai], r_sm], writes=[r_tnum])
            sc.op("dve", lambda e: e.tensor_tensor(out=dst, in0=dst, in1=tnum, op=ALU.add),
                  reads=[r_tnum, r_onsa], writes=[r_onsa])
        return TP

    hcount = [0]

    mbfs = [[r3(M.bf(4 * 128), 4) for _ in range(4)] for _ in range(2)]
    r_mbfs = [[Res() for _ in range(4)] for _ in range(2)]
    for g_ in range(2):
        for s_ in range(4):
            sc.op("pool", lambda e, g_=g_, s_=s_: e.memset(mbfs[g_][s_].rearrange("p a b -> p (a b)"), 0.0),
                  writes=[r_mbfs[g_][s_]])

    def do_chunk(qc):
        q0 = qc * 512
        qb = qc % 2
        for h_ in range(8):
            sc.dma(lambda e, h_=h_: e.dma_start(out=QN[qb][0:64, h_, :],
                                                 in_=scr["QN"][h_ // 2][(h_ % 2) * 64:(h_ % 2) * 64 + 64, q0:q0 + 512]),
                   writes=[r_QN[qb][h_ // 4]])
        sc.dma(lambda e: e.dma_start(out=GN[qb], in_=scr["GN"][q0:q0 + 512, :].rearrange("(s p) c -> p s c", p=128)),
               writes=[r_GN[qb]])
        sc.dma(lambda e: e.dma_start(out=SELM[qb].rearrange("p a b c -> p a (b c)"),
                                     in_=P["c_selm"][q0:q0 + 512].rearrange("(s p) a c -> p s (a c)", p=128)),
               writes=[r_SELM[qb]])
        sc.dma(lambda e: e.dma_start(out=Mc[qb], in_=P["c_mc"][qc]), writes=[r_Mc[qb]])
        nct = 2 if qc >= 4 else 1
        items = []

        def sel_dve(g):
            sc.op("pool", lambda e: e.memset(impacc[:, g, :, 0:1], 0.0), reads=[r_imp], writes=[r_imp])
            sc.op("dve", lambda e: e.tensor_tensor(out=impp, in0=impacc[:, g, :, :], in1=SELM[qb][:, :, 0, :], op=ALU.mult),
                  reads=[r_imp, r_SELM[qb]], writes=[r_sel])
            sc.op("dve", lambda e: e.tensor_tensor(out=impp, in0=impp, in1=SELM[qb][:, :, 1, :], op=ALU.add),
                  reads=[r_sel, r_SELM[qb]], writes=[r_sel])
            for sub in range(4):
                iv = impp[:, sub, :]
                mb_ = mbfs[g][sub]
                sc.op("dve", lambda e, iv=iv: e.max(out=m1, in_=iv), reads=[r_sel], writes=[r_sel])
                sc.op("dve", lambda e, iv=iv: e.match_replace(out=wk, in_to_replace=m1, in_values=iv, imm_value=-3.0e38),
                      reads=[r_sel], writes=[r_sel])
                sc.op("dve", lambda e: e.max(out=m2, in_=wk), reads=[r_sel], writes=[r_sel])
                sc.op("dve", lambda e, iv=iv: e.tensor_scalar(
                    out=wk, in0=iv, scalar1=m2[:, 7:8], scalar2=NEG, op0=ALU.is_lt, op1=ALU.mult),
                    reads=[r_sel], writes=[r_sel])
                for hh_ in range(4):
                    sc.op("dve", lambda e, hh_=hh_, mb_=mb_, sub=sub: e.tensor_scalar(
                        out=mb_[:, hh_, 64:128], in0=wk, scalar1=TQ8[:, g * 4 + hh_, sub:sub + 1], scalar2=None, op0=ALU.add),
                        reads=[r_sel, r_c], writes=[r_mbfs[g][sub]])

        def sel_pe(g):
            Tb = M.bank_bf(7)
            for sub in range(4):
                mb_ = mbfs[g][sub]
                for hh_ in range(4):
                    sc.op("pe", lambda e, mb_=mb_, hh_=hh_: e.transpose(out=Tb[:, hh_ * 128:(hh_ + 1) * 128], in_=mb_[:, hh_, :],
                                                                       identity=ident),
                          reads=[r_mbfs[g][sub]], writes=[r_TP7])
                sc.op("act", lambda e, sub=sub: e.copy(
                    out=QN[qb][64:128, g * 4:(g + 1) * 4, sub * 128:(sub + 1) * 128],
                    in_=r3(Tb[64:128, 0:512], 4)),
                    reads=[r_TP7], writes=[r_QN[qb][g]])

        def mk_after(ai, M_rows, h, br, g, hh):
            def fa():
                a = accS[ai]
                sc.op("dve", lambda e: e.tensor_copy(out=a[0:M_rows, :], in_=M.bank(3 + ai)[0:M_rows, :]),
                      reads=[r_acc[ai]], writes=[r_accS[ai]])

            def after():
                first = (br == 0)
                TP = finish_head(3 + ai, ai, M_rows, h, br, qc, first)
                if br == 0:
                    ia = impacc[:, g, :, 1:64]
                    if hh == 0:
                        sc.op("dve", lambda e: e.tensor_tensor(
                            out=ia, in0=TP[:, :, 65:128], in1=rr_.unsqueeze(2).to_broadcast([128, 4, 63]), op=ALU.mult),
                            reads=[r_TP[ai], r_sm], writes=[r_imp])
                    else:
                        sc.op("dve", lambda e: e.tensor_tensor(
                            out=tnum[:, :, 0:63], in0=TP[:, :, 65:128], in1=rr_.unsqueeze(2).to_broadcast([128, 4, 63]),
                            op=ALU.mult), reads=[r_TP[ai], r_sm], writes=[r_tnum])
                        sc.op("dve", lambda e: e.tensor_tensor(out=ia, in0=ia, in1=tnum[:, :, 0:63], op=ALU.add),
                              reads=[r_tnum, r_imp], writes=[r_imp])
                    if hh == 3:
                        sel_dve(g)
            return (ai, fa, after)

        for g in range(2):
            for hh in range(4):
                h = g * 4 + hh
                pr, half = h // 2, h % 2
                lo, hi = half * 64, half * 64 + 64
                ai = hcount[0] % 2
                hcount[0] += 1
                for ct in range(nct):
                    items.append(dict(
                        qk=[(kcT[:, g, ct * 128:(ct + 1) * 128], QN[qb][:, h, :], [r_kcT, r_QN[qb][g]])],
                        nk=128, c0=0, nc=512, T=TQ[:, h, :], T_reads=[r_c],
                        masks=[(0, Mc[qb][:, ct, :], [r_Mc[qb]])],
                        bias=BCC[:, h, qc, ct:ct + 1], bias_reads=[r_c],
                        pv=VC4[:, ct, g, :], pv_reads=[r_VC],
                        acc=M.bank(3 + ai), r_acc=r_acc[ai], M_rows=128, first=(ct == 0), last=(ct == nct - 1),
                        after=(mk_after(ai, 128, h, 0, g, hh) if ct == nct - 1 else None)))

        def branch_jobs(br, g):
            for hh in range(4):
                h = g * 4 + hh
                pr, half = h // 2, h % 2
                lo, hi = half * 64, half * 64 + 64
                ai = hcount[0] % 2
                hcount[0] += 1
                if br == 1:
                    kts = list(range(0, 4 * qc + 4))
                else:
                    kts = list(range(max(0, 4 * qc - 4), 4 * qc + 4))
                for ki_, kt in enumerate(kts):
                    dlt = 4 * qc - kt
                    masks = []
                    if dlt > 0:
                        c0, ncol = 0, 512
                        Tt = TQ
                        bidx = dlt + 3
                        if br == 2:
                            m = 4 - dlt
                            ncol = 128 * (m + 1)
                            masks = [(128 * m, MK[:, 1, :], [r_c])]
                    else:
                        j = -dlt
                        c0, ncol = 128 * j, 512 - 128 * j
                        Tt = TQA
                        bidx = 3
                    Kt = KS if br == 1 else KW
                    rK = r_KS if br == 1 else r_KW
                    qk = [(Kt[:, g, kt * 128:(kt + 1) * 128], QN[qb][:, h, c0:c0 + ncol], [rK, r_QN[qb][g]])]
                    Vt = VS if br == 1 else VW
                    lastj = (ki_ == len(kts) - 1)
                    Tap = Tt[:, h, 0:ncol]
                    direct = False
                    if br == 1:
                        bidx = dlt + 3
                        if dlt > 0:
                            direct = True
                        else:
                            Tap = MASKO[:, 0:ncol]
                    items.append(dict(
                        qk=qk, nk=128, c0=c0, nc=ncol, T=Tap, T_reads=[r_c], masks=masks, direct=direct,
                        bias=BC[:, h, bidx:bidx + 1], bias_reads=[r_c],
                        pv=Vt[:, kt, g, :], pv_reads=[r_VS if br == 1 else r_VW],
                        acc=M.bank(3 + ai), r_acc=r_acc[ai], M_rows=128, first=(ki_ == 0), last=lastj,
                        after=(mk_after(ai, 65, h, br, g, hh) if lastj else None)))

        branch_jobs(2, 0)
        items.append(lambda: sel_pe(0))
        branch_jobs(2, 1)
        items.append(lambda: sel_pe(1))
        branch_jobs(1, 0)
        branch_jobs(1, 1)
        pipe.run_stream(items)
        sc.op("act", lambda e: e.copy(out=onsab, in_=onsa), reads=[r_onsa], writes=[r_onsab])
        sc.dma(lambda e: e.dma_start(out=scr["ON"][q0:q0 + 512, :].rearrange("(s p) c -> p s c", p=128), in_=onsab),
               reads=[r_onsab])

    for qc in range(8):
        do_chunk(qc)
    sc.barrier()


def dil_phase(sc, M, C, scr, P):
    identf = C["identf"]
    M.reset(C["const_end"])
    KDs = [v4(M.bf(4 * S), 2, 2) for _ in range(2)]
    VDs = [v4(M.bf(32 * 512), 32, 4) for _ in range(2)]
    TDs = [r3(M.f32(4 * 1152), 4) for _ in range(2)]
    QD = [r3(M.bf(2 * 512), 2) for _ in range(2)]
    accS = [M.f32(512) for _ in range(2)]
    osb = [v4(M.f32(4 * 260), 4, 4) for _ in range(2)]
    pipe = AttnPipe(sc, M, [0, 1, 2, 6], ntmp=4)
    r_KDs, r_VDs, r_cs = [Res(), Res()], [Res(), Res()], [Res(), Res()]
    r_QD = [Res(), Res()]
    r_accS = [Res(), Res()]
    r_acc = [Res(), Res()]
    r_TP0 = Res()
    r_TP = [r_TP0, r_TP0]
    r_osb = [Res(), Res()]
    sc.op("dve", lambda e: e.memset(KDs[0].rearrange("p a b c -> p (a b c)"), 0.0), writes=[r_KDs[0]])
    sc.op("pool", lambda e: e.memset(VDs[0].rearrange("p a b c -> p (a b c)"), 0.0), writes=[r_VDs[0]])
    sc.op("dve", lambda e: e.memset(KDs[1].rearrange("p a b c -> p (a b c)"), 0.0), writes=[r_KDs[1]])
    sc.op("pool", lambda e: e.memset(VDs[1].rearrange("p a b c -> p (a b c)"), 0.0), writes=[r_VDs[1]])

    def load_group(gi, q):
        gb = gi % 2
        KD_, VD_, TD_ = KDs[gb], VDs[gb], TDs[gb]
        for pp in range(2):
            for hf in range(2):
                sc.dma(lambda e, pp=pp, hf=hf: e.dma_start(out=KD_[hf * 64:(hf + 1) * 64, pp, hf, :],
                                                            in_=scr["KD"][gi * 2 + pp][hf * 64:(hf + 1) * 64, :]),
                       writes=[r_KDs[gb]], q=q)
        vdv = scr["VD"][gi].rearrange("(kt p) (h c) -> p kt h c", p=128, h=4)
        for q4 in range(4):
            for hh_ in range(4):
                sc.dma(lambda e, q4=q4, hh_=hh_: e.dma_start(out=VD_[:, q4 * 8:(q4 + 1) * 8, hh_, 0:65],
                                                              in_=vdv[:, q4 * 8:(q4 + 1) * 8, hh_, :]),
                       writes=[r_VDs[gb]], q=q)
        sc.dma(lambda e: e.dma_start(out=TD_, in_=P["c_tds"][gi * 4:(gi + 1) * 4].rearrange("h k q -> k h q")),
               writes=[r_cs[gb]], q=q)

    def do_group(gi):
        dil = DILS[gi]
        Ls = S // dil
        Cn = min(512, Ls)
        nsub = Cn // 128
        gb = gi % 2
        KD, VD, TD = KDs[gb], VDs[gb], TDs[gb]
        r_KD, r_VD, r_c = r_KDs[gb], r_VDs[gb], r_cs[gb]
        ODv = scr["OD"][gi].rearrange("(i r) c -> i r c", r=dil)
        hcount = [0]
        items = []
        loadqs = []
        marks = []

        def do_chunk(ci, p0):
            r, i0 = divmod(p0, Ls)
            qb = ci % 2
            ob = osb[qb]

            def loadq():
                for pp in range(2):
                    sc.dma(lambda e, pp=pp: e.dma_start(out=QD[qb][:, pp, 0:Cn], in_=scr["QD"][gi * 2 + pp][:, p0:p0 + Cn]),
                           writes=[r_QD[qb]])
            loadqs.append(loadq)
            if ci == 0:
                items.append(loadq)
            mark = len(items)
            items.append(None)
            marks.append(mark)

            def mk_after(ai, hh):
                a = accS[ai]

                def fa():
                    sc.op("dve", lambda e: e.tensor_copy(out=a[0:65, 0:Cn], in_=M.bank(3 + ai)[0:65, 0:Cn]),
                          reads=[r_acc[ai]], writes=[r_accS[ai]])

                def after():
                    TP = r3(M.bank(5), 4)
                    for sub in range(nsub):
                        sc.op("pe", lambda e, sub=sub: e.transpose(
                            out=TP[:, sub, 0:66], in_=a[0:66, sub * 128:(sub + 1) * 128], identity=identf[0:66, 0:66]),
                            reads=[r_accS[ai]], writes=[r_TP[ai]])
                    sc.op("dve", lambda e: e.tensor_copy(out=ob[:, 0:nsub, hh, :], in_=TP[:, 0:nsub, 0:65]),
                          reads=[r_TP[ai]], writes=[r_osb[qb]])
                    if hh == 3:
                        for sub in range(nsub):
                            ii = i0 + sub * 128
                            sc.dma(lambda e, sub=sub, ii=ii: e.dma_start(
                                out=ODv[ii:ii + 128, r, :], in_=ob[:, sub, :, :].rearrange("p a b -> p (a b)")),
                                reads=[r_osb[qb]], q="act")
                return (ai, fa, after)

            for hh in range(4):
                pp, half = hh // 2, hh % 2
                lo, hi = half * 64, half * 64 + 64
                ai = hcount[0] % 2
                hcount[0] += 1
                js = [j for j in range(0, nsub + 1) if not (j == 0 and i0 == 0)]
                groups, cur, tot = [], [], 0
                for j in js:
                    n_ = 128 if (j == 0 or j == nsub) else 256
                    if tot + n_ > 512:
                        groups.append(cur)
                        cur, tot = [], 0
                    cur.append((j, n_))
                    tot += n_
                if cur:
                    groups.append(cur)
                for gn, grp in enumerate(groups):
                    parts = []
                    off = 0
                    j_first = grp[0][0]
                    t0 = 0 if j_first == 0 else 128 + 256 * (j_first - 1)
                    for (j, n_) in grp:
                        k0 = p0 + 128 * (j - 1)
                        kt = k0 // 128
                        c0 = 0 if j == 0 else 128 * (j - 1)
                        parts.append((KD[:, pp, half, k0:k0 + 128], QD[qb][:, pp, c0:c0 + n_], [r_KD, r_QD[qb]],
                                      off, n_, VD[:, kt, hh, :], [r_VD], c0))
                        off += n_
                    lastg = (gn == len(groups) - 1)
                    items.append(dict(
                        parts=parts, nk=128, nc=off, c0=0, T=TD[:, hh, t0:t0 + off], T_reads=[r_c],
                        acc=M.bank(3 + ai), r_acc=r_acc[ai], M_rows=128, first=(gn == 0), last=lastg, W=Cn,
                        after=(mk_after(ai, hh) if lastg else None)))

        for ci, p0 in enumerate(range(0, S, Cn)):
            do_chunk(ci, p0)
        for ci, mark in enumerate(marks):
            items[mark] = loadqs[ci + 1] if ci + 1 < len(loadqs) else (lambda: None)
        pipe.run_stream(items)

    load_group(0, "sp")
    for gi in range(3):
        if gi + 1 < 3:
            load_group(gi + 1, "pool")
        do_group(gi)
    sc.barrier()


def final_phase(sc, M, C, scr, P, x1, x2, gpost):
    M.reset(C["const_end"])
    ident = C["ident"]
    Wno = r3(M.bf(4 * 1024), 4)
    Wdo = r3(M.bf(2 * 1024), 2)
    Wmo = r3(M.bf(8 * 1024), 8)
    stage = [M.f32(1024) for _ in range(2)]
    gq = M.f32(1024)
    onb = [M.bf(512) for _ in range(2)]
    odf = [v4(M.f32(3 * 260), 3, 4) for _ in range(2)]
    mg = [M.bf(2048) for _ in range(2)]
    xr = [M.f32(1024) for _ in range(2)]
    onT = r3(M.bf(4 * 128), 4)
    odT = r3(M.bf(2 * 128), 2)
    odn = r3(M.f32(4 * 65), 4)
    odb = M.bf(256)
    ya = M.f32(1024)
    yb = M.f32(1024)
    ybf = M.bf(1024)
    yT = r3(M.bf(8 * 128), 8)
    yt = M.f32(1024)
    small = M.f32(8)
    r_W = Res()
    r_stage = [Res(), Res()]
    r_gq = Res()
    r_in = [Res(), Res()]
    r_x = [Res(), Res()]
    r_onT, r_odT, r_odn, r_odb, r_ya, r_yb, r_ybf, r_yT, r_yt, r_sm = (Res() for _ in range(10))
    r_T, r_Y1, r_Y2, r_Y3 = Res(), Res(), Res(), Res()
    cnt = [0]

    def load_piece(dst_ap, src_ap, n):
        i = cnt[0] % 2
        cnt[0] += 1
        st = stage[i][:, 0:n]
        sc.dma(lambda e: e.dma_start(out=st, in_=src_ap), writes=[r_stage[i]])
        sc.op("pool", lambda e: e.tensor_copy(out=dst_ap, in_=st), reads=[r_stage[i]], writes=[r_W])

    for k in range(4):
        load_piece(Wno[:, k, :], P["w_nsa_o"][k * 128:(k + 1) * 128, :], 1024)
    for k in range(2):
        load_piece(Wdo[:, k, :], P["w_dil_o"][k * 128:(k + 1) * 128, :], 1024)
    for k in range(8):
        load_piece(Wmo[:, k, :], P["w_mix_out"][k * 128:(k + 1) * 128, :], 1024)
    sc.dma(lambda e: e.dma_start(out=gq, in_=gpost[0, :].partition_broadcast(128)), writes=[r_gq])
    ybf2 = [ybf, M.bf(1024)]
    r_ybf2 = [Res(), Res()]
    smallA = M.f32(8)
    r_smA = Res()

    def stageA(t):
        b = t % 2
        rows = slice(t * 128, (t + 1) * 128)
        sc.dma(lambda e, b=b, rows=rows: e.dma_start(out=onb[b], in_=scr["ON"][rows, :]), writes=[r_in[b]])
        for gi in range(3):
            sc.dma(lambda e, b=b, rows=rows, gi=gi: e.dma_start(out=odf[b][:, gi, :, :].rearrange("p a b -> p (a b)"),
                                                               in_=scr["OD"][gi][rows, :]), writes=[r_in[b]])
        sc.dma(lambda e, b=b, rows=rows: e.dma_start(out=mg[b], in_=scr["MG"][rows, :]), writes=[r_in[b]])
        sc.dma(lambda e, b=b, rows=rows: e.dma_start(out=xr[b], in_=x1[rows, :]), writes=[r_x[b]])
        sc.op("dve", lambda e, b=b: e.tensor_tensor(out=odn, in0=odf[b][:, 0, :, :], in1=odf[b][:, 1, :, :], op=ALU.add),
              reads=[r_in[b]], writes=[r_odn])
        sc.op("dve", lambda e, b=b: e.tensor_tensor(out=odn, in0=odn, in1=odf[b][:, 2, :, :], op=ALU.add),
              reads=[r_in[b], r_odn], writes=[r_odn])
        sc.op("dve", lambda e: e.reciprocal(out=smallA[:, 0:4].unsqueeze(2), in_=odn[:, :, 64:65]), reads=[r_odn], writes=[r_smA])
        sc.op("dve", lambda e: e.tensor_tensor(out=r3(odb, 4), in0=odn[:, :, 0:64],
                                               in1=smallA[:, 0:4].unsqueeze(2).to_broadcast([128, 4, 64]), op=ALU.mult),
              reads=[r_odn, r_smA], writes=[r_odb])
        Tb = M.bank_bf(0)
        for k in range(4):
            sc.op("pe", lambda e, b=b, k=k: e.transpose(out=Tb[:, k * 128:(k + 1) * 128], in_=onb[b][:, k * 128:(k + 1) * 128],
                                                        identity=ident), reads=[r_in[b]], writes=[r_T])
        for k in range(2):
            sc.op("pe", lambda e, k=k: e.transpose(out=Tb[:, (4 + k) * 128:(5 + k) * 128], in_=odb[:, k * 128:(k + 1) * 128],
                                                   identity=ident), reads=[r_odb], writes=[r_T])
        sc.op("act", lambda e: e.copy(out=onT, in_=r3(Tb[:, 0:512], 4)), reads=[r_T], writes=[r_onT])
        sc.op("act", lambda e: e.copy(out=odT, in_=r3(Tb[:, 512:768], 2)), reads=[r_T], writes=[r_odT])
        for dh in range(2):
            Y1 = M.bank(1 + dh)
            for k in range(4):
                sc.op("pe", lambda e, Y1=Y1, k=k, dh=dh: e.matmul(Y1, lhsT=onT[:, k, :], rhs=Wno[:, k, dh * 512:(dh + 1) * 512],
                                                                  start=(k == 0), stop=(k == 3)),
                      reads=[r_onT, r_W], writes=[r_Y1])
            Y2 = M.bank(3 + dh)
            for k in range(2):
                sc.op("pe", lambda e, Y2=Y2, k=k, dh=dh: e.matmul(Y2, lhsT=odT[:, k, :], rhs=Wdo[:, k, dh * 512:(dh + 1) * 512],
                                                                  start=(k == 0), stop=(k == 1)),
                      reads=[r_odT, r_W], writes=[r_Y2])
            sl = slice(dh * 512, (dh + 1) * 512)
            sc.op("dve", lambda e, Y1=Y1, b=b, sl=sl: e.tensor_tensor(out=ya[:, sl], in0=Y1, in1=mg[b][:, sl], op=ALU.mult),
                  reads=[r_Y1, r_in[b]], writes=[r_ya])
            sl2 = slice(1024 + dh * 512, 1024 + (dh + 1) * 512)
            sc.op("dve", lambda e, Y2=Y2, b=b, sl=sl, sl2=sl2: e.tensor_tensor(out=yb[:, sl], in0=Y2, in1=mg[b][:, sl2], op=ALU.mult),
                  reads=[r_Y2, r_in[b]], writes=[r_yb])
        sc.op("pool", lambda e: e.tensor_tensor(out=ybf2[b], in0=ya, in1=yb, op=ALU.add), reads=[r_ya, r_yb], writes=[r_ybf2[b]])

    def stageB(t):
        b = t % 2
        rows = slice(t * 128, (t + 1) * 128)
        Tb2 = M.bank_bf(7)
        for k in range(8):
            sc.op("pe", lambda e, k=k: e.transpose(out=Tb2[:, k * 128:(k + 1) * 128], in_=ybf2[b][:, k * 128:(k + 1) * 128],
                                                   identity=ident), reads=[r_ybf2[b]], writes=[r_Y3])
        sc.op("act", lambda e: e.copy(out=yT, in_=r3(Tb2, 8)), reads=[r_Y3], writes=[r_yT])
        for dh in range(2):
            Y = M.bank(5 + dh)
            for k in range(8):
                sc.op("pe", lambda e, Y=Y, k=k, dh=dh: e.matmul(Y, lhsT=yT[:, k, :], rhs=Wmo[:, k, dh * 512:(dh + 1) * 512],
                                                                start=(k == 0), stop=(k == 7)),
                      reads=[r_yT, r_W], writes=[r_T if False else r_Y1 if False else r_yt_ps])
        ssa, ssb = small[:, 4:5], small[:, 5:6]
        sc.op("pool", lambda e: e.memset(small[:, 4:6], 0.0), writes=[r_sm])
        sc.op("act", lambda e: e.activation(out=yt[:, 0:512], in_=M.bank(5), func=AF.Square, accum_out=ssa),
              reads=[r_yt_ps], writes=[r_yt, r_sm])
        sc.op("act", lambda e: e.activation(out=yt[:, 512:1024], in_=M.bank(6), func=AF.Square, accum_out=ssb),
              reads=[r_yt_ps], writes=[r_yt, r_sm])
        sc.op("dve", lambda e: e.tensor_tensor(out=ssa, in0=ssa, in1=ssb, op=ALU.add), reads=[r_sm], writes=[r_sm])
        sc.op("dve", lambda e: e.tensor_scalar(out=ssa, in0=ssa, scalar1=1.0 / D, scalar2=EPS, op0=ALU.mult, op1=ALU.add),
              reads=[r_sm], writes=[r_sm])
        sc.op("act", lambda e: e.sqrt(ssa, ssa), reads=[r_sm], writes=[r_sm])
        sc.op("dve", lambda e: e.reciprocal(out=ssa, in_=ssa), reads=[r_sm], writes=[r_sm])
        for dh in range(2):
            sc.op("dve", lambda e, dh=dh: e.scalar_tensor_tensor(
                out=yt[:, dh * 512:(dh + 1) * 512], in0=M.bank(5 + dh), scalar=ssa,
                in1=gq[:, dh * 512:(dh + 1) * 512], op0=ALU.mult, op1=ALU.mult),
                reads=[r_yt_ps, r_sm, r_gq], writes=[r_yt])
        sc.op("dve", lambda e, b=b: e.tensor_tensor(out=xr[b], in0=yt, in1=xr[b], op=ALU.add),
              reads=[r_yt, r_x[b]], writes=[r_x[b]])
        sc.dma(lambda e, b=b, rows=rows: e.dma_start(out=x2[rows, :], in_=xr[b]), reads=[r_x[b]], q="act")

    stageA(0)
    for t in range(32):
        if t + 1 < 32:
            stageA(t + 1)
        stageB(t)
    sc.barrier()


r_yt_ps = Res()


IN_SHAPES = (
    ("ffn1_pre", [1, D]), ("ffn1_post", [1, D]), ("ffn1_w_gate", [D, DFF]), ("ffn1_w_up", [D, DFF]),
    ("ffn1_w_down", [DFF, D]), ("mix_pre", [1, D]), ("mix_post", [1, D]), ("w_in", [D, 5656]),
    ("nsa_pe_k", [32, 64]), ("nsa_w_ck1", [2048, 256]), ("nsa_w_ck2", [256, 64]),
    ("nsa_pe_v", [32, 64]), ("nsa_w_cv1", [2048, 256]), ("nsa_w_cv2", [256, 64]),
    ("w_nsa_o", [512, D]), ("w_dil_o", [256, D]), ("w_mix_out", [D, D]),
    ("ffn2_pre", [1, D]), ("ffn2_post", [1, D]), ("ffn2_w_gate", [D, DFF]),
    ("ffn2_w_up", [D, DFF]), ("ffn2_w_down", [DFF, D]))


def _consts():
    import ml_dtypes
    bf = ml_dtypes.bfloat16
    c = {}
    c["c_ident"] = np.eye(128, dtype=np.float32).astype(bf)
    c["c_identf"] = np.eye(128, dtype=np.float32)
    n_cmp = 255
    cs = np.arange(256) * 16
    ss = np.arange(64) * 64
    ov = np.clip(np.minimum(cs[:, None] + 32, ss[None, :] + 64) - np.maximum(cs[:, None], ss[None, :]), 0, None) / 32.0
    ov[255] = 0
    c["c_ov"] = ov.astype(np.float32).astype(bf)
    qi = np.arange(512, dtype=np.float64)
    c["c_tq"] = np.stack([-s_ * qi for s_ in NSA_SL]).astype(np.float32)
    c["c_td"] = np.stack([-DIL_SL[h] * DILS[h // 4] * qi for h in range(12)]).astype(np.float32)
    ki = np.arange(128)[:, None]
    qq = np.arange(128)[None, :]
    mk = np.zeros((3, 128, 128), np.float32)
    mk[0] = np.where(qq >= ki, 0.0, NEG)
    mk[1] = np.where(qq < ki, 0.0, NEG)
    mk[2] = np.where(qq <= ki, 0.0, NEG)
    c["c_masks"] = mk
    tqa = np.zeros((8, 128, 512), np.float64)
    for h in range(8):
        tqa[h] = -NSA_SL[h] * qi[None, :]
        tqa[h][:, 0:128] += mk[0]
    c["c_tqa"] = tqa.astype(np.float32)
    mo = np.zeros((128, 512), np.float32)
    mo[:, 0:128] = mk[0]
    c["c_masko"] = mo
    tq8 = np.zeros((128, 8, 4), np.float64)
    for h in range(8):
        for sb in range(4):
            tq8[:, h, sb] = -8.0 * NSA_SL[h] * (128 * sb + np.arange(128))
    c["c_tq8"] = tq8.astype(np.float32)
    tdm = np.zeros((12, 128, 384), np.float64)
    for h in range(12):
        sl = DIL_SL[h] * DILS[h // 4]
        tdm[h][:, 0:256] = -sl * qi[None, 0:256]
        tdm[h][:, 0:128] += mk[0]
        tdm[h][:, 128:256] += mk[2]
        tdm[h][:, 256:384] = -sl * qi[None, 0:128] + mk[2]
    c["c_tdm"] = tdm.astype(np.float32)
    tds = np.zeros((12, 128, 1152), np.float64)
    kcol = np.arange(128, dtype=np.float64)[:, None]
    for h in range(12):
        sl = DIL_SL[h] * DILS[h // 4]
        t2a = -sl * (qi[None, 0:256] - kcol)
        t2a[:, 0:128] += mk[0]
        t2a[:, 128:256] += mk[2]
        t2b = -sl * (128.0 + qi[None, 0:128] - kcol) + mk[2]
        tds[h][:, 0:128] = t2b
        for r_ in range(4):
            tds[h][:, 128 + 256 * r_:128 + 256 * (r_ + 1)] = t2a
    c["c_tds"] = tds.astype(np.float32)
    kk = np.arange(128, dtype=np.float64)
    bc = np.zeros((128, 8, 35), np.float64)
    for h in range(8):
        for idx in range(35):
            bc[:, h, idx] = NSA_SL[h] * (kk - 128 * (idx - 3))
    c["c_bc"] = bc.astype(np.float32)
    bcc = np.zeros((128, 8, 8, 2), np.float64)
    for h in range(8):
        for qc in range(8):
            for ct in range(2):
                bcc[:, h, qc, ct] = NSA_SL[h] * (16 * (128 * ct + kk) + 31 - 512 * qc)
    c["c_bcc"] = bcc.astype(np.float32)
    bcd = np.zeros((128, 12, 5), np.float64)
    for h in range(12):
        for j in range(5):
            bcd[:, h, j] = DIL_SL[h] * DILS[h // 4] * (kk - 128 * (1 - j))
    c["c_bcd"] = bcd.astype(np.float32)
    mc = np.zeros((8, 128, 2, 512), np.float32)
    for qc in range(8):
        for ct in range(2):
            cc = 128 * ct + np.arange(128)[:, None]
            qpos = 512 * qc + np.arange(512)[None, :]
            ok = (qpos >= 16 * cc + 31) & (cc < 255)
            mc[qc, :, ct, :] = np.where(ok, 0.0, NEG)
    c["c_mc"] = mc
    oh = np.zeros((64, 32, 128), np.float32)
    for kt in range(32):
        oh[2 * kt, kt, 0:64] = 1
        oh[2 * kt + 1, kt, 64:128] = 1
    c["c_oh"] = oh.astype(bf)
    pos = np.arange(S)[:, None]
    jb = np.arange(64)[None, :]
    own = pos // 64
    forced = (jb == 0) | (jb == own) | (jb == own - 1)
    valid = jb * 64 <= pos
    m1 = np.where(forced, 0.0, np.where(valid, 1.0, 0.0))
    m2 = np.where(forced, 1.0e9 + 1.0e4 * jb, np.where(valid, 0.0, -1.0e9 - 1.0e4 * jb))
    c["c_selm"] = np.stack([m1, m2], axis=1).astype(np.float32)
    return c


CONST_SHAPES = (("c_ident", [128, 128], BF16), ("c_identf", [128, 128], F32), ("c_ov", [256, 64], BF16),
                ("c_tq", [8, 512], F32), ("c_td", [12, 512], F32), ("c_masks", [3, 128, 128], F32),
                ("c_bc", [128, 8, 35], F32), ("c_bcc", [128, 8, 8, 2], F32), ("c_bcd", [128, 12, 5], F32),
                ("c_mc", [8, 128, 2, 512], F32), ("c_oh", [64, 32, 128], BF16), ("c_selm", [S, 2, 64], F32),
                ("c_tqa", [8, 128, 512], F32), ("c_tdm", [12, 128, 384], F32),
                ("c_masko", [128, 512], F32), ("c_tq8", [128, 8, 4], F32),
                ("c_tds", [12, 128, 1152], F32))


def build_nc():
    nc = bass.Bass("TRN2", target_bir_lowering=False)
    dt = lambda name, shape, dtype=F32, kind="ExternalInput": nc.dram_tensor(name, list(shape), dtype, kind=kind).ap()
    x = dt("x", [S, D])
    out = dt("out", [S, D], kind="ExternalOutput")
    P = {}
    for nm, shp in IN_SHAPES:
        P[nm] = dt(nm, shp)
    for nm, shp, ty in CONST_SHAPES:
        P[nm] = dt(nm, shp, ty)
    I = "Internal"
    if DBG_OUT:
        I = "ExternalOutput"
    x1 = dt("x1", [S, D], kind=I)
    x2 = dt("x2", [S, D], kind=I)
    scr = {
        "QN": dt("s_qn", [4, 128, S], BF16, I), "KC": dt("s_kc", [2, 128, S], F32, I),
        "KS": dt("s_ks", [2, 128, S], BF16, I), "KW": dt("s_kw", [2, 128, S], BF16, I),
        "VS": dt("s_vs", [S, 130], BF16, I), "VW": dt("s_vw", [S, 130], BF16, I),
        "GN": dt("s_gn", [S, 24], F32, I), "QD": dt("s_qd", [6, 128, S], BF16, I),
        "KD": dt("s_kd", [6, 128, S], BF16, I), "VD": dt("s_vd", [3, S, 260], BF16, I),
        "MG": dt("s_mg", [S, 2048], BF16, I), "ON": dt("s_on", [S, 512], BF16, I),
        "OD": dt("s_od", [3, S, 260], F32, I),
    }

    import contextlib
    with contextlib.ExitStack() as st:
        big = st.enter_context(nc.sbuf_tensor("big", [128, SBUF_BYTES // 4], F32))
        ps = st.enter_context(nc.psum_tensor("ps", [128, 4096], F32))
        M = Mem(big, ps)
        sc = Sched()
        C = {"r_dram": {}}
        ident = M.bf(128)
        identf = M.f32(128)
        C["kcT"] = r3(M.bf(2 * 256), 2)
        C["VC"] = M.bf(2 * 2 * 128)
        C["r_kcT"], C["r_VC"] = Res(), Res()
        r_ident = Res()
        sc.dma(lambda e: e.dma_start(out=ident, in_=P["c_ident"]), writes=[r_ident])
        sc.dma(lambda e: e.dma_start(out=identf, in_=P["c_identf"]), writes=[r_ident])
        C["ident"] = ident
        C["identf"] = identf
        zeros = M.bf(512)
        r_z = Res()
        sc.op("pool", lambda e: e.memset(zeros, 0.0), writes=[r_z])
        C["zeros"] = zeros
        ZEROS[0] = zeros
        C["const_end"] = M.off
        sc.barrier()
        if STAGE == 1:
            ffn_phase(sc, M, C, x, out, P["ffn1_w_gate"], P["ffn1_w_up"], P["ffn1_w_down"], P["ffn1_pre"], P["ffn1_post"])
        else:
            if DBG_SKIP_FFN1:
                x1 = x
            else:
                ffn_phase(sc, M, C, x, x1, P["ffn1_w_gate"], P["ffn1_w_up"], P["ffn1_w_down"], P["ffn1_pre"], P["ffn1_post"])
            if DBG_UPTO >= 1:
                proj_phase(sc, M, C, x1, P["w_in"], P["mix_pre"], scr)
            if DBG_UPTO >= 2:
                cmp_phase(sc, M, C, scr, P)
            if DBG_UPTO >= 3:
                nsa_phase(sc, M, C, scr, P)
            if DBG_UPTO >= 4:
                dil_phase(sc, M, C, scr, P)
            if DBG_UPTO >= 5:
                final_phase(sc, M, C, scr, P, x1, out if STAGE == 2 else x2, P["mix_post"])
            if STAGE >= 3:
                ffn_phase(sc, M, C, x2, out, P["ffn2_w_gate"], P["ffn2_w_up"], P["ffn2_w_down"], P["ffn2_pre"], P["ffn2_post"])
        sc.barrier()
        sc.emit(nc)
    return nc


DBG_SKIP_FFN1 = False
DBG_UPTO = 9
DBG_OUT = False
DBG_RES = None
_NC = None


def kernel(**inputs):
    global _NC
    if _NC is None:
        _NC = build_nc()
    nc = _NC
    x = np.ascontiguousarray(inputs["x"], dtype=np.float32)
    consts = _consts()
    shared = {}
    for nm, shp in IN_SHAPES:
        shared[nm] = np.ascontiguousarray(np.asarray(inputs[nm], dtype=np.float32)[0])
    in_maps = []
    for b in range(8):
        m = {"x": x[b]}
        m.update(shared)
        m.update(consts)
        in_maps.append(m)
    res = run_bass_kernel_spmd(nc, in_maps, core_ids=list(range(8)))
    if DBG_OUT:
        global DBG_RES
        DBG_RES = res.results[0]
    return np.stack([np.asarray(r["out"]) for r in res.results], axis=0).astype(np.float32)
```

```python
import numpy as np
import concourse.bass as bass
import concourse.mybir as mybir
from concourse.alu_op_type import AluOpType as ALU
from concourse.bass_utils import run_bass_kernel_spmd

F32 = mybir.dt.float32
BF16 = mybir.dt.bfloat16
AF = mybir.ActivationFunctionType

S = 4096
D = 1024
DFF = 2816
NFF = 22
EPS = 1e-6
SBUF_BYTES = 212000
LIMIT = 9
STAGE = 3


class Res:
    __slots__ = ("name", "lw", "rdc", "rdd")

    def __init__(self, name=""):
        self.name = name
        self.lw = None
        self.rdc = {}
        self.rdd = []


ENGS = ("pe", "act", "dve", "pool", "sp")
SAME_ENG_SKIP = ("pe", "sp")


class Sched:
    def __init__(self, nring=28):
        self.ops = {e: [] for e in ENGS}
        self.ndma = 0
        self.nring = nring
        self.last_dma = {}

    def _deps(self, reads, writes):
        dc = {}
        dd = set()

        def add(ev):
            if ev is None:
                return
            if ev[0] == "c":
                if dc.get(ev[1], -1) < ev[2]:
                    dc[ev[1]] = ev[2]
            else:
                dd.add(ev)

        for r in reads:
            add(r.lw)
        for w in writes:
            add(w.lw)
            for e, s in w.rdc.items():
                add(("c", e, s))
            for ev in w.rdd:
                add(ev)
        return dc, dd

    def _mark(self, ev, reads, writes):
        for r in reads:
            if ev[0] == "c":
                if r.rdc.get(ev[1], -1) < ev[2]:
                    r.rdc[ev[1]] = ev[2]
            else:
                r.rdd.append(ev)
        for w in writes:
            w.lw = ev
            w.rdc = {}
            w.rdd = []

    def op(self, eng, fn, reads=(), writes=()):
        dc, dd = self._deps(reads, writes)
        seq = len(self.ops[eng])
        ev = ("c", eng, seq)
        self.ops[eng].append(dict(fn=fn, dc=dc, dd=dd, dma=None))
        self._mark(ev, reads, writes)
        return ev

    def dma(self, fn, reads=(), writes=(), q="sp"):
        dc, dd = self._deps(reads, writes)
        k = self.ndma
        self.ndma += 1
        slot = k % self.nring
        val = 16 * (k // self.nring + 1)
        if k >= self.nring:
            dd.add(("d", slot, val - 16))
        ev = ("d", slot, val)
        self.ops[q].append(dict(fn=fn, dc=dc, dd=dd, dma=slot))
        self._mark(ev, reads, writes)
        self.last_dma[slot] = val
        return ev

    def barrier(self):
        last = {}
        for e in ENGS:
            last[e] = -1
            for i in range(len(self.ops[e]) - 1, -1, -1):
                if self.ops[e][i]["fn"] is not None and self.ops[e][i]["dma"] is None:
                    last[e] = i
                    break
        dds = set(("d", s, v) for s, v in self.last_dma.items())
        for e in ENGS:
            dc = {o: last[o] for o in ENGS if o != e and o != 'sp' and last[o] >= 0}
            self.ops[e].append(dict(fn=None, dc=dc, dd=set(dds), dma=None))

    def emit(self, nc):
        needed = {e: set() for e in ENGS}
        for e in ENGS:
            for o in self.ops[e]:
                for oe, s in o["dc"].items():
                    if oe == e and e in SAME_ENG_SKIP:
                        continue
                    needed[oe].add(s)
        rank = {e: {s: i + 1 for i, s in enumerate(sorted(needed[e]))} for e in ENGS}
        ops = self.ops
        nring = self.nring
        import contextlib

        with contextlib.ExitStack() as st:
            sems = {e: st.enter_context(nc.semaphore("s_" + e)) for e in ENGS}
            ring = [st.enter_context(nc.semaphore("r%d" % i)) for i in range(nring)]
            block = st.enter_context(nc.Block())

            def run(ename):
                def body(eng):
                    known = {}
                    for seq, o in enumerate(ops[ename]):
                        waits = {}
                        for oe, s in o["dc"].items():
                            if oe == ename and ename in SAME_ENG_SKIP:
                                continue
                            key = ("c", oe)
                            v = rank[oe][s]
                            if known.get(key, 0) >= v:
                                continue
                            if waits.get(key, 0) < v:
                                waits[key] = v
                        for ev in o["dd"]:
                            key = ("d", ev[1])
                            v = ev[2]
                            if known.get(key, 0) >= v:
                                continue
                            if waits.get(key, 0) < v:
                                waits[key] = v
                        for key, v in waits.items():
                            sem = sems[key[1]] if key[0] == "c" else ring[key[1]]
                            eng.wait_ge(sem, v)
                            known[key] = v
                        if o["fn"] is None:
                            continue
                        ins = o["fn"](eng)
                        if o["dma"] is not None:
                            ins.then_inc(ring[o["dma"]], 16)
                        elif seq in rank[ename]:
                            ins.then_inc(sems[ename], 1)

                return body

            block.tensor(run("pe"))
            block.scalar(run("act"))
            block.vector(run("dve"))
            block.gpsimd(run("pool"))
            block.sync(run("sp"))


class Mem:
    def __init__(self, big, ps):
        self.big = big
        self.ps = ps
        self.off = 0

    def reset(self, off=0):
        self.off = off

    def f32(self, n, p0=0, p1=128):
        o = (self.off + 31) // 32 * 32
        self.off = o + 4 * n
        assert self.off <= SBUF_BYTES, self.off
        return self.big[p0:p1, o // 4:o // 4 + n]

    def bf(self, n, p0=0, p1=128):
        o = (self.off + 31) // 32 * 32
        self.off = o + 2 * n
        assert self.off <= SBUF_BYTES, self.off
        return self.big[p0:p1, o // 4:o // 4 + (n + 1) // 2].bitcast(BF16)

    def bank(self, b, n=512, o=0):
        return self.ps[:, b * 512 + o:b * 512 + o + n]

    def bank_bf(self, b):
        return self.ps[:, b * 512:(b + 1) * 512].bitcast(BF16)


def r3(ap, a):
    return ap.rearrange("p (a b) -> p a b", a=a)


def ffn_phase(sc, M, C, x_src, x_dst, wg, wu, wd, gpre, gpost):
    M.reset(C["const_end"])
    ident = C["ident"]
    Wg = r3(M.bf(8 * DFF), 8)
    Wu = r3(M.bf(8 * DFF), 8)
    Wd = r3(M.bf(NFF * D), NFF)
    stage = [M.f32(1024) for _ in range(3)]
    gp = M.f32(1024)
    gq = M.f32(1024)
    xp0 = M.f32(1024)
    xp = [xp0, xp0]
    xr = [M.f32(1024) for _ in range(2)]
    hb0 = M.bf(1024)
    hb = [hb0, hb0]
    hT = r3(M.bf(8 * 512), 8)
    AT = r3(M.bf(NFF * 512), NFF)
    sg0 = M.f32(512)
    sg = [sg0, sg0]
    yt = M.f32(1024)
    small = M.f32(32)

    r_stage = [Res("stage%d" % i) for i in range(3)]
    r_Wg = [[Res() for _ in range(4)] for _ in range(8)]
    r_Wu = [[Res() for _ in range(4)] for _ in range(8)]
    r_Wd = [Res() for _ in range(NFF)]
    r_gp, r_gq = Res(), Res()
    r_xp0 = Res()
    r_xp = [r_xp0, r_xp0]
    r_xr = [Res(), Res()]
    r_hb0 = Res()
    r_hb = [r_hb0, r_hb0]
    r_hT, r_AT = Res(), [Res() for _ in range(NFF)]
    r_sg0 = Res()
    r_sg = [r_sg0, r_sg0]
    r_yt = Res()
    r_small = [Res() for _ in range(8)]
    r_T = Res()
    r_G = [Res(), Res()]
    r_U = [Res(), Res()]
    r_Y = Res()
    r_xdst = C["r_dram"][id(x_dst)] if id(x_dst) in C["r_dram"] else Res()
    r_xsrc = C["r_dram"].get(id(x_src), Res())
    C["r_dram"][id(x_dst)] = r_xdst

    sc.dma(lambda e: e.dma_start(out=gp, in_=gpre[0, :].partition_broadcast(128)), writes=[r_gp])
    sc.dma(lambda e: e.dma_start(out=gq, in_=gpost[0, :].partition_broadcast(128)), writes=[r_gq])
    sc.op("act", lambda e: e.mul(gq, gq, 0.5), reads=[r_gq], writes=[r_gq])

    cnt = [0]
    CB = [(0, 768), (768, 768), (1536, 768), (2304, 512)]

    def load_piece(dst_ap, src_ap, n, rdst):
        i = cnt[0] % 3
        eng = ("pool", "act", "dve")[cnt[0] % 3]
        cnt[0] += 1
        st = stage[i][:, 0:n]
        sc.dma(lambda e: e.dma_start(out=st, in_=src_ap), writes=[r_stage[i]])
        if eng == "act":
            sc.op("act", lambda e: e.copy(out=dst_ap, in_=st), reads=[r_stage[i]], writes=[rdst])
        else:
            sc.op(eng, lambda e: e.tensor_copy(out=dst_ap, in_=st), reads=[r_stage[i]], writes=[rdst])

    def load_weights():
        for cb, (c0, w) in enumerate(CB):
            for kc in range(8):
                for (W, wsrc, rW) in ((Wg, wg, r_Wg), (Wu, wu, r_Wu)):
                    load_piece(W[:, kc, c0:c0 + w], wsrc[kc * 128:(kc + 1) * 128, c0:c0 + w], w, rW[kc][cb])
        for f in range(NFF):
            load_piece(Wd[:, f, :], wd[f * 128:(f + 1) * 128, :], 1024, r_Wd[f])

    inv_d = 1.0 / D

    def rstd_from(ss_ap, out_ap, rs):
        sc.op("dve", lambda e: e.tensor_scalar(out=out_ap, in0=ss_ap, scalar1=inv_d, scalar2=EPS,
                                               op0=ALU.mult, op1=ALU.add), reads=[rs], writes=[rs])
        sc.op("act", lambda e: e.sqrt(out_ap, out_ap), reads=[rs], writes=[rs])
        sc.op("dve", lambda e: e.reciprocal(out=out_ap, in_=out_ap), reads=[rs], writes=[rs])

    NT = S // 512
    Tb = M.bank_bf(0)

    def prep(i):
        for j in range(4):
            t = i * 4 + j
            b = t % 2
            x_ap = xp[b]
            sc.dma(lambda e, x_ap=x_ap, t=t: e.dma_start(out=x_ap, in_=x_src[t * 128:(t + 1) * 128, :]),
                   reads=[r_xsrc], writes=[r_xp[b]])
            ss = small[:, b:b + 1]
            sc.op("pool", lambda e, ss=ss: e.memset(ss, 0.0), writes=[r_small[b]])
            sc.op("act", lambda e, x_ap=x_ap, b=b, ss=ss: e.activation(out=hb[b], in_=x_ap, func=AF.Square, accum_out=ss),
                  reads=[r_xp[b]], writes=[r_hb[b], r_small[b]])
            rstd_from(ss, ss, r_small[b])
            sc.op("dve", lambda e, x_ap=x_ap, b=b, ss=ss: e.scalar_tensor_tensor(
                out=hb[b], in0=x_ap, scalar=ss, in1=gp, op0=ALU.mult, op1=ALU.mult),
                reads=[r_xp[b], r_small[b], r_gp], writes=[r_hb[b]])
            for kc in range(8):
                sc.op("pe", lambda e, b=b, kc=kc: e.transpose(out=Tb[:, kc * 128:(kc + 1) * 128],
                                                             in_=hb[b][:, kc * 128:(kc + 1) * 128], identity=ident),
                      reads=[r_hb[b]], writes=[r_T])
            sc.op("act", lambda e, j=j: e.copy(out=hT[:, :, j * 128:(j + 1) * 128], in_=r3(Tb, 8)),
                  reads=[r_T], writes=[r_hT])

    def gateup(i):
        for f in range(NFF):
            pb = f % 2
            G = M.bank(1 + pb)
            U = M.bank(3 + pb)
            for kc in range(8):
                sc.op("pe", lambda e, G=G, kc=kc, f=f: e.matmul(G, lhsT=Wg[:, kc, f * 128:(f + 1) * 128], rhs=hT[:, kc, :],
                                                               start=(kc == 0), stop=(kc == 7)),
                      reads=[r_Wg[kc][min(3, (f * 128) // 768)], r_hT], writes=[r_G[pb]])
            for kc in range(8):
                sc.op("pe", lambda e, U=U, kc=kc, f=f: e.matmul(U, lhsT=Wu[:, kc, f * 128:(f + 1) * 128], rhs=hT[:, kc, :],
                                                               start=(kc == 0), stop=(kc == 7)),
                      reads=[r_Wu[kc][min(3, (f * 128) // 768)], r_hT], writes=[r_U[pb]])
            sc.op("act", lambda e, G=G, pb=pb: e.activation(out=sg[pb], in_=G, func=AF.Silu),
                  reads=[r_G[pb]], writes=[r_sg[pb]])
            sc.op("dve", lambda e, U=U, pb=pb, f=f: e.tensor_tensor(out=AT[:, f, :], in0=sg[pb], in1=U, op=ALU.mult),
                  reads=[r_sg[pb], r_U[pb]], writes=[r_AT[f]])

    YB = [(5, 6), (7, 0)]
    r_Yp = [[Res()], [Res(), r_T]]

    def down(i):
        for j in range(4):
            t = i * 4 + j
            b = t % 2
            yp = j % 2
            rY = r_Yp[yp]
            sc.dma(lambda e, b=b, t=t: e.dma_start(out=xr[b], in_=x_src[t * 128:(t + 1) * 128, :]),
                   reads=[r_xsrc], writes=[r_xr[b]])
            for dh in range(2):
                Y = M.bank(YB[yp][dh])
                for f in range(NFF):
                    sc.op("pe", lambda e, Y=Y, f=f, j=j, dh=dh: e.matmul(
                        Y, lhsT=AT[:, f, j * 128:(j + 1) * 128], rhs=Wd[:, f, dh * 512:(dh + 1) * 512],
                        start=(f == 0), stop=(f == NFF - 1)),
                        reads=[r_AT[f], r_Wd[f]], writes=rY)
            ssa = small[:, 4:5]
            ssb = small[:, 5:6]
            Y0, Y1 = M.bank(YB[yp][0]), M.bank(YB[yp][1])
            sc.op("pool", lambda e: e.memset(small[:, 4:6], 0.0), writes=[r_small[4]])
            sc.op("act", lambda e, ssa=ssa, Y0=Y0: e.activation(out=yt[:, 0:512], in_=Y0, func=AF.Square, accum_out=ssa),
                  reads=rY, writes=[r_yt, r_small[4]])
            sc.op("act", lambda e, ssb=ssb, Y1=Y1: e.activation(out=yt[:, 512:1024], in_=Y1, func=AF.Square, accum_out=ssb),
                  reads=rY, writes=[r_yt, r_small[4]])
            sc.op("dve", lambda e, ssa=ssa, ssb=ssb: e.tensor_tensor(out=ssa, in0=ssa, in1=ssb, op=ALU.add),
                  reads=[r_small[4]], writes=[r_small[4]])
            rstd_from(ssa, ssa, r_small[4])
            for dh in range(2):
                Yd = M.bank(YB[yp][dh])
                sc.op("dve", lambda e, dh=dh, ssa=ssa, Yd=Yd: e.scalar_tensor_tensor(
                    out=yt[:, dh * 512:(dh + 1) * 512], in0=Yd, scalar=ssa,
                    in1=gq[:, dh * 512:(dh + 1) * 512], op0=ALU.mult, op1=ALU.mult),
                    reads=rY + [r_small[4], r_gq], writes=[r_yt])
            sc.op("dve", lambda e, b=b: e.tensor_tensor(out=xr[b], in0=yt, in1=xr[b], op=ALU.add),
                  reads=[r_yt, r_xr[b]], writes=[r_xr[b]])
            sc.dma(lambda e, b=b, t=t: e.dma_start(out=x_dst[t * 128:(t + 1) * 128, :], in_=xr[b]),
                   reads=[r_xr[b]], writes=[r_xdst], q="act")

    if LIMIT >= 1:
        prep(0)
    load_weights()
    for i in range(NT):
        if LIMIT == 2 and i == 0:
            gateup(i)
        if LIMIT == 3 and i == 0:
            gateup(i)
            down(i)
        if LIMIT < 9:
            continue
        gateup(i)
        if i + 1 < NT:
            prep(i + 1)
        down(i)
    sc.barrier()


NEG = -30000.0
ZEROS = [None]
NSA_SL = [2.0 ** (-(i + 1)) for i in range(8)]
DIL_SL = [2.0 ** (-8.0 * (i + 1) / 12) for i in range(12)]
DILS = (1, 4, 16)
C_QN, C_KV, C_GN, C_QD, C_KD, C_VD, C_MG = 0, 512, 1280, 1304, 2072, 2840, 3608


def v4(ap, a, b):
    return ap.rearrange("p (a b c) -> p a b c", a=a, b=b)


def norm_to_hT(sc, M, x_src, t, xp, hb, small, gp, ident, rr, dstT, r_dst, Tb, r_T):
    b = t % 2
    r_xp, r_hb, r_small, r_gp = rr
    sc.dma(lambda e: e.dma_start(out=xp[b], in_=x_src[t * 128:(t + 1) * 128, :]), writes=[r_xp[b]])
    ss = small[:, b:b + 1]
    sc.op("pool", lambda e: e.memset(ss, 0.0), writes=[r_small[b]])
    sc.op("act", lambda e: e.activation(out=hb[b], in_=xp[b], func=AF.Square, accum_out=ss),
          reads=[r_xp[b]], writes=[r_hb[b], r_small[b]])
    sc.op("dve", lambda e: e.tensor_scalar(out=ss, in0=ss, scalar1=1.0 / D, scalar2=EPS, op0=ALU.mult, op1=ALU.add),
          reads=[r_small[b]], writes=[r_small[b]])
    sc.op("act", lambda e: e.sqrt(ss, ss), reads=[r_small[b]], writes=[r_small[b]])
    sc.op("dve", lambda e: e.reciprocal(out=ss, in_=ss), reads=[r_small[b]], writes=[r_small[b]])
    sc.op("dve", lambda e: e.scalar_tensor_tensor(out=hb[b], in0=xp[b], scalar=ss, in1=gp, op0=ALU.mult, op1=ALU.mult),
          reads=[r_xp[b], r_small[b], r_gp], writes=[r_hb[b]])
    for kc in range(8):
        sc.op("pe", lambda e, kc=kc: e.transpose(out=Tb[:, kc * 128:(kc + 1) * 128],
                                                 in_=hb[b][:, kc * 128:(kc + 1) * 128], identity=ident),
              reads=[r_hb[b]], writes=[r_T])
    sc.op("act", lambda e: e.copy(out=dstT, in_=r3(Tb, 8)), reads=[r_T], writes=[r_dst])


def proj_phase(sc, M, C, x1, w_in, gmix, scr):
    M.reset(C["const_end"])
    ident = C["ident"]
    h2T = r3(M.bf(8 * S), 8)
    Wb = [r3(M.bf(8 * 512), 8) for _ in range(2)]
    stage = [M.f32(512) for _ in range(2)]
    gp = M.f32(1024)
    xp = [M.f32(1024) for _ in range(2)]
    hb = [M.bf(1024) for _ in range(2)]
    small = M.f32(8)
    evf = [M.f32(512) for _ in range(2)]
    evb = [M.bf(512) for _ in range(2)]
    vaug = [M.bf(4 * 65) for _ in range(2)]
    r_h2T = [Res() for _ in range(32)]
    r_Wb = [Res(), Res()]
    r_stage = [Res(), Res()]
    r_gp = Res()
    rr = ([Res(), Res()], [Res(), Res()], [Res(), Res()], r_gp)
    r_ev = [Res(), Res()]
    r_va = [Res(), Res()]
    r_T = Res()
    r_ps = [Res(), Res()]
    Tb = M.bank_bf(0)
    sc.dma(lambda e: e.dma_start(out=gp, in_=gmix[0, :].partition_broadcast(128)), writes=[r_gp])
    for i in range(2):
        sc.op("pool", lambda e, i=i: e.memset(vaug[i], 1.0), writes=[r_va[i]])
    for t in range(32):
        norm_to_hT(sc, M, x1, t, xp, hb, small, gp, ident, rr, h2T[:, :, t * 128:(t + 1) * 128], r_h2T[t], Tb, r_T)

    cnt = [0]
    nspec = [0]

    def load_wb(bi, cols):
        for kc in range(8):
            off = 0
            for (c, w) in cols:
                i = cnt[0] % 2
                cnt[0] += 1
                st = stage[i][:, 0:w]
                sc.dma(lambda e, st=st, c=c, w=w, kc=kc: e.dma_start(out=st, in_=w_in[kc * 128:(kc + 1) * 128, c:c + w]),
                       writes=[r_stage[i]])
                dst = Wb[bi][:, kc, off:off + w]
                sc.op("pool", lambda e, st=st, dst=dst: e.tensor_copy(out=dst, in_=st), reads=[r_stage[i]], writes=[r_Wb[bi]])
                off += w

    ecnt = [0]

    def run_fm(cols, dst, fp32, dil):
        bi = nspec[0] % 2
        nspec[0] += 1
        load_wb(bi, cols)
        Ls = S // dil
        Cn = min(512, Ls)
        h4 = h2T.rearrange("p k (i r) -> p k i r", r=dil)
        for p0 in range(0, S, Cn):
            r, i0 = divmod(p0, Ls)
            k = ecnt[0] % 2
            ecnt[0] += 1
            PS = M.bank(1 + k)[:, 0:Cn]
            for kc in range(8):
                sc.op("pe", lambda e, PS=PS, kc=kc, i0=i0, r=r: e.matmul(
                    PS, lhsT=Wb[bi][:, kc, 0:128], rhs=h4[:, kc, i0:i0 + Cn, r], start=(kc == 0), stop=(kc == 7)),
                    reads=[r_Wb[bi]] + r_h2T, writes=[r_ps[k]])
            evt = (evf[k] if fp32 else evb[k])[:, 0:Cn]
            sc.op("act", lambda e, PS=PS, evt=evt: e.copy(out=evt, in_=PS), reads=[r_ps[k]], writes=[r_ev[k]])
            sc.dma(lambda e, evt=evt, p0=p0: e.dma_start(out=dst[:, p0:p0 + Cn], in_=evt), reads=[r_ev[k]], q="act")

    def run_tm(c0, n, dst, kind, dil, nh=0):
        bi = nspec[0] % 2
        nspec[0] += 1
        load_wb(bi, [(c0, n)])
        Ls = S // dil
        h4 = h2T.rearrange("p k (i r) -> p k i r", r=dil)
        for t in range(32):
            p0 = t * 128
            r, i0 = divmod(p0, Ls)
            k = ecnt[0] % 2
            ecnt[0] += 1
            PS = M.bank(1 + k)[:, 0:n]
            for kc in range(8):
                sc.op("pe", lambda e, PS=PS, kc=kc, i0=i0, r=r: e.matmul(
                    PS, lhsT=h4[:, kc, i0:i0 + 128, r], rhs=Wb[bi][:, kc, 0:n], start=(kc == 0), stop=(kc == 7)),
                    reads=[r_Wb[bi]] + r_h2T, writes=[r_ps[k]])
            if kind == "aug":
                va = vaug[k][:, 0:nh * 65]
                sc.op("act", lambda e, PS=PS, va=va: e.copy(out=r3(va, nh)[:, :, 0:64], in_=r3(PS, nh)),
                      reads=[r_ps[k]], writes=[r_va[k]])
                sc.dma(lambda e, va=va, p0=p0: e.dma_start(out=dst[p0:p0 + 128, :], in_=va), reads=[r_va[k]], q="act")
            elif kind == "sigf":
                evt = evf[k][:, 0:n]
                sc.op("act", lambda e, PS=PS, evt=evt: e.activation(out=evt, in_=PS, func=AF.Sigmoid),
                      reads=[r_ps[k]], writes=[r_ev[k]])
                sc.dma(lambda e, evt=evt, p0=p0: e.dma_start(out=dst[p0:p0 + 128, :], in_=evt), reads=[r_ev[k]], q="act")
            else:
                evt = evb[k][:, 0:n]
                sc.op("act", lambda e, PS=PS, evt=evt: e.activation(out=evt, in_=PS, func=AF.Sigmoid),
                      reads=[r_ps[k]], writes=[r_ev[k]])
                sc.dma(lambda e, evt=evt, p0=p0: e.dma_start(out=dst[p0:p0 + 128, :], in_=evt), reads=[r_ev[k]], q="act")

    for p in range(4):
        run_fm([(C_QN + p * 128, 128)], scr["QN"][p], False, 1)
    run_fm([(C_KV, 128)], scr["KC"][0], True, 1)
    run_fm([(C_KV + 128, 128)], scr["KC"][1], True, 1)
    for g in range(2):
        run_fm([(C_KV + 256 + g * 64, 64)] * 2, scr["KS"][g], False, 1)
        run_fm([(C_KV + 512 + g * 64, 64)] * 2, scr["KW"][g], False, 1)
    run_tm(C_KV + 384, 128, scr["VS"], "aug", 1, nh=2)
    run_tm(C_KV + 640, 128, scr["VW"], "aug", 1, nh=2)
    run_tm(C_GN, 24, scr["GN"], "sigf", 1)
    for gi in range(3):
        for pp in range(2):
            run_fm([(C_QD + gi * 256 + pp * 128, 128)], scr["QD"][gi * 2 + pp], False, DILS[gi])
            run_fm([(C_KD + gi * 256 + pp * 128, 128)], scr["KD"][gi * 2 + pp], False, DILS[gi])
        run_tm(C_VD + gi * 256, 256, scr["VD"][gi], "aug", DILS[gi], nh=4)
    for q in range(4):
        run_tm(C_MG + q * 512, 512, scr["MG"][:, q * 512:(q + 1) * 512], "sigb", 1)
    sc.barrier()


def cmp_phase(sc, M, C, scr, P):
    M.reset(C["const_end"])
    KS = r3(M.bf(2 * S), 2)
    KW = r3(M.bf(2 * S), 2)
    VS = v4(M.bf(32 * 256), 32, 2)
    VW = v4(M.bf(32 * 256), 32, 2)
    r_KS, r_KW, r_VS, r_VW = Res(), Res(), Res(), Res()
    C["r_nsa_kv"] = (r_KS, r_KW, r_VS, r_VW)
    sc.op("dve", lambda e: e.memset(KW.rearrange("p a b -> p (a b)"), 0.0), writes=[r_KW])
    sc.op("dve", lambda e: e.memset(VS.rearrange("p a b c -> p (a b c)"), 0.0), writes=[r_VS])
    sc.op("pool", lambda e: e.memset(VW.rearrange("p a b c -> p (a b c)"), 0.0), writes=[r_VW])
    ohf = P["c_oh"].rearrange("j a b -> j (a b)")
    for g in range(2):
        sc.dma(lambda e, g=g: e.dma_start(out=KS[0:64, g, :], in_=scr["KS"][g][0:64, :]), writes=[r_KS], q="act")
        sc.dma(lambda e, g=g: e.dma_start(out=KS[64:128, g, :], in_=ohf), writes=[r_KS], q="act")
        sc.dma(lambda e, g=g: e.dma_start(out=KW[0:64, g, :], in_=scr["KW"][g][0:64, :]), writes=[r_KW], q="act")
    vsv = scr["VS"].rearrange("(kt p) (g c) -> p kt g c", p=128, g=2)
    vwv = scr["VW"].rearrange("(kt p) (g c) -> p kt g c", p=128, g=2)
    for q4 in range(4):
        for g in range(2):
            sc.dma(lambda e, q4=q4, g=g: e.dma_start(out=VS[:, q4 * 8:(q4 + 1) * 8, g, 0:65],
                                                      in_=vsv[:, q4 * 8:(q4 + 1) * 8, g, :]), writes=[r_VS], q="act")
            sc.dma(lambda e, q4=q4, g=g: e.dma_start(out=VW[:, q4 * 8:(q4 + 1) * 8, g, 0:65],
                                                      in_=vwv[:, q4 * 8:(q4 + 1) * 8, g, :]), writes=[r_VW], q="act")
    kcT, VC = C["kcT"], C["VC"]
    kcf = M.f32(S)
    tb = r3(M.bf(32 * 256), 32)
    W1b = r3(M.bf(32 * 256), 32)
    W2b = v4(M.bf(2 * 128), 2, 2)
    st1 = [M.f32(256) for _ in range(2)]
    st2 = M.f32(128)
    peT = M.f32(32)
    hx = [M.f32(256) for _ in range(4)]
    hu = M.f32(256)
    gT = [M.bf(256) for _ in range(4)]
    r_kcf, r_tb, r_W1, r_W2, r_st2, r_pe = Res(), Res(), Res(), Res(), Res(), Res()
    r_st1 = [Res(), Res()]
    r_hx = [Res() for _ in range(4)]
    r_hu = Res()
    r_gT = [Res() for _ in range(4)]
    r_ps = [Res(), Res(), Res()]
    r_kcT, r_VC = C["r_kcT"], C["r_VC"]
    VC4 = v4(VC, 2, 2)
    sc.op("pool", lambda e: e.memset(VC, 1.0), writes=[r_VC])
    for ct in range(2):
        for g in range(2):
            sc.dma(lambda e, ct=ct, g=g: e.dma_start(out=VC4[:, ct, g, 65:128], in_=P["c_ov"][ct * 128:(ct + 1) * 128, 1:64]),
                   writes=[r_VC])
    def do_kv(kv):
        pe_d = P["nsa_pe_k"] if kv == 0 else P["nsa_pe_v"]
        w1_d = P["nsa_w_ck1"] if kv == 0 else P["nsa_w_cv1"]
        w2_d = P["nsa_w_ck2"] if kv == 0 else P["nsa_w_cv2"]
        sc.dma(lambda e: e.dma_start(out=kcf, in_=scr["KC"][kv]), writes=[r_kcf])
        for g in range(2):
            sc.dma(lambda e, g=g, pe_d=pe_d: e.dma_start(out=peT[g * 64:(g + 1) * 64, :], in_=pe_d.rearrange("l d -> d l"),
                                                        allow_slow_non_contiguous=True), writes=[r_pe])
        sc.op("pool", lambda e: e.memset(tb.rearrange("p a b -> p (a b)"), 0.0), writes=[r_tb])
        kcf3 = kcf.rearrange("p (a b) -> p a b", b=16)
        for l in range(32):
            src = kcf3[:, 0:255, l] if l < 16 else kcf3[:, 1:256, l - 16]
            sc.op("dve", lambda e, l=l, src=src: e.tensor_scalar(out=tb[:, l, 0:255], in0=src, scalar1=peT[:, l:l + 1],
                                                                 scalar2=None, op0=ALU.add),
                  reads=[r_kcf, r_pe], writes=[r_tb])
        w1v = w1_d.rearrange("(l d) h -> d l h", d=64)
        for l in range(32):
            i = l % 2
            for g in range(2):
                sc.dma(lambda e, l=l, g=g, i=i: e.dma_start(out=st1[i][g * 64:(g + 1) * 64, :], in_=w1v[:, l, :]),
                       writes=[r_st1[i]])
            sc.op("pool", lambda e, l=l, i=i: e.tensor_copy(out=W1b[:, l, :], in_=st1[i]), reads=[r_st1[i]], writes=[r_W1])
        sc.dma(lambda e: e.dma_start(out=r3(st2, 2), in_=w2_d.rearrange("(hc p) d -> p hc d", p=128)), writes=[r_st2])
        sc.op("pool", lambda e: e.tensor_copy(out=W2b[:, :, 0, :], in_=r3(st2, 2)), reads=[r_st2], writes=[r_W2])
        sc.op("pool", lambda e: e.memset(W2b[:, :, 1, :], 0.0), writes=[r_W2])
        for g in range(2):
            for hc in range(2):
                idx = g * 2 + hc
                PS = M.bank(1 + idx % 2)[:, 0:256]
                rp = r_ps[idx % 2]
                for l in range(32):
                    sc.op("pe", lambda e, PS=PS, l=l, g=g, hc=hc: e.matmul(
                        PS, lhsT=W1b[g * 64:(g + 1) * 64, l, hc * 128:(hc + 1) * 128], rhs=tb[g * 64:(g + 1) * 64, l, :],
                        start=(l == 0), stop=(l == 31)), reads=[r_W1, r_tb], writes=[rp])
                x_ = hx[idx]
                sc.op("act", lambda e, PS=PS, x_=x_: e.copy(out=x_, in_=PS), reads=[rp], writes=[r_hx[idx]])
                sc.op("act", lambda e, x_=x_: e.activation(out=hu, in_=x_, func=AF.Square), reads=[r_hx[idx]], writes=[r_hu])
                sc.op("dve", lambda e: e.tensor_scalar(out=hu, in0=hu, scalar1=0.044715, scalar2=1.0, op0=ALU.mult, op1=ALU.add),
                      reads=[r_hu], writes=[r_hu])
                sc.op("dve", lambda e, x_=x_: e.tensor_tensor(out=hu, in0=hu, in1=x_, op=ALU.mult),
                      reads=[r_hu, r_hx[idx]], writes=[r_hu])
                sc.op("act", lambda e: e.activation(out=hu, in_=hu, func=AF.Sigmoid, scale=1.5957691216057308),
                      reads=[r_hu], writes=[r_hu])
                sc.op("dve", lambda e, x_=x_, idx=idx: e.tensor_tensor(out=gT[idx], in0=hu, in1=x_, op=ALU.mult),
                      reads=[r_hu, r_hx[idx]], writes=[r_gT[idx]])
        for g in range(2):
            if kv == 0:
                PS = M.bank(3)[:, 0:256]
                for hc in range(2):
                    sc.op("pe", lambda e, PS=PS, g=g, hc=hc: e.matmul(
                        PS, lhsT=W2b[:, hc, :, :].rearrange("p a b -> p (a b)"), rhs=gT[g * 2 + hc],
                        start=(hc == 0), stop=(hc == 1)), reads=[r_W2, r_gT[g * 2 + hc]], writes=[r_ps[2]])
                sc.op("act", lambda e, PS=PS, g=g: e.copy(out=kcT[:, g, :], in_=PS), reads=[r_ps[2]], writes=[r_kcT])
            else:
                for ct in range(2):
                    PS = M.bank(3)[:, 0:64]
                    for hc in range(2):
                        sc.op("pe", lambda e, PS=PS, g=g, hc=hc, ct=ct: e.matmul(
                            PS, lhsT=gT[g * 2 + hc][:, ct * 128:(ct + 1) * 128], rhs=W2b[:, hc, 0, :],
                            start=(hc == 0), stop=(hc == 1)), reads=[r_W2, r_gT[g * 2 + hc]], writes=[r_ps[2]])
                    sc.op("act", lambda e, PS=PS, g=g, ct=ct: e.copy(out=VC4[:, ct, g, 0:64], in_=PS),
                          reads=[r_ps[2]], writes=[r_VC])

    for kv in range(2):
        do_kv(kv)
    sc.barrier()


class AttnPipe:
    def __init__(self, sc, M, sbanks, ntmp=3):
        self.sc, self.M = sc, M
        self.sb = sbanks
        self.r_s = [Res() for _ in sbanks]
        self.tmp = [M.f32(512) for _ in range(ntmp)]
        self.Pt = [M.bf(512) for _ in range(ntmp)]
        self.r_tmp = [Res() for _ in range(ntmp)]
        self.r_P = [Res() for _ in range(ntmp)]
        self.n = 0

    def run_stream(self, items, LA=3):
        sc, M = self.sc, self.M
        jobs = [it for it in items if isinstance(it, dict)]
        order = []
        ji = 0
        for it in items:
            if isinstance(it, dict):
                order.append(("job", ji))
                ji += 1
            else:
                order.append(("call", it))
        n = len(jobs)
        slots = {}
        done_pv = [0]
        pending = []
        DELAY = 4

        def emit_pv(j):
            jb = jobs[j]
            ti = slots[j]
            if "parts" in jb:
                while pending and pending[0][0] <= j:
                    pending.pop(0)[2]()
                acc, M_rows, W = jb["acc"], jb["M_rows"], jb["W"]
                nk = jb["nk"]
                if jb["first"]:
                    z = ZEROS[0]
                    sc.op("pe", lambda e: e.matmul(acc[0:M_rows, 0:W], lhsT=z[:, 0:M_rows], rhs=z[:, 0:W], start=True, stop=False),
                          writes=[jb["r_acc"]])
                np_ = len(jb["parts"])
                for pi_, (_l, _r, _rd, off_, n_, pv_, pvr_, c0_) in enumerate(jb["parts"]):
                    Pq = self.Pt[ti][0:nk, off_:off_ + n_]
                    lastp = jb["last"] and pi_ == np_ - 1
                    sc.op("pe", lambda e, Pq=Pq, pv_=pv_, c0_=c0_, n_=n_, lastp=lastp: e.matmul(
                        acc[0:M_rows, c0_:c0_ + n_], lhsT=pv_, rhs=Pq, start=False, stop=lastp),
                        reads=[self.r_P[ti]] + pvr_, writes=[jb["r_acc"]])
                if jb.get("after") is not None:
                    key, fa, fb = jb["after"]
                    while any(p[1] == key for p in pending):
                        pending.pop(0)[2]()
                    fa()
                    pending.append((j + DELAY, key, fb))
                return
            nk, ncol, c0 = jb["nk"], jb["nc"], jb["c0"]
            Pp = self.Pt[ti][0:nk, 0:ncol]
            pv = jb["pv"]
            acc, M_rows = jb["acc"], jb["M_rows"]
            first, last = jb["first"], jb["last"]
            while pending and pending[0][0] <= j:
                pending.pop(0)[2]()
            W = jb.get("W", 512)
            if first and (c0 != 0 or ncol != W):
                z = ZEROS[0]
                sc.op("pe", lambda e: e.matmul(acc[0:M_rows, 0:W], lhsT=z[:, 0:M_rows], rhs=z[:, 0:W], start=True, stop=False),
                      writes=[jb["r_acc"]])
                first = False
            sc.op("pe", lambda e: e.matmul(acc[0:M_rows, c0:c0 + ncol], lhsT=pv, rhs=Pp, start=first, stop=last),
                  reads=[self.r_P[ti]] + jb["pv_reads"], writes=[jb["r_acc"]])
            if jb.get("after") is not None:
                key, fa, fb = jb["after"]
                while any(p[1] == key for p in pending):
                    pending.pop(0)[2]()
                fa()
                pending.append((j + DELAY, key, fb))

        for kind, v in order:
            if kind == "call":
                v()
                continue
            i = v
            jb = jobs[i]
            k = self.n
            self.n += 1
            si = k % len(self.sb)
            ti = k % len(self.tmp)
            slots[i] = ti
            nk, ncol = jb["nk"], jb["nc"]
            Sps = M.bank(self.sb[si])[0:nk, 0:ncol]
            if "parts" in jb:
                for (l_, r_, rd, off_, n_, _pv, _pvr, _c0) in jb["parts"]:
                    Sp_ = M.bank(self.sb[si])[0:nk, off_:off_ + n_]
                    sc.op("pe", lambda e, l_=l_, r_=r_, Sp_=Sp_: e.matmul(Sp_, lhsT=l_, rhs=r_, start=True, stop=True),
                          reads=rd, writes=[self.r_s[si]])
            else:
                nq = len(jb["qk"])
                for qi, (l_, r_, rd) in enumerate(jb["qk"]):
                    sc.op("pe", lambda e, l_=l_, r_=r_, qi=qi, Sps=Sps, nq=nq: e.matmul(
                        Sps, lhsT=l_, rhs=r_, start=(qi == 0), stop=(qi == nq - 1)), reads=rd, writes=[self.r_s[si]])
            Pp = self.Pt[ti][0:nk, 0:ncol]
            bias = jb.get("bias")
            if "parts" in jb:
                tm = self.tmp[ti][0:nk, 0:ncol]
                T = jb["T"]
                sc.op("dve", lambda e, tm=tm, Sps=Sps, T=T: e.scalar_tensor_tensor(
                    out=tm, in0=Sps, scalar=0.125, in1=T, op0=ALU.mult, op1=ALU.add),
                    reads=[self.r_s[si]] + jb.get("T_reads", []), writes=[self.r_tmp[ti]])
                sc.op("act", lambda e, Pp=Pp, tm=tm: e.activation(out=Pp, in_=tm, func=AF.Exp),
                      reads=[self.r_tmp[ti]], writes=[self.r_P[ti]])
            elif jb.get("direct"):
                sc.op("act", lambda e, Pp=Pp, Sps=Sps, bias=bias: e.activation(out=Pp, in_=Sps, func=AF.Exp, bias=bias, scale=0.125),
                      reads=[self.r_s[si]] + jb.get("bias_reads", []), writes=[self.r_P[ti]])
            else:
                tm = self.tmp[ti][0:nk, 0:ncol]
                T = jb["T"]
                sc.op("dve", lambda e, tm=tm, Sps=Sps, T=T: e.scalar_tensor_tensor(
                    out=tm, in0=Sps, scalar=0.125, in1=T, op0=ALU.mult, op1=ALU.add),
                    reads=[self.r_s[si]] + jb.get("T_reads", []), writes=[self.r_tmp[ti]])
                for (off, mk, mrd) in jb["masks"]:
                    w = mk.shape[-1]
                    tmm = self.tmp[ti][0:nk, off:off + w]
                    sc.op("pool", lambda e, tmm=tmm, mk=mk: e.tensor_tensor(out=tmm, in0=tmm, in1=mk, op=ALU.add),
                          reads=[self.r_tmp[ti]] + mrd, writes=[self.r_tmp[ti]])
                sc.op("act", lambda e, Pp=Pp, tm=tm, bias=bias: e.activation(out=Pp, in_=tm, func=AF.Exp, bias=bias),
                      reads=[self.r_tmp[ti]] + jb.get("bias_reads", []), writes=[self.r_P[ti]])
            while done_pv[0] <= i - LA:
                emit_pv(done_pv[0])
                done_pv[0] += 1
        while done_pv[0] < n:
            emit_pv(done_pv[0])
            done_pv[0] += 1
        while pending:
            pending.pop(0)[2]()

    def run(self, jobs, acc, r_acc, M_rows):
        sc, M = self.sc, self.M
        LA = 2
        n = len(jobs)
        slots = []
        for i in range(n + LA):
            if i < n:
                jb = jobs[i]
                k = self.n
                self.n += 1
                si = k % len(self.sb)
                ti = k % len(self.tmp)
                slots.append(ti)
                nk, ncol = jb["nk"], jb["nc"]
                Sps = M.bank(self.sb[si])[0:nk, 0:ncol]
                nq = len(jb["qk"])
                for qi, (l_, r_, rd) in enumerate(jb["qk"]):
                    sc.op("pe", lambda e, Sps=Sps, l_=l_, r_=r_, qi=qi, nq=nq: e.matmul(
                        Sps, lhsT=l_, rhs=r_, start=(qi == 0), stop=(qi == nq - 1)), reads=rd, writes=[self.r_s[si]])
                tm = self.tmp[ti][0:nk, 0:ncol]
                T = jb["T"]
                sc.op("dve", lambda e, tm=tm, Sps=Sps, T=T: e.scalar_tensor_tensor(out=tm, in0=Sps, scalar=0.125, in1=T, op0=ALU.mult, op1=ALU.add),
                      reads=[self.r_s[si]] + jb.get("T_reads", []), writes=[self.r_tmp[ti]])
                for (off, mk, mrd) in jb["masks"]:
                    w = mk.shape[-1]
                    tmm = self.tmp[ti][0:nk, off:off + w]
                    sc.op("pool", lambda e, tmm=tmm, mk=mk: e.tensor_tensor(out=tmm, in0=tmm, in1=mk, op=ALU.add),
                          reads=[self.r_tmp[ti]] + mrd, writes=[self.r_tmp[ti]])
                Pp = self.Pt[ti][0:nk, 0:ncol]
                bias = jb["bias"]
                sc.op("act", lambda e, Pp=Pp, tm=tm, bias=bias: e.activation(out=Pp, in_=tm, func=AF.Exp, bias=bias),
                      reads=[self.r_tmp[ti]] + jb.get("bias_reads", []), writes=[self.r_P[ti]])
            if i >= LA:
                j = i - LA
                jb = jobs[j]
                ti = slots[j]
                nk, ncol, c0 = jb["nk"], jb["nc"], jb["c0"]
                Pp = self.Pt[ti][0:nk, 0:ncol]
                pv = jb["pv"]
                sc.op("pe", lambda e, Pp=Pp, pv=pv, c0=c0, ncol=ncol, j=j: e.matmul(
                    acc[0:M_rows, c0:c0 + ncol], lhsT=pv, rhs=Pp, start=(j == 0), stop=(j == n - 1)),
                    reads=[self.r_P[ti]] + jb["pv_reads"], writes=[r_acc])


def nsa_phase(sc, M, C, scr, P):
    M.reset(C["const_end"])
    identf = C["identf"]
    ident = C["ident"]
    kcT, VC = C["kcT"], C["VC"]
    r_kcT, r_VC = C["r_kcT"], C["r_VC"]
    VC4 = v4(VC, 2, 2)
    KS = r3(M.bf(2 * S), 2)
    KW = r3(M.bf(2 * S), 2)
    VS = v4(M.bf(32 * 256), 32, 2)
    VW = v4(M.bf(32 * 256), 32, 2)
    TQA = r3(M.f32(8 * 512), 8)
    MASKO = M.f32(512)
    TQ8 = r3(M.f32(8 * 4), 8)
    QN = [r3(M.bf(8 * 512), 8) for _ in range(2)]
    TQ = r3(M.f32(8 * 512), 8)
    MK = r3(M.f32(3 * 128), 3)
    BC = r3(M.f32(8 * 35), 8)
    BCC = v4(M.f32(8 * 8 * 2), 8, 8)
    Mc = [r3(M.f32(2 * 512), 2) for _ in range(2)]
    accS = [M.f32(512) for _ in range(2)]
    onsa = r3(M.f32(4 * 512), 4)
    onsab = r3(M.bf(4 * 512), 4)
    impacc = v4(M.f32(2 * 4 * 64), 2, 4)
    GN = [r3(M.f32(4 * 24), 4) for _ in range(2)]
    SELM = [v4(M.f32(4 * 2 * 64), 4, 2) for _ in range(2)]
    rr_ = M.f32(4)
    sc4 = M.f32(4)
    m1 = M.f32(8)
    m2 = M.f32(8)
    wk = M.f32(64)
    impp = r3(M.f32(4 * 64), 4)
    mbf = M.bf(128)
    tnum = r3(M.f32(4 * 64), 4)
    pipe = AttnPipe(sc, M, [0, 1, 2, 6], ntmp=4)
    r_KS, r_KW, r_VS, r_VW = C["r_nsa_kv"]
    r_c = Res()
    r_QN = [[Res(), Res()], [Res(), Res()]]
    r_Mc = [Res(), Res()]
    r_GN = [Res(), Res()]
    r_SELM = [Res(), Res()]
    r_accS = [Res(), Res()]
    r_acc = [Res(), Res()]
    r_TP0 = Res()
    r_TP = [r_TP0, r_TP0]
    r_onsa, r_onsab, r_imp, r_MBT = Res(), Res(), Res(), Res()
    r_sm = Res()
    r_sel = Res()
    r_mbf = Res()
    r_tnum = Res()
    r_TP7 = Res()
    for qb_ in range(2):
        sc.op("pool", lambda e, qb_=qb_: e.memset(QN[qb_].rearrange("p a b -> p (a b)"), 0.0),
              writes=[r_QN[qb_][0], r_QN[qb_][1]])
    sc.dma(lambda e: e.dma_start(out=TQ.rearrange("p a b -> p (a b)"),
                                 in_=P["c_tq"].rearrange("a b -> (a b)").partition_broadcast(128)), writes=[r_c])
    sc.dma(lambda e: e.dma_start(out=TQA, in_=P["c_tqa"].rearrange("h k q -> k h q")), writes=[r_c])
    sc.dma(lambda e: e.dma_start(out=MASKO, in_=P["c_masko"]), writes=[r_c])
    sc.dma(lambda e: e.dma_start(out=TQ8, in_=P["c_tq8"]), writes=[r_c])
    sc.dma(lambda e: e.dma_start(out=MK, in_=P["c_masks"].rearrange("m k q -> k m q")), writes=[r_c])
    sc.dma(lambda e: e.dma_start(out=BC, in_=P["c_bc"]), writes=[r_c])
    sc.dma(lambda e: e.dma_start(out=BCC, in_=P["c_bcc"]), writes=[r_c])
    sc.barrier()

    def finish_head(acc_bank, ai, M_rows, h, br, qc, first):
        a = accS[ai]
        tb_ = 5
        TP = r3(M.bank(tb_), 4)
        for sub in range(4):
            MR = M_rows + (M_rows % 2)
            sc.op("pe", lambda e, sub=sub, MR=MR: e.transpose(out=TP[:, sub, 0:MR], in_=a[0:MR, sub * 128:(sub + 1) * 128],
                                                       identity=identf[0:MR, 0:MR]),
                  reads=[r_accS[ai]], writes=[r_TP[ai]])
        den = TP[:, :, 64:65]
        if br == 0:
            sc.op("dve", lambda e: e.tensor_scalar(out=rr_.unsqueeze(2), in0=den, scalar1=1e-30, scalar2=None, op0=ALU.max),
                  reads=[r_TP[ai]], writes=[r_sm])
            sc.op("dve", lambda e: e.reciprocal(out=rr_, in_=rr_), reads=[r_sm], writes=[r_sm])
        else:
            sc.op("dve", lambda e: e.reciprocal(out=rr_.unsqueeze(2), in_=den), reads=[r_TP[ai]], writes=[r_sm])
        gidx = h * 3 + br
        sc.op("dve", lambda e: e.tensor_tensor(out=sc4, in0=rr_, in1=GN[qc % 2][:, :, gidx], op=ALU.mult),
              reads=[r_sm, r_GN[qc % 2]], writes=[r_sm])
        dst = onsa[:, :, h * 64:(h + 1) * 64]
        if first:
            sc.op("dve", lambda e: e.tensor_tensor(out=dst, in0=TP[:, :, 0:64], in1=sc4.unsqueeze(2).to_broadcast([128, 4, 64]),
                                                   op=ALU.mult), reads=[r_TP[ai], r_sm], writes=[r_onsa])
        else:
            sc.op("dve", lambda e: e.tensor_tensor(out=tnum, in0=TP[:, :, 0:64], in1=sc4.unsqueeze(2).to_broadcast([128, 4, 64]),
                                                   op=ALU.mult), reads=[r_TP[ai], r_sm], writes=[r_tnum])
            sc.op("dve", lambda e: e.tensor_tensor(out=dst, in0=dst, in1=tnum, op=ALU.add),
                  reads=[r_tnum, r_onsa], writes=[r_onsa])
        return TP

    hcount = [0]

    mbfs = [[r3(M.bf(4 * 128), 4) for _ in range(4)] for _ in range(2)]
    r_mbfs = [[Res() for _ in range(4)] for _ in range(2)]
    for g_ in range(2):
        for s_ in range(4):
            sc.op("pool", lambda e, g_=g_, s_=s_: e.memset(mbfs[g_][s_].rearrange("p a b -> p (a b)"), 0.0),
                  writes=[r_mbfs[g_][s_]])

    def do_chunk(qc):
        q0 = qc * 512
        qb = qc % 2
        for h_ in range(8):
            sc.dma(lambda e, h_=h_: e.dma_start(out=QN[qb][0:64, h_, :],
                                                 in_=scr["QN"][h_ // 2][(h_ % 2) * 64:(h_ % 2) * 64 + 64, q0:q0 + 512]),
                   writes=[r_QN[qb][h_ // 4]])
        sc.dma(lambda e: e.dma_start(out=GN[qb], in_=scr["GN"][q0:q0 + 512, :].rearrange("(s p) c -> p s c", p=128)),
               writes=[r_GN[qb]])
        sc.dma(lambda e: e.dma_start(out=SELM[qb].rearrange("p a b c -> p a (b c)"),
                                     in_=P["c_selm"][q0:q0 + 512].rearrange("(s p) a c -> p s (a c)", p=128)),
               writes=[r_SELM[qb]])
        sc.dma(lambda e: e.dma_start(out=Mc[qb], in_=P["c_mc"][qc]), writes=[r_Mc[qb]])
        nct = 2 if qc >= 4 else 1
        items = []

        def sel_dve(g):
            sc.op("pool", lambda e: e.memset(impacc[:, g, :, 0:1], 0.0), reads=[r_imp], writes=[r_imp])
            sc.op("dve", lambda e: e.tensor_tensor(out=impp, in0=impacc[:, g, :, :], in1=SELM[qb][:, :, 0, :], op=ALU.mult),
                  reads=[r_imp, r_SELM[qb]], writes=[r_sel])
            sc.op("dve", lambda e: e.tensor_tensor(out=impp, in0=impp, in1=SELM[qb][:, :, 1, :], op=ALU.add),
                  reads=[r_sel, r_SELM[qb]], writes=[r_sel])
            for sub in range(4):
                iv = impp[:, sub, :]
                mb_ = mbfs[g][sub]
                sc.op("dve", lambda e, iv=iv: e.max(out=m1, in_=iv), reads=[r_sel], writes=[r_sel])
                sc.op("dve", lambda e, iv=iv: e.match_replace(out=wk, in_to_replace=m1, in_values=iv, imm_value=-3.0e38),
                      reads=[r_sel], writes=[r_sel])
                sc.op("dve", lambda e: e.max(out=m2, in_=wk), reads=[r_sel], writes=[r_sel])
                sc.op("dve", lambda e, iv=iv: e.tensor_scalar(
                    out=wk, in0=iv, scalar1=m2[:, 7:8], scalar2=NEG, op0=ALU.is_lt, op1=ALU.mult),
                    reads=[r_sel], writes=[r_sel])
                for hh_ in range(4):
                    sc.op("dve", lambda e, hh_=hh_, mb_=mb_, sub=sub: e.tensor_scalar(
                        out=mb_[:, hh_, 64:128], in0=wk, scalar1=TQ8[:, g * 4 + hh_, sub:sub + 1], scalar2=None, op0=ALU.add),
                        reads=[r_sel, r_c], writes=[r_mbfs[g][sub]])

        def sel_pe(g):
            Tb = M.bank_bf(7)
            for sub in range(4):
                mb_ = mbfs[g][sub]
                for hh_ in range(4):
                    sc.op("pe", lambda e, mb_=mb_, hh_=hh_: e.transpose(out=Tb[:, hh_ * 128:(hh_ + 1) * 128], in_=mb_[:, hh_, :],
                                                                       identity=ident),
                          reads=[r_mbfs[g][sub]], writes=[r_TP7])
                sc.op("act", lambda e, sub=sub: e.copy(
                    out=QN[qb][64:128, g * 4:(g + 1) * 4, sub * 128:(sub + 1) * 128],
                    in_=r3(Tb[64:128, 0:512], 4)),
                    reads=[r_TP7], writes=[r_QN[qb][g]])

        def mk_after(ai, M_rows, h, br, g, hh):
            def fa():
                a = accS[ai]
                sc.op("dve", lambda e: e.tensor_copy(out=a[0:M_rows, :], in_=M.bank(3 + ai)[0:M_rows, :]),
                      reads=[r_acc[ai]], writes=[r_accS[ai]])

            def after():
                first = (br == 0)
                TP = finish_head(3 + ai, ai, M_rows, h, br, qc, first)
                if br == 0:
                    ia = impacc[:, g, :, 1:64]
                    if hh == 0:
                        sc.op("dve", lambda e: e.tensor_tensor(
                            out=ia, in0=TP[:, :, 65:128], in1=rr_.unsqueeze(2).to_broadcast([128, 4, 63]), op=ALU.mult),
                            reads=[r_TP[ai], r_sm], writes=[r_imp])
                    else:
                        sc.op("dve", lambda e: e.tensor_tensor(
                            out=tnum[:, :, 0:63], in0=TP[:, :, 65:128], in1=rr_.unsqueeze(2).to_broadcast([128, 4, 63]),
                            op=ALU.mult), reads=[r_TP[ai], r_sm], writes=[r_tnum])
                        sc.op("dve", lambda e: e.tensor_tensor(out=ia, in0=ia, in1=tnum[:, :, 0:63], op=ALU.add),
                              reads=[r_tnum, r_imp], writes=[r_imp])
                    if hh == 3:
                        sel_dve(g)
            return (ai, fa, after)

        for g in range(2):
            for hh in range(4):
                h = g * 4 + hh
                pr, half = h // 2, h % 2
                lo, hi = half * 64, half * 64 + 64
                ai = hcount[0] % 2
                hcount[0] += 1
                for ct in range(nct):
                    items.append(dict(
                        qk=[(kcT[:, g, ct * 128:(ct + 1) * 128], QN[qb][:, h, :], [r_kcT, r_QN[qb][g]])],
                        nk=128, c0=0, nc=512, T=TQ[:, h, :], T_reads=[r_c],
                        masks=[(0, Mc[qb][:, ct, :], [r_Mc[qb]])],
                        bias=BCC[:, h, qc, ct:ct + 1], bias_reads=[r_c],
                        pv=VC4[:, ct, g, :], pv_reads=[r_VC],
                        acc=M.bank(3 + ai), r_acc=r_acc[ai], M_rows=128, first=(ct == 0), last=(ct == nct - 1),
                        after=(mk_after(ai, 128, h, 0, g, hh) if ct == nct - 1 else None)))

        def branch_jobs(br, g):
            for hh in range(4):
                h = g * 4 + hh
                pr, half = h // 2, h % 2
                lo, hi = half * 64, half * 64 + 64
                ai = hcount[0] % 2
                hcount[0] += 1
                if br == 1:
                    kts = list(range(0, 4 * qc + 4))
                else:
                    kts = list(range(max(0, 4 * qc - 4), 4 * qc + 4))
                for ki_, kt in enumerate(kts):
                    dlt = 4 * qc - kt
                    masks = []
                    if dlt > 0:
                        c0, ncol = 0, 512
                        Tt = TQ
                        bidx = dlt + 3
                        if br == 2:
                            m = 4 - dlt
                            ncol = 128 * (m + 1)
                            masks = [(128 * m, MK[:, 1, :], [r_c])]
                    else:
                        j = -dlt
                        c0, ncol = 128 * j, 512 - 128 * j
                        Tt = TQA
                        bidx = 3
                    Kt = KS if br == 1 else KW
                    rK = r_KS if br == 1 else r_KW
                    qk = [(Kt[:, g, kt * 128:(kt + 1) * 128], QN[qb][:, h, c0:c0 + ncol], [rK, r_QN[qb][g]])]
                    Vt = VS if br == 1 else VW
                    lastj = (ki_ == len(kts) - 1)
                    Tap = Tt[:, h, 0:ncol]
                    direct = False
                    if br == 1:
                        bidx = dlt + 3
                        if dlt > 0:
                            direct = True
                        else:
                            Tap = MASKO[:, 0:ncol]
                    items.append(dict(
                        qk=qk, nk=128, c0=c0, nc=ncol, T=Tap, T_reads=[r_c], masks=masks, direct=direct,
                        bias=BC[:, h, bidx:bidx + 1], bias_reads=[r_c],
                        pv=Vt[:, kt, g, :], pv_reads=[r_VS if br == 1 else r_VW],
                        acc=M.bank(3 + ai), r_acc=r_acc[ai], M_rows=128, first=(ki_ == 0), last=lastj,
                        after=(mk_after(ai, 65, h, br, g, hh) if lastj else None)))

        branch_jobs(2, 0)
        items.append(lambda: sel_pe(0))
        branch_jobs(2, 1)
        items.append(lambda: sel_pe(1))
        branch_jobs(1, 0)
        branch_jobs(1, 1)
        pipe.run_stream(items)
        sc.op("act", lambda e: e.copy(out=onsab, in_=onsa), reads=[r_onsa], writes=[r_onsab])
        sc.dma(lambda e: e.dma_start(out=scr["ON"][q0:q0 + 512, :].rearrange("(s p) c -> p s c", p=128), in_=onsab),
               reads=[r_onsab])

    for qc in range(8):
        do_chunk(qc)
    sc.barrier()


def dil_phase(sc, M, C, scr, P):
    identf = C["identf"]
    M.reset(C["const_end"])
    KDs = [v4(M.bf(4 * S), 2, 2) for _ in range(2)]
    VDs = [v4(M.bf(32 * 512), 32, 4) for _ in range(2)]
    TDs = [r3(M.f32(4 * 1152), 4) for _ in range(2)]
    QD = [r3(M.bf(2 * 512), 2) for _ in range(2)]
    accS = [M.f32(512) for _ in range(2)]
    osb = [v4(M.f32(4 * 260), 4, 4) for _ in range(2)]
    pipe = AttnPipe(sc, M, [0, 1, 2, 6], ntmp=4)
    r_KDs, r_VDs, r_cs = [Res(), Res()], [Res(), Res()], [Res(), Res()]
    r_QD = [Res(), Res()]
    r_accS = [Res(), Res()]
    r_acc = [Res(), Res()]
    r_TP0 = Res()
    r_TP = [r_TP0, r_TP0]
    r_osb = [Res(), Res()]
    sc.op("dve", lambda e: e.memset(KDs[0].rearrange("p a b c -> p (a b c)"), 0.0), writes=[r_KDs[0]])
    sc.op("pool", lambda e: e.memset(VDs[0].rearrange("p a b c -> p (a b c)"), 0.0), writes=[r_VDs[0]])
    sc.op("dve", lambda e: e.memset(KDs[1].rearrange("p a b c -> p (a b c)"), 0.0), writes=[r_KDs[1]])
    sc.op("pool", lambda e: e.memset(VDs[1].rearrange("p a b c -> p (a b c)"), 0.0), writes=[r_VDs[1]])

    def load_group(gi, q):
        gb = gi % 2
        KD_, VD_, TD_ = KDs[gb], VDs[gb], TDs[gb]
        for pp in range(2):
            for hf in range(2):
                sc.dma(lambda e, pp=pp, hf=hf: e.dma_start(out=KD_[hf * 64:(hf + 1) * 64, pp, hf, :],
                                                            in_=scr["KD"][gi * 2 + pp][hf * 64:(hf + 1) * 64, :]),
                       writes=[r_KDs[gb]], q=q)
        vdv = scr["VD"][gi].rearrange("(kt p) (h c) -> p kt h c", p=128, h=4)
        for q4 in range(4):
            for hh_ in range(4):
                sc.dma(lambda e, q4=q4, hh_=hh_: e.dma_start(out=VD_[:, q4 * 8:(q4 + 1) * 8, hh_, 0:65],
                                                              in_=vdv[:, q4 * 8:(q4 + 1) * 8, hh_, :]),
                       writes=[r_VDs[gb]], q=q)
        sc.dma(lambda e: e.dma_start(out=TD_, in_=P["c_tds"][gi * 4:(gi + 1) * 4].rearrange("h k q -> k h q")),
               writes=[r_cs[gb]], q=q)

    def do_group(gi):
        dil = DILS[gi]
        Ls = S // dil
        Cn = min(512, Ls)
        nsub = Cn // 128
        gb = gi % 2
        KD, VD, TD = KDs[gb], VDs[gb], TDs[gb]
        r_KD, r_VD, r_c = r_KDs[gb], r_VDs[gb], r_cs[gb]
        ODv = scr["OD"][gi].rearrange("(i r) c -> i r c", r=dil)
        hcount = [0]
        items = []
        loadqs = []
        marks = []

        def do_chunk(ci, p0):
            r, i0 = divmod(p0, Ls)
            qb = ci % 2
            ob = osb[qb]

            def loadq():
                for pp in range(2):
                    sc.dma(lambda e, pp=pp: e.dma_start(out=QD[qb][:, pp, 0:Cn], in_=scr["QD"][gi * 2 + pp][:, p0:p0 + Cn]),
                           writes=[r_QD[qb]])
            loadqs.append(loadq)
            if ci == 0:
                items.append(loadq)
            mark = len(items)
            items.append(None)
            marks.append(mark)

            def mk_after(ai, hh):
                a = accS[ai]

                def fa():
                    sc.op("dve", lambda e: e.tensor_copy(out=a[0:65, 0:Cn], in_=M.bank(3 + ai)[0:65, 0:Cn]),
                          reads=[r_acc[ai]], writes=[r_accS[ai]])

                def after():
                    TP = r3(M.bank(5), 4)
                    for sub in range(nsub):
                        sc.op("pe", lambda e, sub=sub: e.transpose(
                            out=TP[:, sub, 0:66], in_=a[0:66, sub * 128:(sub + 1) * 128], identity=identf[0:66, 0:66]),
                            reads=[r_accS[ai]], writes=[r_TP[ai]])
                    sc.op("dve", lambda e: e.tensor_copy(out=ob[:, 0:nsub, hh, :], in_=TP[:, 0:nsub, 0:65]),
                          reads=[r_TP[ai]], writes=[r_osb[qb]])
                    if hh == 3:
                        for sub in range(nsub):
                            ii = i0 + sub * 128
                            sc.dma(lambda e, sub=sub, ii=ii: e.dma_start(
                                out=ODv[ii:ii + 128, r, :], in_=ob[:, sub, :, :].rearrange("p a b -> p (a b)")),
                                reads=[r_osb[qb]], q="act")
                return (ai, fa, after)

            for hh in range(4):
                pp, half = hh // 2, hh % 2
                lo, hi = half * 64, half * 64 + 64
                ai = hcount[0] % 2
                hcount[0] += 1
                js = [j for j in range(0, nsub + 1) if not (j == 0 and i0 == 0)]
                groups, cur, tot = [], [], 0
                for j in js:
                    n_ = 128 if (j == 0 or j == nsub) else 256
                    if tot + n_ > 512:
                        groups.append(cur)
                        cur, tot = [], 0
                    cur.append((j, n_))
                    tot += n_
                if cur:
                    groups.append(cur)
                for gn, grp in enumerate(groups):
                    parts = []
                    off = 0
                    j_first = grp[0][0]
                    t0 = 0 if j_first == 0 else 128 + 256 * (j_first - 1)
                    for (j, n_) in grp:
                        k0 = p0 + 128 * (j - 1)
                        kt = k0 // 128
                        c0 = 0 if j == 0 else 128 * (j - 1)
                        parts.append((KD[:, pp, half, k0:k0 + 128], QD[qb][:, pp, c0:c0 + n_], [r_KD, r_QD[qb]],
                                      off, n_, VD[:, kt, hh, :], [r_VD], c0))
                        off += n_
                    lastg = (gn == len(groups) - 1)
                    items.append(dict(
                        parts=parts, nk=128, nc=off, c0=0, T=TD[:, hh, t0:t0 + off], T_reads=[r_c],
                        acc=M.bank(3 + ai), r_acc=r_acc[ai], M_rows=128, first=(gn == 0), last=lastg, W=Cn,
                        after=(mk_after(ai, hh) if lastg else None)))

        for ci, p0 in enumerate(range(0, S, Cn)):
            do_chunk(ci, p0)
        for ci, mark in enumerate(marks):
            items[mark] = loadqs[ci + 1] if ci + 1 < len(loadqs) else (lambda: None)
        pipe.run_stream(items)

    load_group(0, "sp")
    for gi in range(3):
        if gi + 1 < 3:
            load_group(gi + 1, "pool")
        do_group(gi)
    sc.barrier()


def final_phase(sc, M, C, scr, P, x1, x2, gpost):
    M.reset(C["const_end"])
    ident = C["ident"]
    Wno = r3(M.bf(4 * 1024), 4)
    Wdo = r3(M.bf(2 * 1024), 2)
    Wmo = r3(M.bf(8 * 1024), 8)
    stage = [M.f32(1024) for _ in range(2)]
    gq = M.f32(1024)
    onb = [M.bf(512) for _ in range(2)]
    odf = [v4(M.f32(3 * 260), 3, 4) for _ in range(2)]
    mg = [M.bf(2048) for _ in range(2)]
    xr = [M.f32(1024) for _ in range(2)]
    onT = r3(M.bf(4 * 128), 4)
    odT = r3(M.bf(2 * 128), 2)
    odn = r3(M.f32(4 * 65), 4)
    odb = M.bf(256)
    ya = M.f32(1024)
    yb = M.f32(1024)
    ybf = M.bf(1024)
    yT = r3(M.bf(8 * 128), 8)
    yt = M.f32(1024)
    small = M.f32(8)
    r_W = Res()
    r_stage = [Res(), Res()]
    r_gq = Res()
    r_in = [Res(), Res()]
    r_x = [Res(), Res()]
    r_onT, r_odT, r_odn, r_odb, r_ya, r_yb, r_ybf, r_yT, r_yt, r_sm = (Res() for _ in range(10))
    r_T, r_Y1, r_Y2, r_Y3 = Res(), Res(), Res(), Res()
    cnt = [0]

    def load_piece(dst_ap, src_ap, n):
        i = cnt[0] % 2
        cnt[0] += 1
        st = stage[i][:, 0:n]
        sc.dma(lambda e: e.dma_start(out=st, in_=src_ap), writes=[r_stage[i]])
        sc.op("pool", lambda e: e.tensor_copy(out=dst_ap, in_=st), reads=[r_stage[i]], writes=[r_W])

    for k in range(4):
        load_piece(Wno[:, k, :], P["w_nsa_o"][k * 128:(k + 1) * 128, :], 1024)
    for k in range(2):
        load_piece(Wdo[:, k, :], P["w_dil_o"][k * 128:(k + 1) * 128, :], 1024)
    for k in range(8):
        load_piece(Wmo[:, k, :], P["w_mix_out"][k * 128:(k + 1) * 128, :], 1024)
    sc.dma(lambda e: e.dma_start(out=gq, in_=gpost[0, :].partition_broadcast(128)), writes=[r_gq])
    ybf2 = [ybf, M.bf(1024)]
    r_ybf2 = [Res(), Res()]
    smallA = M.f32(8)
    r_smA = Res()

    def stageA(t):
        b = t % 2
        rows = slice(t * 128, (t + 1) * 128)
        sc.dma(lambda e, b=b, rows=rows: e.dma_start(out=onb[b], in_=scr["ON"][rows, :]), writes=[r_in[b]])
        for gi in range(3):
            sc.dma(lambda e, b=b, rows=rows, gi=gi: e.dma_start(out=odf[b][:, gi, :, :].rearrange("p a b -> p (a b)"),
                                                               in_=scr["OD"][gi][rows, :]), writes=[r_in[b]])
        sc.dma(lambda e, b=b, rows=rows: e.dma_start(out=mg[b], in_=scr["MG"][rows, :]), writes=[r_in[b]])
        sc.dma(lambda e, b=b, rows=rows: e.dma_start(out=xr[b], in_=x1[rows, :]), writes=[r_x[b]])
        sc.op("dve", lambda e, b=b: e.tensor_tensor(out=odn, in0=odf[b][:, 0, :, :], in1=odf[b][:, 1, :, :], op=ALU.add),
              reads=[r_in[b]], writes=[r_odn])
        sc.op("dve", lambda e, b=b: e.tensor_tensor(out=odn, in0=odn, in1=odf[b][:, 2, :, :], op=ALU.add),
              reads=[r_in[b], r_odn], writes=[r_odn])
        sc.op("dve", lambda e: e.reciprocal(out=smallA[:, 0:4].unsqueeze(2), in_=odn[:, :, 64:65]), reads=[r_odn], writes=[r_smA])
        sc.op("dve", lambda e: e.tensor_tensor(out=r3(odb, 4), in0=odn[:, :, 0:64],
                                               in1=smallA[:, 0:4].unsqueeze(2).to_broadcast([128, 4, 64]), op=ALU.mult),
              reads=[r_odn, r_smA], writes=[r_odb])
        Tb = M.bank_bf(0)
        for k in range(4):
            sc.op("pe", lambda e, b=b, k=k: e.transpose(out=Tb[:, k * 128:(k + 1) * 128], in_=onb[b][:, k * 128:(k + 1) * 128],
                                                        identity=ident), reads=[r_in[b]], writes=[r_T])
        for k in range(2):
            sc.op("pe", lambda e, k=k: e.transpose(out=Tb[:, (4 + k) * 128:(5 + k) * 128], in_=odb[:, k * 128:(k + 1) * 128],
                                                   identity=ident), reads=[r_odb], writes=[r_T])
        sc.op("act", lambda e: e.copy(out=onT, in_=r3(Tb[:, 0:512], 4)), reads=[r_T], writes=[r_onT])
        sc.op("act", lambda e: e.copy(out=odT, in_=r3(Tb[:, 512:768], 2)), reads=[r_T], writes=[r_odT])
        for dh in range(2):
            Y1 = M.bank(1 + dh)
            for k in range(4):
                sc.op("pe", lambda e, Y1=Y1, k=k, dh=dh: e.matmul(Y1, lhsT=onT[:, k, :], rhs=Wno[:, k, dh * 512:(dh + 1) * 512],
                                                                  start=(k == 0), stop=(k == 3)),
                      reads=[r_onT, r_W], writes=[r_Y1])
            Y2 = M.bank(3 + dh)
            for k in range(2):
                sc.op("pe", lambda e, Y2=Y2, k=k, dh=dh: e.matmul(Y2, lhsT=odT[:, k, :], rhs=Wdo[:, k, dh * 512:(dh + 1) * 512],
                                                                  start=(k == 0), stop=(k == 1)),
                      reads=[r_odT, r_W], writes=[r_Y2])
            sl = slice(dh * 512, (dh + 1) * 512)
            sc.op("dve", lambda e, Y1=Y1, b=b, sl=sl: e.tensor_tensor(out=ya[:, sl], in0=Y1, in1=mg[b][:, sl], op=ALU.mult),
                  reads=[r_Y1, r_in[b]], writes=[r_ya])
            sl2 = slice(1024 + dh * 512, 1024 + (dh + 1) * 512)
            sc.op("dve", lambda e, Y2=Y2, b=b, sl=sl, sl2=sl2: e.tensor_tensor(out=yb[:, sl], in0=Y2, in1=mg[b][:, sl2], op=ALU.mult),
                  reads=[r_Y2, r_in[b]], writes=[r_yb])
        sc.op("pool", lambda e: e.tensor_tensor(out=ybf2[b], in0=ya, in1=yb, op=ALU.add), reads=[r_ya, r_yb], writes=[r_ybf2[b]])

    def stageB(t):
        b = t % 2
        rows = slice(t * 128, (t + 1) * 128)
        Tb2 = M.bank_bf(7)
        for k in range(8):
            sc.op("pe", lambda e, k=k: e.transpose(out=Tb2[:, k * 128:(k + 1) * 128], in_=ybf2[b][:, k * 128:(k + 1) * 128],
                                                   identity=ident), reads=[r_ybf2[b]], writes=[r_Y3])
        sc.op("act", lambda e: e.copy(out=yT, in_=r3(Tb2, 8)), reads=[r_Y3], writes=[r_yT])
        for dh in range(2):
            Y = M.bank(5 + dh)
            for k in range(8):
                sc.op("pe", lambda e, Y=Y, k=k, dh=dh: e.matmul(Y, lhsT=yT[:, k, :], rhs=Wmo[:, k, dh * 512:(dh + 1) * 512],
                                                                start=(k == 0), stop=(k == 7)),
                      reads=[r_yT, r_W], writes=[r_T if False else r_Y1 if False else r_yt_ps])
        ssa, ssb = small[:, 4:5], small[:, 5:6]
        sc.op("pool", lambda e: e.memset(small[:, 4:6], 0.0), writes=[r_sm])
        sc.op("act", lambda e: e.activation(out=yt[:, 0:512], in_=M.bank(5), func=AF.Square, accum_out=ssa),
              reads=[r_yt_ps], writes=[r_yt, r_sm])
        sc.op("act", lambda e: e.activation(out=yt[:, 512:1024], in_=M.bank(6), func=AF.Square, accum_out=ssb),
              reads=[r_yt_ps], writes=[r_yt, r_sm])
        sc.op("dve", lambda e: e.tensor_tensor(out=ssa, in0=ssa, in1=ssb, op=ALU.add), reads=[r_sm], writes=[r_sm])
        sc.op("dve", lambda e: e.tensor_scalar(out=ssa, in0=ssa, scalar1=1.0 / D, scalar2=EPS, op0=ALU.mult, op1=ALU.add),
              reads=[r_sm], writes=[r_sm])
        sc.op("act", lambda e: e.sqrt(ssa, ssa), reads=[r_sm], writes=[r_sm])
        sc.op("dve", lambda e: e.reciprocal(out=ssa, in_=ssa), reads=[r_sm], writes=[r_sm])
        for dh in range(2):
            sc.op("dve", lambda e, dh=dh: e.scalar_tensor_tensor(
                out=yt[:, dh * 512:(dh + 1) * 512], in0=M.bank(5 + dh), scalar=ssa,
                in1=gq[:, dh * 512:(dh + 1) * 512], op0=ALU.mult, op1=ALU.mult),
                reads=[r_yt_ps, r_sm, r_gq], writes=[r_yt])
        sc.op("dve", lambda e, b=b: e.tensor_tensor(out=xr[b], in0=yt, in1=xr[b], op=ALU.add),
              reads=[r_yt, r_x[b]], writes=[r_x[b]])
        sc.dma(lambda e, b=b, rows=rows: e.dma_start(out=x2[rows, :], in_=xr[b]), reads=[r_x[b]], q="act")

    stageA(0)
    for t in range(32):
        if t + 1 < 32:
            stageA(t + 1)
        stageB(t)
    sc.barrier()


r_yt_ps = Res()


IN_SHAPES = (
    ("ffn1_pre", [1, D]), ("ffn1_post", [1, D]), ("ffn1_w_gate", [D, DFF]), ("ffn1_w_up", [D, DFF]),
    ("ffn1_w_down", [DFF, D]), ("mix_pre", [1, D]), ("mix_post", [1, D]), ("w_in", [D, 5656]),
    ("nsa_pe_k", [32, 64]), ("nsa_w_ck1", [2048, 256]), ("nsa_w_ck2", [256, 64]),
    ("nsa_pe_v", [32, 64]), ("nsa_w_cv1", [2048, 256]), ("nsa_w_cv2", [256, 64]),
    ("w_nsa_o", [512, D]), ("w_dil_o", [256, D]), ("w_mix_out", [D, D]),
    ("ffn2_pre", [1, D]), ("ffn2_post", [1, D]), ("ffn2_w_gate", [D, DFF]),
    ("ffn2_w_up", [D, DFF]), ("ffn2_w_down", [DFF, D]))


def _consts():
    import ml_dtypes
    bf = ml_dtypes.bfloat16
    c = {}
    c["c_ident"] = np.eye(128, dtype=np.float32).astype(bf)
    c["c_identf"] = np.eye(128, dtype=np.float32)
    n_cmp = 255
    cs = np.arange(256) * 16
    ss = np.arange(64) * 64
    ov = np.clip(np.minimum(cs[:, None] + 32, ss[None, :] + 64) - np.maximum(cs[:, None], ss[None, :]), 0, None) / 32.0
    ov[255] = 0
    c["c_ov"] = ov.astype(np.float32).astype(bf)
    qi = np.arange(512, dtype=np.float64)
    c["c_tq"] = np.stack([-s_ * qi for s_ in NSA_SL]).astype(np.float32)
    c["c_td"] = np.stack([-DIL_SL[h] * DILS[h // 4] * qi for h in range(12)]).astype(np.float32)
    ki = np.arange(128)[:, None]
    qq = np.arange(128)[None, :]
    mk = np.zeros((3, 128, 128), np.float32)
    mk[0] = np.where(qq >= ki, 0.0, NEG)
    mk[1] = np.where(qq < ki, 0.0, NEG)
    mk[2] = np.where(qq <= ki, 0.0, NEG)
    c["c_masks"] = mk
    tqa = np.zeros((8, 128, 512), np.float64)
    for h in range(8):
        tqa[h] = -NSA_SL[h] * qi[None, :]
        tqa[h][:, 0:128] += mk[0]
    c["c_tqa"] = tqa.astype(np.float32)
    mo = np.zeros((128, 512), np.float32)
    mo[:, 0:128] = mk[0]
    c["c_masko"] = mo
    tq8 = np.zeros((128, 8, 4), np.float64)
    for h in range(8):
        for sb in range(4):
            tq8[:, h, sb] = -8.0 * NSA_SL[h] * (128 * sb + np.arange(128))
    c["c_tq8"] = tq8.astype(np.float32)
    tdm = np.zeros((12, 128, 384), np.float64)
    for h in range(12):
        sl = DIL_SL[h] * DILS[h // 4]
        tdm[h][:, 0:256] = -sl * qi[None, 0:256]
        tdm[h][:, 0:128] += mk[0]
        tdm[h][:, 128:256] += mk[2]
        tdm[h][:, 256:384] = -sl * qi[None, 0:128] + mk[2]
    c["c_tdm"] = tdm.astype(np.float32)
    tds = np.zeros((12, 128, 1152), np.float64)
    kcol = np.arange(128, dtype=np.float64)[:, None]
    for h in range(12):
        sl = DIL_SL[h] * DILS[h // 4]
        t2a = -sl * (qi[None, 0:256] - kcol)
        t2a[:, 0:128] += mk[0]
        t2a[:, 128:256] += mk[2]
        t2b = -sl * (128.0 + qi[None, 0:128] - kcol) + mk[2]
        tds[h][:, 0:128] = t2b
        for r_ in range(4):
            tds[h][:, 128 + 256 * r_:128 + 256 * (r_ + 1)] = t2a
    c["c_tds"] = tds.astype(np.float32)
    kk = np.arange(128, dtype=np.float64)
    bc = np.zeros((128, 8, 35), np.float64)
    for h in range(8):
        for idx in range(35):
            bc[:, h, idx] = NSA_SL[h] * (kk - 128 * (idx - 3))
    c["c_bc"] = bc.astype(np.float32)
    bcc = np.zeros((128, 8, 8, 2), np.float64)
    for h in range(8):
        for qc in range(8):
            for ct in range(2):
                bcc[:, h, qc, ct] = NSA_SL[h] * (16 * (128 * ct + kk) + 31 - 512 * qc)
    c["c_bcc"] = bcc.astype(np.float32)
    bcd = np.zeros((128, 12, 5), np.float64)
    for h in range(12):
        for j in range(5):
            bcd[:, h, j] = DIL_SL[h] * DILS[h // 4] * (kk - 128 * (1 - j))
    c["c_bcd"] = bcd.astype(np.float32)
    mc = np.zeros((8, 128, 2, 512), np.float32)
    for qc in range(8):
        for ct in range(2):
            cc = 128 * ct + np.arange(128)[:, None]
            qpos = 512 * qc + np.arange(512)[None, :]
            ok = (qpos >= 16 * cc + 31) & (cc < 255)
            mc[qc, :, ct, :] = np.where(ok, 0.0, NEG)
    c["c_mc"] = mc
    oh = np.zeros((64, 32, 128), np.float32)
    for kt in range(32):
        oh[2 * kt, kt, 0:64] = 1
        oh[2 * kt + 1, kt, 64:128] = 1
    c["c_oh"] = oh.astype(bf)
    pos = np.arange(S)[:, None]
    jb = np.arange(64)[None, :]
    own = pos // 64
    forced = (jb == 0) | (jb == own) | (jb == own - 1)
    valid = jb * 64 <= pos
    m1 = np.where(forced, 0.0, np.where(valid, 1.0, 0.0))
    m2 = np.where(forced, 1.0e9 + 1.0e4 * jb, np.where(valid, 0.0, -1.0e9 - 1.0e4 * jb))
    c["c_selm"] = np.stack([m1, m2], axis=1).astype(np.float32)
    return c


CONST_SHAPES = (("c_ident", [128, 128], BF16), ("c_identf", [128, 128], F32), ("c_ov", [256, 64], BF16),
                ("c_tq", [8, 512], F32), ("c_td", [12, 512], F32), ("c_masks", [3, 128, 128], F32),
                ("c_bc", [128, 8, 35], F32), ("c_bcc", [128, 8, 8, 2], F32), ("c_bcd", [128, 12, 5], F32),
                ("c_mc", [8, 128, 2, 512], F32), ("c_oh", [64, 32, 128], BF16), ("c_selm", [S, 2, 64], F32),
                ("c_tqa", [8, 128, 512], F32), ("c_tdm", [12, 128, 384], F32),
                ("c_masko", [128, 512], F32), ("c_tq8", [128, 8, 4], F32),
                ("c_tds", [12, 128, 1152], F32))


def build_nc():
    nc = bass.Bass("TRN2", target_bir_lowering=False)
    dt = lambda name, shape, dtype=F32, kind="ExternalInput": nc.dram_tensor(name, list(shape), dtype, kind=kind).ap()
    x = dt("x", [S, D])
    out = dt("out", [S, D], kind="ExternalOutput")
    P = {}
    for nm, shp in IN_SHAPES:
        P[nm] = dt(nm, shp)
    for nm, shp, ty in CONST_SHAPES:
        P[nm] = dt(nm, shp, ty)
    I = "Internal"
    if DBG_OUT:
        I = "ExternalOutput"
    x1 = dt("x1", [S, D], kind=I)
    x2 = dt("x2", [S, D], kind=I)
    scr = {
        "QN": dt("s_qn", [4, 128, S], BF16, I), "KC": dt("s_kc", [2, 128, S], F32, I),
        "KS": dt("s_ks", [2, 128, S], BF16, I), "KW": dt("s_kw", [2, 128, S], BF16, I),
        "VS": dt("s_vs", [S, 130], BF16, I), "VW": dt("s_vw", [S, 130], BF16, I),
        "GN": dt("s_gn", [S, 24], F32, I), "QD": dt("s_qd", [6, 128, S], BF16, I),
        "KD": dt("s_kd", [6, 128, S], BF16, I), "VD": dt("s_vd", [3, S, 260], BF16, I),
        "MG": dt("s_mg", [S, 2048], BF16, I), "ON": dt("s_on", [S, 512], BF16, I),
        "OD": dt("s_od", [3, S, 260], F32, I),
    }

    import contextlib
    with contextlib.ExitStack() as st:
        big = st.enter_context(nc.sbuf_tensor("big", [128, SBUF_BYTES // 4], F32))
        ps = st.enter_context(nc.psum_tensor("ps", [128, 4096], F32))
        M = Mem(big, ps)
        sc = Sched()
        C = {"r_dram": {}}
        ident = M.bf(128)
        identf = M.f32(128)
        C["kcT"] = r3(M.bf(2 * 256), 2)
        C["VC"] = M.bf(2 * 2 * 128)
        C["r_kcT"], C["r_VC"] = Res(), Res()
        r_ident = Res()
        sc.dma(lambda e: e.dma_start(out=ident, in_=P["c_ident"]), writes=[r_ident])
        sc.dma(lambda e: e.dma_start(out=identf, in_=P["c_identf"]), writes=[r_ident])
        C["ident"] = ident
        C["identf"] = identf
        zeros = M.bf(512)
        r_z = Res()
        sc.op("pool", lambda e: e.memset(zeros, 0.0), writes=[r_z])
        C["zeros"] = zeros
        ZEROS[0] = zeros
        C["const_end"] = M.off
        sc.barrier()
        if STAGE == 1:
            ffn_phase(sc, M, C, x, out, P["ffn1_w_gate"], P["ffn1_w_up"], P["ffn1_w_down"], P["ffn1_pre"], P["ffn1_post"])
        else:
            if DBG_SKIP_FFN1:
                x1 = x
            else:
                ffn_phase(sc, M, C, x, x1, P["ffn1_w_gate"], P["ffn1_w_up"], P["ffn1_w_down"], P["ffn1_pre"], P["ffn1_post"])
            if DBG_UPTO >= 1:
                proj_phase(sc, M, C, x1, P["w_in"], P["mix_pre"], scr)
            if DBG_UPTO >= 2:
                cmp_phase(sc, M, C, scr, P)
            if DBG_UPTO >= 3:
                nsa_phase(sc, M, C, scr, P)
            if DBG_UPTO >= 4:
                dil_phase(sc, M, C, scr, P)
            if DBG_UPTO >= 5:
                final_phase(sc, M, C, scr, P, x1, out if STAGE == 2 else x2, P["mix_post"])
            if STAGE >= 3:
                ffn_phase(sc, M, C, x2, out, P["ffn2_w_gate"], P["ffn2_w_up"], P["ffn2_w_down"], P["ffn2_pre"], P["ffn2_post"])
        sc.barrier()
        sc.emit(nc)
    return nc


DBG_SKIP_FFN1 = False
DBG_UPTO = 9
DBG_OUT = False
DBG_RES = None
_NC = None


def kernel(**inputs):
    global _NC
    if _NC is None:
        _NC = build_nc()
    nc = _NC
    x = np.ascontiguousarray(inputs["x"], dtype=np.float32)
    consts = _consts()
    shared = {}
    for nm, shp in IN_SHAPES:
        shared[nm] = np.ascontiguousarray(np.asarray(inputs[nm], dtype=np.float32)[0])
    in_maps = []
    for b in range(8):
        m = {"x": x[b]}
        m.update(shared)
        m.update(consts)
        in_maps.append(m)
    res = run_bass_kernel_spmd(nc, in_maps, core_ids=list(range(8)))
    if DBG_OUT:
        global DBG_RES
        DBG_RES = res.results[0]
    return np.stack([np.asarray(r["out"]) for r in res.results], axis=0).astype(np.float32)
```

```python
import numpy as np
import concourse.bass as bass
import concourse.mybir as mybir
from concourse.alu_op_type import AluOpType as ALU
from concourse.bass_utils import run_bass_kernel_spmd

F32 = mybir.dt.float32
BF16 = mybir.dt.bfloat16
AF = mybir.ActivationFunctionType

S = 4096
D = 1024
DFF = 2816
NFF = 22
EPS = 1e-6
SBUF_BYTES = 212000
LIMIT = 9
STAGE = 3


class Res:
    __slots__ = ("name", "lw", "rdc", "rdd")

    def __init__(self, name=""):
        self.name = name
        self.lw = None
        self.rdc = {}
        self.rdd = []


ENGS = ("pe", "act", "dve", "pool", "sp")
SAME_ENG_SKIP = ("pe", "sp")


class Sched:
    def __init__(self, nring=28):
        self.ops = {e: [] for e in ENGS}
        self.ndma = 0
        self.nring = nring
        self.last_dma = {}

    def _deps(self, reads, writes):
        dc = {}
        dd = set()

        def add(ev):
            if ev is None:
                return
            if ev[0] == "c":
                if dc.get(ev[1], -1) < ev[2]:
                    dc[ev[1]] = ev[2]
            else:
                dd.add(ev)

        for r in reads:
            add(r.lw)
        for w in writes:
            add(w.lw)
            for e, s in w.rdc.items():
                add(("c", e, s))
            for ev in w.rdd:
                add(ev)
        return dc, dd

    def _mark(self, ev, reads, writes):
        for r in reads:
            if ev[0] == "c":
                if r.rdc.get(ev[1], -1) < ev[2]:
                    r.rdc[ev[1]] = ev[2]
            else:
                r.rdd.append(ev)
        for w in writes:
            w.lw = ev
            w.rdc = {}
            w.rdd = []

    def op(self, eng, fn, reads=(), writes=()):
        dc, dd = self._deps(reads, writes)
        seq = len(self.ops[eng])
        ev = ("c", eng, seq)
        self.ops[eng].append(dict(fn=fn, dc=dc, dd=dd, dma=None))
        self._mark(ev, reads, writes)
        return ev

    def dma(self, fn, reads=(), writes=(), q="sp"):
        dc, dd = self._deps(reads, writes)
        k = self.ndma
        self.ndma += 1
        slot = k % self.nring
        val = 16 * (k // self.nring + 1)
        if k >= self.nring:
            dd.add(("d", slot, val - 16))
        ev = ("d", slot, val)
        self.ops[q].append(dict(fn=fn, dc=dc, dd=dd, dma=slot))
        self._mark(ev, reads, writes)
        self.last_dma[slot] = val
        return ev

    def barrier(self):
        last = {}
        for e in ENGS:
            last[e] = -1
            for i in range(len(self.ops[e]) - 1, -1, -1):
                if self.ops[e][i]["fn"] is not None and self.ops[e][i]["dma"] is None:
                    last[e] = i
                    break
        dds = set(("d", s, v) for s, v in self.last_dma.items())
        for e in ENGS:
            dc = {o: last[o] for o in ENGS if o != e and o != 'sp' and last[o] >= 0}
            self.ops[e].append(dict(fn=None, dc=dc, dd=set(dds), dma=None))

    def emit(self, nc):
        needed = {e: set() for e in ENGS}
        for e in ENGS:
            for o in self.ops[e]:
                for oe, s in o["dc"].items():
                    if oe == e and e in SAME_ENG_SKIP:
                        continue
                    needed[oe].add(s)
        rank = {e: {s: i + 1 for i, s in enumerate(sorted(needed[e]))} for e in ENGS}
        ops = self.ops
        nring = self.nring
        import contextlib

        with contextlib.ExitStack() as st:
            sems = {e: st.enter_context(nc.semaphore("s_" + e)) for e in ENGS}
            ring = [st.enter_context(nc.semaphore("r%d" % i)) for i in range(nring)]
            block = st.enter_context(nc.Block())

            def run(ename):
                def body(eng):
                    known = {}
                    for seq, o in enumerate(ops[ename]):
                        waits = {}
                        for oe, s in o["dc"].items():
                            if oe == ename and ename in SAME_ENG_SKIP:
                                continue
                            key = ("c", oe)
                            v = rank[oe][s]
                            if known.get(key, 0) >= v:
                                continue
                            if waits.get(key, 0) < v:
                                waits[key] = v
                        for ev in o["dd"]:
                            key = ("d", ev[1])
                            v = ev[2]
                            if known.get(key, 0) >= v:
                                continue
                            if waits.get(key, 0) < v:
                                waits[key] = v
                        for key, v in waits.items():
                            sem = sems[key[1]] if key[0] == "c" else ring[key[1]]
                            eng.wait_ge(sem, v)
                            known[key] = v
                        if o["fn"] is None:
                            continue
                        ins = o["fn"](eng)
                        if o["dma"] is not None:
                            ins.then_inc(ring[o["dma"]], 16)
                        elif seq in rank[ename]:
                            ins.then_inc(sems[ename], 1)

                return body

            block.tensor(run("pe"))
            block.scalar(run("act"))
            block.vector(run("dve"))
            block.gpsimd(run("pool"))
            block.sync(run("sp"))


class Mem:
    def __init__(self, big, ps):
        self.big = big
        self.ps = ps
        self.off = 0

    def reset(self, off=0):
        self.off = off

    def f32(self, n, p0=0, p1=128):
        o = (self.off + 31) // 32 * 32
        self.off = o + 4 * n
        assert self.off <= SBUF_BYTES, self.off
        return self.big[p0:p1, o // 4:o // 4 + n]

    def bf(self, n, p0=0, p1=128):
        o = (self.off + 31) // 32 * 32
        self.off = o + 2 * n
        assert self.off <= SBUF_BYTES, self.off
        return self.big[p0:p1, o // 4:o // 4 + (n + 1) // 2].bitcast(BF16)

    def bank(self, b, n=512, o=0):
        return self.ps[:, b * 512 + o:b * 512 + o + n]

    def bank_bf(self, b):
        return self.ps[:, b * 512:(b + 1) * 512].bitcast(BF16)


def r3(ap, a):
    return ap.rearrange("p (a b) -> p a b", a=a)


def ffn_phase(sc, M, C, x_src, x_dst, wg, wu, wd, gpre, gpost):
    M.reset(C["const_end"])
    ident = C["ident"]
    Wg = r3(M.bf(8 * DFF), 8)
    Wu = r3(M.bf(8 * DFF), 8)
    Wd = r3(M.bf(NFF * D), NFF)
    stage = [M.f32(1024) for _ in range(3)]
    gp = M.f32(1024)
    gq = M.f32(1024)
    xp0 = M.f32(1024)
    xp = [xp0, xp0]
    xr = [M.f32(1024) for _ in range(2)]
    hb0 = M.bf(1024)
    hb = [hb0, hb0]
    hT = r3(M.bf(8 * 512), 8)
    AT = r3(M.bf(NFF * 512), NFF)
    sg0 = M.f32(512)
    sg = [sg0, sg0]
    yt = M.f32(1024)
    small = M.f32(32)

    r_stage = [Res("stage%d" % i) for i in range(3)]
    r_Wg = [[Res() for _ in range(4)] for _ in range(8)]
    r_Wu = [[Res() for _ in range(4)] for _ in range(8)]
    r_Wd = [Res() for _ in range(NFF)]
    r_gp, r_gq = Res(), Res()
    r_xp0 = Res()
    r_xp = [r_xp0, r_xp0]
    r_xr = [Res(), Res()]
    r_hb0 = Res()
    r_hb = [r_hb0, r_hb0]
    r_hT, r_AT = Res(), [Res() for _ in range(NFF)]
    r_sg0 = Res()
    r_sg = [r_sg0, r_sg0]
    r_yt = Res()
    r_small = [Res() for _ in range(8)]
    r_T = Res()
    r_G = [Res(), Res()]
    r_U = [Res(), Res()]
    r_Y = Res()
    r_xdst = C["r_dram"][id(x_dst)] if id(x_dst) in C["r_dram"] else Res()
    r_xsrc = C["r_dram"].get(id(x_src), Res())
    C["r_dram"][id(x_dst)] = r_xdst

    sc.dma(lambda e: e.dma_start(out=gp, in_=gpre[0, :].partition_broadcast(128)), writes=[r_gp])
    sc.dma(lambda e: e.dma_start(out=gq, in_=gpost[0, :].partition_broadcast(128)), writes=[r_gq])
    sc.op("act", lambda e: e.mul(gq, gq, 0.5), reads=[r_gq], writes=[r_gq])

    cnt = [0]
    CB = [(0, 768), (768, 768), (1536, 768), (2304, 512)]

    def load_piece(dst_ap, src_ap, n, rdst):
        i = cnt[0] % 3
        eng = ("pool", "act", "dve")[cnt[0] % 3]
        cnt[0] += 1
        st = stage[i][:, 0:n]
        sc.dma(lambda e: e.dma_start(out=st, in_=src_ap), writes=[r_stage[i]])
        if eng == "act":
            sc.op("act", lambda e: e.copy(out=dst_ap, in_=st), reads=[r_stage[i]], writes=[rdst])
        else:
            sc.op(eng, lambda e: e.tensor_copy(out=dst_ap, in_=st), reads=[r_stage[i]], writes=[rdst])

    def load_weights():
        for cb, (c0, w) in enumerate(CB):
            for kc in range(8):
                for (W, wsrc, rW) in ((Wg, wg, r_Wg), (Wu, wu, r_Wu)):
                    load_piece(W[:, kc, c0:c0 + w], wsrc[kc * 128:(kc + 1) * 128, c0:c0 + w], w, rW[kc][cb])
        for f in range(NFF):
            load_piece(Wd[:, f, :], wd[f * 128:(f + 1) * 128, :], 1024, r_Wd[f])

    inv_d = 1.0 / D

    def rstd_from(ss_ap, out_ap, rs):
        sc.op("dve", lambda e: e.tensor_scalar(out=out_ap, in0=ss_ap, scalar1=inv_d, scalar2=EPS,
                                               op0=ALU.mult, op1=ALU.add), reads=[rs], writes=[rs])
        sc.op("act", lambda e: e.sqrt(out_ap, out_ap), reads=[rs], writes=[rs])
        sc.op("dve", lambda e: e.reciprocal(out=out_ap, in_=out_ap), reads=[rs], writes=[rs])

    NT = S // 512
    Tb = M.bank_bf(0)

    def prep(i):
        for j in range(4):
            t = i * 4 + j
            b = t % 2
            x_ap = xp[b]
            sc.dma(lambda e, x_ap=x_ap, t=t: e.dma_start(out=x_ap, in_=x_src[t * 128:(t + 1) * 128, :]),
                   reads=[r_xsrc], writes=[r_xp[b]])
            ss = small[:, b:b + 1]
            sc.op("pool", lambda e, ss=ss: e.memset(ss, 0.0), writes=[r_small[b]])
            sc.op("act", lambda e, x_ap=x_ap, b=b, ss=ss: e.activation(out=hb[b], in_=x_ap, func=AF.Square, accum_out=ss),
                  reads=[r_xp[b]], writes=[r_hb[b], r_small[b]])
            rstd_from(ss, ss, r_small[b])
            sc.op("dve", lambda e, x_ap=x_ap, b=b, ss=ss: e.scalar_tensor_tensor(
                out=hb[b], in0=x_ap, scalar=ss, in1=gp, op0=ALU.mult, op1=ALU.mult),
                reads=[r_xp[b], r_small[b], r_gp], writes=[r_hb[b]])
            for kc in range(8):
                sc.op("pe", lambda e, b=b, kc=kc: e.transpose(out=Tb[:, kc * 128:(kc + 1) * 128],
                                                             in_=hb[b][:, kc * 128:(kc + 1) * 128], identity=ident),
                      reads=[r_hb[b]], writes=[r_T])
            sc.op("act", lambda e, j=j: e.copy(out=hT[:, :, j * 128:(j + 1) * 128], in_=r3(Tb, 8)),
                  reads=[r_T], writes=[r_hT])

    def gateup(i):
        for f in range(NFF):
            pb = f % 2
            G = M.bank(1 + pb)
            U = M.bank(3 + pb)
            for kc in range(8):
                sc.op("pe", lambda e, G=G, kc=kc, f=f: e.matmul(G, lhsT=Wg[:, kc, f * 128:(f + 1) * 128], rhs=hT[:, kc, :],
                                                               start=(kc == 0), stop=(kc == 7)),
                      reads=[r_Wg[kc][min(3, (f * 128) // 768)], r_hT], writes=[r_G[pb]])
            for kc in range(8):
                sc.op("pe", lambda e, U=U, kc=kc, f=f: e.matmul(U, lhsT=Wu[:, kc, f * 128:(f + 1) * 128], rhs=hT[:, kc, :],
                                                               start=(kc == 0), stop=(kc == 7)),
                      reads=[r_Wu[kc][min(3, (f * 128) // 768)], r_hT], writes=[r_U[pb]])
            sc.op("act", lambda e, G=G, pb=pb: e.activation(out=sg[pb], in_=G, func=AF.Silu),
                  reads=[r_G[pb]], writes=[r_sg[pb]])
            sc.op("dve", lambda e, U=U, pb=pb, f=f: e.tensor_tensor(out=AT[:, f, :], in0=sg[pb], in1=U, op=ALU.mult),
                  reads=[r_sg[pb], r_U[pb]], writes=[r_AT[f]])

    YB = [(5, 6), (7, 0)]
    r_Yp = [[Res()], [Res(), r_T]]

    def down(i):
        for j in range(4):
            t = i * 4 + j
            b = t % 2
            yp = j % 2
            rY = r_Yp[yp]
            sc.dma(lambda e, b=b, t=t: e.dma_start(out=xr[b], in_=x_src[t * 128:(t + 1) * 128, :]),
                   reads=[r_xsrc], writes=[r_xr[b]])
            for dh in range(2):
                Y = M.bank(YB[yp][dh])
                for f in range(NFF):
                    sc.op("pe", lambda e, Y=Y, f=f, j=j, dh=dh: e.matmul(
                        Y, lhsT=AT[:, f, j * 128:(j + 1) * 128], rhs=Wd[:, f, dh * 512:(dh + 1) * 512],
                        start=(f == 0), stop=(f == NFF - 1)),
                        reads=[r_AT[f], r_Wd[f]], writes=rY)
            ssa = small[:, 4:5]
            ssb = small[:, 5:6]
            Y0, Y1 = M.bank(YB[yp][0]), M.bank(YB[yp][1])
            sc.op("pool", lambda e: e.memset(small[:, 4:6], 0.0), writes=[r_small[4]])
            sc.op("act", lambda e, ssa=ssa, Y0=Y0: e.activation(out=yt[:, 0:512], in_=Y0, func=AF.Square, accum_out=ssa),
                  reads=rY, writes=[r_yt, r_small[4]])
            sc.op("act", lambda e, ssb=ssb, Y1=Y1: e.activation(out=yt[:, 512:1024], in_=Y1, func=AF.Square, accum_out=ssb),
                  reads=rY, writes=[r_yt, r_small[4]])
            sc.op("dve", lambda e, ssa=ssa, ssb=ssb: e.tensor_tensor(out=ssa, in0=ssa, in1=ssb, op=ALU.add),
                  reads=[r_small[4]], writes=[r_small[4]])
            rstd_from(ssa, ssa, r_small[4])
            for dh in range(2):
                Yd = M.bank(YB[yp][dh])
                sc.op("dve", lambda e, dh=dh, ssa=ssa, Yd=Yd: e.scalar_tensor_tensor(
                    out=yt[:, dh * 512:(dh + 1) * 512], in0=Yd, scalar=ssa,
                    in1=gq[:, dh * 512:(dh + 1) * 512], op0=ALU.mult, op1=ALU.mult),
                    reads=rY + [r_small[4], r_gq], writes=[r_yt])
            sc.op("dve", lambda e, b=b: e.tensor_tensor(out=xr[b], in0=yt, in1=xr[b], op=ALU.add),
                  reads=[r_yt, r_xr[b]], writes=[r_xr[b]])
            sc.dma(lambda e, b=b, t=t: e.dma_start(out=x_dst[t * 128:(t + 1) * 128, :], in_=xr[b]),
                   reads=[r_xr[b]], writes=[r_xdst], q="act")

    if LIMIT >= 1:
        prep(0)
    load_weights()
    for i in range(NT):
        if LIMIT == 2 and i == 0:
            gateup(i)
        if LIMIT == 3 and i == 0:
            gateup(i)
            down(i)
        if LIMIT < 9:
            continue
        gateup(i)
        if i + 1 < NT:
            prep(i + 1)
        down(i)
    sc.barrier()


NEG = -30000.0
ZEROS = [None]
NSA_SL = [2.0 ** (-(i + 1)) for i in range(8)]
DIL_SL = [2.0 ** (-8.0 * (i + 1) / 12) for i in range(12)]
DILS = (1, 4, 16)
C_QN, C_KV, C_GN, C_QD, C_KD, C_VD, C_MG = 0, 512, 1280, 1304, 2072, 2840, 3608


def v4(ap, a, b):
    return ap.rearrange("p (a b c) -> p a b c", a=a, b=b)


def norm_to_hT(sc, M, x_src, t, xp, hb, small, gp, ident, rr, dstT, r_dst, Tb, r_T):
    b = t % 2
    r_xp, r_hb, r_small, r_gp = rr
    sc.dma(lambda e: e.dma_start(out=xp[b], in_=x_src[t * 128:(t + 1) * 128, :]), writes=[r_xp[b]])
    ss = small[:, b:b + 1]
    sc.op("pool", lambda e: e.memset(ss, 0.0), writes=[r_small[b]])
    sc.op("act", lambda e: e.activation(out=hb[b], in_=xp[b], func=AF.Square, accum_out=ss),
          reads=[r_xp[b]], writes=[r_hb[b], r_small[b]])
    sc.op("dve", lambda e: e.tensor_scalar(out=ss, in0=ss, scalar1=1.0 / D, scalar2=EPS, op0=ALU.mult, op1=ALU.add),
          reads=[r_small[b]], writes=[r_small[b]])
    sc.op("act", lambda e: e.sqrt(ss, ss), reads=[r_small[b]], writes=[r_small[b]])
    sc.op("dve", lambda e: e.reciprocal(out=ss, in_=ss), reads=[r_small[b]], writes=[r_small[b]])
    sc.op("dve", lambda e: e.scalar_tensor_tensor(out=hb[b], in0=xp[b], scalar=ss, in1=gp, op0=ALU.mult, op1=ALU.mult),
          reads=[r_xp[b], r_small[b], r_gp], writes=[r_hb[b]])
    for kc in range(8):
        sc.op("pe", lambda e, kc=kc: e.transpose(out=Tb[:, kc * 128:(kc + 1) * 128],
                                                 in_=hb[b][:, kc * 128:(kc + 1) * 128], identity=ident),
              reads=[r_hb[b]], writes=[r_T])
    sc.op("act", lambda e: e.copy(out=dstT, in_=r3(Tb, 8)), reads=[r_T], writes=[r_dst])


def proj_phase(sc, M, C, x1, w_in, gmix, scr):
    M.reset(C["const_end"])
    ident = C["ident"]
    h2T = r3(M.bf(8 * S), 8)
    Wb = [r3(M.bf(8 * 512), 8) for _ in range(2)]
    stage = [M.f32(512) for _ in range(2)]
    gp = M.f32(1024)
    xp = [M.f32(1024) for _ in range(2)]
    hb = [M.bf(1024) for _ in range(2)]
    small = M.f32(8)
    evf = [M.f32(512) for _ in range(2)]
    evb = [M.bf(512) for _ in range(2)]
    vaug = [M.bf(4 * 65) for _ in range(2)]
    r_h2T = [Res() for _ in range(32)]
    r_Wb = [Res(), Res()]
    r_stage = [Res(), Res()]
    r_gp = Res()
    rr = ([Res(), Res()], [Res(), Res()], [Res(), Res()], r_gp)
    r_ev = [Res(), Res()]
    r_va = [Res(), Res()]
    r_T = Res()
    r_ps = [Res(), Res()]
    Tb = M.bank_bf(0)
    sc.dma(lambda e: e.dma_start(out=gp, in_=gmix[0, :].partition_broadcast(128)), writes=[r_gp])
    for i in range(2):
        sc.op("pool", lambda e, i=i: e.memset(vaug[i], 1.0), writes=[r_va[i]])
    for t in range(32):
        norm_to_hT(sc, M, x1, t, xp, hb, small, gp, ident, rr, h2T[:, :, t * 128:(t + 1) * 128], r_h2T[t], Tb, r_T)

    cnt = [0]
    nspec = [0]

    def load_wb(bi, cols):
        for kc in range(8):
            off = 0
            for (c, w) in cols:
                i = cnt[0] % 2
                cnt[0] += 1
                st = stage[i][:, 0:w]
                sc.dma(lambda e, st=st, c=c, w=w, kc=kc: e.dma_start(out=st, in_=w_in[kc * 128:(kc + 1) * 128, c:c + w]),
                       writes=[r_stage[i]])
                dst = Wb[bi][:, kc, off:off + w]
                sc.op("pool", lambda e, st=st, dst=dst: e.tensor_copy(out=dst, in_=st), reads=[r_stage[i]], writes=[r_Wb[bi]])
                off += w

    ecnt = [0]

    def run_fm(cols, dst, fp32, dil):
        bi = nspec[0] % 2
        nspec[0] += 1
        load_wb(bi, cols)
        Ls = S // dil
        Cn = min(512, Ls)
        h4 = h2T.rearrange("p k (i r) -> p k i r", r=dil)
        for p0 in range(0, S, Cn):
            r, i0 = divmod(p0, Ls)
            k = ecnt[0] % 2
            ecnt[0] += 1
            PS = M.bank(1 + k)[:, 0:Cn]
            for kc in range(8):
                sc.op("pe", lambda e, PS=PS, kc=kc, i0=i0, r=r: e.matmul(
                    PS, lhsT=Wb[bi][:, kc, 0:128], rhs=h4[:, kc, i0:i0 + Cn, r], start=(kc == 0), stop=(kc == 7)),
                    reads=[r_Wb[bi]] + r_h2T, writes=[r_ps[k]])
            evt = (evf[k] if fp32 else evb[k])[:, 0:Cn]
            sc.op("act", lambda e, PS=PS, evt=evt: e.copy(out=evt, in_=PS), reads=[r_ps[k]], writes=[r_ev[k]])
            sc.dma(lambda e, evt=evt, p0=p0: e.dma_start(out=dst[:, p0:p0 + Cn], in_=evt), reads=[r_ev[k]], q="act")

    def run_tm(c0, n, dst, kind, dil, nh=0):
        bi = nspec[0] % 2
        nspec[0] += 1
        load_wb(bi, [(c0, n)])
        Ls = S // dil
        h4 = h2T.rearrange("p k (i r) -> p k i r", r=dil)
        for t in range(32):
            p0 = t * 128
            r, i0 = divmod(p0, Ls)
            k = ecnt[0] % 2
            ecnt[0] += 1
            PS = M.bank(1 + k)[:, 0:n]
            for kc in range(8):
                sc.op("pe", lambda e, PS=PS, kc=kc, i0=i0, r=r: e.matmul(
                    PS, lhsT=h4[:, kc, i0:i0 + 128, r], rhs=Wb[bi][:, kc, 0:n], start=(kc == 0), stop=(kc == 7)),
                    reads=[r_Wb[bi]] + r_h2T, writes=[r_ps[k]])
            if kind == "aug":
                va = vaug[k][:, 0:nh * 65]
                sc.op("act", lambda e, PS=PS, va=va: e.copy(out=r3(va, nh)[:, :, 0:64], in_=r3(PS, nh)),
                      reads=[r_ps[k]], writes=[r_va[k]])
                sc.dma(lambda e, va=va, p0=p0: e.dma_start(out=dst[p0:p0 + 128, :], in_=va), reads=[r_va[k]], q="act")
            elif kind == "sigf":
                evt = evf[k][:, 0:n]
                sc.op("act", lambda e, PS=PS, evt=evt: e.activation(out=evt, in_=PS, func=AF.Sigmoid),
                      reads=[r_ps[k]], writes=[r_ev[k]])
                sc.dma(lambda e, evt=evt, p0=p0: e.dma_start(out=dst[p0:p0 + 128, :], in_=evt), reads=[r_ev[k]], q="act")
            else:
                evt = evb[k][:, 0:n]
                sc.op("act", lambda e, PS=PS, evt=evt: e.activation(out=evt, in_=PS, func=AF.Sigmoid),
                      reads=[r_ps[k]], writes=[r_ev[k]])
                sc.dma(lambda e, evt=evt, p0=p0: e.dma_start(out=dst[p0:p0 + 128, :], in_=evt), reads=[r_ev[k]], q="act")

    for p in range(4):
        run_fm([(C_QN + p * 128, 128)], scr["QN"][p], False, 1)
    run_fm([(C_KV, 128)], scr["KC"][0], True, 1)
    run_fm([(C_KV + 128, 128)], scr["KC"][1], True, 1)
    for g in range(2):
        run_fm([(C_KV + 256 + g * 64, 64)] * 2, scr["KS"][g], False, 1)
        run_fm([(C_KV + 512 + g * 64, 64)] * 2, scr["KW"][g], False, 1)
    run_tm(C_KV + 384, 128, scr["VS"], "aug", 1, nh=2)
    run_tm(C_KV + 640, 128, scr["VW"], "aug", 1, nh=2)
    run_tm(C_GN, 24, scr["GN"], "sigf", 1)
    for gi in range(3):
        for pp in range(2):
            run_fm([(C_QD + gi * 256 + pp * 128, 128)], scr["QD"][gi * 2 + pp], False, DILS[gi])
            run_fm([(C_KD + gi * 256 + pp * 128, 128)], scr["KD"][gi * 2 + pp], False, DILS[gi])
        run_tm(C_VD + gi * 256, 256, scr["VD"][gi], "aug", DILS[gi], nh=4)
    for q in range(4):
        run_tm(C_MG + q * 512, 512, scr["MG"][:, q * 512:(q + 1) * 512], "sigb", 1)
    sc.barrier()


def cmp_phase(sc, M, C, scr, P):
    M.reset(C["const_end"])
    kcT, VC = C["kcT"], C["VC"]
    kcf = M.f32(S)
    tb = r3(M.bf(32 * 256), 32)
    W1b = r3(M.bf(32 * 256), 32)
    W2b = v4(M.bf(2 * 128), 2, 2)
    st1 = [M.f32(256) for _ in range(2)]
    st2 = M.f32(128)
    peT = M.f32(32)
    hx = [M.f32(256) for _ in range(4)]
    hu = M.f32(256)
    gT = [M.bf(256) for _ in range(4)]
    r_kcf, r_tb, r_W1, r_W2, r_st2, r_pe = Res(), Res(), Res(), Res(), Res(), Res()
    r_st1 = [Res(), Res()]
    r_hx = [Res() for _ in range(4)]
    r_hu = Res()
    r_gT = [Res() for _ in range(4)]
    r_ps = [Res(), Res(), Res()]
    r_kcT, r_VC = C["r_kcT"], C["r_VC"]
    VC4 = v4(VC, 2, 2)
    sc.op("pool", lambda e: e.memset(VC, 1.0), writes=[r_VC])
    for ct in range(2):
        for g in range(2):
            sc.dma(lambda e, ct=ct, g=g: e.dma_start(out=VC4[:, ct, g, 65:128], in_=P["c_ov"][ct * 128:(ct + 1) * 128, 1:64]),
                   writes=[r_VC])
    def do_kv(kv):
        pe_d = P["nsa_pe_k"] if kv == 0 else P["nsa_pe_v"]
        w1_d = P["nsa_w_ck1"] if kv == 0 else P["nsa_w_cv1"]
        w2_d = P["nsa_w_ck2"] if kv == 0 else P["nsa_w_cv2"]
        sc.dma(lambda e: e.dma_start(out=kcf, in_=scr["KC"][kv]), writes=[r_kcf])
        for g in range(2):
            sc.dma(lambda e, g=g, pe_d=pe_d: e.dma_start(out=peT[g * 64:(g + 1) * 64, :], in_=pe_d.rearrange("l d -> d l"),
                                                        allow_slow_non_contiguous=True), writes=[r_pe])
        sc.op("pool", lambda e: e.memset(tb.rearrange("p a b -> p (a b)"), 0.0), writes=[r_tb])
        kcf3 = kcf.rearrange("p (a b) -> p a b", b=16)
        for l in range(32):
            src = kcf3[:, 0:255, l] if l < 16 else kcf3[:, 1:256, l - 16]
            sc.op("dve", lambda e, l=l, src=src: e.tensor_scalar(out=tb[:, l, 0:255], in0=src, scalar1=peT[:, l:l + 1],
                                                                 scalar2=None, op0=ALU.add),
                  reads=[r_kcf, r_pe], writes=[r_tb])
        w1v = w1_d.rearrange("(l d) h -> d l h", d=64)
        for l in range(32):
            i = l % 2
            for g in range(2):
                sc.dma(lambda e, l=l, g=g, i=i: e.dma_start(out=st1[i][g * 64:(g + 1) * 64, :], in_=w1v[:, l, :]),
                       writes=[r_st1[i]])
            sc.op("pool", lambda e, l=l, i=i: e.tensor_copy(out=W1b[:, l, :], in_=st1[i]), reads=[r_st1[i]], writes=[r_W1])
        sc.dma(lambda e: e.dma_start(out=r3(st2, 2), in_=w2_d.rearrange("(hc p) d -> p hc d", p=128)), writes=[r_st2])
        sc.op("pool", lambda e: e.tensor_copy(out=W2b[:, :, 0, :], in_=r3(st2, 2)), reads=[r_st2], writes=[r_W2])
        sc.op("pool", lambda e: e.memset(W2b[:, :, 1, :], 0.0), writes=[r_W2])
        for g in range(2):
            for hc in range(2):
                idx = g * 2 + hc
                PS = M.bank(1 + idx % 2)[:, 0:256]
                rp = r_ps[idx % 2]
                for l in range(32):
                    sc.op("pe", lambda e, PS=PS, l=l, g=g, hc=hc: e.matmul(
                        PS, lhsT=W1b[g * 64:(g + 1) * 64, l, hc * 128:(hc + 1) * 128], rhs=tb[g * 64:(g + 1) * 64, l, :],
                        start=(l == 0), stop=(l == 31)), reads=[r_W1, r_tb], writes=[rp])
                x_ = hx[idx]
                sc.op("act", lambda e, PS=PS, x_=x_: e.copy(out=x_, in_=PS), reads=[rp], writes=[r_hx[idx]])
                sc.op("act", lambda e, x_=x_: e.activation(out=hu, in_=x_, func=AF.Square), reads=[r_hx[idx]], writes=[r_hu])
                sc.op("dve", lambda e: e.tensor_scalar(out=hu, in0=hu, scalar1=0.044715, scalar2=1.0, op0=ALU.mult, op1=ALU.add),
                      reads=[r_hu], writes=[r_hu])
                sc.op("dve", lambda e, x_=x_: e.tensor_tensor(out=hu, in0=hu, in1=x_, op=ALU.mult),
                      reads=[r_hu, r_hx[idx]], writes=[r_hu])
                sc.op("act", lambda e: e.activation(out=hu, in_=hu, func=AF.Sigmoid, scale=1.5957691216057308),
                      reads=[r_hu], writes=[r_hu])
                sc.op("dve", lambda e, x_=x_, idx=idx: e.tensor_tensor(out=gT[idx], in0=hu, in1=x_, op=ALU.mult),
                      reads=[r_hu, r_hx[idx]], writes=[r_gT[idx]])
        for g in range(2):
            if kv == 0:
                PS = M.bank(3)[:, 0:256]
                for hc in range(2):
                    sc.op("pe", lambda e, PS=PS, g=g, hc=hc: e.matmul(
                        PS, lhsT=W2b[:, hc, :, :].rearrange("p a b -> p (a b)"), rhs=gT[g * 2 + hc],
                        start=(hc == 0), stop=(hc == 1)), reads=[r_W2, r_gT[g * 2 + hc]], writes=[r_ps[2]])
                sc.op("act", lambda e, PS=PS, g=g: e.copy(out=kcT[:, g, :], in_=PS), reads=[r_ps[2]], writes=[r_kcT])
            else:
                for ct in range(2):
                    PS = M.bank(3)[:, 0:64]
                    for hc in range(2):
                        sc.op("pe", lambda e, PS=PS, g=g, hc=hc, ct=ct: e.matmul(
                            PS, lhsT=gT[g * 2 + hc][:, ct * 128:(ct + 1) * 128], rhs=W2b[:, hc, 0, :],
                            start=(hc == 0), stop=(hc == 1)), reads=[r_W2, r_gT[g * 2 + hc]], writes=[r_ps[2]])
                    sc.op("act", lambda e, PS=PS, g=g, ct=ct: e.copy(out=VC4[:, ct, g, 0:64], in_=PS),
                          reads=[r_ps[2]], writes=[r_VC])

    for kv in range(2):
        do_kv(kv)
    sc.barrier()


class AttnPipe:
    def __init__(self, sc, M, sbanks, ntmp=3):
        self.sc, self.M = sc, M
        self.sb = sbanks
        self.r_s = [Res() for _ in sbanks]
        self.tmp = [M.f32(512) for _ in range(ntmp)]
        self.Pt = [M.bf(512) for _ in range(ntmp)]
        self.r_tmp = [Res() for _ in range(ntmp)]
        self.r_P = [Res() for _ in range(ntmp)]
        self.n = 0

    def run_stream(self, items, LA=3):
        sc, M = self.sc, self.M
        jobs = [it for it in items if isinstance(it, dict)]
        order = []
        ji = 0
        for it in items:
            if isinstance(it, dict):
                order.append(("job", ji))
                ji += 1
            else:
                order.append(("call", it))
        n = len(jobs)
        slots = {}
        done_pv = [0]
        pending = []
        DELAY = 4

        def emit_pv(j):
            jb = jobs[j]
            ti = slots[j]
            if "parts" in jb:
                while pending and pending[0][0] <= j:
                    pending.pop(0)[2]()
                acc, M_rows, W = jb["acc"], jb["M_rows"], jb["W"]
                nk = jb["nk"]
                if jb["first"]:
                    z = ZEROS[0]
                    sc.op("pe", lambda e: e.matmul(acc[0:M_rows, 0:W], lhsT=z[:, 0:M_rows], rhs=z[:, 0:W], start=True, stop=False),
                          writes=[jb["r_acc"]])
                np_ = len(jb["parts"])
                for pi_, (_l, _r, _rd, off_, n_, pv_, pvr_, c0_) in enumerate(jb["parts"]):
                    Pq = self.Pt[ti][0:nk, off_:off_ + n_]
                    lastp = jb["last"] and pi_ == np_ - 1
                    sc.op("pe", lambda e, Pq=Pq, pv_=pv_, c0_=c0_, n_=n_, lastp=lastp: e.matmul(
                        acc[0:M_rows, c0_:c0_ + n_], lhsT=pv_, rhs=Pq, start=False, stop=lastp),
                        reads=[self.r_P[ti]] + pvr_, writes=[jb["r_acc"]])
                if jb.get("after") is not None:
                    key, fa, fb = jb["after"]
                    while any(p[1] == key for p in pending):
                        pending.pop(0)[2]()
                    fa()
                    pending.append((j + DELAY, key, fb))
                return
            nk, ncol, c0 = jb["nk"], jb["nc"], jb["c0"]
            Pp = self.Pt[ti][0:nk, 0:ncol]
            pv = jb["pv"]
            acc, M_rows = jb["acc"], jb["M_rows"]
            first, last = jb["first"], jb["last"]
            while pending and pending[0][0] <= j:
                pending.pop(0)[2]()
            W = jb.get("W", 512)
            if first and (c0 != 0 or ncol != W):
                z = ZEROS[0]
                sc.op("pe", lambda e: e.matmul(acc[0:M_rows, 0:W], lhsT=z[:, 0:M_rows], rhs=z[:, 0:W], start=True, stop=False),
                      writes=[jb["r_acc"]])
                first = False
            sc.op("pe", lambda e: e.matmul(acc[0:M_rows, c0:c0 + ncol], lhsT=pv, rhs=Pp, start=first, stop=last),
                  reads=[self.r_P[ti]] + jb["pv_reads"], writes=[jb["r_acc"]])
            if jb.get("after") is not None:
                key, fa, fb = jb["after"]
                while any(p[1] == key for p in pending):
                    pending.pop(0)[2]()
                fa()
                pending.append((j + DELAY, key, fb))

        for kind, v in order:
            if kind == "call":
                v()
                continue
            i = v
            jb = jobs[i]
            k = self.n
            self.n += 1
            si = k % len(self.sb)
            ti = k % len(self.tmp)
            slots[i] = ti
            nk, ncol = jb["nk"], jb["nc"]
            Sps = M.bank(self.sb[si])[0:nk, 0:ncol]
            if "parts" in jb:
                for (l_, r_, rd, off_, n_, _pv, _pvr, _c0) in jb["parts"]:
                    Sp_ = M.bank(self.sb[si])[0:nk, off_:off_ + n_]
                    sc.op("pe", lambda e, l_=l_, r_=r_, Sp_=Sp_: e.matmul(Sp_, lhsT=l_, rhs=r_, start=True, stop=True),
                          reads=rd, writes=[self.r_s[si]])
            else:
                nq = len(jb["qk"])
                for qi, (l_, r_, rd) in enumerate(jb["qk"]):
                    sc.op("pe", lambda e, l_=l_, r_=r_, qi=qi, Sps=Sps, nq=nq: e.matmul(
                        Sps, lhsT=l_, rhs=r_, start=(qi == 0), stop=(qi == nq - 1)), reads=rd, writes=[self.r_s[si]])
            Pp = self.Pt[ti][0:nk, 0:ncol]
            bias = jb.get("bias")
            if "parts" in jb:
                tm = self.tmp[ti][0:nk, 0:ncol]
                T = jb["T"]
                sc.op("dve", lambda e, tm=tm, Sps=Sps, T=T: e.scalar_tensor_tensor(
                    out=tm, in0=Sps, scalar=0.125, in1=T, op0=ALU.mult, op1=ALU.add),
                    reads=[self.r_s[si]] + jb.get("T_reads", []), writes=[self.r_tmp[ti]])
                sc.op("act", lambda e, Pp=Pp, tm=tm: e.activation(out=Pp, in_=tm, func=AF.Exp),
                      reads=[self.r_tmp[ti]], writes=[self.r_P[ti]])
            elif jb.get("direct"):
                sc.op("act", lambda e, Pp=Pp, Sps=Sps, bias=bias: e.activation(out=Pp, in_=Sps, func=AF.Exp, bias=bias, scale=0.125),
                      reads=[self.r_s[si]] + jb.get("bias_reads", []), writes=[self.r_P[ti]])
            else:
                tm = self.tmp[ti][0:nk, 0:ncol]
                T = jb["T"]
                sc.op("dve", lambda e, tm=tm, Sps=Sps, T=T: e.scalar_tensor_tensor(
                    out=tm, in0=Sps, scalar=0.125, in1=T, op0=ALU.mult, op1=ALU.add),
                    reads=[self.r_s[si]] + jb.get("T_reads", []), writes=[self.r_tmp[ti]])
                for (off, mk, mrd) in jb["masks"]:
                    w = mk.shape[-1]
                    tmm = self.tmp[ti][0:nk, off:off + w]
                    sc.op("pool", lambda e, tmm=tmm, mk=mk: e.tensor_tensor(out=tmm, in0=tmm, in1=mk, op=ALU.add),
                          reads=[self.r_tmp[ti]] + mrd, writes=[self.r_tmp[ti]])
                sc.op("act", lambda e, Pp=Pp, tm=tm, bias=bias: e.activation(out=Pp, in_=tm, func=AF.Exp, bias=bias),
                      reads=[self.r_tmp[ti]] + jb.get("bias_reads", []), writes=[self.r_P[ti]])
            while done_pv[0] <= i - LA:
                emit_pv(done_pv[0])
                done_pv[0] += 1
        while done_pv[0] < n:
            emit_pv(done_pv[0])
            done_pv[0] += 1
        while pending:
            pending.pop(0)[2]()

    def run(self, jobs, acc, r_acc, M_rows):
        sc, M = self.sc, self.M
        LA = 2
        n = len(jobs)
        slots = []
        for i in range(n + LA):
            if i < n:
                jb = jobs[i]
                k = self.n
                self.n += 1
                si = k % len(self.sb)
                ti = k % len(self.tmp)
                slots.append(ti)
                nk, ncol = jb["nk"], jb["nc"]
                Sps = M.bank(self.sb[si])[0:nk, 0:ncol]
                nq = len(jb["qk"])
                for qi, (l_, r_, rd) in enumerate(jb["qk"]):
                    sc.op("pe", lambda e, Sps=Sps, l_=l_, r_=r_, qi=qi, nq=nq: e.matmul(
                        Sps, lhsT=l_, rhs=r_, start=(qi == 0), stop=(qi == nq - 1)), reads=rd, writes=[self.r_s[si]])
                tm = self.tmp[ti][0:nk, 0:ncol]
                T = jb["T"]
                sc.op("dve", lambda e, tm=tm, Sps=Sps, T=T: e.scalar_tensor_tensor(out=tm, in0=Sps, scalar=0.125, in1=T, op0=ALU.mult, op1=ALU.add),
                      reads=[self.r_s[si]] + jb.get("T_reads", []), writes=[self.r_tmp[ti]])
                for (off, mk, mrd) in jb["masks"]:
                    w = mk.shape[-1]
                    tmm = self.tmp[ti][0:nk, off:off + w]
                    sc.op("pool", lambda e, tmm=tmm, mk=mk: e.tensor_tensor(out=tmm, in0=tmm, in1=mk, op=ALU.add),
                          reads=[self.r_tmp[ti]] + mrd, writes=[self.r_tmp[ti]])
                Pp = self.Pt[ti][0:nk, 0:ncol]
                bias = jb["bias"]
                sc.op("act", lambda e, Pp=Pp, tm=tm, bias=bias: e.activation(out=Pp, in_=tm, func=AF.Exp, bias=bias),
                      reads=[self.r_tmp[ti]] + jb.get("bias_reads", []), writes=[self.r_P[ti]])
            if i >= LA:
                j = i - LA
                jb = jobs[j]
                ti = slots[j]
                nk, ncol, c0 = jb["nk"], jb["nc"], jb["c0"]
                Pp = self.Pt[ti][0:nk, 0:ncol]
                pv = jb["pv"]
                sc.op("pe", lambda e, Pp=Pp, pv=pv, c0=c0, ncol=ncol, j=j: e.matmul(
                    acc[0:M_rows, c0:c0 + ncol], lhsT=pv, rhs=Pp, start=(j == 0), stop=(j == n - 1)),
                    reads=[self.r_P[ti]] + jb["pv_reads"], writes=[r_acc])


def nsa_phase(sc, M, C, scr, P):
    M.reset(C["const_end"])
    identf = C["identf"]
    ident = C["ident"]
    kcT, VC = C["kcT"], C["VC"]
    r_kcT, r_VC = C["r_kcT"], C["r_VC"]
    VC4 = v4(VC, 2, 2)
    KS = r3(M.bf(2 * S), 2)
    KW = r3(M.bf(2 * S), 2)
    VS = v4(M.bf(32 * 256), 32, 2)
    VW = v4(M.bf(32 * 256), 32, 2)
    TQA = r3(M.f32(8 * 512), 8)
    MASKO = M.f32(512)
    TQ8 = r3(M.f32(8 * 4), 8)
    QN = [r3(M.bf(8 * 512), 8) for _ in range(2)]
    TQ = r3(M.f32(8 * 512), 8)
    MK = r3(M.f32(3 * 128), 3)
    BC = r3(M.f32(8 * 35), 8)
    BCC = v4(M.f32(8 * 8 * 2), 8, 8)
    Mc = [r3(M.f32(2 * 512), 2) for _ in range(2)]
    accS = [M.f32(512) for _ in range(2)]
    onsa = r3(M.f32(4 * 512), 4)
    onsab = r3(M.bf(4 * 512), 4)
    impacc = v4(M.f32(2 * 4 * 64), 2, 4)
    GN = [r3(M.f32(4 * 24), 4) for _ in range(2)]
    SELM = [v4(M.f32(4 * 2 * 64), 4, 2) for _ in range(2)]
    rr_ = M.f32(4)
    sc4 = M.f32(4)
    m1 = M.f32(8)
    m2 = M.f32(8)
    wk = M.f32(64)
    impp = r3(M.f32(4 * 64), 4)
    mbf = M.bf(128)
    tnum = r3(M.f32(4 * 64), 4)
    pipe = AttnPipe(sc, M, [0, 1, 2, 6], ntmp=4)
    r_KS, r_KW, r_VS, r_VW, r_c = Res(), Res(), Res(), Res(), Res()
    r_QN = [[Res(), Res()], [Res(), Res()]]
    r_Mc = [Res(), Res()]
    r_GN = [Res(), Res()]
    r_SELM = [Res(), Res()]
    r_accS = [Res(), Res()]
    r_acc = [Res(), Res()]
    r_TP0 = Res()
    r_TP = [r_TP0, r_TP0]
    r_onsa, r_onsab, r_imp, r_MBT = Res(), Res(), Res(), Res()
    r_sm = Res()
    r_sel = Res()
    r_mbf = Res()
    r_tnum = Res()
    r_TP7 = Res()
    sc.op("dve", lambda e: e.memset(KW.rearrange("p a b -> p (a b)"), 0.0), writes=[r_KW])
    sc.op("dve", lambda e: e.memset(VS.rearrange("p a b c -> p (a b c)"), 0.0), writes=[r_VS])
    sc.op("pool", lambda e: e.memset(VW.rearrange("p a b c -> p (a b c)"), 0.0), writes=[r_VW])
    for qb_ in range(2):
        sc.op("pool", lambda e, qb_=qb_: e.memset(QN[qb_].rearrange("p a b -> p (a b)"), 0.0),
              writes=[r_QN[qb_][0], r_QN[qb_][1]])
    ohf = P["c_oh"].rearrange("j a b -> j (a b)")
    for g in range(2):
        sc.dma(lambda e, g=g: e.dma_start(out=KS[0:64, g, :], in_=scr["KS"][g][0:64, :]), writes=[r_KS])
        sc.dma(lambda e, g=g: e.dma_start(out=KS[64:128, g, :], in_=ohf), writes=[r_KS])
        sc.dma(lambda e, g=g: e.dma_start(out=KW[0:64, g, :], in_=scr["KW"][g][0:64, :]), writes=[r_KW])
    vsv = scr["VS"].rearrange("(kt p) (g c) -> p kt g c", p=128, g=2)
    vwv = scr["VW"].rearrange("(kt p) (g c) -> p kt g c", p=128, g=2)
    for q4 in range(4):
        for g in range(2):
            sc.dma(lambda e, q4=q4, g=g: e.dma_start(out=VS[:, q4 * 8:(q4 + 1) * 8, g, 0:65],
                                                      in_=vsv[:, q4 * 8:(q4 + 1) * 8, g, :]), writes=[r_VS])
            sc.dma(lambda e, q4=q4, g=g: e.dma_start(out=VW[:, q4 * 8:(q4 + 1) * 8, g, 0:65],
                                                      in_=vwv[:, q4 * 8:(q4 + 1) * 8, g, :]), writes=[r_VW])
    sc.dma(lambda e: e.dma_start(out=TQ.rearrange("p a b -> p (a b)"),
                                 in_=P["c_tq"].rearrange("a b -> (a b)").partition_broadcast(128)), writes=[r_c])
    sc.dma(lambda e: e.dma_start(out=TQA, in_=P["c_tqa"].rearrange("h k q -> k h q")), writes=[r_c])
    sc.dma(lambda e: e.dma_start(out=MASKO, in_=P["c_masko"]), writes=[r_c])
    sc.dma(lambda e: e.dma_start(out=TQ8, in_=P["c_tq8"]), writes=[r_c])
    sc.dma(lambda e: e.dma_start(out=MK, in_=P["c_masks"].rearrange("m k q -> k m q")), writes=[r_c])
    sc.dma(lambda e: e.dma_start(out=BC, in_=P["c_bc"]), writes=[r_c])
    sc.dma(lambda e: e.dma_start(out=BCC, in_=P["c_bcc"]), writes=[r_c])
    sc.barrier()

    def finish_head(acc_bank, ai, M_rows, h, br, qc, first):
        a = accS[ai]
        tb_ = 5
        TP = r3(M.bank(tb_), 4)
        for sub in range(4):
            MR = M_rows + (M_rows % 2)
            sc.op("pe", lambda e, sub=sub, MR=MR: e.transpose(out=TP[:, sub, 0:MR], in_=a[0:MR, sub * 128:(sub + 1) * 128],
                                                       identity=identf[0:MR, 0:MR]),
                  reads=[r_accS[ai]], writes=[r_TP[ai]])
        den = TP[:, :, 64:65]
        if br == 0:
            sc.op("dve", lambda e: e.tensor_scalar(out=rr_.unsqueeze(2), in0=den, scalar1=1e-30, scalar2=None, op0=ALU.max),
                  reads=[r_TP[ai]], writes=[r_sm])
            sc.op("dve", lambda e: e.reciprocal(out=rr_, in_=rr_), reads=[r_sm], writes=[r_sm])
        else:
            sc.op("dve", lambda e: e.reciprocal(out=rr_.unsqueeze(2), in_=den), reads=[r_TP[ai]], writes=[r_sm])
        gidx = h * 3 + br
        sc.op("dve", lambda e: e.tensor_tensor(out=sc4, in0=rr_, in1=GN[qc % 2][:, :, gidx], op=ALU.mult),
              reads=[r_sm, r_GN[qc % 2]], writes=[r_sm])
        dst = onsa[:, :, h * 64:(h + 1) * 64]
        if first:
            sc.op("dve", lambda e: e.tensor_tensor(out=dst, in0=TP[:, :, 0:64], in1=sc4.unsqueeze(2).to_broadcast([128, 4, 64]),
                                                   op=ALU.mult), reads=[r_TP[ai], r_sm], writes=[r_onsa])
        else:
            sc.op("dve", lambda e: e.tensor_tensor(out=tnum, in0=TP[:, :, 0:64], in1=sc4.unsqueeze(2).to_broadcast([128, 4, 64]),
                                                   op=ALU.mult), reads=[r_TP[ai], r_sm], writes=[r_tnum])
            sc.op("dve", lambda e: e.tensor_tensor(out=dst, in0=dst, in1=tnum, op=ALU.add),
                  reads=[r_tnum, r_onsa], writes=[r_onsa])
        return TP

    hcount = [0]

    mbfs = [[r3(M.bf(4 * 128), 4) for _ in range(4)] for _ in range(2)]
    r_mbfs = [[Res() for _ in range(4)] for _ in range(2)]
    for g_ in range(2):
        for s_ in range(4):
            sc.op("pool", lambda e, g_=g_, s_=s_: e.memset(mbfs[g_][s_].rearrange("p a b -> p (a b)"), 0.0),
                  writes=[r_mbfs[g_][s_]])

    def chunk_loads(qc):
        q0 = qc * 512
        qb = qc % 2
        for h_ in range(8):
            sc.dma(lambda e, h_=h_: e.dma_start(out=QN[qb][0:64, h_, :],
                                                 in_=scr["QN"][h_ // 2][(h_ % 2) * 64:(h_ % 2) * 64 + 64, q0:q0 + 512]),
                   writes=[r_QN[qb][h_ // 4]])
        sc.dma(lambda e: e.dma_start(out=GN[qb], in_=scr["GN"][q0:q0 + 512, :].rearrange("(s p) c -> p s c", p=128)),
               writes=[r_GN[qb]])
        sc.dma(lambda e: e.dma_start(out=SELM[qb].rearrange("p a b c -> p a (b c)"),
                                     in_=P["c_selm"][q0:q0 + 512].rearrange("(s p) a c -> p s (a c)", p=128)),
               writes=[r_SELM[qb]])
        sc.dma(lambda e: e.dma_start(out=Mc[qb], in_=P["c_mc"][qc]), writes=[r_Mc[qb]])

    def do_chunk(qc):
        q0 = qc * 512
        qb = qc % 2
        if qc + 1 < 8:
            chunk_loads(qc + 1)
        nct = 2 if qc >= 4 else 1
        items = []

        def sel_dve(g):
            sc.op("pool", lambda e: e.memset(impacc[:, g, :, 0:1], 0.0), reads=[r_imp], writes=[r_imp])
            sc.op("dve", lambda e: e.tensor_tensor(out=impp, in0=impacc[:, g, :, :], in1=SELM[qb][:, :, 0, :], op=ALU.mult),
                  reads=[r_imp, r_SELM[qb]], writes=[r_sel])
            sc.op("dve", lambda e: e.tensor_tensor(out=impp, in0=impp, in1=SELM[qb][:, :, 1, :], op=ALU.add),
                  reads=[r_sel, r_SELM[qb]], writes=[r_sel])
            for sub in range(4):
                iv = impp[:, sub, :]
                mb_ = mbfs[g][sub]
                sc.op("dve", lambda e, iv=iv: e.max(out=m1, in_=iv), reads=[r_sel], writes=[r_sel])
                sc.op("dve", lambda e, iv=iv: e.match_replace(out=wk, in_to_replace=m1, in_values=iv, imm_value=-3.0e38),
                      reads=[r_sel], writes=[r_sel])
                sc.op("dve", lambda e: e.max(out=m2, in_=wk), reads=[r_sel], writes=[r_sel])
                sc.op("dve", lambda e, iv=iv: e.tensor_scalar(
                    out=wk, in0=iv, scalar1=m2[:, 7:8], scalar2=NEG, op0=ALU.is_lt, op1=ALU.mult),
                    reads=[r_sel], writes=[r_sel])
                for hh_ in range(4):
                    sc.op("dve", lambda e, hh_=hh_, mb_=mb_, sub=sub: e.tensor_scalar(
                        out=mb_[:, hh_, 64:128], in0=wk, scalar1=TQ8[:, g * 4 + hh_, sub:sub + 1], scalar2=None, op0=ALU.add),
                        reads=[r_sel, r_c], writes=[r_mbfs[g][sub]])

        def sel_pe(g):
            Tb = M.bank_bf(7)
            for sub in range(4):
                mb_ = mbfs[g][sub]
                for hh_ in range(4):
                    sc.op("pe", lambda e, mb_=mb_, hh_=hh_: e.transpose(out=Tb[:, hh_ * 128:(hh_ + 1) * 128], in_=mb_[:, hh_, :],
                                                                       identity=ident),
                          reads=[r_mbfs[g][sub]], writes=[r_TP7])
                sc.op("act", lambda e, sub=sub: e.copy(
                    out=QN[qb][64:128, g * 4:(g + 1) * 4, sub * 128:(sub + 1) * 128],
                    in_=r3(Tb[64:128, 0:512], 4)),
                    reads=[r_TP7], writes=[r_QN[qb][g]])

        def mk_after(ai, M_rows, h, br, g, hh):
            def fa():
                a = accS[ai]
                sc.op("dve", lambda e: e.tensor_copy(out=a[0:M_rows, :], in_=M.bank(3 + ai)[0:M_rows, :]),
                      reads=[r_acc[ai]], writes=[r_accS[ai]])

            def after():
                first = (br == 0)
                TP = finish_head(3 + ai, ai, M_rows, h, br, qc, first)
                if br == 0:
                    ia = impacc[:, g, :, 1:64]
                    if hh == 0:
                        sc.op("dve", lambda e: e.tensor_tensor(
                            out=ia, in0=TP[:, :, 65:128], in1=rr_.unsqueeze(2).to_broadcast([128, 4, 63]), op=ALU.mult),
                            reads=[r_TP[ai], r_sm], writes=[r_imp])
                    else:
                        sc.op("dve", lambda e: e.tensor_tensor(
                            out=tnum[:, :, 0:63], in0=TP[:, :, 65:128], in1=rr_.unsqueeze(2).to_broadcast([128, 4, 63]),
                            op=ALU.mult), reads=[r_TP[ai], r_sm], writes=[r_tnum])
                        sc.op("dve", lambda e: e.tensor_tensor(out=ia, in0=ia, in1=tnum[:, :, 0:63], op=ALU.add),
                              reads=[r_tnum, r_imp], writes=[r_imp])
                    if hh == 3:
                        sel_dve(g)
            return (ai, fa, after)

        for g in range(2):
            for hh in range(4):
                h = g * 4 + hh
                pr, half = h // 2, h % 2
                lo, hi = half * 64, half * 64 + 64
                ai = hcount[0] % 2
                hcount[0] += 1
                for ct in range(nct):
                    items.append(dict(
                        qk=[(kcT[:, g, ct * 128:(ct + 1) * 128], QN[qb][:, h, :], [r_kcT, r_QN[qb][g]])],
                        nk=128, c0=0, nc=512, T=TQ[:, h, :], T_reads=[r_c],
                        masks=[(0, Mc[qb][:, ct, :], [r_Mc[qb]])],
                        bias=BCC[:, h, qc, ct:ct + 1], bias_reads=[r_c],
                        pv=VC4[:, ct, g, :], pv_reads=[r_VC],
                        acc=M.bank(3 + ai), r_acc=r_acc[ai], M_rows=128, first=(ct == 0), last=(ct == nct - 1),
                        after=(mk_after(ai, 128, h, 0, g, hh) if ct == nct - 1 else None)))

        def branch_jobs(br, g):
            for hh in range(4):
                h = g * 4 + hh
                pr, half = h // 2, h % 2
                lo, hi = half * 64, half * 64 + 64
                ai = hcount[0] % 2
                hcount[0] += 1
                if br == 1:
                    kts = list(range(0, 4 * qc + 4))
                else:
                    kts = list(range(max(0, 4 * qc - 4), 4 * qc + 4))
                for ki_, kt in enumerate(kts):
                    dlt = 4 * qc - kt
                    masks = []
                    if dlt > 0:
                        c0, ncol = 0, 512
                        Tt = TQ
                        bidx = dlt + 3
                        if br == 2:
                            m = 4 - dlt
                            ncol = 128 * (m + 1)
                            masks = [(128 * m, MK[:, 1, :], [r_c])]
                    else:
                        j = -dlt
                        c0, ncol = 128 * j, 512 - 128 * j
                        Tt = TQA
                        bidx = 3
                    Kt = KS if br == 1 else KW
                    rK = r_KS if br == 1 else r_KW
                    qk = [(Kt[:, g, kt * 128:(kt + 1) * 128], QN[qb][:, h, c0:c0 + ncol], [rK, r_QN[qb][g]])]
                    Vt = VS if br == 1 else VW
                    lastj = (ki_ == len(kts) - 1)
                    Tap = Tt[:, h, 0:ncol]
                    direct = False
                    if br == 1:
                        bidx = dlt + 3
                        if dlt > 0:
                            direct = True
                        else:
                            Tap = MASKO[:, 0:ncol]
                    items.append(dict(
                        qk=qk, nk=128, c0=c0, nc=ncol, T=Tap, T_reads=[r_c], masks=masks, direct=direct,
                        bias=BC[:, h, bidx:bidx + 1], bias_reads=[r_c],
                        pv=Vt[:, kt, g, :], pv_reads=[r_VS if br == 1 else r_VW],
                        acc=M.bank(3 + ai), r_acc=r_acc[ai], M_rows=128, first=(ki_ == 0), last=lastj,
                        after=(mk_after(ai, 65, h, br, g, hh) if lastj else None)))

        branch_jobs(2, 0)
        items.append(lambda: sel_pe(0))
        branch_jobs(2, 1)
        items.append(lambda: sel_pe(1))
        branch_jobs(1, 0)
        branch_jobs(1, 1)
        pipe.run_stream(items)
        sc.op("act", lambda e: e.copy(out=onsab, in_=onsa), reads=[r_onsa], writes=[r_onsab])
        sc.dma(lambda e: e.dma_start(out=scr["ON"][q0:q0 + 512, :].rearrange("(s p) c -> p s c", p=128), in_=onsab),
               reads=[r_onsab])

    chunk_loads(0)
    for qc in range(8):
        do_chunk(qc)
    sc.barrier()


def dil_phase(sc, M, C, scr, P):
    identf = C["identf"]
    M.reset(C["const_end"])
    KDs = [v4(M.bf(4 * S), 2, 2) for _ in range(2)]
    VDs = [v4(M.bf(32 * 512), 32, 4) for _ in range(2)]
    TDs = [r3(M.f32(4 * 1152), 4) for _ in range(2)]
    QD = [r3(M.bf(2 * 512), 2) for _ in range(2)]
    accS = [M.f32(512) for _ in range(2)]
    osb = [v4(M.f32(4 * 260), 4, 4) for _ in range(2)]
    pipe = AttnPipe(sc, M, [0, 1, 2, 6], ntmp=4)
    r_KDs, r_VDs, r_cs = [Res(), Res()], [Res(), Res()], [Res(), Res()]
    r_QD = [Res(), Res()]
    r_accS = [Res(), Res()]
    r_acc = [Res(), Res()]
    r_TP0 = Res()
    r_TP = [r_TP0, r_TP0]
    r_osb = [Res(), Res()]
    sc.op("dve", lambda e: e.memset(KDs[0].rearrange("p a b c -> p (a b c)"), 0.0), writes=[r_KDs[0]])
    sc.op("pool", lambda e: e.memset(VDs[0].rearrange("p a b c -> p (a b c)"), 0.0), writes=[r_VDs[0]])
    sc.op("dve", lambda e: e.memset(KDs[1].rearrange("p a b c -> p (a b c)"), 0.0), writes=[r_KDs[1]])
    sc.op("pool", lambda e: e.memset(VDs[1].rearrange("p a b c -> p (a b c)"), 0.0), writes=[r_VDs[1]])

    def load_group(gi, q):
        gb = gi % 2
        KD_, VD_, TD_ = KDs[gb], VDs[gb], TDs[gb]
        for pp in range(2):
            for hf in range(2):
                sc.dma(lambda e, pp=pp, hf=hf: e.dma_start(out=KD_[hf * 64:(hf + 1) * 64, pp, hf, :],
                                                            in_=scr["KD"][gi * 2 + pp][hf * 64:(hf + 1) * 64, :]),
                       writes=[r_KDs[gb]], q=q)
        vdv = scr["VD"][gi].rearrange("(kt p) (h c) -> p kt h c", p=128, h=4)
        for q4 in range(4):
            for hh_ in range(4):
                sc.dma(lambda e, q4=q4, hh_=hh_: e.dma_start(out=VD_[:, q4 * 8:(q4 + 1) * 8, hh_, 0:65],
                                                              in_=vdv[:, q4 * 8:(q4 + 1) * 8, hh_, :]),
                       writes=[r_VDs[gb]], q=q)
        sc.dma(lambda e: e.dma_start(out=TD_, in_=P["c_tds"][gi * 4:(gi + 1) * 4].rearrange("h k q -> k h q")),
               writes=[r_cs[gb]], q=q)

    def do_group(gi):
        dil = DILS[gi]
        Ls = S // dil
        Cn = min(512, Ls)
        nsub = Cn // 128
        gb = gi % 2
        KD, VD, TD = KDs[gb], VDs[gb], TDs[gb]
        r_KD, r_VD, r_c = r_KDs[gb], r_VDs[gb], r_cs[gb]
        ODv = scr["OD"][gi].rearrange("(i r) c -> i r c", r=dil)
        hcount = [0]
        items = []
        loadqs = []
        marks = []

        def do_chunk(ci, p0):
            r, i0 = divmod(p0, Ls)
            qb = ci % 2
            ob = osb[qb]

            def loadq():
                for pp in range(2):
                    sc.dma(lambda e, pp=pp: e.dma_start(out=QD[qb][:, pp, 0:Cn], in_=scr["QD"][gi * 2 + pp][:, p0:p0 + Cn]),
                           writes=[r_QD[qb]])
            loadqs.append(loadq)
            if ci == 0:
                items.append(loadq)
            mark = len(items)
            items.append(None)
            marks.append(mark)

            def mk_after(ai, hh):
                a = accS[ai]

                def fa():
                    sc.op("dve", lambda e: e.tensor_copy(out=a[0:65, 0:Cn], in_=M.bank(3 + ai)[0:65, 0:Cn]),
                          reads=[r_acc[ai]], writes=[r_accS[ai]])

                def after():
                    TP = r3(M.bank(5), 4)
                    for sub in range(nsub):
                        sc.op("pe", lambda e, sub=sub: e.transpose(
                            out=TP[:, sub, 0:66], in_=a[0:66, sub * 128:(sub + 1) * 128], identity=identf[0:66, 0:66]),
                            reads=[r_accS[ai]], writes=[r_TP[ai]])
                    sc.op("dve", lambda e: e.tensor_copy(out=ob[:, 0:nsub, hh, :], in_=TP[:, 0:nsub, 0:65]),
                          reads=[r_TP[ai]], writes=[r_osb[qb]])
                    if hh == 3:
                        for sub in range(nsub):
                            ii = i0 + sub * 128
                            sc.dma(lambda e, sub=sub, ii=ii: e.dma_start(
                                out=ODv[ii:ii + 128, r, :], in_=ob[:, sub, :, :].rearrange("p a b -> p (a b)")),
                                reads=[r_osb[qb]], q="act")
                return (ai, fa, after)

            for hh in range(4):
                pp, half = hh // 2, hh % 2
                lo, hi = half * 64, half * 64 + 64
                ai = hcount[0] % 2
                hcount[0] += 1
                js = [j for j in range(0, nsub + 1) if not (j == 0 and i0 == 0)]
                groups, cur, tot = [], [], 0
                for j in js:
                    n_ = 128 if (j == 0 or j == nsub) else 256
                    if tot + n_ > 512:
                        groups.append(cur)
                        cur, tot = [], 0
                    cur.append((j, n_))
                    tot += n_
                if cur:
                    groups.append(cur)
                for gn, grp in enumerate(groups):
                    parts = []
                    off = 0
                    j_first = grp[0][0]
                    t0 = 0 if j_first == 0 else 128 + 256 * (j_first - 1)
                    for (j, n_) in grp:
                        k0 = p0 + 128 * (j - 1)
                        kt = k0 // 128
                        c0 = 0 if j == 0 else 128 * (j - 1)
                        parts.append((KD[:, pp, half, k0:k0 + 128], QD[qb][:, pp, c0:c0 + n_], [r_KD, r_QD[qb]],
                                      off, n_, VD[:, kt, hh, :], [r_VD], c0))
                        off += n_
                    lastg = (gn == len(groups) - 1)
                    items.append(dict(
                        parts=parts, nk=128, nc=off, c0=0, T=TD[:, hh, t0:t0 + off], T_reads=[r_c],
                        acc=M.bank(3 + ai), r_acc=r_acc[ai], M_rows=128, first=(gn == 0), last=lastg, W=Cn,
                        after=(mk_after(ai, hh) if lastg else None)))

        for ci, p0 in enumerate(range(0, S, Cn)):
            do_chunk(ci, p0)
        for ci, mark in enumerate(marks):
            items[mark] = loadqs[ci + 1] if ci + 1 < len(loadqs) else (lambda: None)
        pipe.run_stream(items)

    load_group(0, "sp")
    for gi in range(3):
        if gi + 1 < 3:
            load_group(gi + 1, "pool")
        do_group(gi)
    sc.barrier()


def final_phase(sc, M, C, scr, P, x1, x2, gpost):
    M.reset(C["const_end"])
    ident = C["ident"]
    Wno = r3(M.bf(4 * 1024), 4)
    Wdo = r3(M.bf(2 * 1024), 2)
    Wmo = r3(M.bf(8 * 1024), 8)
    stage = [M.f32(1024) for _ in range(2)]
    gq = M.f32(1024)
    onb = [M.bf(512) for _ in range(2)]
    odf = [v4(M.f32(3 * 260), 3, 4) for _ in range(2)]
    mg = [M.bf(2048) for _ in range(2)]
    xr = [M.f32(1024) for _ in range(2)]
    onT = r3(M.bf(4 * 128), 4)
    odT = r3(M.bf(2 * 128), 2)
    odn = r3(M.f32(4 * 65), 4)
    odb = M.bf(256)
    ya = M.f32(1024)
    yb = M.f32(1024)
    ybf = M.bf(1024)
    yT = r3(M.bf(8 * 128), 8)
    yt = M.f32(1024)
    small = M.f32(8)
    r_W = Res()
    r_stage = [Res(), Res()]
    r_gq = Res()
    r_in = [Res(), Res()]
    r_x = [Res(), Res()]
    r_onT, r_odT, r_odn, r_odb, r_ya, r_yb, r_ybf, r_yT, r_yt, r_sm = (Res() for _ in range(10))
    r_T, r_Y1, r_Y2, r_Y3 = Res(), Res(), Res(), Res()
    cnt = [0]

    def load_piece(dst_ap, src_ap, n):
        i = cnt[0] % 2
        cnt[0] += 1
        st = stage[i][:, 0:n]
        sc.dma(lambda e: e.dma_start(out=st, in_=src_ap), writes=[r_stage[i]])
        sc.op("pool", lambda e: e.tensor_copy(out=dst_ap, in_=st), reads=[r_stage[i]], writes=[r_W])

    for k in range(4):
        load_piece(Wno[:, k, :], P["w_nsa_o"][k * 128:(k + 1) * 128, :], 1024)
    for k in range(2):
        load_piece(Wdo[:, k, :], P["w_dil_o"][k * 128:(k + 1) * 128, :], 1024)
    for k in range(8):
        load_piece(Wmo[:, k, :], P["w_mix_out"][k * 128:(k + 1) * 128, :], 1024)
    sc.dma(lambda e: e.dma_start(out=gq, in_=gpost[0, :].partition_broadcast(128)), writes=[r_gq])
    ybf2 = [ybf, M.bf(1024)]
    r_ybf2 = [Res(), Res()]
    smallA = M.f32(8)
    r_smA = Res()

    def stageA(t):
        b = t % 2
        rows = slice(t * 128, (t + 1) * 128)
        sc.dma(lambda e, b=b, rows=rows: e.dma_start(out=onb[b], in_=scr["ON"][rows, :]), writes=[r_in[b]])
        for gi in range(3):
            sc.dma(lambda e, b=b, rows=rows, gi=gi: e.dma_start(out=odf[b][:, gi, :, :].rearrange("p a b -> p (a b)"),
                                                               in_=scr["OD"][gi][rows, :]), writes=[r_in[b]])
        sc.dma(lambda e, b=b, rows=rows: e.dma_start(out=mg[b], in_=scr["MG"][rows, :]), writes=[r_in[b]])
        sc.dma(lambda e, b=b, rows=rows: e.dma_start(out=xr[b], in_=x1[rows, :]), writes=[r_x[b]])
        sc.op("dve", lambda e, b=b: e.tensor_tensor(out=odn, in0=odf[b][:, 0, :, :], in1=odf[b][:, 1, :, :], op=ALU.add),
              reads=[r_in[b]], writes=[r_odn])
        sc.op("dve", lambda e, b=b: e.tensor_tensor(out=odn, in0=odn, in1=odf[b][:, 2, :, :], op=ALU.add),
              reads=[r_in[b], r_odn], writes=[r_odn])
        sc.op("dve", lambda e: e.reciprocal(out=smallA[:, 0:4].unsqueeze(2), in_=odn[:, :, 64:65]), reads=[r_odn], writes=[r_smA])
        sc.op("dve", lambda e: e.tensor_tensor(out=r3(odb, 4), in0=odn[:, :, 0:64],
                                               in1=smallA[:, 0:4].unsqueeze(2).to_broadcast([128, 4, 64]), op=ALU.mult),
              reads=[r_odn, r_smA], writes=[r_odb])
        Tb = M.bank_bf(0)
        for k in range(4):
            sc.op("pe", lambda e, b=b, k=k: e.transpose(out=Tb[:, k * 128:(k + 1) * 128], in_=onb[b][:, k * 128:(k + 1) * 128],
                                                        identity=ident), reads=[r_in[b]], writes=[r_T])
        for k in range(2):
            sc.op("pe", lambda e, k=k: e.transpose(out=Tb[:, (4 + k) * 128:(5 + k) * 128], in_=odb[:, k * 128:(k + 1) * 128],
                                                   identity=ident), reads=[r_odb], writes=[r_T])
        sc.op("act", lambda e: e.copy(out=onT, in_=r3(Tb[:, 0:512], 4)), reads=[r_T], writes=[r_onT])
        sc.op("act", lambda e: e.copy(out=odT, in_=r3(Tb[:, 512:768], 2)), reads=[r_T], writes=[r_odT])
        for dh in range(2):
            Y1 = M.bank(1 + dh)
            for k in range(4):
                sc.op("pe", lambda e, Y1=Y1, k=k, dh=dh: e.matmul(Y1, lhsT=onT[:, k, :], rhs=Wno[:, k, dh * 512:(dh + 1) * 512],
                                                                  start=(k == 0), stop=(k == 3)),
                      reads=[r_onT, r_W], writes=[r_Y1])
            Y2 = M.bank(3 + dh)
            for k in range(2):
                sc.op("pe", lambda e, Y2=Y2, k=k, dh=dh: e.matmul(Y2, lhsT=odT[:, k, :], rhs=Wdo[:, k, dh * 512:(dh + 1) * 512],
                                                                  start=(k == 0), stop=(k == 1)),
                      reads=[r_odT, r_W], writes=[r_Y2])
            sl = slice(dh * 512, (dh + 1) * 512)
            sc.op("dve", lambda e, Y1=Y1, b=b, sl=sl: e.tensor_tensor(out=ya[:, sl], in0=Y1, in1=mg[b][:, sl], op=ALU.mult),
                  reads=[r_Y1, r_in[b]], writes=[r_ya])
            sl2 = slice(1024 + dh * 512, 1024 + (dh + 1) * 512)
            sc.op("dve", lambda e, Y2=Y2, b=b, sl=sl, sl2=sl2: e.tensor_tensor(out=yb[:, sl], in0=Y2, in1=mg[b][:, sl2], op=ALU.mult),
                  reads=[r_Y2, r_in[b]], writes=[r_yb])
        sc.op("pool", lambda e: e.tensor_tensor(out=ybf2[b], in0=ya, in1=yb, op=ALU.add), reads=[r_ya, r_yb], writes=[r_ybf2[b]])

    def stageB(t):
        b = t % 2
        rows = slice(t * 128, (t + 1) * 128)
        Tb2 = M.bank_bf(7)
        for k in range(8):
            sc.op("pe", lambda e, k=k: e.transpose(out=Tb2[:, k * 128:(k + 1) * 128], in_=ybf2[b][:, k * 128:(k + 1) * 128],
                                                   identity=ident), reads=[r_ybf2[b]], writes=[r_Y3])
        sc.op("act", lambda e: e.copy(out=yT, in_=r3(Tb2, 8)), reads=[r_Y3], writes=[r_yT])
        for dh in range(2):
            Y = M.bank(5 + dh)
            for k in range(8):
                sc.op("pe", lambda e, Y=Y, k=k, dh=dh: e.matmul(Y, lhsT=yT[:, k, :], rhs=Wmo[:, k, dh * 512:(dh + 1) * 512],
                                                                start=(k == 0), stop=(k == 7)),
                      reads=[r_yT, r_W], writes=[r_T if False else r_Y1 if False else r_yt_ps])
        ssa, ssb = small[:, 4:5], small[:, 5:6]
        sc.op("pool", lambda e: e.memset(small[:, 4:6], 0.0), writes=[r_sm])
        sc.op("act", lambda e: e.activation(out=yt[:, 0:512], in_=M.bank(5), func=AF.Square, accum_out=ssa),
              reads=[r_yt_ps], writes=[r_yt, r_sm])
        sc.op("act", lambda e: e.activation(out=yt[:, 512:1024], in_=M.bank(6), func=AF.Square, accum_out=ssb),
              reads=[r_yt_ps], writes=[r_yt, r_sm])
        sc.op("dve", lambda e: e.tensor_tensor(out=ssa, in0=ssa, in1=ssb, op=ALU.add), reads=[r_sm], writes=[r_sm])
        sc.op("dve", lambda e: e.tensor_scalar(out=ssa, in0=ssa, scalar1=1.0 / D, scalar2=EPS, op0=ALU.mult, op1=ALU.add),
              reads=[r_sm], writes=[r_sm])
        sc.op("act", lambda e: e.sqrt(ssa, ssa), reads=[r_sm], writes=[r_sm])
        sc.op("dve", lambda e: e.reciprocal(out=ssa, in_=ssa), reads=[r_sm], writes=[r_sm])
        for dh in range(2):
            sc.op("dve", lambda e, dh=dh: e.scalar_tensor_tensor(
                out=yt[:, dh * 512:(dh + 1) * 512], in0=M.bank(5 + dh), scalar=ssa,
                in1=gq[:, dh * 512:(dh + 1) * 512], op0=ALU.mult, op1=ALU.mult),
                reads=[r_yt_ps, r_sm, r_gq], writes=[r_yt])
        sc.op("dve", lambda e, b=b: e.tensor_tensor(out=xr[b], in0=yt, in1=xr[b], op=ALU.add),
              reads=[r_yt, r_x[b]], writes=[r_x[b]])
        sc.dma(lambda e, b=b, rows=rows: e.dma_start(out=x2[rows, :], in_=xr[b]), reads=[r_x[b]], q="act")

    stageA(0)
    for t in range(32):
        if t + 1 < 32:
            stageA(t + 1)
        stageB(t)
    sc.barrier()


r_yt_ps = Res()


IN_SHAPES = (
    ("ffn1_pre", [1, D]), ("ffn1_post", [1, D]), ("ffn1_w_gate", [D, DFF]), ("ffn1_w_up", [D, DFF]),
    ("ffn1_w_down", [DFF, D]), ("mix_pre", [1, D]), ("mix_post", [1, D]), ("w_in", [D, 5656]),
    ("nsa_pe_k", [32, 64]), ("nsa_w_ck1", [2048, 256]), ("nsa_w_ck2", [256, 64]),
    ("nsa_pe_v", [32, 64]), ("nsa_w_cv1", [2048, 256]), ("nsa_w_cv2", [256, 64]),
    ("w_nsa_o", [512, D]), ("w_dil_o", [256, D]), ("w_mix_out", [D, D]),
    ("ffn2_pre", [1, D]), ("ffn2_post", [1, D]), ("ffn2_w_gate", [D, DFF]),
    ("ffn2_w_up", [D, DFF]), ("ffn2_w_down", [DFF, D]))


def _consts():
    import ml_dtypes
    bf = ml_dtypes.bfloat16
    c = {}
    c["c_ident"] = np.eye(128, dtype=np.float32).astype(bf)
    c["c_identf"] = np.eye(128, dtype=np.float32)
    n_cmp = 255
    cs = np.arange(256) * 16
    ss = np.arange(64) * 64
    ov = np.clip(np.minimum(cs[:, None] + 32, ss[None, :] + 64) - np.maximum(cs[:, None], ss[None, :]), 0, None) / 32.0
    ov[255] = 0
    c["c_ov"] = ov.astype(np.float32).astype(bf)
    qi = np.arange(512, dtype=np.float64)
    c["c_tq"] = np.stack([-s_ * qi for s_ in NSA_SL]).astype(np.float32)
    c["c_td"] = np.stack([-DIL_SL[h] * DILS[h // 4] * qi for h in range(12)]).astype(np.float32)
    ki = np.arange(128)[:, None]
    qq = np.arange(128)[None, :]
    mk = np.zeros((3, 128, 128), np.float32)
    mk[0] = np.where(qq >= ki, 0.0, NEG)
    mk[1] = np.where(qq < ki, 0.0, NEG)
    mk[2] = np.where(qq <= ki, 0.0, NEG)
    c["c_masks"] = mk
    tqa = np.zeros((8, 128, 512), np.float64)
    for h in range(8):
        tqa[h] = -NSA_SL[h] * qi[None, :]
        tqa[h][:, 0:128] += mk[0]
    c["c_tqa"] = tqa.astype(np.float32)
    mo = np.zeros((128, 512), np.float32)
    mo[:, 0:128] = mk[0]
    c["c_masko"] = mo
    tq8 = np.zeros((128, 8, 4), np.float64)
    for h in range(8):
        for sb in range(4):
            tq8[:, h, sb] = -8.0 * NSA_SL[h] * (128 * sb + np.arange(128))
    c["c_tq8"] = tq8.astype(np.float32)
    tdm = np.zeros((12, 128, 384), np.float64)
    for h in range(12):
        sl = DIL_SL[h] * DILS[h // 4]
        tdm[h][:, 0:256] = -sl * qi[None, 0:256]
        tdm[h][:, 0:128] += mk[0]
        tdm[h][:, 128:256] += mk[2]
        tdm[h][:, 256:384] = -sl * qi[None, 0:128] + mk[2]
    c["c_tdm"] = tdm.astype(np.float32)
    tds = np.zeros((12, 128, 1152), np.float64)
    kcol = np.arange(128, dtype=np.float64)[:, None]
    for h in range(12):
        sl = DIL_SL[h] * DILS[h // 4]
        t2a = -sl * (qi[None, 0:256] - kcol)
        t2a[:, 0:128] += mk[0]
        t2a[:, 128:256] += mk[2]
        t2b = -sl * (128.0 + qi[None, 0:128] - kcol) + mk[2]
        tds[h][:, 0:128] = t2b
        for r_ in range(4):
            tds[h][:, 128 + 256 * r_:128 + 256 * (r_ + 1)] = t2a
    c["c_tds"] = tds.astype(np.float32)
    kk = np.arange(128, dtype=np.float64)
    bc = np.zeros((128, 8, 35), np.float64)
    for h in range(8):
        for idx in range(35):
            bc[:, h, idx] = NSA_SL[h] * (kk - 128 * (idx - 3))
    c["c_bc"] = bc.astype(np.float32)
    bcc = np.zeros((128, 8, 8, 2), np.float64)
    for h in range(8):
        for qc in range(8):
            for ct in range(2):
                bcc[:, h, qc, ct] = NSA_SL[h] * (16 * (128 * ct + kk) + 31 - 512 * qc)
    c["c_bcc"] = bcc.astype(np.float32)
    bcd = np.zeros((128, 12, 5), np.float64)
    for h in range(12):
        for j in range(5):
            bcd[:, h, j] = DIL_SL[h] * DILS[h // 4] * (kk - 128 * (1 - j))
    c["c_bcd"] = bcd.astype(np.float32)
    mc = np.zeros((8, 128, 2, 512), np.float32)
    for qc in range(8):
        for ct in range(2):
            cc = 128 * ct + np.arange(128)[:, None]
            qpos = 512 * qc + np.arange(512)[None, :]
            ok = (qpos >= 16 * cc + 31) & (cc < 255)
            mc[qc, :, ct, :] = np.where(ok, 0.0, NEG)
    c["c_mc"] = mc
    oh = np.zeros((64, 32, 128), np.float32)
    for kt in range(32):
        oh[2 * kt, kt, 0:64] = 1
        oh[2 * kt + 1, kt, 64:128] = 1
    c["c_oh"] = oh.astype(bf)
    pos = np.arange(S)[:, None]
    jb = np.arange(64)[None, :]
    own = pos // 64
    forced = (jb == 0) | (jb == own) | (jb == own - 1)
    valid = jb * 64 <= pos
    m1 = np.where(forced, 0.0, np.where(valid, 1.0, 0.0))
    m2 = np.where(forced, 1.0e9 + 1.0e4 * jb, np.where(valid, 0.0, -1.0e9 - 1.0e4 * jb))
    c["c_selm"] = np.stack([m1, m2], axis=1).astype(np.float32)
    return c


CONST_SHAPES = (("c_ident", [128, 128], BF16), ("c_identf", [128, 128], F32), ("c_ov", [256, 64], BF16),
                ("c_tq", [8, 512], F32), ("c_td", [12, 512], F32), ("c_masks", [3, 128, 128], F32),
                ("c_bc", [128, 8, 35], F32), ("c_bcc", [128, 8, 8, 2], F32), ("c_bcd", [128, 12, 5], F32),
                ("c_mc", [8, 128, 2, 512], F32), ("c_oh", [64, 32, 128], BF16), ("c_selm", [S, 2, 64], F32),
                ("c_tqa", [8, 128, 512], F32), ("c_tdm", [12, 128, 384], F32),
                ("c_masko", [128, 512], F32), ("c_tq8", [128, 8, 4], F32),
                ("c_tds", [12, 128, 1152], F32))


def build_nc():
    nc = bass.Bass("TRN2", target_bir_lowering=False)
    dt = lambda name, shape, dtype=F32, kind="ExternalInput": nc.dram_tensor(name, list(shape), dtype, kind=kind).ap()
    x = dt("x", [S, D])
    out = dt("out", [S, D], kind="ExternalOutput")
    P = {}
    for nm, shp in IN_SHAPES:
        P[nm] = dt(nm, shp)
    for nm, shp, ty in CONST_SHAPES:
        P[nm] = dt(nm, shp, ty)
    I = "Internal"
    if DBG_OUT:
        I = "ExternalOutput"
    x1 = dt("x1", [S, D], kind=I)
    x2 = dt("x2", [S, D], kind=I)
    scr = {
        "QN": dt("s_qn", [4, 128, S], BF16, I), "KC": dt("s_kc", [2, 128, S], F32, I),
        "KS": dt("s_ks", [2, 128, S], BF16, I), "KW": dt("s_kw", [2, 128, S], BF16, I),
        "VS": dt("s_vs", [S, 130], BF16, I), "VW": dt("s_vw", [S, 130], BF16, I),
        "GN": dt("s_gn", [S, 24], F32, I), "QD": dt("s_qd", [6, 128, S], BF16, I),
        "KD": dt("s_kd", [6, 128, S], BF16, I), "VD": dt("s_vd", [3, S, 260], BF16, I),
        "MG": dt("s_mg", [S, 2048], BF16, I), "ON": dt("s_on", [S, 512], BF16, I),
        "OD": dt("s_od", [3, S, 260], F32, I),
    }

    import contextlib
    with contextlib.ExitStack() as st:
        big = st.enter_context(nc.sbuf_tensor("big", [128, SBUF_BYTES // 4], F32))
        ps = st.enter_context(nc.psum_tensor("ps", [128, 4096], F32))
        M = Mem(big, ps)
        sc = Sched()
        C = {"r_dram": {}}
        ident = M.bf(128)
        identf = M.f32(128)
        C["kcT"] = r3(M.bf(2 * 256), 2)
        C["VC"] = M.bf(2 * 2 * 128)
        C["r_kcT"], C["r_VC"] = Res(), Res()
        r_ident = Res()
        sc.dma(lambda e: e.dma_start(out=ident, in_=P["c_ident"]), writes=[r_ident])
        sc.dma(lambda e: e.dma_start(out=identf, in_=P["c_identf"]), writes=[r_ident])
        C["ident"] = ident
        C["identf"] = identf
        zeros = M.bf(512)
        r_z = Res()
        sc.op("pool", lambda e: e.memset(zeros, 0.0), writes=[r_z])
        C["zeros"] = zeros
        ZEROS[0] = zeros
        C["const_end"] = M.off
        sc.barrier()
        if STAGE == 1:
            ffn_phase(sc, M, C, x, out, P["ffn1_w_gate"], P["ffn1_w_up"], P["ffn1_w_down"], P["ffn1_pre"], P["ffn1_post"])
        else:
            if DBG_SKIP_FFN1:
                x1 = x
            else:
                ffn_phase(sc, M, C, x, x1, P["ffn1_w_gate"], P["ffn1_w_up"], P["ffn1_w_down"], P["ffn1_pre"], P["ffn1_post"])
            if DBG_UPTO >= 1:
                proj_phase(sc, M, C, x1, P["w_in"], P["mix_pre"], scr)
            if DBG_UPTO >= 2:
                cmp_phase(sc, M, C, scr, P)
            if DBG_UPTO >= 3:
                nsa_phase(sc, M, C, scr, P)
            if DBG_UPTO >= 4:
                dil_phase(sc, M, C, scr, P)
            if DBG_UPTO >= 5:
                final_phase(sc, M, C, scr, P, x1, out if STAGE == 2 else x2, P["mix_post"])
            if STAGE >= 3:
                ffn_phase(sc, M, C, x2, out, P["ffn2_w_gate"], P["ffn2_w_up"], P["ffn2_w_down"], P["ffn2_pre"], P["ffn2_post"])
        sc.barrier()
        sc.emit(nc)
    return nc


DBG_SKIP_FFN1 = False
DBG_UPTO = 9
DBG_OUT = False
DBG_RES = None
_NC = None


def kernel(**inputs):
    global _NC
    if _NC is None:
        _NC = build_nc()
    nc = _NC
    x = np.ascontiguousarray(inputs["x"], dtype=np.float32)
    consts = _consts()
    shared = {}
    for nm, shp in IN_SHAPES:
        shared[nm] = np.ascontiguousarray(np.asarray(inputs[nm], dtype=np.float32)[0])
    in_maps = []
    for b in range(8):
        m = {"x": x[b]}
        m.update(shared)
        m.update(consts)
        in_maps.append(m)
    res = run_bass_kernel_spmd(nc, in_maps, core_ids=list(range(8)))
    if DBG_OUT:
        global DBG_RES
        DBG_RES = res.results[0]
    return np.stack([np.asarray(r["out"]) for r in res.results], axis=0).astype(np.float32)
```
